# Optimizing a Trainium2 kernel written in Bass

```python
import math
import jax, jax.numpy as jnp
from jax import lax
import numpy as np

D_MODEL = 1024
BATCH = 2
SEQ = 16384
DEPTH = 2

N_MEM = 256
D_MIX = 2 * D_MODEL
GROUP = D_MIX // 4
A_HEADS = 4
A_DK = GROUP // A_HEADS
A_DV = GROUP // A_HEADS
B_HEADS = 4
B_DQK = GROUP // B_HEADS // 2
B_DV = GROUP // B_HEADS
C_HEADS = 4
C_D = GROUP // C_HEADS
X_HEADS = 4
X_D = GROUP // X_HEADS

ROPE_THETA = 500000.0
ROPE_DIM = B_DQK // 4
CHUNK = 64
Q_BLOCK = 128
CONV_K = 4
EPS = 1e-6
NEG = -1e30
TINY = 1e-30

IN_SIZES = (GROUP, GROUP, GROUP, GROUP,
            GROUP, GROUP, GROUP, GROUP,
            GROUP, GROUP, GROUP, GROUP, C_HEADS, C_HEADS, GROUP,
            GROUP, GROUP)
N_IN = sum(IN_SIZES)

kernel_name = 'hymba_hgrn2_diffattn_mlstm_memxattn'


def rms_norm(x, g):
    xf = x.astype(jnp.float32)
    y = xf * lax.rsqrt(jnp.mean(xf * xf, axis=-1, keepdims=True) + EPS)
    return (y * g.astype(jnp.float32)).astype(x.dtype)


def split_cols(p, sizes):
    outs, off = [], 0
    for s in sizes:
        outs.append(p[..., off:off + s])
        off += s
    return outs


def partial_rope(x, cos, sin):
    half = ROPE_DIM // 2
    x1 = x[..., :half].astype(jnp.float32)
    x2 = x[..., half:ROPE_DIM].astype(jnp.float32)
    rot = jnp.concatenate([x1 * cos - x2 * sin, x2 * cos + x1 * sin], axis=-1).astype(x.dtype)
    return jnp.concatenate([rot, x[..., ROPE_DIM:]], axis=-1)


def causal_dwconv(x, w, b):
    y = lax.conv_general_dilated(x, w[:, None, :].astype(x.dtype), window_strides=(1,),
                                 padding=((CONV_K - 1, 0),),
                                 dimension_numbers=('NWC', 'WIO', 'NWC'),
                                 feature_group_count=x.shape[-1])
    return y + b.astype(x.dtype)


def to_chunks(t):
    bsz, s = t.shape[:2]
    t = t.reshape((bsz, s // CHUNK, CHUNK) + t.shape[2:])
    return jnp.transpose(t, (1, 0, 3, 2) + tuple(range(4, t.ndim)))


def from_chunks(o):
    nc, bsz, h, c, d = o.shape
    return jnp.transpose(o, (1, 0, 3, 2, 4)).reshape(bsz, nc * c, h, d)


def hgrn2_scan(q, k, v, log_f):
    out_dtype = v.dtype
    qc, kc, vc, gc = [to_chunks(t.astype(jnp.float32)) for t in (q, k, v, log_f)]
    bsz, h, dk, dv = q.shape[0], q.shape[2], q.shape[3], v.shape[3]
    causal = jnp.tril(jnp.ones((CHUNK, CHUNK), dtype=bool))[:, :, None]

    def step(state, inp):
        qi, ki, vi, gi = inp
        b = jnp.cumsum(gi, axis=2)
        rel = b[:, :, :, None, :] - b[:, :, None, :, :]
        decay = jnp.where(causal, jnp.exp(jnp.where(causal, rel, 0.0)), 0.0)
        attn = jnp.einsum('bhtk,bhsk,bhtsk->bhts', qi, ki, decay)
        o = (jnp.einsum('bhts,bhsv->bhtv', attn, vi)
             + jnp.einsum('bhtk,bhkv->bhtv', qi * jnp.exp(b), state))
        b_last = b[:, :, -1]
        state = (jnp.exp(b_last)[..., None] * state
                 + jnp.einsum('bhsk,bhsv->bhkv', ki * jnp.exp(b_last[:, :, None] - b), vi))
        return state, o

    s0 = jnp.zeros((bsz, h, dk, dv), jnp.float32)
    _, o = lax.scan(step, s0, (qc, kc, vc, gc))
    return from_chunks(o).astype(out_dtype)


def mlstm_scan(q, k, v, log_i, log_f):
    out_dtype = v.dtype
    qc, kc, vc = [to_chunks(t.astype(jnp.float32)) for t in (q, k, v)]
    ic, fc = [to_chunks(t.astype(jnp.float32)) for t in (log_i, log_f)]
    bsz, h, d = q.shape[0], q.shape[2], q.shape[3]
    causal = jnp.tril(jnp.ones((CHUNK, CHUNK), dtype=bool))

    def step(carry, inp):
        cmat, n, m = carry
        qi, ki, vi, li, lf = inp
        b = jnp.cumsum(lf, axis=-1)
        dmat = jnp.where(causal, b[..., :, None] - b[..., None, :] + li[..., None, :], NEG)
        inter = b + m[..., None]
        m_t = jnp.maximum(jnp.max(dmat, axis=-1), inter)
        w = jnp.where(causal, jnp.exp(dmat - m_t[..., None]), 0.0)
        w_inter = jnp.exp(inter - m_t)
        s = jnp.einsum('bhtd,bhsd->bhts', qi, ki) * w
        num = (jnp.einsum('bhts,bhsv->bhtv', s, vi)
               + w_inter[..., None] * jnp.einsum('bhtk,bhvk->bhtv', qi, cmat))
        nq = jnp.sum(s, axis=-1) + w_inter * jnp.einsum('bhtk,bhk->bht', qi, n)
        hout = num / jnp.maximum(jnp.abs(nq), jnp.exp(-m_t))[..., None]
        b_last = b[..., -1]
        g = b_last[..., None] - b + li
        m_new = jnp.maximum(b_last + m, jnp.max(g, axis=-1))
        wg = jnp.exp(g - m_new[..., None])
        dec = jnp.exp(b_last + m - m_new)
        cmat = dec[..., None, None] * cmat + jnp.einsum('bhs,bhsv,bhsk->bhvk', wg, vi, ki)
        n = dec[..., None] * n + jnp.einsum('bhs,bhsk->bhk', wg, ki)
        return (cmat, n, m_new), hout

    init = (jnp.zeros((bsz, h, d, d), jnp.float32), jnp.zeros((bsz, h, d), jnp.float32),
            jnp.zeros((bsz, h), jnp.float32))
    _, o = lax.scan(step, init, (qc, kc, vc, ic, fc))
    return from_chunks(o).astype(out_dtype)


def diff_attention(q, k, v, lam):
    bsz, s, h = q.shape[:3]
    nb = s // Q_BLOCK
    qb = q.reshape(bsz, nb, Q_BLOCK, h, 2, q.shape[-1]).transpose(1, 0, 2, 3, 4, 5)
    kpos = jnp.arange(s)

    def block(args):
        qi, j = args
        sc = jnp.einsum('bqhcd,bkhcd->bhcqk', qi, k).astype(jnp.float32)
        qpos = j * Q_BLOCK + jnp.arange(Q_BLOCK)
        sc = jnp.where(qpos[:, None] >= kpos[None, :], sc, NEG)
        p = jax.nn.softmax(sc, axis=-1)
        a = (p[:, :, 0] - lam * p[:, :, 1]).astype(v.dtype)
        return jnp.einsum('bhqk,bkhd->bqhd', a, v)

    o = lax.map(block, (qb, jnp.arange(nb)))
    return o.transpose(1, 0, 2, 3, 4).reshape(bsz, s, h, v.shape[-1])


def hybrid_layer(x, mem, cos, sin, layer_idx, lb, norm_g, w_in, mlstm_gate_b, hgrn_norm_g,
                 diff_qk_norm_g, diff_lambda, diff_subln_g, mlstm_conv_w, mlstm_conv_b,
                 mlstm_norm_g, mem_norm_g, w_mem_kv, xattn_qk_norm_g, w_out):
    bsz, s, _ = x.shape
    h = rms_norm(x, norm_g)
    proj = h @ w_in.astype(h.dtype)
    (a_q, a_f, a_i, a_g, b_q, b_k, b_v, b_g,
     c_q, c_k, c_v, c_o, c_i, c_f, c_g, x_q, x_g) = split_cols(proj, IN_SIZES)

    qa = jax.nn.silu(a_q).reshape(bsz, s, A_HEADS, A_DK) * (A_DK ** -0.5)
    fa = a_f.astype(jnp.float32).reshape(bsz, s, A_HEADS, A_DK)
    lbh = lb.astype(jnp.float32).reshape(A_HEADS, A_DK)
    f_gate = lbh + (1.0 - lbh) * jax.nn.sigmoid(fa)
    log_f = jnp.log(jnp.maximum(f_gate, TINY))
    ka = (1.0 - lbh) * jax.nn.sigmoid(-fa)
    oa = hgrn2_scan(qa, ka, a_i.reshape(bsz, s, A_HEADS, A_DV), log_f)
    ya = rms_norm(oa, hgrn_norm_g).reshape(bsz, s, GROUP) * jax.nn.silu(a_g)

    qb = rms_norm(b_q.reshape(bsz, s, B_HEADS, 2, B_DQK), diff_qk_norm_g[0])
    kb = rms_norm(b_k.reshape(bsz, s, B_HEADS, 2, B_DQK), diff_qk_norm_g[1])
    qb = partial_rope(qb, cos, sin) * (B_DQK ** -0.5)
    kb = partial_rope(kb, cos, sin)
    lam_init = 0.8 - 0.6 * math.exp(-0.3 * layer_idx)
    lp = diff_lambda.astype(jnp.float32)
    lam = jnp.exp(jnp.sum(lp[0] * lp[1])) - jnp.exp(jnp.sum(lp[2] * lp[3])) + lam_init
    ob = diff_attention(qb, kb, b_v.reshape(bsz, s, B_HEADS, B_DV), lam)
    yb = (rms_norm(ob, diff_subln_g) * (1.0 - lam_init)).reshape(bsz, s, GROUP) * jax.nn.silu(b_g)

    qk = jax.nn.silu(causal_dwconv(jnp.concatenate([c_q, c_k], axis=-1), mlstm_conv_w, mlstm_conv_b))
    qc = qk[..., :GROUP].reshape(bsz, s, C_HEADS, C_D)
    kc = qk[..., GROUP:].reshape(bsz, s, C_HEADS, C_D) * (C_D ** -0.5)
    vc = c_v.reshape(bsz, s, C_HEADS, C_D)
    gb = mlstm_gate_b.astype(jnp.float32)
    log_i = c_i.astype(jnp.float32) + gb[:C_HEADS]
    log_fc = jax.nn.log_sigmoid(c_f.astype(jnp.float32) + gb[C_HEADS:])
    hc = mlstm_scan(qc, kc, vc, log_i, log_fc)
    hc = jax.nn.sigmoid(c_o).reshape(bsz, s, C_HEADS, C_D) * hc
    yc = rms_norm(hc, mlstm_norm_g).reshape(bsz, s, GROUP) * jax.nn.silu(c_g)

    mn = rms_norm(mem, mem_norm_g)
    kv = mn @ w_mem_kv.astype(mn.dtype)
    km = rms_norm(kv[..., :GROUP].reshape(bsz, N_MEM, X_HEADS, X_D), xattn_qk_norm_g[1])
    vm = kv[..., GROUP:].reshape(bsz, N_MEM, X_HEADS, X_D)
    qx = rms_norm(x_q.reshape(bsz, s, X_HEADS, X_D), xattn_qk_norm_g[0]) * (X_D ** -0.5)
    px = jax.nn.softmax(jnp.einsum('bqhd,bmhd->bhqm', qx, km).astype(jnp.float32), axis=-1)
    ox = jnp.einsum('bhqm,bmhd->bqhd', px.astype(vm.dtype), vm).reshape(bsz, s, GROUP)
    yx = ox * jax.nn.silu(x_g)

    y = jnp.concatenate([ya, yb, yc, yx], axis=-1).astype(x.dtype) @ w_out.astype(x.dtype)
    return x + y


def setup_inputs(seed: int = 0) -> dict:
    key = jax.random.key(seed)
    ks = jax.random.split(key, 20)
    f32 = jnp.float32
    nrm = lambda k, shape: jax.random.normal(k, shape, f32)
    x = nrm(ks[0], (BATCH, SEQ, D_MODEL))
    mem = nrm(ks[1], (BATCH, N_MEM, D_MODEL))
    offs = jax.random.randint(ks[2], (BATCH, 1), 0, 4096, dtype=jnp.int32)
    positions = offs + jnp.arange(SEQ, dtype=jnp.int32)[None, :]
    norm_g = 1.0 + 0.02 * nrm(ks[3], (DEPTH, D_MODEL))
    w_in = nrm(ks[4], (DEPTH, D_MODEL, N_IN)) * (D_MODEL ** -0.5)
    mlstm_gate_b = jnp.concatenate([
        0.1 * nrm(ks[5], (DEPTH, C_HEADS)),
        jnp.linspace(3.0, 6.0, C_HEADS, dtype=f32)[None, :] + 0.1 * nrm(ks[6], (DEPTH, C_HEADS))], axis=-1)
    hgrn_lb_logits = nrm(ks[7], (DEPTH, GROUP))
    hgrn_norm_g = 1.0 + 0.02 * nrm(ks[8], (DEPTH, A_DV))
    diff_qk_norm_g = 1.0 + 0.02 * nrm(ks[9], (DEPTH, 2, B_DQK))
    diff_lambda = 0.1 * nrm(ks[10], (DEPTH, 4, B_DQK))
    diff_subln_g = 1.0 + 0.02 * nrm(ks[11], (DEPTH, B_DV))
    mlstm_conv_w = nrm(ks[12], (DEPTH, CONV_K, 2 * GROUP)) * (CONV_K ** -0.5)
    mlstm_conv_b = 0.01 * nrm(ks[13], (DEPTH, 2 * GROUP))
    mlstm_norm_g = 1.0 + 0.02 * nrm(ks[14], (DEPTH, C_D))
    mem_norm_g = 1.0 + 0.02 * nrm(ks[15], (DEPTH, D_MODEL))
    w_mem_kv = nrm(ks[16], (DEPTH, D_MODEL, 2 * GROUP)) * (D_MODEL ** -0.5)
    xattn_qk_norm_g = 1.0 + 0.02 * nrm(ks[17], (DEPTH, 2, X_D))
    w_out = nrm(ks[18], (DEPTH, D_MIX, D_MODEL)) * (D_MIX ** -0.5)
    return {'x': x, 'mem': mem, 'positions': positions, 'norm_g': norm_g, 'w_in': w_in,
            'mlstm_gate_b': mlstm_gate_b, 'hgrn_lb_logits': hgrn_lb_logits,
            'hgrn_norm_g': hgrn_norm_g, 'diff_qk_norm_g': diff_qk_norm_g,
            'diff_lambda': diff_lambda, 'diff_subln_g': diff_subln_g,
            'mlstm_conv_w': mlstm_conv_w, 'mlstm_conv_b': mlstm_conv_b,
            'mlstm_norm_g': mlstm_norm_g, 'mem_norm_g': mem_norm_g, 'w_mem_kv': w_mem_kv,
            'xattn_qk_norm_g': xattn_qk_norm_g, 'w_out': w_out}


def reference(x, mem, positions, norm_g, w_in, mlstm_gate_b, hgrn_lb_logits, hgrn_norm_g,
              diff_qk_norm_g, diff_lambda, diff_subln_g, mlstm_conv_w, mlstm_conv_b,
              mlstm_norm_g, mem_norm_g, w_mem_kv, xattn_qk_norm_g, w_out):
    inv_freq = ROPE_THETA ** (-jnp.arange(0, ROPE_DIM, 2, dtype=jnp.float32) / ROPE_DIM)
    ang = positions.astype(jnp.float32)[..., None] * inv_freq
    cos = jnp.cos(ang)[:, :, None, None, :]
    sin = jnp.sin(ang)[:, :, None, None, :]
    sm = jax.nn.softmax(hgrn_lb_logits.astype(jnp.float32), axis=0)
    lower_bounds = jnp.cumsum(sm, axis=0) - sm[0]
    for l in range(DEPTH):
        x = hybrid_layer(x, mem, cos, sin, l, lower_bounds[l], norm_g[l], w_in[l],
                         mlstm_gate_b[l], hgrn_norm_g[l], diff_qk_norm_g[l], diff_lambda[l],
                         diff_subln_g[l], mlstm_conv_w[l], mlstm_conv_b[l], mlstm_norm_g[l],
                         mem_norm_g[l], w_mem_kv[l], xattn_qk_norm_g[l], w_out[l])
    return x
```

```python
import math
import os
from contextlib import ExitStack

import numpy as np
import concourse.bass as bass
import concourse.mybir as mybir
from concourse.bass_utils import run_bass_kernel_spmd

F32 = mybir.dt.float32
BF16 = mybir.dt.bfloat16
I32 = mybir.dt.int32
AF = mybir.ActivationFunctionType
ALU = mybir.AluOpType

D_MODEL = 1024
BATCH = 2
SEQ = 16384
DEPTH = 2
N_MEM = 256
GROUP = 512
EPS = 1e-6
ROPE_THETA = 500000.0
TB = 512
NFM = 7
NTM = 1026


class Sched:
    SAME = ('act', 'dve', 'pool')

    def __init__(self, nc, es):
        self.nc = nc
        self.engs = {'pe': nc.tensor, 'act': nc.scalar, 'dve': nc.vector, 'pool': nc.gpsimd, 'sp': nc.sync}
        self.sem = {e: es.enter_context(nc.semaphore('s_' + e)) for e in ('pe', 'act', 'dve', 'pool')}
        self.cnt = {e: 0 for e in self.sem}
        self.pending = {e: [] for e in self.sem}
        self.csem = {}
        self.ccnt = {}
        self.es = es
        self.waited = {}
        self.acc = {}
        self.ops = []
        self.nwaits = 0

    @staticmethod
    def box(ap):
        dims = ap.ap
        esz = mybir.dt.size(ap.dtype)
        off = ap.offset
        sp = str(ap.space)
        if sp == 'DRAM':
            ext = 1
            for s, c in dims:
                ext += (c - 1) * abs(s)
            return (ap.tensor.name, 0, 1, off * esz, (off + ext) * esz)
        pstep = dims[0][0]
        if pstep == 0:
            p0, f0 = 0, off
        else:
            p0, f0 = off // pstep, off % pstep
        ext = 1
        for s, c in dims[1:]:
            ext += (c - 1) * abs(s)
        if sp == 'PSUM':
            return (ap.tensor.name, (p0 // 32) * 32, ((p0 + dims[0][1] + 31) // 32) * 32, 0, 2048)
        return (ap.tensor.name, p0, p0 + dims[0][1], f0 * esz, (f0 + ext) * esz)

    def _sem_of(self, key):
        return self.sem[key] if key in self.sem else self.csem[key]

    def emit(self, eng, fn, outs=(), ins=(), sig=True, chan=None, inc=16):
        is_dma = chan is not None
        oboxes = [self.box(a) for a in outs]
        iboxes = [self.box(a) for a in ins]
        deps = set()
        for (name, p0, p1, f0, f1) in iboxes:
            for ent in self.acc.get(name, ()):
                if ent[5] and ent[0] < p1 and p0 < ent[1] and ent[2] < f1 and f0 < ent[3]:
                    deps.add(ent[4])
        for (name, p0, p1, f0, f1) in oboxes:
            for ent in self.acc.get(name, ()):
                if ent[0] < p1 and p0 < ent[1] and ent[2] < f1 and f0 < ent[3]:
                    deps.add(ent[4])
        need = {}
        for d in deps:
            deng, key, val = self.ops[d]
            if deng is not None and deng == eng and eng not in self.SAME:
                continue
            if val is None:
                raise RuntimeError('dependency on unsignalled op')
            if need.get(key, 0) < val:
                need[key] = val
        if is_dma:
            if chan not in self.csem:
                self.csem[chan] = self.es.enter_context(self.nc.semaphore('c_' + chan))
                self.ccnt[chan] = 0
            if self.ccnt[chan] > 0 and need.get(chan, 0) < self.ccnt[chan]:
                need[chan] = self.ccnt[chan]
        e = self.engs[eng]
        for key, val in need.items():
            if self.waited.get((eng, key), 0) >= val:
                continue
            e.wait_ge(self._sem_of(key), val)
            self.waited[(eng, key)] = val
            self.nwaits += 1
        inst = fn(e)
        opid = len(self.ops)
        if is_dma:
            self.ccnt[chan] += inc
            inst.then_inc(self.csem[chan], inc)
            self.ops.append([None, chan, self.ccnt[chan]])
        elif sig:
            self.cnt[eng] += 1
            inst.then_inc(self.sem[eng], 1)
            self.ops.append([eng, eng, self.cnt[eng]])
            for p in self.pending[eng]:
                self.ops[p][2] = self.cnt[eng]
            self.pending[eng] = []
        else:
            self.ops.append([eng, eng, None])
            self.pending[eng].append(opid)
        for (name, p0, p1, f0, f1) in oboxes:
            lst = self.acc.setdefault(name, [])
            lst[:] = [t for t in lst if not (p0 <= t[0] and t[1] <= p1 and f0 <= t[2] and t[3] <= f1)]
            lst.append((p0, p1, f0, f1, opid, True))
        rkey = None if is_dma else eng
        for (name, p0, p1, f0, f1) in iboxes:
            lst = self.acc.setdefault(name, [])
            for i, t in enumerate(lst):
                if (not t[5]) and t[0] == p0 and t[1] == p1 and t[2] == f0 and t[3] == f1 \
                        and self.ops[t[4]][0] == rkey and rkey is not None:
                    lst[i] = (p0, p1, f0, f1, opid, False)
                    break
            else:
                lst.append((p0, p1, f0, f1, opid, False))
        return inst

    def barrier(self):
        keys = [(key, self.cnt[key]) for key in self.sem] + [(key, self.ccnt[key]) for key in self.csem]
        for eng, e in self.engs.items():
            for key, val in keys:
                if val > 0 and self.waited.get((eng, key), 0) < val:
                    e.wait_ge(self._sem_of(key), val)
                    self.waited[(eng, key)] = val

    def finish(self):
        sp = self.engs['sp']
        for key in list(self.csem):
            if self.ccnt[key] > 0:
                sp.wait_ge(self.csem[key], self.ccnt[key])
        for key in self.sem:
            if self.cnt[key] > 0:
                sp.wait_ge(self.sem[key], self.cnt[key])


class _Stop(Exception):
    pass


def _lvl(n):
    if n > float(os.environ.get('KLVL', '99')):
        raise _Stop()


class K:
    def __init__(self, nc, es):
        self.nc = nc
        self.es = es
        self.s = Sched(nc, es)
        self.n = 0

    def sb(self, name, shape, dt):
        return self.es.enter_context(self.nc.sbuf_tensor(name, shape, dt))

    def ps(self, name, shape, dt=F32):
        return self.es.enter_context(self.nc.psum_tensor(name, shape, dt))

    def mm(self, out, lhsT, rhs, start=True, stop=True, sig=None, skip=False):
        if sig is None:
            sig = True
        return self.s.emit('pe', lambda e: e.matmul(out, lhsT, rhs, start=start, stop=stop,
                                                     skip_group_check=skip),
                           outs=[out], ins=[lhsT, rhs], sig=sig)

    def tr(self, out, in_, ident):
        return self.s.emit('pe', lambda e: e.transpose(out, in_, ident), outs=[out], ins=[in_, ident])

    def act(self, out, in_, func, bias=0.0, scale=1.0, accum_out=None):
        ins = [in_]
        outs = [out]
        if not isinstance(bias, (int, float)):
            ins.append(bias)
        if not isinstance(scale, (int, float)):
            ins.append(scale)
        if accum_out is not None:
            outs.append(accum_out)
        kw = {}
        if accum_out is not None:
            kw['accum_out'] = accum_out
        return self.s.emit('act', lambda e: e.activation(out, in_, func, bias=bias, scale=scale, **kw),
                           outs=outs, ins=ins)

    def ts(self, eng, out, in0, s1, s2=None, op0=ALU.mult, op1=None, accum_out=None):
        ins = [in0]
        outs = [out]
        for s_ in (s1, s2):
            if s_ is not None and not isinstance(s_, (int, float)):
                ins.append(s_)
        kw = {}
        if op1 is not None:
            kw['op1'] = op1
        if accum_out is not None:
            kw['accum_out'] = accum_out
            outs.append(accum_out)
        return self.s.emit(eng, lambda e: e.tensor_scalar(out, in0, s1, s2, op0, **kw), outs=outs, ins=ins)

    def tt(self, eng, out, in0, in1, op):
        return self.s.emit(eng, lambda e: e.tensor_tensor(out, in0, in1, op), outs=[out], ins=[in0, in1])

    def stt(self, out, in0, scalar, in1, op0, op1):
        ins = [in0, in1]
        if not isinstance(scalar, (int, float)):
            ins.append(scalar)
        return self.s.emit('dve', lambda e: e.scalar_tensor_tensor(out, in0, scalar, in1, op0, op1),
                           outs=[out], ins=ins)

    def cp(self, eng, out, in_):
        if eng == 'act':
            return self.s.emit('act', lambda e: e.copy(out, in_), outs=[out], ins=[in_])
        return self.s.emit(eng, lambda e: e.tensor_copy(out, in_), outs=[out], ins=[in_])

    def recip(self, out, in_):
        return self.s.emit('dve', lambda e: e.reciprocal(out, in_), outs=[out], ins=[in_])

    def memset(self, eng, out, val):
        return self.s.emit(eng, lambda e: e.memset(out, val), outs=[out], ins=[])

    def scan(self, out, d0, d1, init, op0, op1):
        return self.s.emit('dve', lambda e: e.tensor_tensor_scan(out, d0, d1, init, op0, op1),
                           outs=[out], ins=[d0, d1])

    def dma(self, out, in_, chan, eng='sp'):
        return self.s.emit(eng, lambda e: e.dma_start(out=out, in_=in_), outs=[out], ins=[in_], chan=chan)

    def cc(self, kind, op, groups, in_ap, out_ap, chan):
        return self.s.emit('pool', lambda e: e.collective_compute(kind, op, replica_groups=groups,
                                                                 ins=[in_ap.opt()], outs=[out_ap.opt()]),
                           outs=[out_ap], ins=[in_ap], chan=chan, inc=1)

    def rsqrt(self, out, in_, scale, epscol, tmp):
        self.act(tmp, in_, AF.Ln, bias=epscol, scale=scale)
        self.act(out, tmp, AF.Exp, scale=-0.5)


CM_IDENT, CM_TRI128, CM_TRI64, CM_BLK64, CM_ONES, CM_PROPE, CM_ZERO = range(7)
NCM = 7
(CC_NG, CC_MNG, CC_INVF, CC_SGN, CC_GQ, CC_GK, CC_CWQ, CC_CBQ, CC_CWK, CC_CBK, CC_GXQ, CC_GXK, CC_LBL,
 CC_IND0, CC_IND1, CC_GBI, CC_GBF, CC_EPS, CC_ONE, CC_HALFPI) = (0, 8, 16, 17, 18, 19, 20, 24, 25, 29, 30, 31,
                                                                 32, 34, 35, 36, 37, 38, 39, 40)
CC_RANK = 41
NCC = 45
CR_HG, CR_SUB, CR_MG = range(3)
TWO_PI = 2.0 * math.pi


def const_mats():
    m = np.zeros((NCM, 128, 128), np.float32)
    i = np.arange(128)
    m[CM_IDENT] = np.eye(128)
    m[CM_TRI128] = (i[None, :] >= i[:, None])
    same = (i[:, None] // 64) == (i[None, :] // 64)
    m[CM_TRI64] = m[CM_TRI128] * same
    m[CM_BLK64] = same
    m[CM_ONES] = 1.0
    for base in (0, 64):
        for j in range(8):
            m[CM_PROPE, base + 8 + j, base + j] = 1.0
            m[CM_PROPE, base + j, base + 8 + j] = 1.0
    return np.ascontiguousarray(m.transpose(1, 0, 2))


def emit_layer(nc, k, PSB, S, layer_idx, sfx, D, do_rope=True, fused=False):
    NBLK = S // TB
    NT = S // 128
    xT_d, pos_d, wfm_d, wtm_d = D['xT'], D['pos'], D['w_fm'], D['w_tm']
    memT_d, wkv_d, cmat_d, ccol_d, crow_d, lam_d, rm_d = (D['memT'], D['w_kv'], D['cmat'], D['ccol'], D['crow'],
                                                          D['lam'], D['rmask'])
    cs_d = D['cs']
    yT_d = D.get('yT')

    lam_init = 0.8 - 0.6 * math.exp(-0.3 * layer_idx)

    with ExitStack() as es:
        k.es = es
        sb = lambda name, shape, dt: k.sb(name + sfx, shape, dt)
        kT = sb("kT", [128, S], BF16)
        vA = sb("vA", [128, NT, 129], BF16)
        wfm = sb("wfm", [128, 8, NFM * 128], BF16)
        wtm = sb("wtm", [128, 8, NTM], BF16)
        cmf3 = sb("cmf3", [128, 3, 128], F32)
        cmb = sb("cmb", [128, NCM, 128], BF16)
        cc = sb("cc", [128, NCC], F32)
        crow = sb("crow_sb", [128, 3 * 128], F32)
        lamt = sb("lamt", [128, 256], F32)
        rmask = sb("rmask_sb", [128, TB], F32)
        zrow = sb("zrow", [128, TB], BF16)
        kmT = sb("kmT", [128, 256], BF16)
        vmA = sb("vmA", [128, 2, 129], BF16)
        S_h = sb("S_h", [128, 128], F32)
        Sb_h = sb("Sb_h", [128, 8, 128], BF16)
        dS = sb("dS", [128, 8, 128], F32)
        C_m = sb("C_m", [128, 129], F32)
        Cb_m = sb("Cb_m", [128, 8, 129], BF16)
        dC = sb("dC", [128, 8, 129], F32)
        sm = sb("sm", [128, 64], F32)
        stage = sb("stage", [128, NTM], F32)
        xt = sb("xt", [128, 4, TB], F32)
        xtm = xt[:].rearrange("p c t -> p (c t)").rearrange("p (c t) -> p c t", c=8)
        xb = sb("xb", [128, 8, TB], BF16)
        sqb = sb("sqb", [128, 2, TB], BF16)
        RX = sb("RX", [128, TB], F32)
        rxc = sb("rxc", [128, 8], F32)
        T = [sb(f"T{i}", [128, TB], F32) for i in range(6)]
        Bt = [sb(f"B{i}", [128, TB], BF16) for i in range(3)]
        cosT = sb("cosT", [128, TB], F32)
        sinT = sb("sinT", [128, TB], F32)
        qhT = sb("qhT", [128, TB], BF16)
        qtT = sb("qtT", [128, TB], BF16)
        ktT = sb("ktT", [128, TB], BF16)
        qcT = sb("qcT", [128, TB], BF16)
        kcT = sb("kcT", [128, TB], BF16)
        cqx = sb("cqx", [128, TB + 3], F32)
        ckx = sb("ckx", [128, TB + 3], F32)
        qxn = sb("qxn", [128, TB], BF16)
        bcum = sb("bcum", [128, TB], F32)
        av = sb("av", [128, 4, 128], BF16)
        cvA = sb("cvA", [128, 4, 129], BF16)
        gates = sb("gates", [128, 4, 5, 128], BF16)
        gcol = sb("gcol", [128, 4, 16], F32)
        kTM = sb("kTM", [128, 4, 128], BF16)
        kTM2 = sb("kTM2", [128, 4, 128], BF16)
        AT = sb("AT", [128, 2, 128], BF16)
        PT = sb("PT", [128, 4, TB], BF16)
        yTM = sb("yTM", [128, 4, 4, 128], BF16)
        yT = sb("yT_sb", [128, 4, TB], BF16)
        yTr = sb("yTr", [128, 2, 4, TB], BF16) if fused else None
        G = PSB[0:3]
        PS_S = PSB[3:5]
        PS_O = PSB[5:8]
        gi = [0]

        def gbank():
            b = G[gi[0] % 3]
            gi[0] += 1
            return b

        cmF = lambda i: cmf3[:, i - CM_TRI64, :]
        cmB = lambda i: cmb[:, i, :]
        col = lambda i, n=1: cc[:, i:i + n]
        epsc = col(CC_EPS)

        try:
            k.dma(stage[:, 0:NCM * 128], cmat_d.rearrange("p m n -> p (m n)"), "c0")
            k.dma(cmf3[:], cmat_d[:, CM_TRI64:CM_TRI64 + 3, :], "c5")
            k.dma(cc[:], ccol_d[:, :], "c1")
            k.dma(crow[:], crow_d[0:1, :].partition_broadcast(128), "c2")
            k.dma(lamt[:], lam_d[0:1, :].partition_broadcast(128), "c3")
            k.dma(rmask[:], rm_d[:, :], "c4")
            k.cp('dve', cmb[:].rearrange("p m n -> p (m n)"), stage[:, 0:NCM * 128])
            k.memset('pool', zrow[:], 0.0)
            k.memset('pool', vA[:, :, 128:129], 1.0)
            k.memset('pool', cvA[:, :, 128:129], 1.0)
            k.memset('pool', vmA[:, :, 128:129], 1.0)
            k.memset('pool', S_h[:], 0.0)
            k.memset('pool', C_m[:], 0.0)
            k.memset('pool', cqx[:, 0:3], 0.0)
            k.memset('pool', ckx[:, 0:3], 0.0)
            k.memset('pool', AT[:], 0.0)
            k.memset('dve', sm[:], 0.0)
            wv = wfm_d.rearrange("(c p) n -> p c n", p=128)
            for c in range(8):
                k.dma(stage[:, 0:NFM * 128], wv[:, c, :], "st")
                k.ts('dve', wfm[:, c, :], stage[:, 0:NFM * 128], col(CC_NG + c))
            wv = wtm_d.rearrange("(c p) n -> p c n", p=128)
            for c in range(8):
                k.dma(stage[:, 0:NTM], wv[:, c, :], "st")
                k.ts('dve', wtm[:, c, :], stage[:, 0:NTM], col(CC_NG + c))

            _lvl(2)
            LB, OMLB, NLAM, R1 = sm[:, 0:1], sm[:, 1:2], sm[:, 2:3], sm[:, 3:4]
            if layer_idx == 0:
                k.memset('dve', LB, 0.0)
            else:
                k.tt('dve', sm[:, 4:5], col(CC_LBL), col(CC_LBL + 1), ALU.subtract)
                k.act(sm[:, 5:6], sm[:, 4:5], AF.Exp)
                k.ts('dve', sm[:, 5:6], sm[:, 5:6], 1.0, None, op0=ALU.add)
                k.recip(LB, sm[:, 5:6])
            k.ts('dve', OMLB, LB, -1.0, 1.0, op0=ALU.mult, op1=ALU.add)
            k.tt('dve', T[0][:, 0:64], lamt[:, 0:64], lamt[:, 64:128], ALU.mult)
            k.tt('dve', T[0][:, 64:128], lamt[:, 128:192], lamt[:, 192:256], ALU.mult)
            k.s.emit('dve', lambda e: e.reduce_sum(sm[:, 6:7], T[0][:, 0:64], mybir.AxisListType.X),
                     outs=[sm[:, 6:7]], ins=[T[0][:, 0:64]])
            k.s.emit('dve', lambda e: e.reduce_sum(sm[:, 7:8], T[0][:, 64:128], mybir.AxisListType.X),
                     outs=[sm[:, 7:8]], ins=[T[0][:, 64:128]])
            k.act(sm[:, 8:10], sm[:, 6:8], AF.Exp)
            k.tt('dve', sm[:, 10:11], sm[:, 9:10], sm[:, 8:9], ALU.subtract)
            k.ts('dve', NLAM, sm[:, 10:11], -lam_init, None, op0=ALU.add)

            _lvl(3)
            mv = memT_d.rearrange("(c p) n -> p c n", p=128)
            k.dma(xtm, mv[:, :, :], "x0")
            wk = wkv_d.rearrange("(c p) n -> p c n", p=128)
            mw = xb[:, :, 256:512]
            for c in range(8):
                k.dma(stage[:, 0:256], wk[:, c, :], "st")
                k.ts('dve', mw[:, c, :], stage[:, 0:256], col(CC_MNG + c))
                k.cp('pool', xb[:, c, 0:256], xtm[:, c, :])
            for mt in range(2):
                pc = gbank()
                for c in range(8):
                    k.act(sqb[:, c % 2, 0:128], xtm[:, c, mt * 128:(mt + 1) * 128], AF.Square)
                    k.mm(pc[:, 0:1], sqb[:, c % 2, 0:128], cmB(CM_ONES)[:, 0:1], start=(c == 0), stop=(c == 7))
                k.rsqrt(gcol[:, 0, 0:1], pc[:, 0:1], 1.0 / D_MODEL, epsc, gcol[:, 0, 1:2])
                pk = gbank()
                for c in range(8):
                    k.mm(pk[:, 0:256], xb[:, c, mt * 128:(mt + 1) * 128], mw[:, c, :], start=(c == 0), stop=(c == 7))
                k.ts('dve', T[0][:, 0:128], pk[:, 0:128], gcol[:, 0, 0:1])
                k.ts('dve', vmA[:, mt, 0:128], pk[:, 128:256], gcol[:, 0, 0:1])
                k.act(T[1][:, 0:128], T[0][:, 0:128], AF.Square, accum_out=gcol[:, 0, 2:3])
                k.rsqrt(gcol[:, 0, 3:4], gcol[:, 0, 2:3], 1.0 / 128, epsc, gcol[:, 0, 4:5])
                k.ts('dve', Bt[0][:, 0:128], T[0][:, 0:128], gcol[:, 0, 3:4])
                pt_ = gbank()
                ptb = pt_[:, 0:64].bitcast(BF16)
                k.tr(ptb, Bt[0][:, 0:128], cmB(CM_IDENT))
                k.tt('dve', sm[:, 11:12], col(CC_GXQ), col(CC_GXK), ALU.mult)
                k.ts('dve', kmT[:, mt * 128:(mt + 1) * 128], ptb, sm[:, 11:12], 128 ** -0.5, op0=ALU.mult, op1=ALU.mult)

            _lvl(4)
            for bi in (range(NBLK) if do_rope else ()):
                tsl = slice(bi * TB, (bi + 1) * TB)
                pi_ = T[5].bitcast(I32)
                k.dma(pi_[:, :], pos_d[0:1, tsl].partition_broadcast(128), "x1")
                k.cp('dve', T[0][:], pi_[:, :])
                k.ts('dve', T[1][:], T[0][:], col(CC_INVF))
                for which, dst in ((0, cosT), (1, sinT)):
                    src = T[1]
                    if which == 0:
                        k.ts('dve', T[2][:], T[1][:], math.pi / 2, None, op0=ALU.add)
                        src = T[2]
                    k.ts('dve', T[3][:], src[:], 1.0 / TWO_PI, 12582912.0, op0=ALU.mult, op1=ALU.add)
                    k.ts('dve', T[3][:], T[3][:], -12582912.0, None, op0=ALU.add)
                    k.stt(T[4][:], T[3][:], -6.28125, src[:], ALU.mult, ALU.add)
                    k.stt(T[4][:], T[3][:], -(TWO_PI - 6.28125), T[4][:], ALU.mult, ALU.add)
                    k.ts('dve', T[4][:], T[4][:], 3.1415925, -3.1415925, op0=ALU.min, op1=ALU.max)
                    k.act(dst[:], T[4][:], AF.Sin)
                k.ts('dve', sinT[:], sinT[:], col(CC_SGN))
                k.dma(cs_d[0, :, tsl], cosT[:], "y0")
                k.dma(cs_d[1, :, tsl], sinT[:], "y1")

            xv = xT_d.rearrange("(c p) t -> p c t", p=128)
            yv = yT_d.rearrange("(g p) t -> p g t", p=128) if yT_d is not None else None

            for bi in range(NBLK):
                t0 = bi * TB
                tsl = slice(t0, t0 + TB)
                _lvl(5)
                if bi == 0:
                    k.dma(cosT[:], cs_d[0, :, tsl], "x2")
                    k.dma(sinT[:], cs_d[1, :, tsl], "x3")
                    k.dma(xt[:], xv[:, 0:4, tsl], "x0")
                prx = gbank()
                prc = gbank()
                for c in range(8):
                    if c == 4:
                        k.dma(xt[:], xv[:, 4:8, tsl], "x1")
                    k.cp('pool', xb[:, c, :], xt[:, c % 4, :])
                    k.act(sqb[:, c % 2, :], xt[:, c % 4, :], AF.Square)
                    k.mm(prx[:], cmB(CM_ONES), sqb[:, c % 2, :], start=(c == 0), stop=(c == 7))
                    for tt_ in range(4):
                        k.mm(prc[:, tt_:tt_ + 1], sqb[:, c % 2, tt_ * 128:(tt_ + 1) * 128], cmB(CM_ONES)[:, 0:1],
                             start=(c == 0 and tt_ == 0), stop=(c == 7), skip=True)
                if bi + 1 < NBLK:
                    k.dma(xt[:], xv[:, 0:4, t0 + TB:t0 + 2 * TB], "x0")
                k.rsqrt(RX[:], prx[:], 1.0 / D_MODEL, epsc, T[0][:])
                k.rsqrt(rxc[:, 0:4], prc[:, 0:4], 1.0 / D_MODEL, epsc, rxc[:, 4:8])
                k.ts('dve', rxc[:, 4:8], rxc[:, 0:4], -1.0)

                def fm_proj(j):
                    p = gbank()
                    for c in range(8):
                        k.mm(p[:], wfm[:, c, j * 128:(j + 1) * 128], xb[:, c, :], start=(c == 0), stop=(c == 7))
                    return p

                _lvl(6)
                p = fm_proj(0)
                k.tt('dve', T[0][:], p[:], RX[:], ALU.mult)
                k.act(T[1][:], T[0][:], AF.Exp, scale=-1.0)
                k.ts('dve', T[1][:], T[1][:], 1.0, None, op0=ALU.add)
                k.recip(T[1][:], T[1][:])
                k.stt(T[0][:], T[0][:], 128 ** -0.5, T[1][:], ALU.mult, ALU.mult)
                p = fm_proj(1)
                k.tt('dve', T[1][:], p[:], RX[:], ALU.mult)
                k.act(T[2][:], T[1][:], AF.Exp, scale=-1.0)
                k.ts('dve', T[2][:], T[2][:], 1.0, None, op0=ALU.add)
                k.recip(T[2][:], T[2][:])
                k.ts('dve', T[2][:], T[2][:], OMLB, LB, op0=ALU.mult, op1=ALU.add)
                k.act(T[3][:], T[2][:], AF.Ln)
                k.ts('dve', T[2][:], T[2][:], -1.0, 1.0, op0=ALU.mult, op1=ALU.add)
                k.scan(bcum[:], rmask[:], T[3][:], 0.0, ALU.mult, ALU.add)
                for c8 in range(8):
                    cs_ = slice(c8 * 64, (c8 + 1) * 64)
                    k.ts('dve', T[3][:, cs_], bcum[:, cs_], bcum[:, c8 * 64 + 31:c8 * 64 + 32], None, op0=ALU.subtract)
                k.act(T[4][:], T[3][:], AF.Exp)
                k.act(T[5][:], T[3][:], AF.Exp, scale=-1.0)
                k.tt('dve', qtT[:], T[0][:], T[4][:], ALU.mult)
                k.tt('dve', ktT[:], T[2][:], T[5][:], ALU.mult)
                EREF, ELAST, CFAC = sm[:, 16:24], sm[:, 24:32], sm[:, 32:40]
                k.act(EREF, bcum[:, 31:TB:64], AF.Exp)
                k.act(ELAST, bcum[:, 63:TB:64], AF.Exp)
                k.tt('dve', sm[:, 40:48], bcum[:, 63:TB:64], bcum[:, 31:TB:64], ALU.subtract)
                k.act(CFAC, sm[:, 40:48], AF.Exp)

                _lvl(7)
                def qk_norm_rope(j, gcol_, scale_, dst):
                    p_ = fm_proj(j)
                    k.tt('dve', T[0][:], p_[:], RX[:], ALU.mult)
                    k.act(Bt[0][:], T[0][:], AF.Square)
                    pss = gbank()
                    k.mm(pss[:], cmB(CM_BLK64), Bt[0][:])
                    k.rsqrt(T[1][:], pss[:], 1.0 / 64, epsc, T[2][:])
                    k.ts('dve', Bt[1][:], T[0][:], gcol_)
                    pp = gbank()
                    k.mm(pp[:], cmB(CM_PROPE), Bt[1][:])
                    k.tt('dve', T[2][:], Bt[1][:], cosT[:], ALU.mult)
                    k.tt('dve', T[3][:], pp[:], sinT[:], ALU.mult)
                    k.tt('dve', T[2][:], T[2][:], T[3][:], ALU.add)
                    k.stt(dst, T[2][:], scale_, T[1][:], ALU.mult, ALU.mult)

                qk_norm_rope(2, col(CC_GQ), 0.125, qhT[:])
                qk_norm_rope(3, col(CC_GK), 1.0, kT[:, tsl])
                if bi + 1 < NBLK:
                    k.dma(cosT[:], cs_d[0, :, t0 + TB:t0 + 2 * TB], "x2")
                    k.dma(sinT[:], cs_d[1, :, t0 + TB:t0 + 2 * TB], "x3")

                _lvl(8)
                def conv_silu(j, ext, wcol, bcol, scale_, dst):
                    p_ = fm_proj(j)
                    k.tt('dve', ext[:, 3:TB + 3], p_[:], RX[:], ALU.mult)
                    k.ts('dve', T[0][:], ext[:, 0:TB], cc[:, wcol:wcol + 1], cc[:, bcol:bcol + 1], op0=ALU.mult, op1=ALU.add)
                    for jj in range(1, 4):
                        k.stt(T[0][:], ext[:, jj:TB + jj], cc[:, wcol + jj:wcol + jj + 1], T[0][:], ALU.mult, ALU.add)
                    k.cp('pool', ext[:, 0:3], ext[:, TB:TB + 3])
                    k.act(T[1][:], T[0][:], AF.Exp, scale=-1.0)
                    k.ts('dve', T[1][:], T[1][:], 1.0, None, op0=ALU.add)
                    k.recip(T[1][:], T[1][:])
                    k.stt(dst, T[0][:], scale_, T[1][:], ALU.mult, ALU.mult)

                conv_silu(4, cqx, CC_CWQ, CC_CBQ, 1.0, qcT[:])
                conv_silu(5, ckx, CC_CWK, CC_CBK, 128 ** -0.5, kcT[:])

                _lvl(9)
                p = fm_proj(6)
                k.tt('dve', T[0][:], p[:], RX[:], ALU.mult)
                k.act(Bt[0][:], T[0][:], AF.Square)
                pss = gbank()
                k.mm(pss[:], cmB(CM_ONES), Bt[0][:])
                k.rsqrt(T[1][:], pss[:], 1.0 / 128, epsc, T[2][:])
                k.tt('dve', qxn[:], T[0][:], T[1][:], ALU.mult)
                for mt in range(2):
                    pS = gbank()
                    k.mm(pS[:], kmT[:, mt * 128:(mt + 1) * 128], qxn[:])
                    k.act(PT[:, mt, :], pS[:], AF.Exp)

                _lvl(10)
                for tt_ in range(4):
                    tcs = slice(tt_ * 128, (tt_ + 1) * 128)
                    pA, pB, pC = gbank(), gbank(), gbank()
                    for c in range(8):
                        k.mm(pA[:], xb[:, c, tcs], wtm[:, c, 0:512], start=(c == 0), stop=(c == 7))
                    for c in range(8):
                        k.mm(pB[:], xb[:, c, tcs], wtm[:, c, 512:1024], start=(c == 0), stop=(c == 7))
                    for c in range(8):
                        k.mm(pC[:, 0:2], xb[:, c, tcs], wtm[:, c, 1024:1026], start=(c == 0), stop=(c == 7))
                    rc = rxc[:, tt_:tt_ + 1]
                    nrc = rxc[:, 4 + tt_:5 + tt_]
                    k.act(T[0][:], pA[:], AF.Exp, scale=nrc)
                    k.ts('dve', T[0][:], T[0][:], 1.0, None, op0=ALU.add)
                    k.recip(T[0][:], T[0][:])
                    k.stt(gates[:, tt_, 0:4, :].rearrange("p g d -> p (g d)"), pA[:], rc, T[0][:], ALU.mult, ALU.mult)
                    k.act(T[1][:, 0:128], pB[:, 0:128], AF.Exp, scale=nrc)
                    k.ts('dve', T[1][:, 0:128], T[1][:, 0:128], 1.0, None, op0=ALU.add)
                    k.recip(gates[:, tt_, 4, :], T[1][:, 0:128])
                    k.ts('dve', av[:, tt_, :], pB[:, 128:256], rc)
                    k.ts('dve', vA[:, bi * 4 + tt_, 0:128], pB[:, 256:384], rc)
                    k.ts('dve', cvA[:, tt_, 0:128], pB[:, 384:512], rc)
                    g_ = gcol[:, tt_, :]
                    k.stt(g_[:, 0:1], pC[:, 0:1], rc, col(CC_GBI), ALU.mult, ALU.add)
                    k.stt(g_[:, 2:3], pC[:, 1:2], rc, col(CC_GBF), ALU.mult, ALU.add)
                    k.act(g_[:, 3:4], g_[:, 2:3], AF.Exp, scale=-1.0)
                    k.act(g_[:, 4:5], g_[:, 3:4], AF.Ln, bias=col(CC_ONE))
                    k.ts('dve', g_[:, 1:2], g_[:, 4:5], -1.0)
                    k.ts('dve', g_[:, 5:6], g_[:, 1:2], col(CC_IND0))
                    k.ts('dve', g_[:, 6:7], g_[:, 1:2], col(CC_IND1))
                    pc2 = gbank()
                    k.mm(pc2[:, 0:1], cmF(CM_TRI64), g_[:, 1:2])
                    k.mm(pc2[:, 1:2], cmF(CM_BLK64), g_[:, 1:2], skip=True)
                    k.mm(pc2[:, 2:4], cmF(CM_ONES), g_[:, 5:7], skip=True)
                    k.cp('dve', g_[:, 7:11], pc2[:, 0:4])
                    k.tt('dve', g_[:, 11:12], g_[:, 0:1], g_[:, 7:8], ALU.subtract)
                    k.tt('dve', g_[:, 12:13], g_[:, 11:12], g_[:, 8:9], ALU.add)
                    k.act(g_[:, 13:14], g_[:, 11:12], AF.Exp)
                    k.act(g_[:, 14:15], g_[:, 12:13], AF.Exp)
                    k.act(g_[:, 15:16], g_[:, 7:8], AF.Exp)
                    k.act(sm[:, 48 + 2 * tt_:50 + 2 * tt_], g_[:, 9:11], AF.Exp)

                _lvl(11)
                for tt_ in range(4):
                    tcs = slice(tt_ * 128, (tt_ + 1) * 128)
                    po = gbank()
                    for mt in range(2):
                        k.mm(po[:, 0:129], PT[:, mt, tcs], vmA[:, mt, :], start=(mt == 0), stop=(mt == 1))
                    k.recip(gcol[:, tt_, 5:6], po[:, 128:129])
                    k.stt(yTM[:, tt_, 3, :], po[:, 0:128], gcol[:, tt_, 5:6], gates[:, tt_, 3, :], ALU.mult, ALU.mult)

                _lvl(12)
                for tt_ in range(4):
                    tcs = slice(tt_ * 128, (tt_ + 1) * 128)
                    ptr = gbank()
                    ptb = ptr[:, 0:64].bitcast(BF16)
                    k.tr(ptb, ktT[:, tcs], cmB(CM_IDENT))
                    k.cp('dve', kTM[:, tt_, :], ptb)
                _lvl(12.1)
                for tt_ in range(4):
                    pd = gbank()
                    for hh in range(2):
                        c8 = tt_ * 2 + hh
                        rs = slice(hh * 64, hh * 64 + 64)
                        k.mm(pd[:, hh * 128:(hh + 1) * 128], kTM[rs, tt_, :], av[rs, tt_, :], skip=(hh == 1))
                        k.ts('dve', dS[:, c8, :], pd[:, hh * 128:(hh + 1) * 128], CFAC[:, c8:c8 + 1])
                _lvl(12.2)
                for c8 in range(8):
                    k.ts('dve', Sb_h[:, c8, :], S_h[:], EREF[:, c8:c8 + 1])
                    k.stt(S_h[:], S_h[:], ELAST[:, c8:c8 + 1], dS[:, c8, :], ALU.mult, ALU.add)
                _lvl(12.3)
                for tt_ in range(4):
                    tcs = slice(tt_ * 128, (tt_ + 1) * 128)
                    pa = gbank()
                    for hh in range(2):
                        cs_ = slice(tt_ * 128 + hh * 64, tt_ * 128 + hh * 64 + 64)
                        rs = slice(hh * 64, hh * 64 + 64)
                        k.mm(pa[rs, hh * 64:hh * 64 + 64], ktT[:, cs_], qtT[:, cs_], skip=(hh == 1))
                        k.stt(AT[rs, 0, hh * 64:hh * 64 + 64], pa[rs, hh * 64:hh * 64 + 64], 3.0e38,
                              cmB(CM_TRI64)[rs, hh * 64:hh * 64 + 64], ALU.min, ALU.mult)
                    _lvl(12.4)
                    po = gbank()
                    for hh in range(2):
                        cs_ = slice(tt_ * 128 + hh * 64, tt_ * 128 + hh * 64 + 64)
                        rs = slice(hh * 64, hh * 64 + 64)
                        k.mm(po[rs, 0:128], AT[rs, 0, hh * 64:hh * 64 + 64], av[rs, tt_, :], start=True, stop=False, skip=True)
                        k.mm(po[rs, 0:128], qtT[:, cs_], Sb_h[:, tt_ * 2 + hh, :], start=False, stop=True, skip=True)
                    g_ = gcol[:, tt_, :]
                    k.act(T[0][:, 0:128], po[:, 0:128], AF.Square, accum_out=g_[:, 5:6])
                    k.rsqrt(g_[:, 6:7], g_[:, 5:6], 1.0 / 128, epsc, g_[:, 5:6])
                    k.stt(T[0][:, 0:128], po[:, 0:128], g_[:, 6:7], crow[:, CR_HG * 128:(CR_HG + 1) * 128], ALU.mult, ALU.mult)
                    k.tt('dve', yTM[:, tt_, 0, :], T[0][:, 0:128], gates[:, tt_, 0, :], ALU.mult)

                _lvl(13)
                for tt_ in range(4):
                    tcs = slice(tt_ * 128, (tt_ + 1) * 128)
                    ptr = gbank()
                    ptb = ptr[:, 0:64].bitcast(BF16)
                    k.tr(ptb, kcT[:, tcs], cmB(CM_IDENT))
                    k.ts('dve', kTM2[:, tt_, :], ptb, gcol[:, tt_, 14:15])
                for tt_ in range(4):
                    for hh in range(2):
                        c8 = tt_ * 2 + hh
                        rs = slice(hh * 64, hh * 64 + 64)
                        pd = gbank()
                        k.mm(pd[:, 0:129], kTM2[rs, tt_, :], cvA[rs, tt_, :])
                        k.cp('dve', dC[:, c8, :], pd[:, 0:129])
                for c8 in range(8):
                    k.cp('dve', Cb_m[:, c8, :], C_m[:])
                    k.stt(C_m[:], C_m[:], sm[:, 48 + c8:49 + c8], dC[:, c8, :], ALU.mult, ALU.add)
                for tt_ in range(4):
                    tcs = slice(tt_ * 128, (tt_ + 1) * 128)
                    g_ = gcol[:, tt_, :]
                    pa = gbank()
                    k.mm(pa[:, 0:128], kcT[:, tcs], qcT[:, tcs])
                    k.stt(AT[:, 1, :], pa[:, 0:128], g_[:, 13:14], cmB(CM_TRI64), ALU.mult, ALU.mult)
                    po = gbank()
                    for hh in range(2):
                        cs_ = slice(tt_ * 128 + hh * 64, tt_ * 128 + hh * 64 + 64)
                        rs = slice(hh * 64, hh * 64 + 64)
                        k.mm(po[rs, 0:129], AT[rs, 1, hh * 64:hh * 64 + 64], cvA[rs, tt_, :], start=True, stop=False, skip=True)
                        k.mm(po[rs, 0:129], qcT[:, cs_], Cb_m[:, tt_ * 2 + hh, :], start=False, stop=True, skip=True)
                    k.act(g_[:, 5:6], po[:, 128:129], AF.Abs, scale=g_[:, 15:16])
                    k.ts('dve', g_[:, 5:6], g_[:, 5:6], 1.0, None, op0=ALU.max)
                    k.recip(g_[:, 5:6], g_[:, 5:6])
                    k.tt('dve', g_[:, 5:6], g_[:, 5:6], g_[:, 15:16], ALU.mult)
                    k.stt(T[0][:, 0:128], po[:, 0:128], g_[:, 5:6], gates[:, tt_, 4, :], ALU.mult, ALU.mult)
                    k.act(T[1][:, 0:128], T[0][:, 0:128], AF.Square, accum_out=g_[:, 6:7])
                    k.rsqrt(g_[:, 5:6], g_[:, 6:7], 1.0 / 128, epsc, g_[:, 6:7])
                    k.stt(T[0][:, 0:128], T[0][:, 0:128], g_[:, 5:6], crow[:, CR_MG * 128:(CR_MG + 1) * 128], ALU.mult, ALU.mult)
                    k.tt('dve', yTM[:, tt_, 2, :], T[0][:, 0:128], gates[:, tt_, 2, :], ALU.mult)

                _lvl(14)
                for ob in range(3):
                    k.mm(PS_O[ob][:], cmB(CM_ZERO), zrow[:], start=True, stop=False, sig=True, skip=True)
                oslot = {}
                for mp in range(2):
                    for qq in range(4):
                        idx = mp * 4 + qq
                        oslot[(mp, qq)] = (PS_O[idx // 3], (idx % 3) * 129)
                nkt = bi * 4 + 4
                def s_mm(j_, mp_):
                    q0_ = max(j_ - bi * 4, 0) * 128
                    rs_ = slice(mp_ * 64, mp_ * 64 + 64)
                    k.mm(PS_S[mp_][:, q0_:TB], kT[rs_, j_ * 128:(j_ + 1) * 128], qhT[rs_, q0_:TB])

                s_mm(0, 0)
                s_mm(0, 1)
                for j in range(nkt):
                    jj = j - bi * 4
                    q0 = max(jj, 0) * 128
                    for mp in range(2):
                        rs = slice(mp * 64, mp * 64 + 64)
                        pt_ = PT[:, 2 * (j % 2) + mp, :]
                        k.act(pt_[:, q0:TB], PS_S[mp][:, q0:TB], AF.Exp)
                        if jj >= 0:
                            k.tt('pool', pt_[:, q0:q0 + 128], pt_[:, q0:q0 + 128], cmB(CM_TRI128), ALU.mult)
                        for qq in range(max(jj, 0), 4):
                            ob, oc = oslot[(mp, qq)]
                            last = (j == min(nkt - 1, bi * 4 + qq))
                            k.mm(ob[:, oc:oc + 129], pt_[:, qq * 128:(qq + 1) * 128], vA[:, j, :],
                                 start=False, stop=last, sig=last, skip=True)
                        if j + 1 < nkt:
                            s_mm(j + 1, mp)
                for tt_ in range(4):
                    g_ = gcol[:, tt_, :]
                    ob1, oc1 = oslot[(0, tt_)]
                    ob2, oc2 = oslot[(1, tt_)]
                    k.recip(g_[:, 5:6], ob1[:, oc1 + 128:oc1 + 129])
                    k.recip(g_[:, 6:7], ob2[:, oc2 + 128:oc2 + 129])
                    k.tt('dve', g_[:, 6:7], g_[:, 6:7], NLAM, ALU.mult)
                    k.ts('dve', T[0][:, 0:128], ob1[:, oc1:oc1 + 128], g_[:, 5:6])
                    k.stt(T[0][:, 0:128], ob2[:, oc2:oc2 + 128], g_[:, 6:7], T[0][:, 0:128], ALU.mult, ALU.add)
                    k.act(T[1][:, 0:128], T[0][:, 0:128], AF.Square, accum_out=g_[:, 5:6])
                    k.rsqrt(g_[:, 6:7], g_[:, 5:6], 1.0 / 128, epsc, g_[:, 5:6])
                    k.ts('dve', g_[:, 6:7], g_[:, 6:7], 1.0 - lam_init)
                    k.stt(T[0][:, 0:128], T[0][:, 0:128], g_[:, 6:7], crow[:, CR_SUB * 128:(CR_SUB + 1) * 128], ALU.mult, ALU.mult)
                    k.tt('dve', yTM[:, tt_, 1, :], T[0][:, 0:128], gates[:, tt_, 1, :], ALU.mult)

                _lvl(15)
                for tt_ in range(4):
                    pty = gbank()
                    ptb = pty[:, 0:256].bitcast(BF16)
                    for g4 in range(4):
                        k.tr(ptb[:, g4 * 128:(g4 + 1) * 128], yTM[:, tt_, g4, :], cmB(CM_IDENT))
                    k.cp('dve', yT[:, :, tt_ * 128:(tt_ + 1) * 128], ptb.rearrange("p (g t) -> p g t", g=4))
                if not fused:
                    k.dma(yv[:, :, tsl], yT[:], "yu", eng='pool')
                else:
                    ch, half = bi // 2, bi % 2
                    for r in range(4):
                        k.ts('pool', yTr[:, r % 2, :, :], yT[:], col(CC_RANK + r))
                        k.dma(D['yin'][ch, r].rearrange("(g p) t -> p g t", p=128)[:, :, half * TB:(half + 1) * TB],
                              yTr[:, r % 2, :, :], "ys%d" % r, eng='pool')
                    if half == 1 or bi == NBLK - 1:
                        k.cc("AllReduce", ALU.add, [[0, 1, 2, 3], [4, 5, 6, 7]],
                             D['yin'][ch].rearrange("r p t -> (r p) t"), D['yout'][ch], "cc")

        except _Stop:
            pass
        k.s.barrier()


def layer_dram(nc, S, sfx, shared=None):
    dr = lambda name, shape, dt=F32: nc.dram_tensor(name, shape, dt, kind="ExternalInput").ap()
    D = {}
    if shared is None:
        D['pos'] = dr("pos", [1, S], I32)
        D['memT'] = dr("memT", [D_MODEL, N_MEM])
        D['cmat'] = dr("cmat", [128, NCM, 128])
        D['rmask'] = dr("rmask", [128, TB])
        D['cs'] = nc.dram_tensor("cs_scr", [2, 128, S], F32, kind="Internal").ap()
    else:
        for n_ in ('pos', 'memT', 'cmat', 'rmask', 'cs'):
            D[n_] = shared[n_]
    D['w_fm'] = dr("w_fm" + sfx, [D_MODEL, NFM * 128])
    D['w_tm'] = dr("w_tm" + sfx, [D_MODEL, NTM])
    D['w_kv'] = dr("w_kv" + sfx, [D_MODEL, 256])
    D['ccol'] = dr("ccol" + sfx, [128, NCC])
    D['crow'] = dr("crow" + sfx, [1, 3 * 128])
    D['lam'] = dr("lam" + sfx, [1, 256])
    return D


def build_layer(S, layer_idx):
    nc = bass.Bass("TRN2", target_bir_lowering=False)
    D = layer_dram(nc, S, "")
    D['xT'] = nc.dram_tensor("xT", [D_MODEL, S], F32, kind="ExternalInput").ap()
    D['yT'] = nc.dram_tensor("yT", [4 * 128, S], BF16, kind="ExternalOutput").ap()
    with ExitStack() as es:
        es.enter_context(nc.allow_low_precision(reason="bf16 matmul operands"))
        k = K(nc, es)
        PSB = [k.ps(f"pb{i}", [128, 512]) for i in range(8)]
        emit_layer(nc, k, PSB, S, layer_idx, "", D, do_rope=True, fused=False)
        k.s.finish()
    return nc


def emit_outproj(nc, k, PSB, S, sfx, D, full):
    NBLK = S // TB
    with ExitStack() as es:
        k.es = es
        sb = lambda name, shape, dt: k.sb(name + sfx, shape, dt)
        stage = sb("ostage", [128, D_MODEL], F32)
        wo = sb("wo", [128, 16, 256], BF16)
        yt = sb("yt", [128, 16, TB], BF16)
        xo = sb("xo", [128, 2, TB], F32)
        oo = sb("oo", [128, 2, TB], F32)
        wv = D['w_own'].rearrange("(c p) n -> p c n", p=128)
        for c in range(16):
            k.dma(stage[:, 0:256], wv[:, c, :], "st")
            k.cp('dve', wo[:, c, :], stage[:, 0:256])
        if full:
            w = sb("wfull", [128, 16, D_MODEL], BF16)
            xt = sb("oxt", [128, 8, TB], F32)
            ot = sb("oot", [128, 8, TB], F32)
            wv = D['w_out'].rearrange("(c p) n -> p c n", p=128)
            for c in range(16):
                k.dma(stage[:], wv[:, c, :], "st")
                k.cp('dve', w[:, c, :], stage[:])
            xv = D['x_prev'].rearrange("(c p) t -> p c t", p=128)
            x1v = D['x1'].rearrange("(c p) t -> p c t", p=128)
        xov = D['x_own_prev'].rearrange("(c p) t -> p c t", p=128)
        outv = (D['x1own'] if full else D['out']).rearrange("(c p) t -> p c t", p=128)
        pi = [0]
        for bi in range(NBLK):
            tsl = slice(bi * TB, (bi + 1) * TB)
            ch, half = bi // 2, bi % 2
            k.dma(yt[:], D['yout'][ch].rearrange("(c p) t -> p c t", p=128)[:, :, half * TB:(half + 1) * TB], "x0")
            k.dma(xo[:], xov[:, :, tsl], "x1")
            if full:
                k.dma(xt[:], xv[:, :, tsl], "x2")
                for oc in range(8):
                    p = PSB[pi[0] % 8]
                    pi[0] += 1
                    for c in range(16):
                        k.mm(p[:], w[:, c, oc * 128:(oc + 1) * 128], yt[:, c, :], start=(c == 0), stop=(c == 15),
                             sig=(c == 15))
                    k.tt('dve', ot[:, oc, :], p[:], xt[:, oc, :], ALU.add)
                k.dma(x1v[:, :, tsl], ot[:], "y0")
            for oc in range(2):
                p = PSB[pi[0] % 8]
                pi[0] += 1
                for c in range(16):
                    k.mm(p[:], wo[:, c, oc * 128:(oc + 1) * 128], yt[:, c, :], start=(c == 0), stop=(c == 15),
                         sig=(c == 15))
                k.tt('dve', oo[:, oc, :], p[:], xo[:, oc, :], ALU.add)
            k.dma(outv[:, :, tsl], oo[:], "y1")
        k.s.barrier()


def build_fused(S):
    nc = bass.Bass("TRN2", target_bir_lowering=False)
    NCH = max(S // 1024, 1)
    D0 = layer_dram(nc, S, "_0")
    D1 = layer_dram(nc, S, "_1", shared=D0)
    x0 = nc.dram_tensor("xT", [D_MODEL, S], F32, kind="ExternalInput").ap()
    x0own = nc.dram_tensor("x0own", [256, S], F32, kind="ExternalInput").ap()
    w_out0 = nc.dram_tensor("w_out0", [2048, D_MODEL], F32, kind="ExternalInput").ap()
    w_own0 = nc.dram_tensor("w_own0", [2048, 256], F32, kind="ExternalInput").ap()
    w_own1 = nc.dram_tensor("w_own1", [2048, 256], F32, kind="ExternalInput").ap()
    out = nc.dram_tensor("out", [256, S], F32, kind="ExternalOutput").ap()
    yin = nc.dram_tensor("yin", [NCH, 4, 512, min(S, 1024)], BF16).ap()
    yout = nc.dram_tensor("yout", [NCH, 2048, min(S, 1024)], BF16).ap()
    x1 = nc.dram_tensor("x1", [D_MODEL, S], F32).ap()
    x1own = nc.dram_tensor("x1own", [256, S], F32).ap()
    for D in (D0, D1):
        D['yin'], D['yout'] = yin, yout
    D0['xT'] = x0
    D1['xT'] = x1
    with ExitStack() as es:
        es.enter_context(nc.allow_low_precision(reason="bf16 matmul operands"))
        k = K(nc, es)
        PSB = [k.ps(f"pb{i}", [128, 512]) for i in range(8)]
        emit_layer(nc, k, PSB, S, 0, "_a", D0, do_rope=True, fused=True)
        emit_outproj(nc, k, PSB, S, "_p", {'w_own': w_own0, 'w_out': w_out0, 'x_prev': x0, 'x1': x1,
                                           'x_own_prev': x0own, 'x1own': x1own, 'yout': yout}, full=True)
        emit_layer(nc, k, PSB, S, 1, "_b", D1, do_rope=False, fused=True)
        emit_outproj(nc, k, PSB, S, "_q", {'w_own': w_own1, 'x_own_prev': x1own, 'out': out, 'yout': yout}, full=False)
        k.s.finish()
    return nc


def head_cols(h):
    G = GROUP
    base = np.arange(128)
    off = {}
    names = ['a_q', 'a_f', 'a_i', 'a_g', 'b_q', 'b_k', 'b_v', 'b_g', 'c_q', 'c_k', 'c_v', 'c_o']
    o = 0
    for n in names:
        off[n] = o
        o += G
    off['c_i'] = o
    o += 4
    off['c_f'] = o
    o += 4
    off['c_g'] = o
    o += G
    off['x_q'] = o
    o += G
    off['x_g'] = o
    o += G
    hc = lambda n: off[n] + h * 128 + base
    fm = np.concatenate([hc('a_q'), hc('a_f'), hc('b_q'), hc('b_k'), hc('c_q'), hc('c_k'), hc('x_q')])
    tm = np.concatenate([hc('a_g'), hc('b_g'), hc('c_g'), hc('x_g'), hc('c_o'), hc('a_i'), hc('b_v'), hc('c_v'),
                         np.array([off['c_i'] + h, off['c_f'] + h])])
    return fm, tm


def layer_inputs(l, b, h, xT_b, inp, cmat, rmask):
    fm, tm = head_cols(h)
    w_in = inp['w_in'][l]
    cc = np.zeros((128, NCC), np.float32)
    cc[:, CC_NG:CC_NG + 8] = inp['norm_g'][l].reshape(8, 128).T
    cc[:, CC_MNG:CC_MNG + 8] = inp['mem_norm_g'][l].reshape(8, 128).T
    invf = (ROPE_THETA ** (-np.arange(0, 16, 2, dtype=np.float32) / 16)).astype(np.float32)
    for base in (0, 64):
        cc[base:base + 8, CC_INVF] = invf
        cc[base + 8:base + 16, CC_INVF] = invf
        cc[base:base + 8, CC_SGN] = -1.0
        cc[base + 8:base + 16, CC_SGN] = 1.0
    cc[:, CC_GQ] = np.tile(inp['diff_qk_norm_g'][l, 0], 2)
    cc[:, CC_GK] = np.tile(inp['diff_qk_norm_g'][l, 1], 2)
    cw = inp['mlstm_conv_w'][l]
    cb = inp['mlstm_conv_b'][l]
    cc[:, CC_CWQ:CC_CWQ + 4] = cw[:, h * 128:(h + 1) * 128].T
    cc[:, CC_CBQ] = cb[h * 128:(h + 1) * 128]
    cc[:, CC_CWK:CC_CWK + 4] = cw[:, GROUP + h * 128:GROUP + (h + 1) * 128].T
    cc[:, CC_CBK] = cb[GROUP + h * 128:GROUP + (h + 1) * 128]
    cc[:, CC_GXQ] = inp['xattn_qk_norm_g'][l, 0]
    cc[:, CC_GXK] = inp['xattn_qk_norm_g'][l, 1]
    cc[:, CC_LBL:CC_LBL + 2] = inp['hgrn_lb_logits'][:, h * 128:(h + 1) * 128].T
    cc[0:64, CC_IND0] = 1.0
    cc[64:128, CC_IND1] = 1.0
    cc[:, CC_GBI] = inp['mlstm_gate_b'][l, h]
    cc[:, CC_GBF] = inp['mlstm_gate_b'][l, 4 + h]
    cc[:, CC_EPS] = EPS
    cc[:, CC_ONE] = 1.0
    cc[:, CC_RANK + h] = 1.0
    crow = np.concatenate([inp['hgrn_norm_g'][l], inp['diff_subln_g'][l], inp['mlstm_norm_g'][l]])[None, :]
    wkv = inp['w_mem_kv'][l]
    wkv_h = np.concatenate([wkv[:, h * 128:(h + 1) * 128], wkv[:, GROUP + h * 128:GROUP + (h + 1) * 128]], axis=1)
    return {
        "xT": xT_b,
        "pos": np.ascontiguousarray(inp['positions'][b][None, :xT_b.shape[1]]).astype(np.int32),
        "w_fm": np.ascontiguousarray(w_in[:, fm]),
        "w_tm": np.ascontiguousarray(w_in[:, tm]),
        "memT": np.ascontiguousarray(inp['mem'][b].T),
        "w_kv": np.ascontiguousarray(wkv_h),
        "cmat": cmat,
        "ccol": cc,
        "crow": np.ascontiguousarray(crow.astype(np.float32)),
        "lam": np.ascontiguousarray(inp['diff_lambda'][l].reshape(1, 256)),
        "rmask": rmask,
    }


_CACHE = {}


def run_layer(l, xT, inp, S):
    key = ('L', l, S)
    if key not in _CACHE:
        _CACHE[key] = build_layer(S, l)
    nc = _CACHE[key]
    cmat = const_mats()
    rmask = np.ones((128, TB), np.float32)
    rmask[:, ::64] = 0.0
    in_maps = []
    for c in range(8):
        b, h = c // 4, c % 4
        in_maps.append(layer_inputs(l, b, h, np.ascontiguousarray(xT[b]), inp, cmat, rmask))
    res = run_bass_kernel_spmd(nc, in_maps, core_ids=list(range(8)))
    ys = [np.asarray(r["yT"]) for r in res.results]
    return np.stack([np.concatenate(ys[b * 4:(b + 1) * 4], axis=0) for b in range(BATCH)])


def perm_w_out(w):
    return np.ascontiguousarray(w.reshape(4, 4, 128, D_MODEL).transpose(1, 0, 2, 3).reshape(2048, D_MODEL))


def run_fused(xT, inp, S):
    key = ('F', S)
    if key not in _CACHE:
        _CACHE[key] = build_fused(S)
    nc = _CACHE[key]
    cmat = const_mats()
    rmask = np.ones((128, TB), np.float32)
    rmask[:, ::64] = 0.0
    wp = [perm_w_out(inp['w_out'][l]) for l in range(DEPTH)]
    in_maps = []
    for c in range(8):
        b, h = c // 4, c % 4
        xb_ = np.ascontiguousarray(xT[b])
        m = {}
        for l in range(DEPTH):
            li = layer_inputs(l, b, h, xb_, inp, cmat, rmask)
            for n_ in ('w_fm', 'w_tm', 'w_kv', 'ccol', 'crow', 'lam'):
                m[n_ + "_%d" % l] = li[n_]
            if l == 0:
                for n_ in ('xT', 'pos', 'memT', 'cmat', 'rmask'):
                    m[n_] = li[n_]
        fs = slice(h * 256, (h + 1) * 256)
        m['x0own'] = np.ascontiguousarray(xb_[fs])
        m['w_out0'] = wp[0]
        m['w_own0'] = np.ascontiguousarray(wp[0][:, fs])
        m['w_own1'] = np.ascontiguousarray(wp[1][:, fs])
        in_maps.append(m)
    res = run_bass_kernel_spmd(nc, in_maps, core_ids=list(range(8)))
    outs = [np.asarray(r["out"]) for r in res.results]
    return np.stack([np.concatenate(outs[b * 4:(b + 1) * 4], axis=0) for b in range(BATCH)])


def kernel(**inputs):
    inp = {k_: np.asarray(v) for k_, v in inputs.items()}
    x = inp['x']
    S = x.shape[1]
    xT = np.ascontiguousarray(x.transpose(0, 2, 1))
    oT = run_fused(xT, inp, S)
    return np.ascontiguousarray(oT.transpose(0, 2, 1)).astype(np.float32)
```

```python
import math
import os
from contextlib import ExitStack

import numpy as np
import concourse.bass as bass
import concourse.mybir as mybir
from concourse.bass_utils import run_bass_kernel_spmd

F32 = mybir.dt.float32
BF16 = mybir.dt.bfloat16
I32 = mybir.dt.int32
AF = mybir.ActivationFunctionType
ALU = mybir.AluOpType

D_MODEL = 1024
BATCH = 2
SEQ = 16384
DEPTH = 2
N_MEM = 256
GROUP = 512
EPS = 1e-6
ROPE_THETA = 500000.0
TB = 512
NFM = 7
NTM = 1026


class Sched:
    SAME = ('act', 'dve', 'pool')

    def __init__(self, nc, es):
        self.nc = nc
        self.engs = {'pe': nc.tensor, 'act': nc.scalar, 'dve': nc.vector, 'pool': nc.gpsimd, 'sp': nc.sync}
        self.sem = {e: es.enter_context(nc.semaphore('s_' + e)) for e in ('pe', 'act', 'dve', 'pool')}
        self.cnt = {e: 0 for e in self.sem}
        self.pending = {e: [] for e in self.sem}
        self.csem = {}
        self.ccnt = {}
        self.es = es
        self.waited = {}
        self.acc = {}
        self.ops = []
        self.nwaits = 0

    @staticmethod
    def box(ap):
        dims = ap.ap
        esz = mybir.dt.size(ap.dtype)
        off = ap.offset
        sp = str(ap.space)
        if sp == 'DRAM':
            ext = 1
            for s, c in dims:
                ext += (c - 1) * abs(s)
            return (ap.tensor.name, 0, 1, off * esz, (off + ext) * esz)
        pstep = dims[0][0]
        if pstep == 0:
            p0, f0 = 0, off
        else:
            p0, f0 = off // pstep, off % pstep
        ext = 1
        for s, c in dims[1:]:
            ext += (c - 1) * abs(s)
        if sp == 'PSUM':
            return (ap.tensor.name, (p0 // 32) * 32, ((p0 + dims[0][1] + 31) // 32) * 32, 0, 2048)
        return (ap.tensor.name, p0, p0 + dims[0][1], f0 * esz, (f0 + ext) * esz)

    def _sem_of(self, key):
        return self.sem[key] if key in self.sem else self.csem[key]

    def emit(self, eng, fn, outs=(), ins=(), sig=True, chan=None, inc=16):
        is_dma = chan is not None
        oboxes = [self.box(a) for a in outs]
        iboxes = [self.box(a) for a in ins]
        deps = set()
        for (name, p0, p1, f0, f1) in iboxes:
            for ent in self.acc.get(name, ()):
                if ent[5] and ent[0] < p1 and p0 < ent[1] and ent[2] < f1 and f0 < ent[3]:
                    deps.add(ent[4])
        for (name, p0, p1, f0, f1) in oboxes:
            for ent in self.acc.get(name, ()):
                if ent[0] < p1 and p0 < ent[1] and ent[2] < f1 and f0 < ent[3]:
                    deps.add(ent[4])
        need = {}
        for d in deps:
            deng, key, val = self.ops[d]
            if deng is not None and deng == eng and eng not in self.SAME:
                continue
            if val is None:
                raise RuntimeError('dependency on unsignalled op')
            if need.get(key, 0) < val:
                need[key] = val
        if is_dma:
            if chan not in self.csem:
                self.csem[chan] = self.es.enter_context(self.nc.semaphore('c_' + chan))
                self.ccnt[chan] = 0
            if self.ccnt[chan] > 0 and need.get(chan, 0) < self.ccnt[chan]:
                need[chan] = self.ccnt[chan]
        e = self.engs[eng]
        for key, val in need.items():
            if self.waited.get((eng, key), 0) >= val:
                continue
            e.wait_ge(self._sem_of(key), val)
            self.waited[(eng, key)] = val
            self.nwaits += 1
        inst = fn(e)
        opid = len(self.ops)
        if is_dma:
            self.ccnt[chan] += inc
            inst.then_inc(self.csem[chan], inc)
            self.ops.append([None, chan, self.ccnt[chan]])
        elif sig:
            self.cnt[eng] += 1
            inst.then_inc(self.sem[eng], 1)
            self.ops.append([eng, eng, self.cnt[eng]])
            for p in self.pending[eng]:
                self.ops[p][2] = self.cnt[eng]
            self.pending[eng] = []
        else:
            self.ops.append([eng, eng, None])
            self.pending[eng].append(opid)
        for (name, p0, p1, f0, f1) in oboxes:
            lst = self.acc.setdefault(name, [])
            lst[:] = [t for t in lst if not (p0 <= t[0] and t[1] <= p1 and f0 <= t[2] and t[3] <= f1)]
            lst.append((p0, p1, f0, f1, opid, True))
        rkey = None if is_dma else eng
        for (name, p0, p1, f0, f1) in iboxes:
            lst = self.acc.setdefault(name, [])
            for i, t in enumerate(lst):
                if (not t[5]) and t[0] == p0 and t[1] == p1 and t[2] == f0 and t[3] == f1 \
                        and self.ops[t[4]][0] == rkey and rkey is not None:
                    lst[i] = (p0, p1, f0, f1, opid, False)
                    break
            else:
                lst.append((p0, p1, f0, f1, opid, False))
        return inst

    def barrier(self):
        keys = [(key, self.cnt[key]) for key in self.sem] + [(key, self.ccnt[key]) for key in self.csem]
        for eng, e in self.engs.items():
            for key, val in keys:
                if val > 0 and self.waited.get((eng, key), 0) < val:
                    e.wait_ge(self._sem_of(key), val)
                    self.waited[(eng, key)] = val

    def finish(self):
        sp = self.engs['sp']
        for key in list(self.csem):
            if self.ccnt[key] > 0:
                sp.wait_ge(self.csem[key], self.ccnt[key])
        for key in self.sem:
            if self.cnt[key] > 0:
                sp.wait_ge(self.sem[key], self.cnt[key])


class _Stop(Exception):
    pass


def _lvl(n):
    if n > float(os.environ.get('KLVL', '99')):
        raise _Stop()


class K:
    def __init__(self, nc, es):
        self.nc = nc
        self.es = es
        self.s = Sched(nc, es)
        self.n = 0

    def sb(self, name, shape, dt):
        return self.es.enter_context(self.nc.sbuf_tensor(name, shape, dt))

    def ps(self, name, shape, dt=F32):
        return self.es.enter_context(self.nc.psum_tensor(name, shape, dt))

    def mm(self, out, lhsT, rhs, start=True, stop=True, sig=None, skip=False):
        if sig is None:
            sig = True
        return self.s.emit('pe', lambda e: e.matmul(out, lhsT, rhs, start=start, stop=stop,
                                                     skip_group_check=skip),
                           outs=[out], ins=[lhsT, rhs], sig=sig)

    def tr(self, out, in_, ident):
        return self.s.emit('pe', lambda e: e.transpose(out, in_, ident), outs=[out], ins=[in_, ident])

    def act(self, out, in_, func, bias=0.0, scale=1.0, accum_out=None):
        ins = [in_]
        outs = [out]
        if not isinstance(bias, (int, float)):
            ins.append(bias)
        if not isinstance(scale, (int, float)):
            ins.append(scale)
        if accum_out is not None:
            outs.append(accum_out)
        kw = {}
        if accum_out is not None:
            kw['accum_out'] = accum_out
        return self.s.emit('act', lambda e: e.activation(out, in_, func, bias=bias, scale=scale, **kw),
                           outs=outs, ins=ins)

    def ts(self, eng, out, in0, s1, s2=None, op0=ALU.mult, op1=None, accum_out=None):
        ins = [in0]
        outs = [out]
        for s_ in (s1, s2):
            if s_ is not None and not isinstance(s_, (int, float)):
                ins.append(s_)
        kw = {}
        if op1 is not None:
            kw['op1'] = op1
        if accum_out is not None:
            kw['accum_out'] = accum_out
            outs.append(accum_out)
        return self.s.emit(eng, lambda e: e.tensor_scalar(out, in0, s1, s2, op0, **kw), outs=outs, ins=ins)

    def tt(self, eng, out, in0, in1, op):
        return self.s.emit(eng, lambda e: e.tensor_tensor(out, in0, in1, op), outs=[out], ins=[in0, in1])

    def stt(self, out, in0, scalar, in1, op0, op1):
        ins = [in0, in1]
        if not isinstance(scalar, (int, float)):
            ins.append(scalar)
        return self.s.emit('dve', lambda e: e.scalar_tensor_tensor(out, in0, scalar, in1, op0, op1),
                           outs=[out], ins=ins)

    def cp(self, eng, out, in_):
        if eng == 'act':
            return self.s.emit('act', lambda e: e.copy(out, in_), outs=[out], ins=[in_])
        return self.s.emit(eng, lambda e: e.tensor_copy(out, in_), outs=[out], ins=[in_])

    def recip(self, out, in_):
        return self.s.emit('dve', lambda e: e.reciprocal(out, in_), outs=[out], ins=[in_])

    def memset(self, eng, out, val):
        return self.s.emit(eng, lambda e: e.memset(out, val), outs=[out], ins=[])

    def scan(self, out, d0, d1, init, op0, op1):
        return self.s.emit('dve', lambda e: e.tensor_tensor_scan(out, d0, d1, init, op0, op1),
                           outs=[out], ins=[d0, d1])

    def dma(self, out, in_, chan, eng='sp'):
        return self.s.emit(eng, lambda e: e.dma_start(out=out, in_=in_), outs=[out], ins=[in_], chan=chan)

    def cc(self, kind, op, groups, in_ap, out_ap, chan):
        return self.s.emit('pool', lambda e: e.collective_compute(kind, op, replica_groups=groups,
                                                                 ins=[in_ap.opt()], outs=[out_ap.opt()]),
                           outs=[out_ap], ins=[in_ap], chan=chan, inc=1)

    def rsqrt(self, out, in_, scale, epscol, tmp):
        self.act(tmp, in_, AF.Ln, bias=epscol, scale=scale)
        self.act(out, tmp, AF.Exp, scale=-0.5)


CM_IDENT, CM_TRI128, CM_TRI64, CM_BLK64, CM_ONES, CM_PROPE, CM_ZERO = range(7)
NCM = 7
(CC_NG, CC_MNG, CC_INVF, CC_SGN, CC_GQ, CC_GK, CC_CWQ, CC_CBQ, CC_CWK, CC_CBK, CC_GXQ, CC_GXK, CC_LBL,
 CC_IND0, CC_IND1, CC_GBI, CC_GBF, CC_EPS, CC_ONE, CC_HALFPI) = (0, 8, 16, 17, 18, 19, 20, 24, 25, 29, 30, 31,
                                                                 32, 34, 35, 36, 37, 38, 39, 40)
CC_RANK = 41
NCC = 45
CR_HG, CR_SUB, CR_MG = range(3)
TWO_PI = 2.0 * math.pi


def const_mats():
    m = np.zeros((NCM, 128, 128), np.float32)
    i = np.arange(128)
    m[CM_IDENT] = np.eye(128)
    m[CM_TRI128] = (i[None, :] >= i[:, None])
    same = (i[:, None] // 64) == (i[None, :] // 64)
    m[CM_TRI64] = m[CM_TRI128] * same
    m[CM_BLK64] = same
    m[CM_ONES] = 1.0
    for base in (0, 64):
        for j in range(8):
            m[CM_PROPE, base + 8 + j, base + j] = 1.0
            m[CM_PROPE, base + j, base + 8 + j] = 1.0
    return np.ascontiguousarray(m.transpose(1, 0, 2))


def emit_layer(nc, k, PSB, S, layer_idx, sfx, D, do_rope=True, fused=False):
    NBLK = S // TB
    NT = S // 128
    xT_d, pos_d, wfm_d, wtm_d = D['xT'], D['pos'], D['w_fm'], D['w_tm']
    memT_d, wkv_d, cmat_d, ccol_d, crow_d, lam_d, rm_d = (D['memT'], D['w_kv'], D['cmat'], D['ccol'], D['crow'],
                                                          D['lam'], D['rmask'])
    cs_d = D['cs']
    yT_d = D.get('yT')

    lam_init = 0.8 - 0.6 * math.exp(-0.3 * layer_idx)

    with ExitStack() as es:
        k.es = es
        sb = lambda name, shape, dt: k.sb(name + sfx, shape, dt)
        kT = sb("kT", [128, S], BF16)
        vA = sb("vA", [128, NT, 129], BF16)
        wfm = sb("wfm", [128, 8, NFM * 128], BF16)
        wtm = sb("wtm", [128, 8, NTM], BF16)
        cmf3 = sb("cmf3", [128, 3, 128], F32)
        cmb = sb("cmb", [128, NCM, 128], BF16)
        cc = sb("cc", [128, NCC], F32)
        crow = sb("crow_sb", [128, 3 * 128], F32)
        lamt = sb("lamt", [128, 256], F32)
        rmask = sb("rmask_sb", [128, TB], F32)
        zrow = sb("zrow", [128, TB], BF16)
        kmT = sb("kmT", [128, 256], BF16)
        vmA = sb("vmA", [128, 2, 129], BF16)
        S_h = sb("S_h", [128, 128], F32)
        Sb_h = sb("Sb_h", [128, 8, 128], BF16)
        dS = sb("dS", [128, 8, 128], F32)
        C_m = sb("C_m", [128, 129], F32)
        Cb_m = sb("Cb_m", [128, 8, 129], BF16)
        dC = sb("dC", [128, 8, 129], F32)
        sm = sb("sm", [128, 64], F32)
        stage = sb("stage", [128, NTM], F32)
        xt = sb("xt", [128, 4, TB], F32)
        xtm = xt[:].rearrange("p c t -> p (c t)").rearrange("p (c t) -> p c t", c=8)
        xb = sb("xb", [128, 8, TB], BF16)
        sqb = sb("sqb", [128, 2, TB], BF16)
        RX = sb("RX", [128, TB], F32)
        rxc = sb("rxc", [128, 8], F32)
        Tall = sb("Tall", [128, 6, TB], F32)
        T = [Tall[:, i, :] for i in range(6)]
        xt2 = Tall[:, 2:6, :]
        Bt = [sb(f"B{i}", [128, TB], BF16) for i in range(3)]
        cosT = sb("cosT", [128, TB], F32)
        sinT = sb("sinT", [128, TB], F32)
        qhT = sb("qhT", [128, TB], BF16)
        qtT = sb("qtT", [128, TB], BF16)
        ktT = sb("ktT", [128, TB], BF16)
        qcT = sb("qcT", [128, TB], BF16)
        kcT = sb("kcT", [128, TB], BF16)
        cqx = sb("cqx", [128, TB + 3], F32)
        ckx = sb("ckx", [128, TB + 3], F32)
        qxn = sb("qxn", [128, TB], BF16)
        bcum = sb("bcum", [128, TB], F32)
        av = sb("av", [128, 4, 128], BF16)
        cvA = sb("cvA", [128, 4, 129], BF16)
        gates = sb("gates", [128, 4, 5, 128], BF16)
        gcol = sb("gcol", [128, 4, 16], F32)
        kTM = sb("kTM", [128, 4, 128], BF16)
        kTM2 = sb("kTM2", [128, 4, 128], BF16)
        AT = sb("AT", [128, 2, 128], BF16)
        PT = sb("PT", [128, 4, TB], BF16)
        yTM = sb("yTM", [128, 4, 4, 128], BF16)
        yT = sb("yT_sb", [128, 4, TB], BF16)
        yTr = sb("yTr", [128, 2, 4, TB], BF16) if fused else None
        G = PSB[0:3]
        PS_S = PSB[3:5]
        PS_O = PSB[5:8]
        gi = [0]

        def gbank():
            b = G[gi[0] % 3]
            gi[0] += 1
            return b

        cmF = lambda i: cmf3[:, i - CM_TRI64, :]
        cmB = lambda i: cmb[:, i, :]
        col = lambda i, n=1: cc[:, i:i + n]
        epsc = col(CC_EPS)

        try:
            k.dma(stage[:, 0:NCM * 128], cmat_d.rearrange("p m n -> p (m n)"), "c0")
            k.dma(cmf3[:], cmat_d[:, CM_TRI64:CM_TRI64 + 3, :], "c5")
            k.dma(cc[:], ccol_d[:, :], "c1")
            k.dma(crow[:], crow_d[0:1, :].partition_broadcast(128), "c2")
            k.dma(lamt[:], lam_d[0:1, :].partition_broadcast(128), "c3")
            k.dma(rmask[:], rm_d[:, :], "c4")
            k.cp('dve', cmb[:].rearrange("p m n -> p (m n)"), stage[:, 0:NCM * 128])
            k.memset('pool', zrow[:], 0.0)
            k.memset('pool', vA[:, :, 128:129], 1.0)
            k.memset('pool', cvA[:, :, 128:129], 1.0)
            k.memset('pool', vmA[:, :, 128:129], 1.0)
            k.memset('pool', S_h[:], 0.0)
            k.memset('pool', C_m[:], 0.0)
            k.memset('pool', cqx[:, 0:3], 0.0)
            k.memset('pool', ckx[:, 0:3], 0.0)
            k.memset('pool', AT[:], 0.0)
            k.memset('dve', sm[:], 0.0)
            wv = wfm_d.rearrange("(c p) n -> p c n", p=128)
            for c in range(8):
                k.dma(stage[:, 0:NFM * 128], wv[:, c, :], "st")
                k.ts('dve', wfm[:, c, :], stage[:, 0:NFM * 128], col(CC_NG + c))
            wv = wtm_d.rearrange("(c p) n -> p c n", p=128)
            for c in range(8):
                k.dma(stage[:, 0:NTM], wv[:, c, :], "st")
                k.ts('dve', wtm[:, c, :], stage[:, 0:NTM], col(CC_NG + c))

            _lvl(2)
            LB, OMLB, NLAM, R1 = sm[:, 0:1], sm[:, 1:2], sm[:, 2:3], sm[:, 3:4]
            if layer_idx == 0:
                k.memset('dve', LB, 0.0)
            else:
                k.tt('dve', sm[:, 4:5], col(CC_LBL), col(CC_LBL + 1), ALU.subtract)
                k.act(sm[:, 5:6], sm[:, 4:5], AF.Exp)
                k.ts('dve', sm[:, 5:6], sm[:, 5:6], 1.0, None, op0=ALU.add)
                k.recip(LB, sm[:, 5:6])
            k.ts('dve', OMLB, LB, -1.0, 1.0, op0=ALU.mult, op1=ALU.add)
            k.tt('dve', T[0][:, 0:64], lamt[:, 0:64], lamt[:, 64:128], ALU.mult)
            k.tt('dve', T[0][:, 64:128], lamt[:, 128:192], lamt[:, 192:256], ALU.mult)
            k.s.emit('dve', lambda e: e.reduce_sum(sm[:, 6:7], T[0][:, 0:64], mybir.AxisListType.X),
                     outs=[sm[:, 6:7]], ins=[T[0][:, 0:64]])
            k.s.emit('dve', lambda e: e.reduce_sum(sm[:, 7:8], T[0][:, 64:128], mybir.AxisListType.X),
                     outs=[sm[:, 7:8]], ins=[T[0][:, 64:128]])
            k.act(sm[:, 8:10], sm[:, 6:8], AF.Exp)
            k.tt('dve', sm[:, 10:11], sm[:, 9:10], sm[:, 8:9], ALU.subtract)
            k.ts('dve', NLAM, sm[:, 10:11], -lam_init, None, op0=ALU.add)

            _lvl(3)
            mv = memT_d.rearrange("(c p) n -> p c n", p=128)
            k.dma(xtm, mv[:, :, :], "x0")
            wk = wkv_d.rearrange("(c p) n -> p c n", p=128)
            mw = xb[:, :, 256:512]
            for c in range(8):
                k.dma(stage[:, 0:256], wk[:, c, :], "st")
                k.ts('dve', mw[:, c, :], stage[:, 0:256], col(CC_MNG + c))
                k.cp('pool', xb[:, c, 0:256], xtm[:, c, :])
            for mt in range(2):
                pc = gbank()
                for c in range(8):
                    k.act(sqb[:, c % 2, 0:128], xtm[:, c, mt * 128:(mt + 1) * 128], AF.Square)
                    k.mm(pc[:, 0:1], sqb[:, c % 2, 0:128], cmB(CM_ONES)[:, 0:1], start=(c == 0), stop=(c == 7))
                k.rsqrt(gcol[:, 0, 0:1], pc[:, 0:1], 1.0 / D_MODEL, epsc, gcol[:, 0, 1:2])
                pk = gbank()
                for c in range(8):
                    k.mm(pk[:, 0:256], xb[:, c, mt * 128:(mt + 1) * 128], mw[:, c, :], start=(c == 0), stop=(c == 7))
                k.ts('dve', T[0][:, 0:128], pk[:, 0:128], gcol[:, 0, 0:1])
                k.ts('dve', vmA[:, mt, 0:128], pk[:, 128:256], gcol[:, 0, 0:1])
                k.act(T[1][:, 0:128], T[0][:, 0:128], AF.Square, accum_out=gcol[:, 0, 2:3])
                k.rsqrt(gcol[:, 0, 3:4], gcol[:, 0, 2:3], 1.0 / 128, epsc, gcol[:, 0, 4:5])
                k.ts('dve', Bt[0][:, 0:128], T[0][:, 0:128], gcol[:, 0, 3:4])
                pt_ = gbank()
                ptb = pt_[:, 0:64].bitcast(BF16)
                k.tr(ptb, Bt[0][:, 0:128], cmB(CM_IDENT))
                k.tt('dve', sm[:, 11:12], col(CC_GXQ), col(CC_GXK), ALU.mult)
                k.ts('dve', kmT[:, mt * 128:(mt + 1) * 128], ptb, sm[:, 11:12], 128 ** -0.5, op0=ALU.mult, op1=ALU.mult)

            _lvl(4)
            for bi in (range(NBLK) if do_rope else ()):
                tsl = slice(bi * TB, (bi + 1) * TB)
                pi_ = T[5].bitcast(I32)
                k.dma(pi_[:, :], pos_d[0:1, tsl].partition_broadcast(128), "x1")
                k.cp('dve', T[0][:], pi_[:, :])
                k.ts('dve', T[1][:], T[0][:], col(CC_INVF))
                for which, dst in ((0, cosT), (1, sinT)):
                    src = T[1]
                    if which == 0:
                        k.ts('dve', T[2][:], T[1][:], math.pi / 2, None, op0=ALU.add)
                        src = T[2]
                    k.ts('dve', T[3][:], src[:], 1.0 / TWO_PI, 12582912.0, op0=ALU.mult, op1=ALU.add)
                    k.ts('dve', T[3][:], T[3][:], -12582912.0, None, op0=ALU.add)
                    k.stt(T[4][:], T[3][:], -6.28125, src[:], ALU.mult, ALU.add)
                    k.stt(T[4][:], T[3][:], -(TWO_PI - 6.28125), T[4][:], ALU.mult, ALU.add)
                    k.ts('dve', T[4][:], T[4][:], 3.1415925, -3.1415925, op0=ALU.min, op1=ALU.max)
                    k.act(dst[:], T[4][:], AF.Sin)
                k.ts('dve', sinT[:], sinT[:], col(CC_SGN))
                k.dma(cs_d[0, :, tsl], cosT[:], "y0")
                k.dma(cs_d[1, :, tsl], sinT[:], "y1")

            xv = xT_d.rearrange("(c p) t -> p c t", p=128)
            yv = yT_d.rearrange("(g p) t -> p g t", p=128) if yT_d is not None else None

            for bi in range(NBLK):
                t0 = bi * TB
                tsl = slice(t0, t0 + TB)
                _lvl(5)
                if bi == 0:
                    k.dma(cosT[:], cs_d[0, :, tsl], "x2")
                    k.dma(sinT[:], cs_d[1, :, tsl], "x3")
                    k.dma(xt[:], xv[:, 0:4, tsl], "x0")
                prx = gbank()
                prc = gbank()
                if bi == 0:
                    k.dma(xt2, xv[:, 4:8, tsl], "x1")
                for c in range(8):
                    xsrc = xt[:, c, :] if c < 4 else xt2[:, c - 4, :]
                    k.cp('pool', xb[:, c, :], xsrc)
                    k.act(sqb[:, c % 2, :], xsrc, AF.Square)
                    k.mm(prx[:], cmB(CM_ONES), sqb[:, c % 2, :], start=(c == 0), stop=(c == 7))
                    for tt_ in range(4):
                        k.mm(prc[:, tt_:tt_ + 1], sqb[:, c % 2, tt_ * 128:(tt_ + 1) * 128], cmB(CM_ONES)[:, 0:1],
                             start=(c == 0 and tt_ == 0), stop=(c == 7), skip=True)
                if bi + 1 < NBLK:
                    k.dma(xt[:], xv[:, 0:4, t0 + TB:t0 + 2 * TB], "x0")
                k.rsqrt(RX[:], prx[:], 1.0 / D_MODEL, epsc, T[0][:])
                k.rsqrt(rxc[:, 0:4], prc[:, 0:4], 1.0 / D_MODEL, epsc, rxc[:, 4:8])
                k.ts('dve', rxc[:, 4:8], rxc[:, 0:4], -1.0)

                def fm_proj(j):
                    p = gbank()
                    for c in range(8):
                        k.mm(p[:], wfm[:, c, j * 128:(j + 1) * 128], xb[:, c, :], start=(c == 0), stop=(c == 7))
                    return p

                _lvl(6)
                p = fm_proj(0)
                k.tt('dve', T[0][:], p[:], RX[:], ALU.mult)
                k.act(T[1][:], T[0][:], AF.Exp, scale=-1.0)
                k.ts('dve', T[1][:], T[1][:], 1.0, None, op0=ALU.add)
                k.recip(T[1][:], T[1][:])
                k.stt(T[0][:], T[0][:], 128 ** -0.5, T[1][:], ALU.mult, ALU.mult)
                p = fm_proj(1)
                k.tt('dve', T[1][:], p[:], RX[:], ALU.mult)
                k.act(T[2][:], T[1][:], AF.Exp, scale=-1.0)
                k.ts('dve', T[2][:], T[2][:], 1.0, None, op0=ALU.add)
                k.recip(T[2][:], T[2][:])
                k.ts('dve', T[2][:], T[2][:], OMLB, LB, op0=ALU.mult, op1=ALU.add)
                k.act(T[3][:], T[2][:], AF.Ln)
                k.ts('dve', T[2][:], T[2][:], -1.0, 1.0, op0=ALU.mult, op1=ALU.add)
                k.scan(bcum[:], rmask[:], T[3][:], 0.0, ALU.mult, ALU.add)
                for c8 in range(8):
                    cs_ = slice(c8 * 64, (c8 + 1) * 64)
                    k.ts('dve', T[3][:, cs_], bcum[:, cs_], bcum[:, c8 * 64 + 31:c8 * 64 + 32], None, op0=ALU.subtract)
                k.act(T[4][:], T[3][:], AF.Exp)
                k.act(T[5][:], T[3][:], AF.Exp, scale=-1.0)
                k.tt('dve', qtT[:], T[0][:], T[4][:], ALU.mult)
                k.tt('dve', ktT[:], T[2][:], T[5][:], ALU.mult)
                EREF, ELAST, CFAC = sm[:, 16:24], sm[:, 24:32], sm[:, 32:40]
                k.act(EREF, bcum[:, 31:TB:64], AF.Exp)
                k.act(ELAST, bcum[:, 63:TB:64], AF.Exp)
                k.tt('dve', sm[:, 40:48], bcum[:, 63:TB:64], bcum[:, 31:TB:64], ALU.subtract)
                k.act(CFAC, sm[:, 40:48], AF.Exp)

                _lvl(7)
                def qk_norm_rope(j, gcol_, scale_, dst):
                    p_ = fm_proj(j)
                    k.tt('dve', T[0][:], p_[:], RX[:], ALU.mult)
                    k.act(Bt[0][:], T[0][:], AF.Square)
                    pss = gbank()
                    k.mm(pss[:], cmB(CM_BLK64), Bt[0][:])
                    k.rsqrt(T[1][:], pss[:], 1.0 / 64, epsc, T[2][:])
                    k.ts('dve', Bt[1][:], T[0][:], gcol_)
                    pp = gbank()
                    k.mm(pp[:], cmB(CM_PROPE), Bt[1][:])
                    k.tt('dve', T[2][:], Bt[1][:], cosT[:], ALU.mult)
                    k.tt('dve', T[3][:], pp[:], sinT[:], ALU.mult)
                    k.tt('dve', T[2][:], T[2][:], T[3][:], ALU.add)
                    k.stt(dst, T[2][:], scale_, T[1][:], ALU.mult, ALU.mult)

                qk_norm_rope(2, col(CC_GQ), 0.125, qhT[:])
                qk_norm_rope(3, col(CC_GK), 1.0, kT[:, tsl])
                if bi + 1 < NBLK:
                    k.dma(cosT[:], cs_d[0, :, t0 + TB:t0 + 2 * TB], "x2")
                    k.dma(sinT[:], cs_d[1, :, t0 + TB:t0 + 2 * TB], "x3")

                _lvl(8)
                def conv_silu(j, ext, wcol, bcol, scale_, dst):
                    p_ = fm_proj(j)
                    k.tt('dve', ext[:, 3:TB + 3], p_[:], RX[:], ALU.mult)
                    k.ts('dve', T[0][:], ext[:, 0:TB], cc[:, wcol:wcol + 1], cc[:, bcol:bcol + 1], op0=ALU.mult, op1=ALU.add)
                    for jj in range(1, 4):
                        k.stt(T[0][:], ext[:, jj:TB + jj], cc[:, wcol + jj:wcol + jj + 1], T[0][:], ALU.mult, ALU.add)
                    k.cp('pool', ext[:, 0:3], ext[:, TB:TB + 3])
                    k.act(T[1][:], T[0][:], AF.Exp, scale=-1.0)
                    k.ts('dve', T[1][:], T[1][:], 1.0, None, op0=ALU.add)
                    k.recip(T[1][:], T[1][:])
                    k.stt(dst, T[0][:], scale_, T[1][:], ALU.mult, ALU.mult)

                conv_silu(4, cqx, CC_CWQ, CC_CBQ, 1.0, qcT[:])
                conv_silu(5, ckx, CC_CWK, CC_CBK, 128 ** -0.5, kcT[:])

                _lvl(9)
                p = fm_proj(6)
                k.tt('dve', T[0][:], p[:], RX[:], ALU.mult)
                k.act(Bt[0][:], T[0][:], AF.Square)
                pss = gbank()
                k.mm(pss[:], cmB(CM_ONES), Bt[0][:])
                k.rsqrt(T[1][:], pss[:], 1.0 / 128, epsc, T[2][:])
                k.tt('dve', qxn[:], T[0][:], T[1][:], ALU.mult)
                for mt in range(2):
                    pS = gbank()
                    k.mm(pS[:], kmT[:, mt * 128:(mt + 1) * 128], qxn[:])
                    k.act(PT[:, mt, :], pS[:], AF.Exp)

                _lvl(10)
                for tt_ in range(4):
                    tcs = slice(tt_ * 128, (tt_ + 1) * 128)
                    pA, pB, pC = gbank(), gbank(), gbank()
                    for c in range(8):
                        k.mm(pA[:], xb[:, c, tcs], wtm[:, c, 0:512], start=(c == 0), stop=(c == 7))
                    for c in range(8):
                        k.mm(pB[:], xb[:, c, tcs], wtm[:, c, 512:1024], start=(c == 0), stop=(c == 7))
                    for c in range(8):
                        k.mm(pC[:, 0:2], xb[:, c, tcs], wtm[:, c, 1024:1026], start=(c == 0), stop=(c == 7))
                    rc = rxc[:, tt_:tt_ + 1]
                    nrc = rxc[:, 4 + tt_:5 + tt_]
                    k.act(T[0][:], pA[:], AF.Exp, scale=nrc)
                    k.ts('dve', T[0][:], T[0][:], 1.0, None, op0=ALU.add)
                    k.recip(T[0][:], T[0][:])
                    k.stt(gates[:, tt_, 0:4, :].rearrange("p g d -> p (g d)"), pA[:], rc, T[0][:], ALU.mult, ALU.mult)
                    k.act(T[1][:, 0:128], pB[:, 0:128], AF.Exp, scale=nrc)
                    k.ts('dve', T[1][:, 0:128], T[1][:, 0:128], 1.0, None, op0=ALU.add)
                    k.recip(gates[:, tt_, 4, :], T[1][:, 0:128])
                    k.ts('dve', av[:, tt_, :], pB[:, 128:256], rc)
                    k.ts('dve', vA[:, bi * 4 + tt_, 0:128], pB[:, 256:384], rc)
                    k.ts('dve', cvA[:, tt_, 0:128], pB[:, 384:512], rc)
                    g_ = gcol[:, tt_, :]
                    k.stt(g_[:, 0:1], pC[:, 0:1], rc, col(CC_GBI), ALU.mult, ALU.add)
                    k.stt(g_[:, 2:3], pC[:, 1:2], rc, col(CC_GBF), ALU.mult, ALU.add)
                    k.act(g_[:, 3:4], g_[:, 2:3], AF.Exp, scale=-1.0)
                    k.act(g_[:, 4:5], g_[:, 3:4], AF.Ln, bias=col(CC_ONE))
                    k.ts('dve', g_[:, 1:2], g_[:, 4:5], -1.0)
                    k.ts('dve', g_[:, 5:6], g_[:, 1:2], col(CC_IND0))
                    k.ts('dve', g_[:, 6:7], g_[:, 1:2], col(CC_IND1))
                    pc2 = gbank()
                    k.mm(pc2[:, 0:1], cmF(CM_TRI64), g_[:, 1:2])
                    k.mm(pc2[:, 1:2], cmF(CM_BLK64), g_[:, 1:2], skip=True)
                    k.mm(pc2[:, 2:4], cmF(CM_ONES), g_[:, 5:7], skip=True)
                    k.cp('dve', g_[:, 7:11], pc2[:, 0:4])
                    k.tt('dve', g_[:, 11:12], g_[:, 0:1], g_[:, 7:8], ALU.subtract)
                    k.tt('dve', g_[:, 12:13], g_[:, 11:12], g_[:, 8:9], ALU.add)
                    k.act(g_[:, 13:14], g_[:, 11:12], AF.Exp)
                    k.act(g_[:, 14:15], g_[:, 12:13], AF.Exp)
                    k.act(g_[:, 15:16], g_[:, 7:8], AF.Exp)
                    k.act(sm[:, 48 + 2 * tt_:50 + 2 * tt_], g_[:, 9:11], AF.Exp)

                _lvl(11)
                for tt_ in range(4):
                    tcs = slice(tt_ * 128, (tt_ + 1) * 128)
                    po = gbank()
                    for mt in range(2):
                        k.mm(po[:, 0:129], PT[:, mt, tcs], vmA[:, mt, :], start=(mt == 0), stop=(mt == 1))
                    k.recip(gcol[:, tt_, 5:6], po[:, 128:129])
                    k.stt(yTM[:, tt_, 3, :], po[:, 0:128], gcol[:, tt_, 5:6], gates[:, tt_, 3, :], ALU.mult, ALU.mult)

                _lvl(12)
                for tt_ in range(4):
                    tcs = slice(tt_ * 128, (tt_ + 1) * 128)
                    ptr = gbank()
                    ptb = ptr[:, 0:64].bitcast(BF16)
                    k.tr(ptb, ktT[:, tcs], cmB(CM_IDENT))
                    k.cp('dve', kTM[:, tt_, :], ptb)
                _lvl(12.1)
                for tt_ in range(4):
                    pd = gbank()
                    for hh in range(2):
                        c8 = tt_ * 2 + hh
                        rs = slice(hh * 64, hh * 64 + 64)
                        k.mm(pd[:, hh * 128:(hh + 1) * 128], kTM[rs, tt_, :], av[rs, tt_, :], skip=(hh == 1))
                        k.ts('dve', dS[:, c8, :], pd[:, hh * 128:(hh + 1) * 128], CFAC[:, c8:c8 + 1])
                _lvl(12.2)
                for c8 in range(8):
                    k.ts('dve', Sb_h[:, c8, :], S_h[:], EREF[:, c8:c8 + 1])
                    k.stt(S_h[:], S_h[:], ELAST[:, c8:c8 + 1], dS[:, c8, :], ALU.mult, ALU.add)
                _lvl(12.3)
                for tt_ in range(4):
                    tcs = slice(tt_ * 128, (tt_ + 1) * 128)
                    pa = gbank()
                    for hh in range(2):
                        cs_ = slice(tt_ * 128 + hh * 64, tt_ * 128 + hh * 64 + 64)
                        rs = slice(hh * 64, hh * 64 + 64)
                        k.mm(pa[rs, hh * 64:hh * 64 + 64], ktT[:, cs_], qtT[:, cs_], skip=(hh == 1))
                        k.stt(AT[rs, 0, hh * 64:hh * 64 + 64], pa[rs, hh * 64:hh * 64 + 64], 3.0e38,
                              cmB(CM_TRI64)[rs, hh * 64:hh * 64 + 64], ALU.min, ALU.mult)
                    _lvl(12.4)
                    po = gbank()
                    for hh in range(2):
                        cs_ = slice(tt_ * 128 + hh * 64, tt_ * 128 + hh * 64 + 64)
                        rs = slice(hh * 64, hh * 64 + 64)
                        k.mm(po[rs, 0:128], AT[rs, 0, hh * 64:hh * 64 + 64], av[rs, tt_, :], start=True, stop=False, skip=True)
                        k.mm(po[rs, 0:128], qtT[:, cs_], Sb_h[:, tt_ * 2 + hh, :], start=False, stop=True, skip=True)
                    g_ = gcol[:, tt_, :]
                    k.act(T[0][:, 0:128], po[:, 0:128], AF.Square, accum_out=g_[:, 5:6])
                    k.rsqrt(g_[:, 6:7], g_[:, 5:6], 1.0 / 128, epsc, g_[:, 5:6])
                    k.stt(T[0][:, 0:128], po[:, 0:128], g_[:, 6:7], crow[:, CR_HG * 128:(CR_HG + 1) * 128], ALU.mult, ALU.mult)
                    k.tt('dve', yTM[:, tt_, 0, :], T[0][:, 0:128], gates[:, tt_, 0, :], ALU.mult)

                _lvl(13)
                for tt_ in range(4):
                    tcs = slice(tt_ * 128, (tt_ + 1) * 128)
                    ptr = gbank()
                    ptb = ptr[:, 0:64].bitcast(BF16)
                    k.tr(ptb, kcT[:, tcs], cmB(CM_IDENT))
                    k.ts('dve', kTM2[:, tt_, :], ptb, gcol[:, tt_, 14:15])
                for tt_ in range(4):
                    for hh in range(2):
                        c8 = tt_ * 2 + hh
                        rs = slice(hh * 64, hh * 64 + 64)
                        pd = gbank()
                        k.mm(pd[:, 0:129], kTM2[rs, tt_, :], cvA[rs, tt_, :])
                        k.cp('dve', dC[:, c8, :], pd[:, 0:129])
                for c8 in range(8):
                    k.cp('dve', Cb_m[:, c8, :], C_m[:])
                    k.stt(C_m[:], C_m[:], sm[:, 48 + c8:49 + c8], dC[:, c8, :], ALU.mult, ALU.add)
                for tt_ in range(4):
                    tcs = slice(tt_ * 128, (tt_ + 1) * 128)
                    g_ = gcol[:, tt_, :]
                    pa = gbank()
                    k.mm(pa[:, 0:128], kcT[:, tcs], qcT[:, tcs])
                    k.stt(AT[:, 1, :], pa[:, 0:128], g_[:, 13:14], cmB(CM_TRI64), ALU.mult, ALU.mult)
                    po = gbank()
                    for hh in range(2):
                        cs_ = slice(tt_ * 128 + hh * 64, tt_ * 128 + hh * 64 + 64)
                        rs = slice(hh * 64, hh * 64 + 64)
                        k.mm(po[rs, 0:129], AT[rs, 1, hh * 64:hh * 64 + 64], cvA[rs, tt_, :], start=True, stop=False, skip=True)
                        k.mm(po[rs, 0:129], qcT[:, cs_], Cb_m[:, tt_ * 2 + hh, :], start=False, stop=True, skip=True)
                    k.act(g_[:, 5:6], po[:, 128:129], AF.Abs, scale=g_[:, 15:16])
                    k.ts('dve', g_[:, 5:6], g_[:, 5:6], 1.0, None, op0=ALU.max)
                    k.recip(g_[:, 5:6], g_[:, 5:6])
                    k.tt('dve', g_[:, 5:6], g_[:, 5:6], g_[:, 15:16], ALU.mult)
                    k.stt(T[0][:, 0:128], po[:, 0:128], g_[:, 5:6], gates[:, tt_, 4, :], ALU.mult, ALU.mult)
                    k.act(T[1][:, 0:128], T[0][:, 0:128], AF.Square, accum_out=g_[:, 6:7])
                    k.rsqrt(g_[:, 5:6], g_[:, 6:7], 1.0 / 128, epsc, g_[:, 6:7])
                    k.stt(T[0][:, 0:128], T[0][:, 0:128], g_[:, 5:6], crow[:, CR_MG * 128:(CR_MG + 1) * 128], ALU.mult, ALU.mult)
                    k.tt('dve', yTM[:, tt_, 2, :], T[0][:, 0:128], gates[:, tt_, 2, :], ALU.mult)

                _lvl(14)
                if bi + 1 < NBLK:
                    k.dma(xt2, xv[:, 4:8, t0 + TB:t0 + 2 * TB], "x1")
                for ob in range(3):
                    k.mm(PS_O[ob][:], cmB(CM_ZERO), zrow[:], start=True, stop=False, sig=True, skip=True)
                oslot = {}
                for mp in range(2):
                    for qq in range(4):
                        idx = mp * 4 + qq
                        oslot[(mp, qq)] = (PS_O[idx // 3], (idx % 3) * 129)
                nkt = bi * 4 + 4
                def s_mm(j_, mp_):
                    q0_ = max(j_ - bi * 4, 0) * 128
                    rs_ = slice(mp_ * 64, mp_ * 64 + 64)
                    k.mm(PS_S[mp_][:, q0_:TB], kT[rs_, j_ * 128:(j_ + 1) * 128], qhT[rs_, q0_:TB])

                s_mm(0, 0)
                s_mm(0, 1)
                for j in range(nkt):
                    jj = j - bi * 4
                    q0 = max(jj, 0) * 128
                    for mp in range(2):
                        rs = slice(mp * 64, mp * 64 + 64)
                        pt_ = PT[:, 2 * (j % 2) + mp, :]
                        k.act(pt_[:, q0:TB], PS_S[mp][:, q0:TB], AF.Exp)
                        if jj >= 0:
                            k.tt('pool', pt_[:, q0:q0 + 128], pt_[:, q0:q0 + 128], cmB(CM_TRI128), ALU.mult)
                        for qq in range(max(jj, 0), 4):
                            ob, oc = oslot[(mp, qq)]
                            last = (j == min(nkt - 1, bi * 4 + qq))
                            k.mm(ob[:, oc:oc + 129], pt_[:, qq * 128:(qq + 1) * 128], vA[:, j, :],
                                 start=False, stop=last, sig=last, skip=True)
                        if j + 1 < nkt:
                            s_mm(j + 1, mp)
                for tt_ in range(4):
                    g_ = gcol[:, tt_, :]
                    ob1, oc1 = oslot[(0, tt_)]
                    ob2, oc2 = oslot[(1, tt_)]
                    k.recip(g_[:, 5:6], ob1[:, oc1 + 128:oc1 + 129])
                    k.recip(g_[:, 6:7], ob2[:, oc2 + 128:oc2 + 129])
                    k.tt('dve', g_[:, 6:7], g_[:, 6:7], NLAM, ALU.mult)
                    k.ts('dve', T[0][:, 0:128], ob1[:, oc1:oc1 + 128], g_[:, 5:6])
                    k.stt(T[0][:, 0:128], ob2[:, oc2:oc2 + 128], g_[:, 6:7], T[0][:, 0:128], ALU.mult, ALU.add)
                    k.act(T[1][:, 0:128], T[0][:, 0:128], AF.Square, accum_out=g_[:, 5:6])
                    k.rsqrt(g_[:, 6:7], g_[:, 5:6], 1.0 / 128, epsc, g_[:, 5:6])
                    k.ts('dve', g_[:, 6:7], g_[:, 6:7], 1.0 - lam_init)
                    k.stt(T[0][:, 0:128], T[0][:, 0:128], g_[:, 6:7], crow[:, CR_SUB * 128:(CR_SUB + 1) * 128], ALU.mult, ALU.mult)
                    k.tt('dve', yTM[:, tt_, 1, :], T[0][:, 0:128], gates[:, tt_, 1, :], ALU.mult)

                _lvl(15)
                for tt_ in range(4):
                    pty = gbank()
                    ptb = pty[:, 0:256].bitcast(BF16)
                    for g4 in range(4):
                        k.tr(ptb[:, g4 * 128:(g4 + 1) * 128], yTM[:, tt_, g4, :], cmB(CM_IDENT))
                    k.cp('dve', yT[:, :, tt_ * 128:(tt_ + 1) * 128], ptb.rearrange("p (g t) -> p g t", g=4))
                if not fused:
                    k.dma(yv[:, :, tsl], yT[:], "yu", eng='pool')
                else:
                    ch, half = bi // 2, bi % 2
                    for r in range(4):
                        k.ts('pool', yTr[:, r % 2, :, :], yT[:], col(CC_RANK + r))
                        k.dma(D['yin'][ch, r].rearrange("(g p) t -> p g t", p=128)[:, :, half * TB:(half + 1) * TB],
                              yTr[:, r % 2, :, :], "ys%d" % r, eng='pool')
                    if half == 1 or bi == NBLK - 1:
                        k.cc("AllReduce", ALU.add, [[0, 1, 2, 3], [4, 5, 6, 7]],
                             D['yin'][ch].rearrange("r p t -> (r p) t"), D['yout'][ch], "cc")

        except _Stop:
            pass
        k.s.barrier()


def layer_dram(nc, S, sfx, shared=None):
    dr = lambda name, shape, dt=F32: nc.dram_tensor(name, shape, dt, kind="ExternalInput").ap()
    D = {}
    if shared is None:
        D['pos'] = dr("pos", [1, S], I32)
        D['memT'] = dr("memT", [D_MODEL, N_MEM])
        D['cmat'] = dr("cmat", [128, NCM, 128])
        D['rmask'] = dr("rmask", [128, TB])
        D['cs'] = nc.dram_tensor("cs_scr", [2, 128, S], F32, kind="Internal").ap()
    else:
        for n_ in ('pos', 'memT', 'cmat', 'rmask', 'cs'):
            D[n_] = shared[n_]
    D['w_fm'] = dr("w_fm" + sfx, [D_MODEL, NFM * 128])
    D['w_tm'] = dr("w_tm" + sfx, [D_MODEL, NTM])
    D['w_kv'] = dr("w_kv" + sfx, [D_MODEL, 256])
    D['ccol'] = dr("ccol" + sfx, [128, NCC])
    D['crow'] = dr("crow" + sfx, [1, 3 * 128])
    D['lam'] = dr("lam" + sfx, [1, 256])
    return D


def build_layer(S, layer_idx):
    nc = bass.Bass("TRN2", target_bir_lowering=False)
    D = layer_dram(nc, S, "")
    D['xT'] = nc.dram_tensor("xT", [D_MODEL, S], F32, kind="ExternalInput").ap()
    D['yT'] = nc.dram_tensor("yT", [4 * 128, S], BF16, kind="ExternalOutput").ap()
    with ExitStack() as es:
        es.enter_context(nc.allow_low_precision(reason="bf16 matmul operands"))
        k = K(nc, es)
        PSB = [k.ps(f"pb{i}", [128, 512]) for i in range(8)]
        emit_layer(nc, k, PSB, S, layer_idx, "", D, do_rope=True, fused=False)
        k.s.finish()
    return nc


def emit_outproj(nc, k, PSB, S, sfx, D, full):
    NBLK = S // TB
    with ExitStack() as es:
        k.es = es
        sb = lambda name, shape, dt: k.sb(name + sfx, shape, dt)
        stage = sb("ostage", [128, D_MODEL], F32)
        wo = sb("wo", [128, 16, 256], BF16)
        yt = sb("yt", [128, 16, TB], BF16)
        xo = sb("xo", [128, 2, TB], F32)
        oo = sb("oo", [128, 2, TB], F32)
        wv = D['w_own'].rearrange("(c p) n -> p c n", p=128)
        for c in range(16):
            k.dma(stage[:, 0:256], wv[:, c, :], "st")
            k.cp('dve', wo[:, c, :], stage[:, 0:256])
        if full:
            w = sb("wfull", [128, 16, D_MODEL], BF16)
            xt = sb("oxt", [128, 8, TB], F32)
            ot = sb("oot", [128, 8, TB], F32)
            wv = D['w_out'].rearrange("(c p) n -> p c n", p=128)
            for c in range(16):
                k.dma(stage[:], wv[:, c, :], "st")
                k.cp('dve', w[:, c, :], stage[:])
            xv = D['x_prev'].rearrange("(c p) t -> p c t", p=128)
            x1v = D['x1'].rearrange("(c p) t -> p c t", p=128)
        xov = D['x_own_prev'].rearrange("(c p) t -> p c t", p=128)
        outv = (D['x1own'] if full else D['out']).rearrange("(c p) t -> p c t", p=128)
        pi = [0]
        for bi in range(NBLK):
            tsl = slice(bi * TB, (bi + 1) * TB)
            ch, half = bi // 2, bi % 2
            k.dma(yt[:], D['yout'][ch].rearrange("(c p) t -> p c t", p=128)[:, :, half * TB:(half + 1) * TB], "x0")
            k.dma(xo[:], xov[:, :, tsl], "x1")
            if full:
                k.dma(xt[:], xv[:, :, tsl], "x2")
                for oc in range(8):
                    p = PSB[pi[0] % 8]
                    pi[0] += 1
                    for c in range(16):
                        k.mm(p[:], w[:, c, oc * 128:(oc + 1) * 128], yt[:, c, :], start=(c == 0), stop=(c == 15),
                             sig=(c == 15))
                    k.tt('dve', ot[:, oc, :], p[:], xt[:, oc, :], ALU.add)
                k.dma(x1v[:, :, tsl], ot[:], "y0")
            for oc in range(2):
                p = PSB[pi[0] % 8]
                pi[0] += 1
                for c in range(16):
                    k.mm(p[:], wo[:, c, oc * 128:(oc + 1) * 128], yt[:, c, :], start=(c == 0), stop=(c == 15),
                         sig=(c == 15))
                k.tt('dve', oo[:, oc, :], p[:], xo[:, oc, :], ALU.add)
            k.dma(outv[:, :, tsl], oo[:], "y1")
        k.s.barrier()


def build_fused(S):
    nc = bass.Bass("TRN2", target_bir_lowering=False)
    NCH = max(S // 1024, 1)
    D0 = layer_dram(nc, S, "_0")
    D1 = layer_dram(nc, S, "_1", shared=D0)
    x0 = nc.dram_tensor("xT", [D_MODEL, S], F32, kind="ExternalInput").ap()
    x0own = nc.dram_tensor("x0own", [256, S], F32, kind="ExternalInput").ap()
    w_out0 = nc.dram_tensor("w_out0", [2048, D_MODEL], F32, kind="ExternalInput").ap()
    w_own0 = nc.dram_tensor("w_own0", [2048, 256], F32, kind="ExternalInput").ap()
    w_own1 = nc.dram_tensor("w_own1", [2048, 256], F32, kind="ExternalInput").ap()
    out = nc.dram_tensor("out", [256, S], F32, kind="ExternalOutput").ap()
    yin = nc.dram_tensor("yin", [NCH, 4, 512, min(S, 1024)], BF16).ap()
    yout = nc.dram_tensor("yout", [NCH, 2048, min(S, 1024)], BF16).ap()
    x1 = nc.dram_tensor("x1", [D_MODEL, S], F32).ap()
    x1own = nc.dram_tensor("x1own", [256, S], F32).ap()
    for D in (D0, D1):
        D['yin'], D['yout'] = yin, yout
    D0['xT'] = x0
    D1['xT'] = x1
    with ExitStack() as es:
        es.enter_context(nc.allow_low_precision(reason="bf16 matmul operands"))
        k = K(nc, es)
        PSB = [k.ps(f"pb{i}", [128, 512]) for i in range(8)]
        emit_layer(nc, k, PSB, S, 0, "_a", D0, do_rope=True, fused=True)
        emit_outproj(nc, k, PSB, S, "_p", {'w_own': w_own0, 'w_out': w_out0, 'x_prev': x0, 'x1': x1,
                                           'x_own_prev': x0own, 'x1own': x1own, 'yout': yout}, full=True)
        emit_layer(nc, k, PSB, S, 1, "_b", D1, do_rope=False, fused=True)
        emit_outproj(nc, k, PSB, S, "_q", {'w_own': w_own1, 'x_own_prev': x1own, 'out': out, 'yout': yout}, full=False)
        k.s.finish()
    return nc


def head_cols(h):
    G = GROUP
    base = np.arange(128)
    off = {}
    names = ['a_q', 'a_f', 'a_i', 'a_g', 'b_q', 'b_k', 'b_v', 'b_g', 'c_q', 'c_k', 'c_v', 'c_o']
    o = 0
    for n in names:
        off[n] = o
        o += G
    off['c_i'] = o
    o += 4
    off['c_f'] = o
    o += 4
    off['c_g'] = o
    o += G
    off['x_q'] = o
    o += G
    off['x_g'] = o
    o += G
    hc = lambda n: off[n] + h * 128 + base
    fm = np.concatenate([hc('a_q'), hc('a_f'), hc('b_q'), hc('b_k'), hc('c_q'), hc('c_k'), hc('x_q')])
    tm = np.concatenate([hc('a_g'), hc('b_g'), hc('c_g'), hc('x_g'), hc('c_o'), hc('a_i'), hc('b_v'), hc('c_v'),
                         np.array([off['c_i'] + h, off['c_f'] + h])])
    return fm, tm


def layer_inputs(l, b, h, xT_b, inp, cmat, rmask):
    fm, tm = head_cols(h)
    w_in = inp['w_in'][l]
    cc = np.zeros((128, NCC), np.float32)
    cc[:, CC_NG:CC_NG + 8] = inp['norm_g'][l].reshape(8, 128).T
    cc[:, CC_MNG:CC_MNG + 8] = inp['mem_norm_g'][l].reshape(8, 128).T
    invf = (ROPE_THETA ** (-np.arange(0, 16, 2, dtype=np.float32) / 16)).astype(np.float32)
    for base in (0, 64):
        cc[base:base + 8, CC_INVF] = invf
        cc[base + 8:base + 16, CC_INVF] = invf
        cc[base:base + 8, CC_SGN] = -1.0
        cc[base + 8:base + 16, CC_SGN] = 1.0
    cc[:, CC_GQ] = np.tile(inp['diff_qk_norm_g'][l, 0], 2)
    cc[:, CC_GK] = np.tile(inp['diff_qk_norm_g'][l, 1], 2)
    cw = inp['mlstm_conv_w'][l]
    cb = inp['mlstm_conv_b'][l]
    cc[:, CC_CWQ:CC_CWQ + 4] = cw[:, h * 128:(h + 1) * 128].T
    cc[:, CC_CBQ] = cb[h * 128:(h + 1) * 128]
    cc[:, CC_CWK:CC_CWK + 4] = cw[:, GROUP + h * 128:GROUP + (h + 1) * 128].T
    cc[:, CC_CBK] = cb[GROUP + h * 128:GROUP + (h + 1) * 128]
    cc[:, CC_GXQ] = inp['xattn_qk_norm_g'][l, 0]
    cc[:, CC_GXK] = inp['xattn_qk_norm_g'][l, 1]
    cc[:, CC_LBL:CC_LBL + 2] = inp['hgrn_lb_logits'][:, h * 128:(h + 1) * 128].T
    cc[0:64, CC_IND0] = 1.0
    cc[64:128, CC_IND1] = 1.0
    cc[:, CC_GBI] = inp['mlstm_gate_b'][l, h]
    cc[:, CC_GBF] = inp['mlstm_gate_b'][l, 4 + h]
    cc[:, CC_EPS] = EPS
    cc[:, CC_ONE] = 1.0
    cc[:, CC_RANK + h] = 1.0
    crow = np.concatenate([inp['hgrn_norm_g'][l], inp['diff_subln_g'][l], inp['mlstm_norm_g'][l]])[None, :]
    wkv = inp['w_mem_kv'][l]
    wkv_h = np.concatenate([wkv[:, h * 128:(h + 1) * 128], wkv[:, GROUP + h * 128:GROUP + (h + 1) * 128]], axis=1)
    return {
        "xT": xT_b,
        "pos": np.ascontiguousarray(inp['positions'][b][None, :xT_b.shape[1]]).astype(np.int32),
        "w_fm": np.ascontiguousarray(w_in[:, fm]),
        "w_tm": np.ascontiguousarray(w_in[:, tm]),
        "memT": np.ascontiguousarray(inp['mem'][b].T),
        "w_kv": np.ascontiguousarray(wkv_h),
        "cmat": cmat,
        "ccol": cc,
        "crow": np.ascontiguousarray(crow.astype(np.float32)),
        "lam": np.ascontiguousarray(inp['diff_lambda'][l].reshape(1, 256)),
        "rmask": rmask,
    }


_CACHE = {}


def run_layer(l, xT, inp, S):
    key = ('L', l, S)
    if key not in _CACHE:
        _CACHE[key] = build_layer(S, l)
    nc = _CACHE[key]
    cmat = const_mats()
    rmask = np.ones((128, TB), np.float32)
    rmask[:, ::64] = 0.0
    in_maps = []
    for c in range(8):
        b, h = c // 4, c % 4
        in_maps.append(layer_inputs(l, b, h, np.ascontiguousarray(xT[b]), inp, cmat, rmask))
    res = run_bass_kernel_spmd(nc, in_maps, core_ids=list(range(8)))
    ys = [np.asarray(r["yT"]) for r in res.results]
    return np.stack([np.concatenate(ys[b * 4:(b + 1) * 4], axis=0) for b in range(BATCH)])


def perm_w_out(w):
    return np.ascontiguousarray(w.reshape(4, 4, 128, D_MODEL).transpose(1, 0, 2, 3).reshape(2048, D_MODEL))


def run_fused(xT, inp, S):
    key = ('F', S)
    if key not in _CACHE:
        _CACHE[key] = build_fused(S)
    nc = _CACHE[key]
    cmat = const_mats()
    rmask = np.ones((128, TB), np.float32)
    rmask[:, ::64] = 0.0
    wp = [perm_w_out(inp['w_out'][l]) for l in range(DEPTH)]
    in_maps = []
    for c in range(8):
        b, h = c // 4, c % 4
        xb_ = np.ascontiguousarray(xT[b])
        m = {}
        for l in range(DEPTH):
            li = layer_inputs(l, b, h, xb_, inp, cmat, rmask)
            for n_ in ('w_fm', 'w_tm', 'w_kv', 'ccol', 'crow', 'lam'):
                m[n_ + "_%d" % l] = li[n_]
            if l == 0:
                for n_ in ('xT', 'pos', 'memT', 'cmat', 'rmask'):
                    m[n_] = li[n_]
        fs = slice(h * 256, (h + 1) * 256)
        m['x0own'] = np.ascontiguousarray(xb_[fs])
        m['w_out0'] = wp[0]
        m['w_own0'] = np.ascontiguousarray(wp[0][:, fs])
        m['w_own1'] = np.ascontiguousarray(wp[1][:, fs])
        in_maps.append(m)
    res = run_bass_kernel_spmd(nc, in_maps, core_ids=list(range(8)))
    outs = [np.asarray(r["out"]) for r in res.results]
    return np.stack([np.concatenate(outs[b * 4:(b + 1) * 4], axis=0) for b in range(BATCH)])


def kernel(**inputs):
    inp = {k_: np.asarray(v) for k_, v in inputs.items()}
    x = inp['x']
    S = x.shape[1]
    xT = np.ascontiguousarray(x.transpose(0, 2, 1))
    oT = run_fused(xT, inp, S)
    return np.ascontiguousarray(oT.transpose(0, 2, 1)).astype(np.float32)
```

```python
import math
import os
from contextlib import ExitStack

import numpy as np
import concourse.bass as bass
import concourse.mybir as mybir
from concourse.bass_utils import run_bass_kernel_spmd

F32 = mybir.dt.float32
BF16 = mybir.dt.bfloat16
I32 = mybir.dt.int32
AF = mybir.ActivationFunctionType
ALU = mybir.AluOpType

D_MODEL = 1024
BATCH = 2
SEQ = 16384
DEPTH = 2
N_MEM = 256
GROUP = 512
EPS = 1e-6
ROPE_THETA = 500000.0
TB = 512
NFM = 7
NTM = 1026


class Sched:
    SAME = ('act', 'dve', 'pool')

    def __init__(self, nc, es):
        self.nc = nc
        self.engs = {'pe': nc.tensor, 'act': nc.scalar, 'dve': nc.vector, 'pool': nc.gpsimd, 'sp': nc.sync}
        self.sem = {e: es.enter_context(nc.semaphore('s_' + e)) for e in ('pe', 'act', 'dve', 'pool')}
        self.cnt = {e: 0 for e in self.sem}
        self.pending = {e: [] for e in self.sem}
        self.csem = {}
        self.ccnt = {}
        self.es = es
        self.waited = {}
        self.acc = {}
        self.ops = []
        self.nwaits = 0

    @staticmethod
    def box(ap):
        dims = ap.ap
        esz = mybir.dt.size(ap.dtype)
        off = ap.offset
        sp = str(ap.space)
        if sp == 'DRAM':
            ext = 1
            for s, c in dims:
                ext += (c - 1) * abs(s)
            return (ap.tensor.name, 0, 1, off * esz, (off + ext) * esz)
        pstep = dims[0][0]
        if pstep == 0:
            p0, f0 = 0, off
        else:
            p0, f0 = off // pstep, off % pstep
        ext = 1
        for s, c in dims[1:]:
            ext += (c - 1) * abs(s)
        if sp == 'PSUM':
            return (ap.tensor.name, (p0 // 32) * 32, ((p0 + dims[0][1] + 31) // 32) * 32, 0, 2048)
        return (ap.tensor.name, p0, p0 + dims[0][1], f0 * esz, (f0 + ext) * esz)

    def _sem_of(self, key):
        return self.sem[key] if key in self.sem else self.csem[key]

    def emit(self, eng, fn, outs=(), ins=(), sig=True, chan=None, inc=16):
        is_dma = chan is not None
        oboxes = [self.box(a) for a in outs]
        iboxes = [self.box(a) for a in ins]
        deps = set()
        for (name, p0, p1, f0, f1) in iboxes:
            for ent in self.acc.get(name, ()):
                if ent[5] and ent[0] < p1 and p0 < ent[1] and ent[2] < f1 and f0 < ent[3]:
                    deps.add(ent[4])
        for (name, p0, p1, f0, f1) in oboxes:
            for ent in self.acc.get(name, ()):
                if ent[0] < p1 and p0 < ent[1] and ent[2] < f1 and f0 < ent[3]:
                    deps.add(ent[4])
        need = {}
        for d in deps:
            deng, key, val = self.ops[d]
            if deng is not None and deng == eng and eng not in self.SAME:
                continue
            if val is None:
                raise RuntimeError('dependency on unsignalled op')
            if need.get(key, 0) < val:
                need[key] = val
        if is_dma:
            if chan not in self.csem:
                self.csem[chan] = self.es.enter_context(self.nc.semaphore('c_' + chan))
                self.ccnt[chan] = 0
            if self.ccnt[chan] > 0 and need.get(chan, 0) < self.ccnt[chan]:
                need[chan] = self.ccnt[chan]
        e = self.engs[eng]
        for key, val in need.items():
            if self.waited.get((eng, key), 0) >= val:
                continue
            e.wait_ge(self._sem_of(key), val)
            self.waited[(eng, key)] = val
            self.nwaits += 1
        inst = fn(e)
        opid = len(self.ops)
        if is_dma:
            self.ccnt[chan] += inc
            inst.then_inc(self.csem[chan], inc)
            self.ops.append([None, chan, self.ccnt[chan]])
        elif sig:
            self.cnt[eng] += 1
            inst.then_inc(self.sem[eng], 1)
            self.ops.append([eng, eng, self.cnt[eng]])
            for p in self.pending[eng]:
                self.ops[p][2] = self.cnt[eng]
            self.pending[eng] = []
        else:
            self.ops.append([eng, eng, None])
            self.pending[eng].append(opid)
        for (name, p0, p1, f0, f1) in oboxes:
            lst = self.acc.setdefault(name, [])
            lst[:] = [t for t in lst if not (p0 <= t[0] and t[1] <= p1 and f0 <= t[2] and t[3] <= f1)]
            lst.append((p0, p1, f0, f1, opid, True))
        rkey = None if is_dma else eng
        for (name, p0, p1, f0, f1) in iboxes:
            lst = self.acc.setdefault(name, [])
            for i, t in enumerate(lst):
                if (not t[5]) and t[0] == p0 and t[1] == p1 and t[2] == f0 and t[3] == f1 \
                        and self.ops[t[4]][0] == rkey and rkey is not None:
                    lst[i] = (p0, p1, f0, f1, opid, False)
                    break
            else:
                lst.append((p0, p1, f0, f1, opid, False))
        return inst

    def barrier(self):
        keys = [(key, self.cnt[key]) for key in self.sem] + [(key, self.ccnt[key]) for key in self.csem]
        for eng, e in self.engs.items():
            for key, val in keys:
                if val > 0 and self.waited.get((eng, key), 0) < val:
                    e.wait_ge(self._sem_of(key), val)
                    self.waited[(eng, key)] = val

    def finish(self):
        sp = self.engs['sp']
        for key in list(self.csem):
            if self.ccnt[key] > 0:
                sp.wait_ge(self.csem[key], self.ccnt[key])
        for key in self.sem:
            if self.cnt[key] > 0:
                sp.wait_ge(self.sem[key], self.cnt[key])


class _Stop(Exception):
    pass


def _lvl(n):
    if n > float(os.environ.get('KLVL', '99')):
        raise _Stop()


class K:
    def __init__(self, nc, es):
        self.nc = nc
        self.es = es
        self.s = Sched(nc, es)
        self.n = 0

    def sb(self, name, shape, dt):
        return self.es.enter_context(self.nc.sbuf_tensor(name, shape, dt))

    def ps(self, name, shape, dt=F32):
        return self.es.enter_context(self.nc.psum_tensor(name, shape, dt))

    def mm(self, out, lhsT, rhs, start=True, stop=True, sig=None, skip=False):
        if sig is None:
            sig = True
        return self.s.emit('pe', lambda e: e.matmul(out, lhsT, rhs, start=start, stop=stop,
                                                     skip_group_check=skip),
                           outs=[out], ins=[lhsT, rhs], sig=sig)

    def tr(self, out, in_, ident):
        return self.s.emit('pe', lambda e: e.transpose(out, in_, ident), outs=[out], ins=[in_, ident])

    def act(self, out, in_, func, bias=0.0, scale=1.0, accum_out=None):
        ins = [in_]
        outs = [out]
        if not isinstance(bias, (int, float)):
            ins.append(bias)
        if not isinstance(scale, (int, float)):
            ins.append(scale)
        if accum_out is not None:
            outs.append(accum_out)
        kw = {}
        if accum_out is not None:
            kw['accum_out'] = accum_out
        return self.s.emit('act', lambda e: e.activation(out, in_, func, bias=bias, scale=scale, **kw),
                           outs=outs, ins=ins)

    def ts(self, eng, out, in0, s1, s2=None, op0=ALU.mult, op1=None, accum_out=None):
        ins = [in0]
        outs = [out]
        for s_ in (s1, s2):
            if s_ is not None and not isinstance(s_, (int, float)):
                ins.append(s_)
        kw = {}
        if op1 is not None:
            kw['op1'] = op1
        if accum_out is not None:
            kw['accum_out'] = accum_out
            outs.append(accum_out)
        return self.s.emit(eng, lambda e: e.tensor_scalar(out, in0, s1, s2, op0, **kw), outs=outs, ins=ins)

    def tt(self, eng, out, in0, in1, op):
        return self.s.emit(eng, lambda e: e.tensor_tensor(out, in0, in1, op), outs=[out], ins=[in0, in1])

    def stt(self, out, in0, scalar, in1, op0, op1):
        ins = [in0, in1]
        if not isinstance(scalar, (int, float)):
            ins.append(scalar)
        return self.s.emit('dve', lambda e: e.scalar_tensor_tensor(out, in0, scalar, in1, op0, op1),
                           outs=[out], ins=ins)

    def cp(self, eng, out, in_):
        if eng == 'act':
            return self.s.emit('act', lambda e: e.copy(out, in_), outs=[out], ins=[in_])
        return self.s.emit(eng, lambda e: e.tensor_copy(out, in_), outs=[out], ins=[in_])

    def recip(self, out, in_):
        return self.s.emit('dve', lambda e: e.reciprocal(out, in_), outs=[out], ins=[in_])

    def memset(self, eng, out, val):
        return self.s.emit(eng, lambda e: e.memset(out, val), outs=[out], ins=[])

    def scan(self, out, d0, d1, init, op0, op1):
        return self.s.emit('dve', lambda e: e.tensor_tensor_scan(out, d0, d1, init, op0, op1),
                           outs=[out], ins=[d0, d1])

    def dma(self, out, in_, chan, eng='sp'):
        return self.s.emit(eng, lambda e: e.dma_start(out=out, in_=in_), outs=[out], ins=[in_], chan=chan)

    def cc(self, kind, op, groups, in_ap, out_ap, chan):
        return self.s.emit('pool', lambda e: e.collective_compute(kind, op, replica_groups=groups,
                                                                 ins=[in_ap.opt()], outs=[out_ap.opt()]),
                           outs=[out_ap], ins=[in_ap], chan=chan, inc=1)

    def rsqrt(self, out, in_, scale, epscol, tmp):
        self.act(tmp, in_, AF.Ln, bias=epscol, scale=scale)
        self.act(out, tmp, AF.Exp, scale=-0.5)


CM_IDENT, CM_TRI128, CM_TRI64, CM_BLK64, CM_ONES, CM_PROPE, CM_ZERO = range(7)
NCM = 7
(CC_NG, CC_MNG, CC_INVF, CC_SGN, CC_GQ, CC_GK, CC_CWQ, CC_CBQ, CC_CWK, CC_CBK, CC_GXQ, CC_GXK, CC_LBL,
 CC_IND0, CC_IND1, CC_GBI, CC_GBF, CC_EPS, CC_ONE, CC_HALFPI) = (0, 8, 16, 17, 18, 19, 20, 24, 25, 29, 30, 31,
                                                                 32, 34, 35, 36, 37, 38, 39, 40)
CC_RANK = 41
NCC = 45
CR_HG, CR_SUB, CR_MG = range(3)
TWO_PI = 2.0 * math.pi


def const_mats():
    m = np.zeros((NCM, 128, 128), np.float32)
    i = np.arange(128)
    m[CM_IDENT] = np.eye(128)
    m[CM_TRI128] = (i[None, :] >= i[:, None])
    same = (i[:, None] // 64) == (i[None, :] // 64)
    m[CM_TRI64] = m[CM_TRI128] * same
    m[CM_BLK64] = same
    m[CM_ONES] = 1.0
    for base in (0, 64):
        for j in range(8):
            m[CM_PROPE, base + 8 + j, base + j] = 1.0
            m[CM_PROPE, base + j, base + 8 + j] = 1.0
    return np.ascontiguousarray(m.transpose(1, 0, 2))


def emit_layer(nc, k, PSB, S, layer_idx, sfx, D, do_rope=True, fused=False):
    NBLK = S // TB
    NT = S // 128
    xT_d, pos_d, wfm_d, wtm_d = D['xT'], D['pos'], D['w_fm'], D['w_tm']
    memT_d, wkv_d, cmat_d, ccol_d, crow_d, lam_d, rm_d = (D['memT'], D['w_kv'], D['cmat'], D['ccol'], D['crow'],
                                                          D['lam'], D['rmask'])
    cs_d = D['cs']
    yT_d = D.get('yT')

    lam_init = 0.8 - 0.6 * math.exp(-0.3 * layer_idx)

    with ExitStack() as es:
        k.es = es
        sb = lambda name, shape, dt: k.sb(name + sfx, shape, dt)
        kT = sb("kT", [128, S], BF16)
        vA = sb("vA", [128, NT, 129], BF16)
        wfm = sb("wfm", [128, 8, NFM * 128], BF16)
        wtm = sb("wtm", [128, 8, NTM], BF16)
        cmf3 = sb("cmf3", [128, 3, 128], F32)
        cmb = sb("cmb", [128, NCM, 128], BF16)
        cc = sb("cc", [128, NCC], F32)
        crow = sb("crow_sb", [128, 3 * 128], F32)
        lamt = sb("lamt", [128, 256], F32)
        rmask = sb("rmask_sb", [128, TB], F32)
        zrow = sb("zrow", [128, TB], BF16)
        kmT = sb("kmT", [128, 256], BF16)
        vmA = sb("vmA", [128, 2, 129], BF16)
        S_h = sb("S_h", [128, 128], F32)
        Sb_h = sb("Sb_h", [128, 8, 128], BF16)
        dS = sb("dS", [128, 8, 128], F32)
        C_m = sb("C_m", [128, 129], F32)
        Cb_m = sb("Cb_m", [128, 8, 129], BF16)
        dC = sb("dC", [128, 8, 129], F32)
        sm = sb("sm", [128, 64], F32)
        stage = sb("stage", [128, NTM], F32)
        xt = sb("xt", [128, 4, TB], F32)
        xtm = xt[:].rearrange("p c t -> p (c t)").rearrange("p (c t) -> p c t", c=8)
        xb = sb("xb", [128, 8, TB], BF16)
        sqb = sb("sqb", [128, 2, TB], BF16)
        RX = sb("RX", [128, TB], F32)
        rxc = sb("rxc", [128, 8], F32)
        Tall = sb("Tall", [128, 6, TB], F32)
        T = [Tall[:, i, :] for i in range(6)]
        xt2 = Tall[:, 2:6, :]
        Bt = [sb(f"B{i}", [128, TB], BF16) for i in range(3)]
        cosT = sb("cosT", [128, TB], F32)
        sinT = sb("sinT", [128, TB], F32)
        qhT = sb("qhT", [128, TB], BF16)
        qtT = sb("qtT", [128, TB], BF16)
        ktT = sb("ktT", [128, TB], BF16)
        qcT = sb("qcT", [128, TB], BF16)
        kcT = sb("kcT", [128, TB], BF16)
        cqx = sb("cqx", [128, TB + 3], F32)
        ckx = sb("ckx", [128, TB + 3], F32)
        qxn = sb("qxn", [128, TB], BF16)
        bcum = sb("bcum", [128, TB], F32)
        av = sb("av", [128, 4, 128], BF16)
        cvA = sb("cvA", [128, 4, 129], BF16)
        gates = sb("gates", [128, 4, 5, 128], BF16)
        gcol = sb("gcol", [128, 4, 16], F32)
        kTM = sb("kTM", [128, 4, 128], BF16)
        kTM2 = sb("kTM2", [128, 4, 128], BF16)
        AT = sb("AT", [128, 2, 128], BF16)
        PT = sb("PT", [128, 4, TB], BF16)
        yTM = sb("yTM", [128, 4, 4, 128], BF16)
        yT = sb("yT_sb", [128, 4, TB], BF16)
        yTr = sb("yTr", [128, 2, 4, TB], BF16) if fused else None
        G = PSB[0:3]
        PS_S = PSB[3:5]
        PS_O = PSB[5:8]
        gi = [0]

        def gbank():
            b = G[gi[0] % 3]
            gi[0] += 1
            return b

        cmF = lambda i: cmf3[:, i - CM_TRI64, :]
        cmB = lambda i: cmb[:, i, :]
        col = lambda i, n=1: cc[:, i:i + n]
        epsc = col(CC_EPS)

        try:
            k.dma(stage[:, 0:NCM * 128], cmat_d.rearrange("p m n -> p (m n)"), "c0")
            k.dma(cmf3[:], cmat_d[:, CM_TRI64:CM_TRI64 + 3, :], "c5")
            k.dma(cc[:], ccol_d[:, :], "c1")
            k.dma(crow[:], crow_d[0:1, :].partition_broadcast(128), "c2")
            k.dma(lamt[:], lam_d[0:1, :].partition_broadcast(128), "c3")
            k.dma(rmask[:], rm_d[:, :], "c4")
            k.cp('dve', cmb[:].rearrange("p m n -> p (m n)"), stage[:, 0:NCM * 128])
            k.memset('pool', zrow[:], 0.0)
            k.memset('pool', vA[:, :, 128:129], 1.0)
            k.memset('pool', cvA[:, :, 128:129], 1.0)
            k.memset('pool', vmA[:, :, 128:129], 1.0)
            k.memset('pool', S_h[:], 0.0)
            k.memset('pool', C_m[:], 0.0)
            k.memset('pool', cqx[:, 0:3], 0.0)
            k.memset('pool', ckx[:, 0:3], 0.0)
            k.memset('pool', AT[:], 0.0)
            k.memset('dve', sm[:], 0.0)
            wv = wfm_d.rearrange("(c p) n -> p c n", p=128)
            for c in range(8):
                k.dma(stage[:, 0:NFM * 128], wv[:, c, :], "st")
                k.ts('dve', wfm[:, c, :], stage[:, 0:NFM * 128], col(CC_NG + c))
            wv = wtm_d.rearrange("(c p) n -> p c n", p=128)
            for c in range(8):
                k.dma(stage[:, 0:NTM], wv[:, c, :], "st")
                k.ts('dve', wtm[:, c, :], stage[:, 0:NTM], col(CC_NG + c))

            _lvl(2)
            LB, OMLB, NLAM, R1 = sm[:, 0:1], sm[:, 1:2], sm[:, 2:3], sm[:, 3:4]
            if layer_idx == 0:
                k.memset('dve', LB, 0.0)
            else:
                k.tt('dve', sm[:, 4:5], col(CC_LBL), col(CC_LBL + 1), ALU.subtract)
                k.act(sm[:, 5:6], sm[:, 4:5], AF.Exp)
                k.ts('dve', sm[:, 5:6], sm[:, 5:6], 1.0, None, op0=ALU.add)
                k.recip(LB, sm[:, 5:6])
            k.ts('dve', OMLB, LB, -1.0, 1.0, op0=ALU.mult, op1=ALU.add)
            k.tt('dve', T[0][:, 0:64], lamt[:, 0:64], lamt[:, 64:128], ALU.mult)
            k.tt('dve', T[0][:, 64:128], lamt[:, 128:192], lamt[:, 192:256], ALU.mult)
            k.s.emit('dve', lambda e: e.reduce_sum(sm[:, 6:7], T[0][:, 0:64], mybir.AxisListType.X),
                     outs=[sm[:, 6:7]], ins=[T[0][:, 0:64]])
            k.s.emit('dve', lambda e: e.reduce_sum(sm[:, 7:8], T[0][:, 64:128], mybir.AxisListType.X),
                     outs=[sm[:, 7:8]], ins=[T[0][:, 64:128]])
            k.act(sm[:, 8:10], sm[:, 6:8], AF.Exp)
            k.tt('dve', sm[:, 10:11], sm[:, 9:10], sm[:, 8:9], ALU.subtract)
            k.ts('dve', NLAM, sm[:, 10:11], -lam_init, None, op0=ALU.add)

            _lvl(3)
            mv = memT_d.rearrange("(c p) n -> p c n", p=128)
            k.dma(xtm, mv[:, :, :], "x0")
            wk = wkv_d.rearrange("(c p) n -> p c n", p=128)
            mw = xb[:, :, 256:512]
            for c in range(8):
                k.dma(stage[:, 0:256], wk[:, c, :], "st")
                k.ts('dve', mw[:, c, :], stage[:, 0:256], col(CC_MNG + c))
                k.cp('pool', xb[:, c, 0:256], xtm[:, c, :])
            for mt in range(2):
                pc = gbank()
                for c in range(8):
                    k.act(sqb[:, c % 2, 0:128], xtm[:, c, mt * 128:(mt + 1) * 128], AF.Square)
                    k.mm(pc[:, 0:1], sqb[:, c % 2, 0:128], cmB(CM_ONES)[:, 0:1], start=(c == 0), stop=(c == 7))
                k.rsqrt(gcol[:, 0, 0:1], pc[:, 0:1], 1.0 / D_MODEL, epsc, gcol[:, 0, 1:2])
                pk = gbank()
                for c in range(8):
                    k.mm(pk[:, 0:256], xb[:, c, mt * 128:(mt + 1) * 128], mw[:, c, :], start=(c == 0), stop=(c == 7))
                k.ts('dve', T[0][:, 0:128], pk[:, 0:128], gcol[:, 0, 0:1])
                k.ts('dve', vmA[:, mt, 0:128], pk[:, 128:256], gcol[:, 0, 0:1])
                k.act(T[1][:, 0:128], T[0][:, 0:128], AF.Square, accum_out=gcol[:, 0, 2:3])
                k.rsqrt(gcol[:, 0, 3:4], gcol[:, 0, 2:3], 1.0 / 128, epsc, gcol[:, 0, 4:5])
                k.ts('dve', Bt[0][:, 0:128], T[0][:, 0:128], gcol[:, 0, 3:4])
                pt_ = gbank()
                ptb = pt_[:, 0:64].bitcast(BF16)
                k.tr(ptb, Bt[0][:, 0:128], cmB(CM_IDENT))
                k.tt('dve', sm[:, 11:12], col(CC_GXQ), col(CC_GXK), ALU.mult)
                k.ts('dve', kmT[:, mt * 128:(mt + 1) * 128], ptb, sm[:, 11:12], 128 ** -0.5, op0=ALU.mult, op1=ALU.mult)

            _lvl(4)
            for bi in (range(NBLK) if do_rope else ()):
                tsl = slice(bi * TB, (bi + 1) * TB)
                pi_ = T[5].bitcast(I32)
                k.dma(pi_[:, :], pos_d[0:1, tsl].partition_broadcast(128), "x1")
                k.cp('dve', T[0][:], pi_[:, :])
                k.ts('dve', T[1][:], T[0][:], col(CC_INVF))
                for which, dst in ((0, cosT), (1, sinT)):
                    src = T[1]
                    if which == 0:
                        k.ts('dve', T[2][:], T[1][:], math.pi / 2, None, op0=ALU.add)
                        src = T[2]
                    k.ts('dve', T[3][:], src[:], 1.0 / TWO_PI, 12582912.0, op0=ALU.mult, op1=ALU.add)
                    k.ts('dve', T[3][:], T[3][:], -12582912.0, None, op0=ALU.add)
                    k.stt(T[4][:], T[3][:], -6.28125, src[:], ALU.mult, ALU.add)
                    k.stt(T[4][:], T[3][:], -(TWO_PI - 6.28125), T[4][:], ALU.mult, ALU.add)
                    k.ts('dve', T[4][:], T[4][:], 3.1415925, -3.1415925, op0=ALU.min, op1=ALU.max)
                    k.act(dst[:], T[4][:], AF.Sin)
                k.ts('dve', sinT[:], sinT[:], col(CC_SGN))
                k.dma(cs_d[0, :, tsl], cosT[:], "y0")
                k.dma(cs_d[1, :, tsl], sinT[:], "y1")

            xv = xT_d.rearrange("(c p) t -> p c t", p=128)
            yv = yT_d.rearrange("(g p) t -> p g t", p=128) if yT_d is not None else None

            for bi in range(NBLK):
                t0 = bi * TB
                tsl = slice(t0, t0 + TB)
                _lvl(5)
                if bi == 0:
                    k.dma(cosT[:], cs_d[0, :, tsl], "x2")
                    k.dma(sinT[:], cs_d[1, :, tsl], "x3")
                    k.dma(xt[:], xv[:, 0:4, tsl], "x0")
                prx = gbank()
                prc = gbank()
                if bi == 0:
                    k.dma(xt2, xv[:, 4:8, tsl], "x1")
                for c in range(8):
                    xsrc = xt[:, c, :] if c < 4 else xt2[:, c - 4, :]
                    k.cp('act', xb[:, c, :], xsrc)
                    k.act(sqb[:, c % 2, :], xsrc, AF.Square)
                    k.mm(prx[:], cmB(CM_ONES), sqb[:, c % 2, :], start=(c == 0), stop=(c == 7))
                    for tt_ in range(4):
                        k.mm(prc[:, tt_:tt_ + 1], sqb[:, c % 2, tt_ * 128:(tt_ + 1) * 128], cmB(CM_ONES)[:, 0:1],
                             start=(c == 0 and tt_ == 0), stop=(c == 7), skip=True)
                if bi + 1 < NBLK:
                    k.dma(xt[:], xv[:, 0:4, t0 + TB:t0 + 2 * TB], "x0")
                k.rsqrt(RX[:], prx[:], 1.0 / D_MODEL, epsc, T[0][:])
                k.rsqrt(rxc[:, 0:4], prc[:, 0:4], 1.0 / D_MODEL, epsc, rxc[:, 4:8])
                k.ts('dve', rxc[:, 4:8], rxc[:, 0:4], -1.0)

                def fm_proj(j):
                    p = gbank()
                    for c in range(8):
                        k.mm(p[:], wfm[:, c, j * 128:(j + 1) * 128], xb[:, c, :], start=(c == 0), stop=(c == 7))
                    return p

                _lvl(6)
                p = fm_proj(0)
                k.tt('dve', T[0][:], p[:], RX[:], ALU.mult)
                k.act(T[1][:], T[0][:], AF.Exp, scale=-1.0)
                k.ts('dve', T[1][:], T[1][:], 1.0, None, op0=ALU.add)
                k.recip(T[1][:], T[1][:])
                k.stt(T[0][:], T[0][:], 128 ** -0.5, T[1][:], ALU.mult, ALU.mult)
                p = fm_proj(1)
                k.tt('dve', T[1][:], p[:], RX[:], ALU.mult)
                k.act(T[2][:], T[1][:], AF.Exp, scale=-1.0)
                k.ts('dve', T[2][:], T[2][:], 1.0, None, op0=ALU.add)
                k.recip(T[2][:], T[2][:])
                k.ts('dve', T[2][:], T[2][:], OMLB, LB, op0=ALU.mult, op1=ALU.add)
                k.act(T[3][:], T[2][:], AF.Ln)
                k.ts('dve', T[2][:], T[2][:], -1.0, 1.0, op0=ALU.mult, op1=ALU.add)
                k.scan(bcum[:], rmask[:], T[3][:], 0.0, ALU.mult, ALU.add)
                for c8 in range(8):
                    cs_ = slice(c8 * 64, (c8 + 1) * 64)
                    k.ts('dve', T[3][:, cs_], bcum[:, cs_], bcum[:, c8 * 64 + 31:c8 * 64 + 32], None, op0=ALU.subtract)
                k.act(T[4][:], T[3][:], AF.Exp)
                k.act(T[5][:], T[3][:], AF.Exp, scale=-1.0)
                k.tt('dve', qtT[:], T[0][:], T[4][:], ALU.mult)
                k.tt('dve', ktT[:], T[2][:], T[5][:], ALU.mult)
                EREF, ELAST, CFAC = sm[:, 16:24], sm[:, 24:32], sm[:, 32:40]
                k.act(EREF, bcum[:, 31:TB:64], AF.Exp)
                k.act(ELAST, bcum[:, 63:TB:64], AF.Exp)
                k.tt('dve', sm[:, 40:48], bcum[:, 63:TB:64], bcum[:, 31:TB:64], ALU.subtract)
                k.act(CFAC, sm[:, 40:48], AF.Exp)

                _lvl(7)
                def qk_norm_rope(j, gcol_, scale_, dst):
                    p_ = fm_proj(j)
                    k.tt('dve', T[0][:], p_[:], RX[:], ALU.mult)
                    k.act(Bt[0][:], T[0][:], AF.Square)
                    pss = gbank()
                    k.mm(pss[:], cmB(CM_BLK64), Bt[0][:])
                    k.rsqrt(T[1][:], pss[:], 1.0 / 64, epsc, T[2][:])
                    k.ts('dve', Bt[1][:], T[0][:], gcol_)
                    pp = gbank()
                    k.mm(pp[:], cmB(CM_PROPE), Bt[1][:])
                    k.tt('dve', T[2][:], Bt[1][:], cosT[:], ALU.mult)
                    k.tt('dve', T[3][:], pp[:], sinT[:], ALU.mult)
                    k.tt('dve', T[2][:], T[2][:], T[3][:], ALU.add)
                    k.stt(dst, T[2][:], scale_, T[1][:], ALU.mult, ALU.mult)

                qk_norm_rope(2, col(CC_GQ), 0.125, qhT[:])
                qk_norm_rope(3, col(CC_GK), 1.0, kT[:, tsl])
                if bi + 1 < NBLK:
                    k.dma(cosT[:], cs_d[0, :, t0 + TB:t0 + 2 * TB], "x2")
                    k.dma(sinT[:], cs_d[1, :, t0 + TB:t0 + 2 * TB], "x3")

                _lvl(8)
                def conv_silu(j, ext, wcol, bcol, scale_, dst):
                    p_ = fm_proj(j)
                    k.tt('dve', ext[:, 3:TB + 3], p_[:], RX[:], ALU.mult)
                    k.ts('dve', T[0][:], ext[:, 0:TB], cc[:, wcol:wcol + 1], cc[:, bcol:bcol + 1], op0=ALU.mult, op1=ALU.add)
                    for jj in range(1, 4):
                        k.stt(T[0][:], ext[:, jj:TB + jj], cc[:, wcol + jj:wcol + jj + 1], T[0][:], ALU.mult, ALU.add)
                    k.cp('pool', ext[:, 0:3], ext[:, TB:TB + 3])
                    k.act(T[1][:], T[0][:], AF.Exp, scale=-1.0)
                    k.ts('dve', T[1][:], T[1][:], 1.0, None, op0=ALU.add)
                    k.recip(T[1][:], T[1][:])
                    k.stt(dst, T[0][:], scale_, T[1][:], ALU.mult, ALU.mult)

                conv_silu(4, cqx, CC_CWQ, CC_CBQ, 1.0, qcT[:])
                conv_silu(5, ckx, CC_CWK, CC_CBK, 128 ** -0.5, kcT[:])

                _lvl(9)
                p = fm_proj(6)
                k.tt('dve', T[0][:], p[:], RX[:], ALU.mult)
                k.act(Bt[0][:], T[0][:], AF.Square)
                pss = gbank()
                k.mm(pss[:], cmB(CM_ONES), Bt[0][:])
                k.rsqrt(T[1][:], pss[:], 1.0 / 128, epsc, T[2][:])
                k.tt('dve', qxn[:], T[0][:], T[1][:], ALU.mult)
                for mt in range(2):
                    pS = gbank()
                    k.mm(pS[:], kmT[:, mt * 128:(mt + 1) * 128], qxn[:])
                    k.act(PT[:, mt, :], pS[:], AF.Exp)

                _lvl(10)
                for tt_ in range(4):
                    tcs = slice(tt_ * 128, (tt_ + 1) * 128)
                    pA, pB, pC = gbank(), gbank(), gbank()
                    for c in range(8):
                        k.mm(pA[:], xb[:, c, tcs], wtm[:, c, 0:512], start=(c == 0), stop=(c == 7))
                    for c in range(8):
                        k.mm(pB[:], xb[:, c, tcs], wtm[:, c, 512:1024], start=(c == 0), stop=(c == 7))
                    for c in range(8):
                        k.mm(pC[:, 0:2], xb[:, c, tcs], wtm[:, c, 1024:1026], start=(c == 0), stop=(c == 7))
                    rc = rxc[:, tt_:tt_ + 1]
                    nrc = rxc[:, 4 + tt_:5 + tt_]
                    k.act(T[0][:], pA[:], AF.Exp, scale=nrc)
                    k.ts('dve', T[0][:], T[0][:], 1.0, None, op0=ALU.add)
                    k.recip(T[0][:], T[0][:])
                    k.stt(gates[:, tt_, 0:4, :].rearrange("p g d -> p (g d)"), pA[:], rc, T[0][:], ALU.mult, ALU.mult)
                    k.act(T[1][:, 0:128], pB[:, 0:128], AF.Exp, scale=nrc)
                    k.ts('dve', T[1][:, 0:128], T[1][:, 0:128], 1.0, None, op0=ALU.add)
                    k.recip(gates[:, tt_, 4, :], T[1][:, 0:128])
                    k.ts('dve', av[:, tt_, :], pB[:, 128:256], rc)
                    k.ts('dve', vA[:, bi * 4 + tt_, 0:128], pB[:, 256:384], rc)
                    k.ts('dve', cvA[:, tt_, 0:128], pB[:, 384:512], rc)
                    g_ = gcol[:, tt_, :]
                    k.stt(g_[:, 0:1], pC[:, 0:1], rc, col(CC_GBI), ALU.mult, ALU.add)
                    k.stt(g_[:, 2:3], pC[:, 1:2], rc, col(CC_GBF), ALU.mult, ALU.add)
                    k.act(g_[:, 3:4], g_[:, 2:3], AF.Exp, scale=-1.0)
                    k.act(g_[:, 4:5], g_[:, 3:4], AF.Ln, bias=col(CC_ONE))
                    k.ts('dve', g_[:, 1:2], g_[:, 4:5], -1.0)
                    k.ts('dve', g_[:, 5:6], g_[:, 1:2], col(CC_IND0))
                    k.ts('dve', g_[:, 6:7], g_[:, 1:2], col(CC_IND1))
                    pc2 = gbank()
                    k.mm(pc2[:, 0:1], cmF(CM_TRI64), g_[:, 1:2])
                    k.mm(pc2[:, 1:2], cmF(CM_BLK64), g_[:, 1:2], skip=True)
                    k.mm(pc2[:, 2:4], cmF(CM_ONES), g_[:, 5:7], skip=True)
                    k.cp('dve', g_[:, 7:11], pc2[:, 0:4])
                    k.tt('dve', g_[:, 11:12], g_[:, 0:1], g_[:, 7:8], ALU.subtract)
                    k.tt('dve', g_[:, 12:13], g_[:, 11:12], g_[:, 8:9], ALU.add)
                    k.act(g_[:, 13:14], g_[:, 11:12], AF.Exp)
                    k.act(g_[:, 14:15], g_[:, 12:13], AF.Exp)
                    k.act(g_[:, 15:16], g_[:, 7:8], AF.Exp)
                    k.act(sm[:, 48 + 2 * tt_:50 + 2 * tt_], g_[:, 9:11], AF.Exp)

                _lvl(11)
                for tt_ in range(4):
                    tcs = slice(tt_ * 128, (tt_ + 1) * 128)
                    po = gbank()
                    for mt in range(2):
                        k.mm(po[:, 0:129], PT[:, mt, tcs], vmA[:, mt, :], start=(mt == 0), stop=(mt == 1))
                    k.recip(gcol[:, tt_, 5:6], po[:, 128:129])
                    k.stt(yTM[:, tt_, 3, :], po[:, 0:128], gcol[:, tt_, 5:6], gates[:, tt_, 3, :], ALU.mult, ALU.mult)

                _lvl(12)
                for tt_ in range(4):
                    tcs = slice(tt_ * 128, (tt_ + 1) * 128)
                    ptr = gbank()
                    ptb = ptr[:, 0:64].bitcast(BF16)
                    k.tr(ptb, ktT[:, tcs], cmB(CM_IDENT))
                    k.cp('dve', kTM[:, tt_, :], ptb)
                _lvl(12.1)
                for tt_ in range(4):
                    pd = gbank()
                    for hh in range(2):
                        c8 = tt_ * 2 + hh
                        rs = slice(hh * 64, hh * 64 + 64)
                        k.mm(pd[:, hh * 128:(hh + 1) * 128], kTM[rs, tt_, :], av[rs, tt_, :], skip=(hh == 1))
                        k.ts('dve', dS[:, c8, :], pd[:, hh * 128:(hh + 1) * 128], CFAC[:, c8:c8 + 1])
                _lvl(12.2)
                for c8 in range(8):
                    k.ts('dve', Sb_h[:, c8, :], S_h[:], EREF[:, c8:c8 + 1])
                    k.stt(S_h[:], S_h[:], ELAST[:, c8:c8 + 1], dS[:, c8, :], ALU.mult, ALU.add)
                _lvl(12.3)
                for tt_ in range(4):
                    tcs = slice(tt_ * 128, (tt_ + 1) * 128)
                    pa = gbank()
                    for hh in range(2):
                        cs_ = slice(tt_ * 128 + hh * 64, tt_ * 128 + hh * 64 + 64)
                        rs = slice(hh * 64, hh * 64 + 64)
                        k.mm(pa[rs, hh * 64:hh * 64 + 64], ktT[:, cs_], qtT[:, cs_], skip=(hh == 1))
                        k.stt(AT[rs, 0, hh * 64:hh * 64 + 64], pa[rs, hh * 64:hh * 64 + 64], 3.0e38,
                              cmB(CM_TRI64)[rs, hh * 64:hh * 64 + 64], ALU.min, ALU.mult)
                    _lvl(12.4)
                    po = gbank()
                    for hh in range(2):
                        cs_ = slice(tt_ * 128 + hh * 64, tt_ * 128 + hh * 64 + 64)
                        rs = slice(hh * 64, hh * 64 + 64)
                        k.mm(po[rs, 0:128], AT[rs, 0, hh * 64:hh * 64 + 64], av[rs, tt_, :], start=True, stop=False, skip=True)
                        k.mm(po[rs, 0:128], qtT[:, cs_], Sb_h[:, tt_ * 2 + hh, :], start=False, stop=True, skip=True)
                    g_ = gcol[:, tt_, :]
                    k.act(T[0][:, 0:128], po[:, 0:128], AF.Square, accum_out=g_[:, 5:6])
                    k.rsqrt(g_[:, 6:7], g_[:, 5:6], 1.0 / 128, epsc, g_[:, 5:6])
                    k.stt(T[0][:, 0:128], po[:, 0:128], g_[:, 6:7], crow[:, CR_HG * 128:(CR_HG + 1) * 128], ALU.mult, ALU.mult)
                    k.tt('dve', yTM[:, tt_, 0, :], T[0][:, 0:128], gates[:, tt_, 0, :], ALU.mult)

                _lvl(13)
                for tt_ in range(4):
                    tcs = slice(tt_ * 128, (tt_ + 1) * 128)
                    ptr = gbank()
                    ptb = ptr[:, 0:64].bitcast(BF16)
                    k.tr(ptb, kcT[:, tcs], cmB(CM_IDENT))
                    k.ts('dve', kTM2[:, tt_, :], ptb, gcol[:, tt_, 14:15])
                for tt_ in range(4):
                    for hh in range(2):
                        c8 = tt_ * 2 + hh
                        rs = slice(hh * 64, hh * 64 + 64)
                        pd = gbank()
                        k.mm(pd[:, 0:129], kTM2[rs, tt_, :], cvA[rs, tt_, :])
                        k.cp('dve', dC[:, c8, :], pd[:, 0:129])
                for c8 in range(8):
                    k.cp('dve', Cb_m[:, c8, :], C_m[:])
                    k.stt(C_m[:], C_m[:], sm[:, 48 + c8:49 + c8], dC[:, c8, :], ALU.mult, ALU.add)
                for tt_ in range(4):
                    tcs = slice(tt_ * 128, (tt_ + 1) * 128)
                    g_ = gcol[:, tt_, :]
                    pa = gbank()
                    k.mm(pa[:, 0:128], kcT[:, tcs], qcT[:, tcs])
                    k.stt(AT[:, 1, :], pa[:, 0:128], g_[:, 13:14], cmB(CM_TRI64), ALU.mult, ALU.mult)
                    po = gbank()
                    for hh in range(2):
                        cs_ = slice(tt_ * 128 + hh * 64, tt_ * 128 + hh * 64 + 64)
                        rs = slice(hh * 64, hh * 64 + 64)
                        k.mm(po[rs, 0:129], AT[rs, 1, hh * 64:hh * 64 + 64], cvA[rs, tt_, :], start=True, stop=False, skip=True)
                        k.mm(po[rs, 0:129], qcT[:, cs_], Cb_m[:, tt_ * 2 + hh, :], start=False, stop=True, skip=True)
                    k.act(g_[:, 5:6], po[:, 128:129], AF.Abs, scale=g_[:, 15:16])
                    k.ts('dve', g_[:, 5:6], g_[:, 5:6], 1.0, None, op0=ALU.max)
                    k.recip(g_[:, 5:6], g_[:, 5:6])
                    k.tt('dve', g_[:, 5:6], g_[:, 5:6], g_[:, 15:16], ALU.mult)
                    k.stt(T[0][:, 0:128], po[:, 0:128], g_[:, 5:6], gates[:, tt_, 4, :], ALU.mult, ALU.mult)
                    k.act(T[1][:, 0:128], T[0][:, 0:128], AF.Square, accum_out=g_[:, 6:7])
                    k.rsqrt(g_[:, 5:6], g_[:, 6:7], 1.0 / 128, epsc, g_[:, 6:7])
                    k.stt(T[0][:, 0:128], T[0][:, 0:128], g_[:, 5:6], crow[:, CR_MG * 128:(CR_MG + 1) * 128], ALU.mult, ALU.mult)
                    k.tt('dve', yTM[:, tt_, 2, :], T[0][:, 0:128], gates[:, tt_, 2, :], ALU.mult)

                _lvl(14)
                if bi + 1 < NBLK:
                    k.dma(xt2, xv[:, 4:8, t0 + TB:t0 + 2 * TB], "x1")
                for ob in range(3):
                    k.mm(PS_O[ob][:], cmB(CM_ZERO), zrow[:], start=True, stop=False, sig=True, skip=True)
                oslot = {}
                for mp in range(2):
                    for qq in range(4):
                        idx = mp * 4 + qq
                        oslot[(mp, qq)] = (PS_O[idx // 3], (idx % 3) * 129)
                nkt = bi * 4 + 4
                def s_mm(j_, mp_):
                    q0_ = max(j_ - bi * 4, 0) * 128
                    rs_ = slice(mp_ * 64, mp_ * 64 + 64)
                    k.mm(PS_S[mp_][:, q0_:TB], kT[rs_, j_ * 128:(j_ + 1) * 128], qhT[rs_, q0_:TB])

                s_mm(0, 0)
                s_mm(0, 1)
                for j in range(nkt):
                    jj = j - bi * 4
                    q0 = max(jj, 0) * 128
                    for mp in range(2):
                        rs = slice(mp * 64, mp * 64 + 64)
                        pt_ = PT[:, 2 * (j % 2) + mp, :]
                        k.act(pt_[:, q0:TB], PS_S[mp][:, q0:TB], AF.Exp)
                        if jj >= 0:
                            k.tt('pool', pt_[:, q0:q0 + 128], pt_[:, q0:q0 + 128], cmB(CM_TRI128), ALU.mult)
                        for qq in range(max(jj, 0), 4):
                            ob, oc = oslot[(mp, qq)]
                            last = (j == min(nkt - 1, bi * 4 + qq))
                            k.mm(ob[:, oc:oc + 129], pt_[:, qq * 128:(qq + 1) * 128], vA[:, j, :],
                                 start=False, stop=last, sig=last, skip=True)
                        if j + 1 < nkt:
                            s_mm(j + 1, mp)
                for tt_ in range(4):
                    g_ = gcol[:, tt_, :]
                    ob1, oc1 = oslot[(0, tt_)]
                    ob2, oc2 = oslot[(1, tt_)]
                    k.recip(g_[:, 5:6], ob1[:, oc1 + 128:oc1 + 129])
                    k.recip(g_[:, 6:7], ob2[:, oc2 + 128:oc2 + 129])
                    k.tt('dve', g_[:, 6:7], g_[:, 6:7], NLAM, ALU.mult)
                    k.ts('dve', T[0][:, 0:128], ob1[:, oc1:oc1 + 128], g_[:, 5:6])
                    k.stt(T[0][:, 0:128], ob2[:, oc2:oc2 + 128], g_[:, 6:7], T[0][:, 0:128], ALU.mult, ALU.add)
                    k.act(T[1][:, 0:128], T[0][:, 0:128], AF.Square, accum_out=g_[:, 5:6])
                    k.rsqrt(g_[:, 6:7], g_[:, 5:6], 1.0 / 128, epsc, g_[:, 5:6])
                    k.ts('dve', g_[:, 6:7], g_[:, 6:7], 1.0 - lam_init)
                    k.stt(T[0][:, 0:128], T[0][:, 0:128], g_[:, 6:7], crow[:, CR_SUB * 128:(CR_SUB + 1) * 128], ALU.mult, ALU.mult)
                    k.tt('dve', yTM[:, tt_, 1, :], T[0][:, 0:128], gates[:, tt_, 1, :], ALU.mult)

                _lvl(15)
                for tt_ in range(4):
                    pty = gbank()
                    ptb = pty[:, 0:256].bitcast(BF16)
                    for g4 in range(4):
                        k.tr(ptb[:, g4 * 128:(g4 + 1) * 128], yTM[:, tt_, g4, :], cmB(CM_IDENT))
                    k.cp('dve', yT[:, :, tt_ * 128:(tt_ + 1) * 128], ptb.rearrange("p (g t) -> p g t", g=4))
                if not fused:
                    k.dma(yv[:, :, tsl], yT[:], "yu", eng='pool')
                else:
                    ch, half = bi // 2, bi % 2
                    for r in range(4):
                        k.ts('pool', yTr[:, r % 2, :, :], yT[:], col(CC_RANK + r))
                        k.dma(D['yin'][ch, r].rearrange("(g p) t -> p g t", p=128)[:, :, half * TB:(half + 1) * TB],
                              yTr[:, r % 2, :, :], "ys%d" % r, eng='pool')
                    if half == 1 or bi == NBLK - 1:
                        k.cc("AllReduce", ALU.add, [[0, 1, 2, 3], [4, 5, 6, 7]],
                             D['yin'][ch].rearrange("r p t -> (r p) t"), D['yout'][ch], "cc")

        except _Stop:
            pass
        k.s.barrier()


def layer_dram(nc, S, sfx, shared=None):
    dr = lambda name, shape, dt=F32: nc.dram_tensor(name, shape, dt, kind="ExternalInput").ap()
    D = {}
    if shared is None:
        D['pos'] = dr("pos", [1, S], I32)
        D['memT'] = dr("memT", [D_MODEL, N_MEM])
        D['cmat'] = dr("cmat", [128, NCM, 128])
        D['rmask'] = dr("rmask", [128, TB])
        D['cs'] = nc.dram_tensor("cs_scr", [2, 128, S], F32, kind="Internal").ap()
    else:
        for n_ in ('pos', 'memT', 'cmat', 'rmask', 'cs'):
            D[n_] = shared[n_]
    D['w_fm'] = dr("w_fm" + sfx, [D_MODEL, NFM * 128])
    D['w_tm'] = dr("w_tm" + sfx, [D_MODEL, NTM])
    D['w_kv'] = dr("w_kv" + sfx, [D_MODEL, 256])
    D['ccol'] = dr("ccol" + sfx, [128, NCC])
    D['crow'] = dr("crow" + sfx, [1, 3 * 128])
    D['lam'] = dr("lam" + sfx, [1, 256])
    return D


def build_layer(S, layer_idx):
    nc = bass.Bass("TRN2", target_bir_lowering=False)
    D = layer_dram(nc, S, "")
    D['xT'] = nc.dram_tensor("xT", [D_MODEL, S], F32, kind="ExternalInput").ap()
    D['yT'] = nc.dram_tensor("yT", [4 * 128, S], BF16, kind="ExternalOutput").ap()
    with ExitStack() as es:
        es.enter_context(nc.allow_low_precision(reason="bf16 matmul operands"))
        k = K(nc, es)
        PSB = [k.ps(f"pb{i}", [128, 512]) for i in range(8)]
        emit_layer(nc, k, PSB, S, layer_idx, "", D, do_rope=True, fused=False)
        k.s.finish()
    return nc


def emit_outproj(nc, k, PSB, S, sfx, D, full):
    NBLK = S // TB
    with ExitStack() as es:
        k.es = es
        sb = lambda name, shape, dt: k.sb(name + sfx, shape, dt)
        stage = sb("ostage", [128, D_MODEL], F32)
        wo = sb("wo", [128, 16, 256], BF16)
        yt = sb("yt", [128, 16, TB], BF16)
        xo = sb("xo", [128, 2, TB], F32)
        oo = sb("oo", [128, 2, TB], F32)
        wv = D['w_own'].rearrange("(c p) n -> p c n", p=128)
        for c in range(16):
            k.dma(stage[:, 0:256], wv[:, c, :], "st")
            k.cp('dve', wo[:, c, :], stage[:, 0:256])
        if full:
            w = sb("wfull", [128, 16, D_MODEL], BF16)
            xt = sb("oxt", [128, 8, TB], F32)
            ot = sb("oot", [128, 8, TB], F32)
            wv = D['w_out'].rearrange("(c p) n -> p c n", p=128)
            for c in range(16):
                k.dma(stage[:], wv[:, c, :], "st")
                k.cp('dve', w[:, c, :], stage[:])
            xv = D['x_prev'].rearrange("(c p) t -> p c t", p=128)
            x1v = D['x1'].rearrange("(c p) t -> p c t", p=128)
        xov = D['x_own_prev'].rearrange("(c p) t -> p c t", p=128)
        outv = (D['x1own'] if full else D['out']).rearrange("(c p) t -> p c t", p=128)
        pi = [0]
        for bi in range(NBLK):
            tsl = slice(bi * TB, (bi + 1) * TB)
            ch, half = bi // 2, bi % 2
            k.dma(yt[:], D['yout'][ch].rearrange("(c p) t -> p c t", p=128)[:, :, half * TB:(half + 1) * TB], "x0")
            k.dma(xo[:], xov[:, :, tsl], "x1")
            if full:
                k.dma(xt[:], xv[:, :, tsl], "x2")
                for oc in range(8):
                    p = PSB[pi[0] % 8]
                    pi[0] += 1
                    for c in range(16):
                        k.mm(p[:], w[:, c, oc * 128:(oc + 1) * 128], yt[:, c, :], start=(c == 0), stop=(c == 15),
                             sig=(c == 15))
                    k.tt('dve', ot[:, oc, :], p[:], xt[:, oc, :], ALU.add)
                k.dma(x1v[:, :, tsl], ot[:], "y0")
            for oc in range(2):
                p = PSB[pi[0] % 8]
                pi[0] += 1
                for c in range(16):
                    k.mm(p[:], wo[:, c, oc * 128:(oc + 1) * 128], yt[:, c, :], start=(c == 0), stop=(c == 15),
                         sig=(c == 15))
                k.tt('dve', oo[:, oc, :], p[:], xo[:, oc, :], ALU.add)
            k.dma(outv[:, :, tsl], oo[:], "y1")
        k.s.barrier()


def build_fused(S):
    nc = bass.Bass("TRN2", target_bir_lowering=False)
    NCH = max(S // 1024, 1)
    D0 = layer_dram(nc, S, "_0")
    D1 = layer_dram(nc, S, "_1", shared=D0)
    x0 = nc.dram_tensor("xT", [D_MODEL, S], F32, kind="ExternalInput").ap()
    x0own = nc.dram_tensor("x0own", [256, S], F32, kind="ExternalInput").ap()
    w_out0 = nc.dram_tensor("w_out0", [2048, D_MODEL], F32, kind="ExternalInput").ap()
    w_own0 = nc.dram_tensor("w_own0", [2048, 256], F32, kind="ExternalInput").ap()
    w_own1 = nc.dram_tensor("w_own1", [2048, 256], F32, kind="ExternalInput").ap()
    out = nc.dram_tensor("out", [256, S], F32, kind="ExternalOutput").ap()
    yin = nc.dram_tensor("yin", [NCH, 4, 512, min(S, 1024)], BF16).ap()
    yout = nc.dram_tensor("yout", [NCH, 2048, min(S, 1024)], BF16).ap()
    x1 = nc.dram_tensor("x1", [D_MODEL, S], F32).ap()
    x1own = nc.dram_tensor("x1own", [256, S], F32).ap()
    for D in (D0, D1):
        D['yin'], D['yout'] = yin, yout
    D0['xT'] = x0
    D1['xT'] = x1
    with ExitStack() as es:
        es.enter_context(nc.allow_low_precision(reason="bf16 matmul operands"))
        k = K(nc, es)
        PSB = [k.ps(f"pb{i}", [128, 512]) for i in range(8)]
        emit_layer(nc, k, PSB, S, 0, "_a", D0, do_rope=True, fused=True)
        emit_outproj(nc, k, PSB, S, "_p", {'w_own': w_own0, 'w_out': w_out0, 'x_prev': x0, 'x1': x1,
                                           'x_own_prev': x0own, 'x1own': x1own, 'yout': yout}, full=True)
        emit_layer(nc, k, PSB, S, 1, "_b", D1, do_rope=False, fused=True)
        emit_outproj(nc, k, PSB, S, "_q", {'w_own': w_own1, 'x_own_prev': x1own, 'out': out, 'yout': yout}, full=False)
        k.s.finish()
    return nc


def head_cols(h):
    G = GROUP
    base = np.arange(128)
    off = {}
    names = ['a_q', 'a_f', 'a_i', 'a_g', 'b_q', 'b_k', 'b_v', 'b_g', 'c_q', 'c_k', 'c_v', 'c_o']
    o = 0
    for n in names:
        off[n] = o
        o += G
    off['c_i'] = o
    o += 4
    off['c_f'] = o
    o += 4
    off['c_g'] = o
    o += G
    off['x_q'] = o
    o += G
    off['x_g'] = o
    o += G
    hc = lambda n: off[n] + h * 128 + base
    fm = np.concatenate([hc('a_q'), hc('a_f'), hc('b_q'), hc('b_k'), hc('c_q'), hc('c_k'), hc('x_q')])
    tm = np.concatenate([hc('a_g'), hc('b_g'), hc('c_g'), hc('x_g'), hc('c_o'), hc('a_i'), hc('b_v'), hc('c_v'),
                         np.array([off['c_i'] + h, off['c_f'] + h])])
    return fm, tm


def layer_inputs(l, b, h, xT_b, inp, cmat, rmask):
    fm, tm = head_cols(h)
    w_in = inp['w_in'][l]
    cc = np.zeros((128, NCC), np.float32)
    cc[:, CC_NG:CC_NG + 8] = inp['norm_g'][l].reshape(8, 128).T
    cc[:, CC_MNG:CC_MNG + 8] = inp['mem_norm_g'][l].reshape(8, 128).T
    invf = (ROPE_THETA ** (-np.arange(0, 16, 2, dtype=np.float32) / 16)).astype(np.float32)
    for base in (0, 64):
        cc[base:base + 8, CC_INVF] = invf
        cc[base + 8:base + 16, CC_INVF] = invf
        cc[base:base + 8, CC_SGN] = -1.0
        cc[base + 8:base + 16, CC_SGN] = 1.0
    cc[:, CC_GQ] = np.tile(inp['diff_qk_norm_g'][l, 0], 2)
    cc[:, CC_GK] = np.tile(inp['diff_qk_norm_g'][l, 1], 2)
    cw = inp['mlstm_conv_w'][l]
    cb = inp['mlstm_conv_b'][l]
    cc[:, CC_CWQ:CC_CWQ + 4] = cw[:, h * 128:(h + 1) * 128].T
    cc[:, CC_CBQ] = cb[h * 128:(h + 1) * 128]
    cc[:, CC_CWK:CC_CWK + 4] = cw[:, GROUP + h * 128:GROUP + (h + 1) * 128].T
    cc[:, CC_CBK] = cb[GROUP + h * 128:GROUP + (h + 1) * 128]
    cc[:, CC_GXQ] = inp['xattn_qk_norm_g'][l, 0]
    cc[:, CC_GXK] = inp['xattn_qk_norm_g'][l, 1]
    cc[:, CC_LBL:CC_LBL + 2] = inp['hgrn_lb_logits'][:, h * 128:(h + 1) * 128].T
    cc[0:64, CC_IND0] = 1.0
    cc[64:128, CC_IND1] = 1.0
    cc[:, CC_GBI] = inp['mlstm_gate_b'][l, h]
    cc[:, CC_GBF] = inp['mlstm_gate_b'][l, 4 + h]
    cc[:, CC_EPS] = EPS
    cc[:, CC_ONE] = 1.0
    cc[:, CC_RANK + h] = 1.0
    crow = np.concatenate([inp['hgrn_norm_g'][l], inp['diff_subln_g'][l], inp['mlstm_norm_g'][l]])[None, :]
    wkv = inp['w_mem_kv'][l]
    wkv_h = np.concatenate([wkv[:, h * 128:(h + 1) * 128], wkv[:, GROUP + h * 128:GROUP + (h + 1) * 128]], axis=1)
    return {
        "xT": xT_b,
        "pos": np.ascontiguousarray(inp['positions'][b][None, :xT_b.shape[1]]).astype(np.int32),
        "w_fm": np.ascontiguousarray(w_in[:, fm]),
        "w_tm": np.ascontiguousarray(w_in[:, tm]),
        "memT": np.ascontiguousarray(inp['mem'][b].T),
        "w_kv": np.ascontiguousarray(wkv_h),
        "cmat": cmat,
        "ccol": cc,
        "crow": np.ascontiguousarray(crow.astype(np.float32)),
        "lam": np.ascontiguousarray(inp['diff_lambda'][l].reshape(1, 256)),
        "rmask": rmask,
    }


_CACHE = {}


def run_layer(l, xT, inp, S):
    key = ('L', l, S)
    if key not in _CACHE:
        _CACHE[key] = build_layer(S, l)
    nc = _CACHE[key]
    cmat = const_mats()
    rmask = np.ones((128, TB), np.float32)
    rmask[:, ::64] = 0.0
    in_maps = []
    for c in range(8):
        b, h = c // 4, c % 4
        in_maps.append(layer_inputs(l, b, h, np.ascontiguousarray(xT[b]), inp, cmat, rmask))
    res = run_bass_kernel_spmd(nc, in_maps, core_ids=list(range(8)))
    ys = [np.asarray(r["yT"]) for r in res.results]
    return np.stack([np.concatenate(ys[b * 4:(b + 1) * 4], axis=0) for b in range(BATCH)])


def perm_w_out(w):
    return np.ascontiguousarray(w.reshape(4, 4, 128, D_MODEL).transpose(1, 0, 2, 3).reshape(2048, D_MODEL))


def run_fused(xT, inp, S):
    key = ('F', S)
    if key not in _CACHE:
        _CACHE[key] = build_fused(S)
    nc = _CACHE[key]
    cmat = const_mats()
    rmask = np.ones((128, TB), np.float32)
    rmask[:, ::64] = 0.0
    wp = [perm_w_out(inp['w_out'][l]) for l in range(DEPTH)]
    in_maps = []
    for c in range(8):
        b, h = c // 4, c % 4
        xb_ = np.ascontiguousarray(xT[b])
        m = {}
        for l in range(DEPTH):
            li = layer_inputs(l, b, h, xb_, inp, cmat, rmask)
            for n_ in ('w_fm', 'w_tm', 'w_kv', 'ccol', 'crow', 'lam'):
                m[n_ + "_%d" % l] = li[n_]
            if l == 0:
                for n_ in ('xT', 'pos', 'memT', 'cmat', 'rmask'):
                    m[n_] = li[n_]
        fs = slice(h * 256, (h + 1) * 256)
        m['x0own'] = np.ascontiguousarray(xb_[fs])
        m['w_out0'] = wp[0]
        m['w_own0'] = np.ascontiguousarray(wp[0][:, fs])
        m['w_own1'] = np.ascontiguousarray(wp[1][:, fs])
        in_maps.append(m)
    res = run_bass_kernel_spmd(nc, in_maps, core_ids=list(range(8)))
    outs = [np.asarray(r["out"]) for r in res.results]
    return np.stack([np.concatenate(outs[b * 4:(b + 1) * 4], axis=0) for b in range(BATCH)])


def kernel(**inputs):
    inp = {k_: np.asarray(v) for k_, v in inputs.items()}
    x = inp['x']
    S = x.shape[1]
    xT = np.ascontiguousarray(x.transpose(0, 2, 1))
    oT = run_fused(xT, inp, S)
    return np.ascontiguousarray(oT.transpose(0, 2, 1)).astype(np.float32)
```

```python
import math
import os
from contextlib import ExitStack

import numpy as np
import concourse.bass as bass
import concourse.mybir as mybir
from concourse.bass_utils import run_bass_kernel_spmd

F32 = mybir.dt.float32
BF16 = mybir.dt.bfloat16
I32 = mybir.dt.int32
AF = mybir.ActivationFunctionType
ALU = mybir.AluOpType

D_MODEL = 1024
BATCH = 2
SEQ = 16384
DEPTH = 2
N_MEM = 256
GROUP = 512
EPS = 1e-6
ROPE_THETA = 500000.0
TB = 512
NFM = 7
NTM = 1026


class Sched:
    SAME = ('act', 'dve', 'pool')

    def __init__(self, nc, es):
        self.nc = nc
        self.engs = {'pe': nc.tensor, 'act': nc.scalar, 'dve': nc.vector, 'pool': nc.gpsimd, 'sp': nc.sync}
        self.sem = {e: es.enter_context(nc.semaphore('s_' + e)) for e in ('pe', 'act', 'dve', 'pool')}
        self.cnt = {e: 0 for e in self.sem}
        self.pending = {e: [] for e in self.sem}
        self.csem = {}
        self.ccnt = {}
        self.es = es
        self.waited = {}
        self.acc = {}
        self.ops = []
        self.nwaits = 0

    @staticmethod
    def box(ap):
        dims = ap.ap
        esz = mybir.dt.size(ap.dtype)
        off = ap.offset
        sp = str(ap.space)
        if sp == 'DRAM':
            ext = 1
            for s, c in dims:
                ext += (c - 1) * abs(s)
            return (ap.tensor.name, 0, 1, off * esz, (off + ext) * esz)
        pstep = dims[0][0]
        if pstep == 0:
            p0, f0 = 0, off
        else:
            p0, f0 = off // pstep, off % pstep
        ext = 1
        for s, c in dims[1:]:
            ext += (c - 1) * abs(s)
        if sp == 'PSUM':
            return (ap.tensor.name, (p0 // 32) * 32, ((p0 + dims[0][1] + 31) // 32) * 32, 0, 2048)
        return (ap.tensor.name, p0, p0 + dims[0][1], f0 * esz, (f0 + ext) * esz)

    def _sem_of(self, key):
        return self.sem[key] if key in self.sem else self.csem[key]

    def emit(self, eng, fn, outs=(), ins=(), sig=True, chan=None, inc=16):
        is_dma = chan is not None
        oboxes = [self.box(a) for a in outs]
        iboxes = [self.box(a) for a in ins]
        deps = set()
        for (name, p0, p1, f0, f1) in iboxes:
            for ent in self.acc.get(name, ()):
                if ent[5] and ent[0] < p1 and p0 < ent[1] and ent[2] < f1 and f0 < ent[3]:
                    deps.add(ent[4])
        for (name, p0, p1, f0, f1) in oboxes:
            for ent in self.acc.get(name, ()):
                if ent[0] < p1 and p0 < ent[1] and ent[2] < f1 and f0 < ent[3]:
                    deps.add(ent[4])
        need = {}
        for d in deps:
            deng, key, val = self.ops[d]
            if deng is not None and deng == eng and eng not in self.SAME:
                continue
            if val is None:
                raise RuntimeError('dependency on unsignalled op')
            if need.get(key, 0) < val:
                need[key] = val
        if is_dma:
            if chan not in self.csem:
                self.csem[chan] = self.es.enter_context(self.nc.semaphore('c_' + chan))
                self.ccnt[chan] = 0
            if self.ccnt[chan] > 0 and need.get(chan, 0) < self.ccnt[chan]:
                need[chan] = self.ccnt[chan]
        e = self.engs[eng]
        for key, val in need.items():
            if self.waited.get((eng, key), 0) >= val:
                continue
            e.wait_ge(self._sem_of(key), val)
            self.waited[(eng, key)] = val
            self.nwaits += 1
        inst = fn(e)
        opid = len(self.ops)
        if is_dma:
            self.ccnt[chan] += inc
            inst.then_inc(self.csem[chan], inc)
            self.ops.append([None, chan, self.ccnt[chan]])
        elif sig:
            self.cnt[eng] += 1
            inst.then_inc(self.sem[eng], 1)
            self.ops.append([eng, eng, self.cnt[eng]])
            for p in self.pending[eng]:
                self.ops[p][2] = self.cnt[eng]
            self.pending[eng] = []
        else:
            self.ops.append([eng, eng, None])
            self.pending[eng].append(opid)
        for (name, p0, p1, f0, f1) in oboxes:
            lst = self.acc.setdefault(name, [])
            lst[:] = [t for t in lst if not (p0 <= t[0] and t[1] <= p1 and f0 <= t[2] and t[3] <= f1)]
            lst.append((p0, p1, f0, f1, opid, True))
        rkey = None if is_dma else eng
        for (name, p0, p1, f0, f1) in iboxes:
            lst = self.acc.setdefault(name, [])
            for i, t in enumerate(lst):
                if (not t[5]) and t[0] == p0 and t[1] == p1 and t[2] == f0 and t[3] == f1 \
                        and self.ops[t[4]][0] == rkey and rkey is not None:
                    lst[i] = (p0, p1, f0, f1, opid, False)
                    break
            else:
                lst.append((p0, p1, f0, f1, opid, False))
        return inst

    def barrier(self):
        keys = [(key, self.cnt[key]) for key in self.sem] + [(key, self.ccnt[key]) for key in self.csem]
        for eng, e in self.engs.items():
            for key, val in keys:
                if val > 0 and self.waited.get((eng, key), 0) < val:
                    e.wait_ge(self._sem_of(key), val)
                    self.waited[(eng, key)] = val

    def finish(self):
        sp = self.engs['sp']
        for key in list(self.csem):
            if self.ccnt[key] > 0:
                sp.wait_ge(self.csem[key], self.ccnt[key])
        for key in self.sem:
            if self.cnt[key] > 0:
                sp.wait_ge(self.sem[key], self.cnt[key])


class _Stop(Exception):
    pass


def _lvl(n):
    if n > float(os.environ.get('KLVL', '99')):
        raise _Stop()


class K:
    def __init__(self, nc, es):
        self.nc = nc
        self.es = es
        self.s = Sched(nc, es)
        self.n = 0

    def sb(self, name, shape, dt):
        return self.es.enter_context(self.nc.sbuf_tensor(name, shape, dt))

    def ps(self, name, shape, dt=F32):
        return self.es.enter_context(self.nc.psum_tensor(name, shape, dt))

    def mm(self, out, lhsT, rhs, start=True, stop=True, sig=None, skip=False):
        if sig is None:
            sig = True
        return self.s.emit('pe', lambda e: e.matmul(out, lhsT, rhs, start=start, stop=stop,
                                                     skip_group_check=skip),
                           outs=[out], ins=[lhsT, rhs], sig=sig)

    def tr(self, out, in_, ident):
        return self.s.emit('pe', lambda e: e.transpose(out, in_, ident), outs=[out], ins=[in_, ident])

    def act(self, out, in_, func, bias=0.0, scale=1.0, accum_out=None):
        ins = [in_]
        outs = [out]
        if not isinstance(bias, (int, float)):
            ins.append(bias)
        if not isinstance(scale, (int, float)):
            ins.append(scale)
        if accum_out is not None:
            outs.append(accum_out)
        kw = {}
        if accum_out is not None:
            kw['accum_out'] = accum_out
        return self.s.emit('act', lambda e: e.activation(out, in_, func, bias=bias, scale=scale, **kw),
                           outs=outs, ins=ins)

    def ts(self, eng, out, in0, s1, s2=None, op0=ALU.mult, op1=None, accum_out=None):
        ins = [in0]
        outs = [out]
        for s_ in (s1, s2):
            if s_ is not None and not isinstance(s_, (int, float)):
                ins.append(s_)
        kw = {}
        if op1 is not None:
            kw['op1'] = op1
        if accum_out is not None:
            kw['accum_out'] = accum_out
            outs.append(accum_out)
        return self.s.emit(eng, lambda e: e.tensor_scalar(out, in0, s1, s2, op0, **kw), outs=outs, ins=ins)

    def tt(self, eng, out, in0, in1, op):
        return self.s.emit(eng, lambda e: e.tensor_tensor(out, in0, in1, op), outs=[out], ins=[in0, in1])

    def stt(self, out, in0, scalar, in1, op0, op1):
        ins = [in0, in1]
        if not isinstance(scalar, (int, float)):
            ins.append(scalar)
        return self.s.emit('dve', lambda e: e.scalar_tensor_tensor(out, in0, scalar, in1, op0, op1),
                           outs=[out], ins=ins)

    def cp(self, eng, out, in_):
        if eng == 'act':
            return self.s.emit('act', lambda e: e.copy(out, in_), outs=[out], ins=[in_])
        return self.s.emit(eng, lambda e: e.tensor_copy(out, in_), outs=[out], ins=[in_])

    def recip(self, out, in_):
        return self.s.emit('dve', lambda e: e.reciprocal(out, in_), outs=[out], ins=[in_])

    def memset(self, eng, out, val):
        return self.s.emit(eng, lambda e: e.memset(out, val), outs=[out], ins=[])

    def scan(self, out, d0, d1, init, op0, op1):
        return self.s.emit('dve', lambda e: e.tensor_tensor_scan(out, d0, d1, init, op0, op1),
                           outs=[out], ins=[d0, d1])

    def dma(self, out, in_, chan, eng='sp'):
        return self.s.emit(eng, lambda e: e.dma_start(out=out, in_=in_), outs=[out], ins=[in_], chan=chan)

    def cc(self, kind, op, groups, in_ap, out_ap, chan):
        return self.s.emit('pool', lambda e: e.collective_compute(kind, op, replica_groups=groups,
                                                                 ins=[in_ap.opt()], outs=[out_ap.opt()]),
                           outs=[out_ap], ins=[in_ap], chan=chan, inc=1)

    def rsqrt(self, out, in_, scale, epscol, tmp):
        self.act(tmp, in_, AF.Ln, bias=epscol, scale=scale)
        self.act(out, tmp, AF.Exp, scale=-0.5)


CM_IDENT, CM_TRI128, CM_TRI64, CM_BLK64, CM_ONES, CM_PROPE, CM_ZERO = range(7)
NCM = 7
(CC_NG, CC_MNG, CC_INVF, CC_SGN, CC_GQ, CC_GK, CC_CWQ, CC_CBQ, CC_CWK, CC_CBK, CC_GXQ, CC_GXK, CC_LBL,
 CC_IND0, CC_IND1, CC_GBI, CC_GBF, CC_EPS, CC_ONE, CC_HALFPI) = (0, 8, 16, 17, 18, 19, 20, 24, 25, 29, 30, 31,
                                                                 32, 34, 35, 36, 37, 38, 39, 40)
CC_RANK = 41
NCC = 45
CR_HG, CR_SUB, CR_MG = range(3)
TWO_PI = 2.0 * math.pi


def const_mats():
    m = np.zeros((NCM, 128, 128), np.float32)
    i = np.arange(128)
    m[CM_IDENT] = np.eye(128)
    m[CM_TRI128] = (i[None, :] >= i[:, None])
    same = (i[:, None] // 64) == (i[None, :] // 64)
    m[CM_TRI64] = m[CM_TRI128] * same
    m[CM_BLK64] = same
    m[CM_ONES] = 1.0
    for base in (0, 64):
        for j in range(8):
            m[CM_PROPE, base + 8 + j, base + j] = 1.0
            m[CM_PROPE, base + j, base + 8 + j] = 1.0
    return np.ascontiguousarray(m.transpose(1, 0, 2))


def emit_layer(nc, k, PSB, S, layer_idx, sfx, D, do_rope=True, fused=False):
    NBLK = S // TB
    NT = S // 128
    xT_d, pos_d, wfm_d, wtm_d = D['xT'], D['pos'], D['w_fm'], D['w_tm']
    memT_d, wkv_d, cmat_d, ccol_d, crow_d, lam_d, rm_d = (D['memT'], D['w_kv'], D['cmat'], D['ccol'], D['crow'],
                                                          D['lam'], D['rmask'])
    cs_d = D['cs']
    yT_d = D.get('yT')

    lam_init = 0.8 - 0.6 * math.exp(-0.3 * layer_idx)

    with ExitStack() as es:
        k.es = es
        sb = lambda name, shape, dt: k.sb(name + sfx, shape, dt)
        kT = sb("kT", [128, S], BF16)
        vA = sb("vA", [128, NT, 129], BF16)
        wfm = sb("wfm", [128, 8, NFM * 128], BF16)
        wtm = sb("wtm", [128, 8, NTM], BF16)
        cmf3 = sb("cmf3", [128, 3, 128], F32)
        cmb = sb("cmb", [128, NCM, 128], BF16)
        cc = sb("cc", [128, NCC], F32)
        crow = sb("crow_sb", [128, 3 * 128], F32)
        lamt = sb("lamt", [128, 256], F32)
        rmask = sb("rmask_sb", [128, TB], F32)
        zrow = sb("zrow", [128, TB], BF16)
        kmT = sb("kmT", [128, 256], BF16)
        vmA = sb("vmA", [128, 2, 129], BF16)
        S_h = sb("S_h", [128, 128], F32)
        Sb_h = sb("Sb_h", [128, 8, 128], BF16)
        dS = sb("dS", [128, 8, 128], F32)
        C_m = sb("C_m", [128, 129], F32)
        Cb_m = sb("Cb_m", [128, 8, 129], BF16)
        dC = sb("dC", [128, 8, 129], F32)
        sm = sb("sm", [128, 64], F32)
        stage = sb("stage", [128, NTM], F32)
        xt = sb("xt", [128, 4, TB], F32)
        xtm = xt[:].rearrange("p c t -> p (c t)").rearrange("p (c t) -> p c t", c=8)
        xb = sb("xb", [128, 8, TB], BF16)
        sqb = sb("sqb", [128, 2, TB], BF16)
        RX = sb("RX", [128, TB], F32)
        rxc = sb("rxc", [128, 8], F32)
        Tall = sb("Tall", [128, 6, TB], F32)
        T = [Tall[:, i, :] for i in range(6)]
        xt2 = Tall[:, 2:6, :]
        Bt = [sb(f"B{i}", [128, TB], BF16) for i in range(3)]
        cosT = sb("cosT", [128, TB], F32)
        sinT = sb("sinT", [128, TB], F32)
        qhT = sb("qhT", [128, TB], BF16)
        qtT = sb("qtT", [128, TB], BF16)
        ktT = sb("ktT", [128, TB], BF16)
        qcT = sb("qcT", [128, TB], BF16)
        kcT = sb("kcT", [128, TB], BF16)
        cqx = sb("cqx", [128, TB + 3], F32)
        ckx = sb("ckx", [128, TB + 3], F32)
        qxn = sb("qxn", [128, TB], BF16)
        bcum = sb("bcum", [128, TB], F32)
        av = sb("av", [128, 4, 128], BF16)
        cvA = sb("cvA", [128, 4, 129], BF16)
        gates = sb("gates", [128, 4, 5, 128], BF16)
        gcol = sb("gcol", [128, 4, 16], F32)
        kTM = sb("kTM", [128, 4, 128], BF16)
        kTM2 = sb("kTM2", [128, 4, 128], BF16)
        AT = sb("AT", [128, 2, 128], BF16)
        PT = sb("PT", [128, 4, TB], BF16)
        yTM = sb("yTM", [128, 4, 4, 128], BF16)
        yT = sb("yT_sb", [128, 4, TB], BF16)
        yTr = sb("yTr", [128, 2, 4, TB], BF16) if fused else None
        G = PSB[0:3]
        PS_S = PSB[3:5]
        PS_O = PSB[5:8]
        gi = [0]

        def gbank():
            b = G[gi[0] % 3]
            gi[0] += 1
            return b

        cmF = lambda i: cmf3[:, i - CM_TRI64, :]
        cmB = lambda i: cmb[:, i, :]
        col = lambda i, n=1: cc[:, i:i + n]
        epsc = col(CC_EPS)

        try:
            k.dma(stage[:, 0:NCM * 128], cmat_d.rearrange("p m n -> p (m n)"), "c0")
            k.dma(cmf3[:], cmat_d[:, CM_TRI64:CM_TRI64 + 3, :], "c5")
            k.dma(cc[:], ccol_d[:, :], "c1")
            k.dma(crow[:], crow_d[0:1, :].partition_broadcast(128), "c2")
            k.dma(lamt[:], lam_d[0:1, :].partition_broadcast(128), "c3")
            k.dma(rmask[:], rm_d[:, :], "c4")
            k.cp('dve', cmb[:].rearrange("p m n -> p (m n)"), stage[:, 0:NCM * 128])
            k.memset('pool', zrow[:], 0.0)
            k.memset('pool', vA[:, :, 128:129], 1.0)
            k.memset('pool', cvA[:, :, 128:129], 1.0)
            k.memset('pool', vmA[:, :, 128:129], 1.0)
            k.memset('pool', S_h[:], 0.0)
            k.memset('pool', C_m[:], 0.0)
            k.memset('pool', cqx[:, 0:3], 0.0)
            k.memset('pool', ckx[:, 0:3], 0.0)
            k.memset('pool', AT[:], 0.0)
            k.memset('dve', sm[:], 0.0)
            wv = wfm_d.rearrange("(c p) n -> p c n", p=128)
            for c in range(8):
                k.dma(stage[:, 0:NFM * 128], wv[:, c, :], "st")
                k.ts('dve', wfm[:, c, :], stage[:, 0:NFM * 128], col(CC_NG + c))
            wv = wtm_d.rearrange("(c p) n -> p c n", p=128)
            for c in range(8):
                k.dma(stage[:, 0:NTM], wv[:, c, :], "st")
                k.ts('dve', wtm[:, c, :], stage[:, 0:NTM], col(CC_NG + c))

            _lvl(2)
            LB, OMLB, NLAM, R1 = sm[:, 0:1], sm[:, 1:2], sm[:, 2:3], sm[:, 3:4]
            if layer_idx == 0:
                k.memset('dve', LB, 0.0)
            else:
                k.tt('dve', sm[:, 4:5], col(CC_LBL), col(CC_LBL + 1), ALU.subtract)
                k.act(sm[:, 5:6], sm[:, 4:5], AF.Exp)
                k.ts('dve', sm[:, 5:6], sm[:, 5:6], 1.0, None, op0=ALU.add)
                k.recip(LB, sm[:, 5:6])
            k.ts('dve', OMLB, LB, -1.0, 1.0, op0=ALU.mult, op1=ALU.add)
            k.tt('dve', T[0][:, 0:64], lamt[:, 0:64], lamt[:, 64:128], ALU.mult)
            k.tt('dve', T[0][:, 64:128], lamt[:, 128:192], lamt[:, 192:256], ALU.mult)
            k.s.emit('dve', lambda e: e.reduce_sum(sm[:, 6:7], T[0][:, 0:64], mybir.AxisListType.X),
                     outs=[sm[:, 6:7]], ins=[T[0][:, 0:64]])
            k.s.emit('dve', lambda e: e.reduce_sum(sm[:, 7:8], T[0][:, 64:128], mybir.AxisListType.X),
                     outs=[sm[:, 7:8]], ins=[T[0][:, 64:128]])
            k.act(sm[:, 8:10], sm[:, 6:8], AF.Exp)
            k.tt('dve', sm[:, 10:11], sm[:, 9:10], sm[:, 8:9], ALU.subtract)
            k.ts('dve', NLAM, sm[:, 10:11], -lam_init, None, op0=ALU.add)

            _lvl(3)
            mv = memT_d.rearrange("(c p) n -> p c n", p=128)
            k.dma(xtm, mv[:, :, :], "x0")
            wk = wkv_d.rearrange("(c p) n -> p c n", p=128)
            mw = xb[:, :, 256:512]
            for c in range(8):
                k.dma(stage[:, 0:256], wk[:, c, :], "st")
                k.ts('dve', mw[:, c, :], stage[:, 0:256], col(CC_MNG + c))
                k.cp('pool', xb[:, c, 0:256], xtm[:, c, :])
            for mt in range(2):
                pc = gbank()
                for c in range(8):
                    k.act(sqb[:, c % 2, 0:128], xtm[:, c, mt * 128:(mt + 1) * 128], AF.Square)
                    k.mm(pc[:, 0:1], sqb[:, c % 2, 0:128], cmB(CM_ONES)[:, 0:1], start=(c == 0), stop=(c == 7))
                k.rsqrt(gcol[:, 0, 0:1], pc[:, 0:1], 1.0 / D_MODEL, epsc, gcol[:, 0, 1:2])
                pk = gbank()
                for c in range(8):
                    k.mm(pk[:, 0:256], xb[:, c, mt * 128:(mt + 1) * 128], mw[:, c, :], start=(c == 0), stop=(c == 7))
                k.ts('dve', T[0][:, 0:128], pk[:, 0:128], gcol[:, 0, 0:1])
                k.ts('dve', vmA[:, mt, 0:128], pk[:, 128:256], gcol[:, 0, 0:1])
                k.act(T[1][:, 0:128], T[0][:, 0:128], AF.Square, accum_out=gcol[:, 0, 2:3])
                k.rsqrt(gcol[:, 0, 3:4], gcol[:, 0, 2:3], 1.0 / 128, epsc, gcol[:, 0, 4:5])
                k.ts('dve', Bt[0][:, 0:128], T[0][:, 0:128], gcol[:, 0, 3:4])
                pt_ = gbank()
                ptb = pt_[:, 0:64].bitcast(BF16)
                k.tr(ptb, Bt[0][:, 0:128], cmB(CM_IDENT))
                k.tt('dve', sm[:, 11:12], col(CC_GXQ), col(CC_GXK), ALU.mult)
                k.ts('dve', kmT[:, mt * 128:(mt + 1) * 128], ptb, sm[:, 11:12], 128 ** -0.5, op0=ALU.mult, op1=ALU.mult)

            _lvl(4)
            for bi in (range(NBLK) if do_rope else ()):
                tsl = slice(bi * TB, (bi + 1) * TB)
                pi_ = T[5].bitcast(I32)
                k.dma(pi_[:, :], pos_d[0:1, tsl].partition_broadcast(128), "x1")
                k.cp('dve', T[0][:], pi_[:, :])
                k.ts('dve', T[1][:], T[0][:], col(CC_INVF))
                for which, dst in ((0, cosT), (1, sinT)):
                    src = T[1]
                    if which == 0:
                        k.ts('dve', T[2][:], T[1][:], math.pi / 2, None, op0=ALU.add)
                        src = T[2]
                    k.ts('dve', T[3][:], src[:], 1.0 / TWO_PI, 12582912.0, op0=ALU.mult, op1=ALU.add)
                    k.ts('dve', T[3][:], T[3][:], -12582912.0, None, op0=ALU.add)
                    k.stt(T[4][:], T[3][:], -6.28125, src[:], ALU.mult, ALU.add)
                    k.stt(T[4][:], T[3][:], -(TWO_PI - 6.28125), T[4][:], ALU.mult, ALU.add)
                    k.ts('dve', T[4][:], T[4][:], 3.1415925, -3.1415925, op0=ALU.min, op1=ALU.max)
                    k.act(dst[:], T[4][:], AF.Sin)
                k.ts('dve', sinT[:], sinT[:], col(CC_SGN))
                k.dma(cs_d[0, :, tsl], cosT[:], "y0")
                k.dma(cs_d[1, :, tsl], sinT[:], "y1")

            xv = xT_d.rearrange("(c p) t -> p c t", p=128)
            yv = yT_d.rearrange("(g p) t -> p g t", p=128) if yT_d is not None else None

            for bi in range(NBLK):
                t0 = bi * TB
                tsl = slice(t0, t0 + TB)
                _lvl(5)
                if bi == 0:
                    k.dma(cosT[:], cs_d[0, :, tsl], "x2")
                    k.dma(sinT[:], cs_d[1, :, tsl], "x3")
                    k.dma(xt[:], xv[:, 0:4, tsl], "x0")
                prx = gbank()
                prc = gbank()
                if bi == 0:
                    k.dma(xt2, xv[:, 4:8, tsl], "x1")
                for c in range(8):
                    xsrc = xt[:, c, :] if c < 4 else xt2[:, c - 4, :]
                    k.cp('act', xb[:, c, :], xsrc)
                    k.act(sqb[:, c % 2, :], xsrc, AF.Square)
                    k.mm(prx[:], cmB(CM_ONES), sqb[:, c % 2, :], start=(c == 0), stop=(c == 7))
                    for tt_ in range(4):
                        k.mm(prc[:, tt_:tt_ + 1], sqb[:, c % 2, tt_ * 128:(tt_ + 1) * 128], cmB(CM_ONES)[:, 0:1],
                             start=(c == 0 and tt_ == 0), stop=(c == 7), skip=True)
                if bi + 1 < NBLK:
                    k.dma(xt[:], xv[:, 0:4, t0 + TB:t0 + 2 * TB], "x0")
                k.rsqrt(RX[:], prx[:], 1.0 / D_MODEL, epsc, T[0][:])
                k.rsqrt(rxc[:, 0:4], prc[:, 0:4], 1.0 / D_MODEL, epsc, rxc[:, 4:8])
                k.ts('dve', rxc[:, 4:8], rxc[:, 0:4], -1.0)

                def fm_proj(j):
                    p = gbank()
                    for c in range(8):
                        k.mm(p[:], wfm[:, c, j * 128:(j + 1) * 128], xb[:, c, :], start=(c == 0), stop=(c == 7))
                    return p

                _lvl(6)
                p = fm_proj(0)
                k.tt('dve', T[0][:], p[:], RX[:], ALU.mult)
                k.act(T[1][:], T[0][:], AF.Exp, scale=-1.0)
                k.ts('dve', T[1][:], T[1][:], 1.0, None, op0=ALU.add)
                k.recip(T[1][:], T[1][:])
                k.stt(T[0][:], T[0][:], 128 ** -0.5, T[1][:], ALU.mult, ALU.mult)
                p = fm_proj(1)
                k.tt('dve', T[1][:], p[:], RX[:], ALU.mult)
                k.act(T[2][:], T[1][:], AF.Exp, scale=-1.0)
                k.ts('dve', T[2][:], T[2][:], 1.0, None, op0=ALU.add)
                k.recip(T[2][:], T[2][:])
                k.ts('dve', T[2][:], T[2][:], OMLB, LB, op0=ALU.mult, op1=ALU.add)
                k.act(T[3][:], T[2][:], AF.Ln)
                k.ts('dve', T[2][:], T[2][:], -1.0, 1.0, op0=ALU.mult, op1=ALU.add)
                k.scan(bcum[:], rmask[:], T[3][:], 0.0, ALU.mult, ALU.add)
                for c8 in range(8):
                    cs_ = slice(c8 * 64, (c8 + 1) * 64)
                    k.ts('dve', T[3][:, cs_], bcum[:, cs_], bcum[:, c8 * 64 + 31:c8 * 64 + 32], None, op0=ALU.subtract)
                k.act(T[4][:], T[3][:], AF.Exp)
                k.act(T[5][:], T[3][:], AF.Exp, scale=-1.0)
                k.tt('dve', qtT[:], T[0][:], T[4][:], ALU.mult)
                k.tt('dve', ktT[:], T[2][:], T[5][:], ALU.mult)
                EREF, ELAST, CFAC = sm[:, 16:24], sm[:, 24:32], sm[:, 32:40]
                k.act(EREF, bcum[:, 31:TB:64], AF.Exp)
                k.act(ELAST, bcum[:, 63:TB:64], AF.Exp)
                k.tt('dve', sm[:, 40:48], bcum[:, 63:TB:64], bcum[:, 31:TB:64], ALU.subtract)
                k.act(CFAC, sm[:, 40:48], AF.Exp)

                _lvl(7)
                def qk_norm_rope(j, gcol_, scale_, dst):
                    p_ = fm_proj(j)
                    k.tt('dve', T[0][:], p_[:], RX[:], ALU.mult)
                    k.act(Bt[0][:], T[0][:], AF.Square)
                    pss = gbank()
                    k.mm(pss[:], cmB(CM_BLK64), Bt[0][:])
                    k.rsqrt(T[1][:], pss[:], 1.0 / 64, epsc, T[2][:])
                    k.ts('dve', Bt[1][:], T[0][:], gcol_)
                    pp = gbank()
                    k.mm(pp[:], cmB(CM_PROPE), Bt[1][:])
                    k.tt('dve', T[2][:], Bt[1][:], cosT[:], ALU.mult)
                    k.tt('dve', T[3][:], pp[:], sinT[:], ALU.mult)
                    k.tt('dve', T[2][:], T[2][:], T[3][:], ALU.add)
                    k.stt(dst, T[2][:], scale_, T[1][:], ALU.mult, ALU.mult)

                qk_norm_rope(2, col(CC_GQ), 0.125, qhT[:])
                qk_norm_rope(3, col(CC_GK), 1.0, kT[:, tsl])
                if bi + 1 < NBLK:
                    k.dma(cosT[:], cs_d[0, :, t0 + TB:t0 + 2 * TB], "x2")
                    k.dma(sinT[:], cs_d[1, :, t0 + TB:t0 + 2 * TB], "x3")

                _lvl(8)
                def conv_silu(j, ext, wcol, bcol, scale_, dst):
                    p_ = fm_proj(j)
                    k.tt('dve', ext[:, 3:TB + 3], p_[:], RX[:], ALU.mult)
                    k.ts('dve', T[0][:], ext[:, 0:TB], cc[:, wcol:wcol + 1], cc[:, bcol:bcol + 1], op0=ALU.mult, op1=ALU.add)
                    for jj in range(1, 4):
                        k.stt(T[0][:], ext[:, jj:TB + jj], cc[:, wcol + jj:wcol + jj + 1], T[0][:], ALU.mult, ALU.add)
                    k.cp('pool', ext[:, 0:3], ext[:, TB:TB + 3])
                    k.act(T[1][:], T[0][:], AF.Exp, scale=-1.0)
                    k.ts('dve', T[1][:], T[1][:], 1.0, None, op0=ALU.add)
                    k.recip(T[1][:], T[1][:])
                    k.stt(dst, T[0][:], scale_, T[1][:], ALU.mult, ALU.mult)

                conv_silu(4, cqx, CC_CWQ, CC_CBQ, 1.0, qcT[:])
                conv_silu(5, ckx, CC_CWK, CC_CBK, 128 ** -0.5, kcT[:])

                _lvl(9)
                p = fm_proj(6)
                k.tt('dve', T[0][:], p[:], RX[:], ALU.mult)
                k.act(Bt[0][:], T[0][:], AF.Square)
                pss = gbank()
                k.mm(pss[:], cmB(CM_ONES), Bt[0][:])
                k.rsqrt(T[1][:], pss[:], 1.0 / 128, epsc, T[2][:])
                k.tt('dve', qxn[:], T[0][:], T[1][:], ALU.mult)
                for mt in range(2):
                    pS = gbank()
                    k.mm(pS[:], kmT[:, mt * 128:(mt + 1) * 128], qxn[:])
                    k.act(PT[:, mt, :], pS[:], AF.Exp)

                _lvl(10)
                for tt_ in range(4):
                    tcs = slice(tt_ * 128, (tt_ + 1) * 128)
                    pA, pB, pC = gbank(), gbank(), gbank()
                    for c in range(8):
                        k.mm(pA[:], xb[:, c, tcs], wtm[:, c, 0:512], start=(c == 0), stop=(c == 7))
                    for c in range(8):
                        k.mm(pB[:], xb[:, c, tcs], wtm[:, c, 512:1024], start=(c == 0), stop=(c == 7))
                    for c in range(8):
                        k.mm(pC[:, 0:2], xb[:, c, tcs], wtm[:, c, 1024:1026], start=(c == 0), stop=(c == 7))
                    rc = rxc[:, tt_:tt_ + 1]
                    nrc = rxc[:, 4 + tt_:5 + tt_]
                    k.act(T[0][:], pA[:], AF.Exp, scale=nrc)
                    k.ts('dve', T[0][:], T[0][:], 1.0, None, op0=ALU.add)
                    k.recip(T[0][:], T[0][:])
                    k.stt(gates[:, tt_, 0:4, :].rearrange("p g d -> p (g d)"), pA[:], rc, T[0][:], ALU.mult, ALU.mult)
                    k.act(T[1][:, 0:128], pB[:, 0:128], AF.Exp, scale=nrc)
                    k.ts('dve', T[1][:, 0:128], T[1][:, 0:128], 1.0, None, op0=ALU.add)
                    k.recip(gates[:, tt_, 4, :], T[1][:, 0:128])
                    k.ts('dve', av[:, tt_, :], pB[:, 128:256], rc)
                    k.ts('dve', vA[:, bi * 4 + tt_, 0:128], pB[:, 256:384], rc)
                    k.ts('dve', cvA[:, tt_, 0:128], pB[:, 384:512], rc)
                    g_ = gcol[:, tt_, :]
                    k.stt(g_[:, 0:1], pC[:, 0:1], rc, col(CC_GBI), ALU.mult, ALU.add)
                    k.stt(g_[:, 2:3], pC[:, 1:2], rc, col(CC_GBF), ALU.mult, ALU.add)
                    k.act(g_[:, 3:4], g_[:, 2:3], AF.Exp, scale=-1.0)
                    k.act(g_[:, 4:5], g_[:, 3:4], AF.Ln, bias=col(CC_ONE))
                    k.ts('dve', g_[:, 1:2], g_[:, 4:5], -1.0)
                    k.ts('dve', g_[:, 5:6], g_[:, 1:2], col(CC_IND0))
                    k.ts('dve', g_[:, 6:7], g_[:, 1:2], col(CC_IND1))
                    pc2 = gbank()
                    k.mm(pc2[:, 0:1], cmF(CM_TRI64), g_[:, 1:2])
                    k.mm(pc2[:, 1:2], cmF(CM_BLK64), g_[:, 1:2], skip=True)
                    k.mm(pc2[:, 2:4], cmF(CM_ONES), g_[:, 5:7], skip=True)
                    k.cp('dve', g_[:, 7:11], pc2[:, 0:4])
                    k.tt('dve', g_[:, 11:12], g_[:, 0:1], g_[:, 7:8], ALU.subtract)
                    k.tt('dve', g_[:, 12:13], g_[:, 11:12], g_[:, 8:9], ALU.add)
                    k.act(g_[:, 13:14], g_[:, 11:12], AF.Exp)
                    k.act(g_[:, 14:15], g_[:, 12:13], AF.Exp)
                    k.act(g_[:, 15:16], g_[:, 7:8], AF.Exp)
                    k.act(sm[:, 48 + 2 * tt_:50 + 2 * tt_], g_[:, 9:11], AF.Exp)

                _lvl(11)
                for tt_ in range(4):
                    tcs = slice(tt_ * 128, (tt_ + 1) * 128)
                    po = gbank()
                    for mt in range(2):
                        k.mm(po[:, 0:129], PT[:, mt, tcs], vmA[:, mt, :], start=(mt == 0), stop=(mt == 1))
                    k.recip(gcol[:, tt_, 5:6], po[:, 128:129])
                    k.stt(yTM[:, tt_, 3, :], po[:, 0:128], gcol[:, tt_, 5:6], gates[:, tt_, 3, :], ALU.mult, ALU.mult)

                _lvl(12)
                for tt_ in range(4):
                    tcs = slice(tt_ * 128, (tt_ + 1) * 128)
                    ptr = gbank()
                    ptb = ptr[:, 0:64].bitcast(BF16)
                    k.tr(ptb, ktT[:, tcs], cmB(CM_IDENT))
                    k.cp('dve', kTM[:, tt_, :], ptb)
                _lvl(12.1)
                for tt_ in range(4):
                    pd = gbank()
                    for hh in range(2):
                        c8 = tt_ * 2 + hh
                        rs = slice(hh * 64, hh * 64 + 64)
                        k.mm(pd[:, hh * 128:(hh + 1) * 128], kTM[rs, tt_, :], av[rs, tt_, :], skip=(hh == 1))
                        k.ts('dve', dS[:, c8, :], pd[:, hh * 128:(hh + 1) * 128], CFAC[:, c8:c8 + 1])
                _lvl(12.2)
                for c8 in range(8):
                    k.ts('dve', Sb_h[:, c8, :], S_h[:], EREF[:, c8:c8 + 1])
                    k.stt(S_h[:], S_h[:], ELAST[:, c8:c8 + 1], dS[:, c8, :], ALU.mult, ALU.add)
                _lvl(12.3)
                for tt_ in range(4):
                    tcs = slice(tt_ * 128, (tt_ + 1) * 128)
                    pa = gbank()
                    for hh in range(2):
                        cs_ = slice(tt_ * 128 + hh * 64, tt_ * 128 + hh * 64 + 64)
                        rs = slice(hh * 64, hh * 64 + 64)
                        k.mm(pa[rs, hh * 64:hh * 64 + 64], ktT[:, cs_], qtT[:, cs_], skip=(hh == 1))
                        k.stt(AT[rs, 0, hh * 64:hh * 64 + 64], pa[rs, hh * 64:hh * 64 + 64], 3.0e38,
                              cmB(CM_TRI64)[rs, hh * 64:hh * 64 + 64], ALU.min, ALU.mult)
                    _lvl(12.4)
                    po = gbank()
                    for hh in range(2):
                        cs_ = slice(tt_ * 128 + hh * 64, tt_ * 128 + hh * 64 + 64)
                        rs = slice(hh * 64, hh * 64 + 64)
                        k.mm(po[rs, 0:128], AT[rs, 0, hh * 64:hh * 64 + 64], av[rs, tt_, :], start=True, stop=False, skip=True)
                        k.mm(po[rs, 0:128], qtT[:, cs_], Sb_h[:, tt_ * 2 + hh, :], start=False, stop=True, skip=True)
                    g_ = gcol[:, tt_, :]
                    k.act(T[0][:, 0:128], po[:, 0:128], AF.Square, accum_out=g_[:, 5:6])
                    k.rsqrt(g_[:, 6:7], g_[:, 5:6], 1.0 / 128, epsc, g_[:, 5:6])
                    k.stt(T[0][:, 0:128], po[:, 0:128], g_[:, 6:7], crow[:, CR_HG * 128:(CR_HG + 1) * 128], ALU.mult, ALU.mult)
                    k.tt('dve', yTM[:, tt_, 0, :], T[0][:, 0:128], gates[:, tt_, 0, :], ALU.mult)

                _lvl(13)
                for tt_ in range(4):
                    tcs = slice(tt_ * 128, (tt_ + 1) * 128)
                    ptr = gbank()
                    ptb = ptr[:, 0:64].bitcast(BF16)
                    k.tr(ptb, kcT[:, tcs], cmB(CM_IDENT))
                    k.ts('dve', kTM2[:, tt_, :], ptb, gcol[:, tt_, 14:15])
                for tt_ in range(4):
                    for hh in range(2):
                        c8 = tt_ * 2 + hh
                        rs = slice(hh * 64, hh * 64 + 64)
                        pd = gbank()
                        k.mm(pd[:, 0:129], kTM2[rs, tt_, :], cvA[rs, tt_, :])
                        k.cp('dve', dC[:, c8, :], pd[:, 0:129])
                for c8 in range(8):
                    k.cp('dve', Cb_m[:, c8, :], C_m[:])
                    k.stt(C_m[:], C_m[:], sm[:, 48 + c8:49 + c8], dC[:, c8, :], ALU.mult, ALU.add)
                for tt_ in range(4):
                    tcs = slice(tt_ * 128, (tt_ + 1) * 128)
                    g_ = gcol[:, tt_, :]
                    pa = gbank()
                    k.mm(pa[:, 0:128], kcT[:, tcs], qcT[:, tcs])
                    k.stt(AT[:, 1, :], pa[:, 0:128], g_[:, 13:14], cmB(CM_TRI64), ALU.mult, ALU.mult)
                    po = gbank()
                    for hh in range(2):
                        cs_ = slice(tt_ * 128 + hh * 64, tt_ * 128 + hh * 64 + 64)
                        rs = slice(hh * 64, hh * 64 + 64)
                        k.mm(po[rs, 0:129], AT[rs, 1, hh * 64:hh * 64 + 64], cvA[rs, tt_, :], start=True, stop=False, skip=True)
                        k.mm(po[rs, 0:129], qcT[:, cs_], Cb_m[:, tt_ * 2 + hh, :], start=False, stop=True, skip=True)
                    k.act(g_[:, 5:6], po[:, 128:129], AF.Abs, scale=g_[:, 15:16])
                    k.ts('dve', g_[:, 5:6], g_[:, 5:6], 1.0, None, op0=ALU.max)
                    k.recip(g_[:, 5:6], g_[:, 5:6])
                    k.tt('dve', g_[:, 5:6], g_[:, 5:6], g_[:, 15:16], ALU.mult)
                    k.stt(T[0][:, 0:128], po[:, 0:128], g_[:, 5:6], gates[:, tt_, 4, :], ALU.mult, ALU.mult)
                    k.act(T[1][:, 0:128], T[0][:, 0:128], AF.Square, accum_out=g_[:, 6:7])
                    k.rsqrt(g_[:, 5:6], g_[:, 6:7], 1.0 / 128, epsc, g_[:, 6:7])
                    k.stt(T[0][:, 0:128], T[0][:, 0:128], g_[:, 5:6], crow[:, CR_MG * 128:(CR_MG + 1) * 128], ALU.mult, ALU.mult)
                    k.tt('dve', yTM[:, tt_, 2, :], T[0][:, 0:128], gates[:, tt_, 2, :], ALU.mult)

                _lvl(14)
                if bi + 1 < NBLK:
                    k.dma(xt2, xv[:, 4:8, t0 + TB:t0 + 2 * TB], "x1")
                for ob in range(3):
                    k.mm(PS_O[ob][:], cmB(CM_ZERO), zrow[:], start=True, stop=False, sig=True, skip=True)
                oslot = {}
                for mp in range(2):
                    for qq in range(4):
                        idx = mp * 4 + qq
                        oslot[(mp, qq)] = (PS_O[idx // 3], (idx % 3) * 129)
                nkt = bi * 4 + 4
                def s_mm(j_, mp_):
                    q0_ = max(j_ - bi * 4, 0) * 128
                    rs_ = slice(mp_ * 64, mp_ * 64 + 64)
                    k.mm(PS_S[mp_][:, q0_:TB], kT[rs_, j_ * 128:(j_ + 1) * 128], qhT[rs_, q0_:TB])

                s_mm(0, 0)
                s_mm(0, 1)
                for j in range(nkt):
                    jj = j - bi * 4
                    q0 = max(jj, 0) * 128
                    for mp in range(2):
                        rs = slice(mp * 64, mp * 64 + 64)
                        pt_ = PT[:, 2 * (j % 2) + mp, :]
                        k.act(pt_[:, q0:TB], PS_S[mp][:, q0:TB], AF.Exp)
                        if jj >= 0:
                            k.tt('pool', pt_[:, q0:q0 + 128], pt_[:, q0:q0 + 128], cmB(CM_TRI128), ALU.mult)
                        for qq in range(max(jj, 0), 4):
                            ob, oc = oslot[(mp, qq)]
                            last = (j == min(nkt - 1, bi * 4 + qq))
                            k.mm(ob[:, oc:oc + 129], pt_[:, qq * 128:(qq + 1) * 128], vA[:, j, :],
                                 start=False, stop=last, sig=last, skip=True)
                        if j + 1 < nkt:
                            s_mm(j + 1, mp)
                for tt_ in range(4):
                    g_ = gcol[:, tt_, :]
                    ob1, oc1 = oslot[(0, tt_)]
                    ob2, oc2 = oslot[(1, tt_)]
                    k.recip(g_[:, 5:6], ob1[:, oc1 + 128:oc1 + 129])
                    k.recip(g_[:, 6:7], ob2[:, oc2 + 128:oc2 + 129])
                    k.tt('dve', g_[:, 6:7], g_[:, 6:7], NLAM, ALU.mult)
                    k.ts('dve', T[0][:, 0:128], ob1[:, oc1:oc1 + 128], g_[:, 5:6])
                    k.stt(T[0][:, 0:128], ob2[:, oc2:oc2 + 128], g_[:, 6:7], T[0][:, 0:128], ALU.mult, ALU.add)
                    k.act(T[1][:, 0:128], T[0][:, 0:128], AF.Square, accum_out=g_[:, 5:6])
                    k.rsqrt(g_[:, 6:7], g_[:, 5:6], 1.0 / 128, epsc, g_[:, 5:6])
                    k.ts('dve', g_[:, 6:7], g_[:, 6:7], 1.0 - lam_init)
                    k.stt(T[0][:, 0:128], T[0][:, 0:128], g_[:, 6:7], crow[:, CR_SUB * 128:(CR_SUB + 1) * 128], ALU.mult, ALU.mult)
                    k.tt('dve', yTM[:, tt_, 1, :], T[0][:, 0:128], gates[:, tt_, 1, :], ALU.mult)

                _lvl(15)
                for tt_ in range(4):
                    pty = gbank()
                    ptb = pty[:, 0:256].bitcast(BF16)
                    for g4 in range(4):
                        k.tr(ptb[:, g4 * 128:(g4 + 1) * 128], yTM[:, tt_, g4, :], cmB(CM_IDENT))
                    k.cp('dve', yT[:, :, tt_ * 128:(tt_ + 1) * 128], ptb.rearrange("p (g t) -> p g t", g=4))
                if not fused:
                    k.dma(yv[:, :, tsl], yT[:], "yu", eng='pool')
                else:
                    ch, half = bi // 2, bi % 2
                    for r in range(4):
                        k.act(yTr[:, r % 2, :, :], yT[:], AF.Copy, scale=col(CC_RANK + r))
                        k.dma(D['yin'][ch, r].rearrange("(g p) t -> p g t", p=128)[:, :, half * TB:(half + 1) * TB],
                              yTr[:, r % 2, :, :], "ys%d" % r, eng='pool')
                    if half == 1 or bi == NBLK - 1:
                        k.cc("AllReduce", ALU.add, [[0, 1, 2, 3], [4, 5, 6, 7]],
                             D['yin'][ch].rearrange("r p t -> (r p) t"), D['yout'][ch], "cc")

        except _Stop:
            pass
        k.s.barrier()


def layer_dram(nc, S, sfx, shared=None):
    dr = lambda name, shape, dt=F32: nc.dram_tensor(name, shape, dt, kind="ExternalInput").ap()
    D = {}
    if shared is None:
        D['pos'] = dr("pos", [1, S], I32)
        D['memT'] = dr("memT", [D_MODEL, N_MEM])
        D['cmat'] = dr("cmat", [128, NCM, 128])
        D['rmask'] = dr("rmask", [128, TB])
        D['cs'] = nc.dram_tensor("cs_scr", [2, 128, S], F32, kind="Internal").ap()
    else:
        for n_ in ('pos', 'memT', 'cmat', 'rmask', 'cs'):
            D[n_] = shared[n_]
    D['w_fm'] = dr("w_fm" + sfx, [D_MODEL, NFM * 128])
    D['w_tm'] = dr("w_tm" + sfx, [D_MODEL, NTM])
    D['w_kv'] = dr("w_kv" + sfx, [D_MODEL, 256])
    D['ccol'] = dr("ccol" + sfx, [128, NCC])
    D['crow'] = dr("crow" + sfx, [1, 3 * 128])
    D['lam'] = dr("lam" + sfx, [1, 256])
    return D


def build_layer(S, layer_idx):
    nc = bass.Bass("TRN2", target_bir_lowering=False)
    D = layer_dram(nc, S, "")
    D['xT'] = nc.dram_tensor("xT", [D_MODEL, S], F32, kind="ExternalInput").ap()
    D['yT'] = nc.dram_tensor("yT", [4 * 128, S], BF16, kind="ExternalOutput").ap()
    with ExitStack() as es:
        es.enter_context(nc.allow_low_precision(reason="bf16 matmul operands"))
        k = K(nc, es)
        PSB = [k.ps(f"pb{i}", [128, 512]) for i in range(8)]
        emit_layer(nc, k, PSB, S, layer_idx, "", D, do_rope=True, fused=False)
        k.s.finish()
    return nc


def emit_outproj(nc, k, PSB, S, sfx, D, full):
    NBLK = S // TB
    with ExitStack() as es:
        k.es = es
        sb = lambda name, shape, dt: k.sb(name + sfx, shape, dt)
        stage = sb("ostage", [128, D_MODEL], F32)
        wo = sb("wo", [128, 16, 256], BF16)
        yt = sb("yt", [128, 16, TB], BF16)
        xo = sb("xo", [128, 2, TB], F32)
        oo = sb("oo", [128, 2, TB], F32)
        wv = D['w_own'].rearrange("(c p) n -> p c n", p=128)
        for c in range(16):
            k.dma(stage[:, 0:256], wv[:, c, :], "st")
            k.cp('dve', wo[:, c, :], stage[:, 0:256])
        if full:
            w = sb("wfull", [128, 16, D_MODEL], BF16)
            xt = sb("oxt", [128, 8, TB], F32)
            ot = sb("oot", [128, 8, TB], F32)
            wv = D['w_out'].rearrange("(c p) n -> p c n", p=128)
            for c in range(16):
                k.dma(stage[:], wv[:, c, :], "st")
                k.cp('dve', w[:, c, :], stage[:])
            xv = D['x_prev'].rearrange("(c p) t -> p c t", p=128)
            x1v = D['x1'].rearrange("(c p) t -> p c t", p=128)
        xov = D['x_own_prev'].rearrange("(c p) t -> p c t", p=128)
        outv = (D['x1own'] if full else D['out']).rearrange("(c p) t -> p c t", p=128)
        pi = [0]
        for bi in range(NBLK):
            tsl = slice(bi * TB, (bi + 1) * TB)
            ch, half = bi // 2, bi % 2
            k.dma(yt[:], D['yout'][ch].rearrange("(c p) t -> p c t", p=128)[:, :, half * TB:(half + 1) * TB], "x0")
            k.dma(xo[:], xov[:, :, tsl], "x1")
            if full:
                k.dma(xt[:], xv[:, :, tsl], "x2")
                for oc in range(8):
                    p = PSB[pi[0] % 8]
                    pi[0] += 1
                    for c in range(16):
                        k.mm(p[:], w[:, c, oc * 128:(oc + 1) * 128], yt[:, c, :], start=(c == 0), stop=(c == 15),
                             sig=(c == 15))
                    k.tt('dve', ot[:, oc, :], p[:], xt[:, oc, :], ALU.add)
                k.dma(x1v[:, :, tsl], ot[:], "y0")
            for oc in range(2):
                p = PSB[pi[0] % 8]
                pi[0] += 1
                for c in range(16):
                    k.mm(p[:], wo[:, c, oc * 128:(oc + 1) * 128], yt[:, c, :], start=(c == 0), stop=(c == 15),
                         sig=(c == 15))
                k.tt('dve', oo[:, oc, :], p[:], xo[:, oc, :], ALU.add)
            k.dma(outv[:, :, tsl], oo[:], "y1")
        k.s.barrier()


def build_fused(S):
    nc = bass.Bass("TRN2", target_bir_lowering=False)
    NCH = max(S // 1024, 1)
    D0 = layer_dram(nc, S, "_0")
    D1 = layer_dram(nc, S, "_1", shared=D0)
    x0 = nc.dram_tensor("xT", [D_MODEL, S], F32, kind="ExternalInput").ap()
    x0own = nc.dram_tensor("x0own", [256, S], F32, kind="ExternalInput").ap()
    w_out0 = nc.dram_tensor("w_out0", [2048, D_MODEL], F32, kind="ExternalInput").ap()
    w_own0 = nc.dram_tensor("w_own0", [2048, 256], F32, kind="ExternalInput").ap()
    w_own1 = nc.dram_tensor("w_own1", [2048, 256], F32, kind="ExternalInput").ap()
    out = nc.dram_tensor("out", [256, S], F32, kind="ExternalOutput").ap()
    yin = nc.dram_tensor("yin", [NCH, 4, 512, min(S, 1024)], BF16).ap()
    yout = nc.dram_tensor("yout", [NCH, 2048, min(S, 1024)], BF16).ap()
    x1 = nc.dram_tensor("x1", [D_MODEL, S], F32).ap()
    x1own = nc.dram_tensor("x1own", [256, S], F32).ap()
    for D in (D0, D1):
        D['yin'], D['yout'] = yin, yout
    D0['xT'] = x0
    D1['xT'] = x1
    with ExitStack() as es:
        es.enter_context(nc.allow_low_precision(reason="bf16 matmul operands"))
        k = K(nc, es)
        PSB = [k.ps(f"pb{i}", [128, 512]) for i in range(8)]
        emit_layer(nc, k, PSB, S, 0, "_a", D0, do_rope=True, fused=True)
        emit_outproj(nc, k, PSB, S, "_p", {'w_own': w_own0, 'w_out': w_out0, 'x_prev': x0, 'x1': x1,
                                           'x_own_prev': x0own, 'x1own': x1own, 'yout': yout}, full=True)
        emit_layer(nc, k, PSB, S, 1, "_b", D1, do_rope=False, fused=True)
        emit_outproj(nc, k, PSB, S, "_q", {'w_own': w_own1, 'x_own_prev': x1own, 'out': out, 'yout': yout}, full=False)
        k.s.finish()
    return nc


def head_cols(h):
    G = GROUP
    base = np.arange(128)
    off = {}
    names = ['a_q', 'a_f', 'a_i', 'a_g', 'b_q', 'b_k', 'b_v', 'b_g', 'c_q', 'c_k', 'c_v', 'c_o']
    o = 0
    for n in names:
        off[n] = o
        o += G
    off['c_i'] = o
    o += 4
    off['c_f'] = o
    o += 4
    off['c_g'] = o
    o += G
    off['x_q'] = o
    o += G
    off['x_g'] = o
    o += G
    hc = lambda n: off[n] + h * 128 + base
    fm = np.concatenate([hc('a_q'), hc('a_f'), hc('b_q'), hc('b_k'), hc('c_q'), hc('c_k'), hc('x_q')])
    tm = np.concatenate([hc('a_g'), hc('b_g'), hc('c_g'), hc('x_g'), hc('c_o'), hc('a_i'), hc('b_v'), hc('c_v'),
                         np.array([off['c_i'] + h, off['c_f'] + h])])
    return fm, tm


def layer_inputs(l, b, h, xT_b, inp, cmat, rmask):
    fm, tm = head_cols(h)
    w_in = inp['w_in'][l]
    cc = np.zeros((128, NCC), np.float32)
    cc[:, CC_NG:CC_NG + 8] = inp['norm_g'][l].reshape(8, 128).T
    cc[:, CC_MNG:CC_MNG + 8] = inp['mem_norm_g'][l].reshape(8, 128).T
    invf = (ROPE_THETA ** (-np.arange(0, 16, 2, dtype=np.float32) / 16)).astype(np.float32)
    for base in (0, 64):
        cc[base:base + 8, CC_INVF] = invf
        cc[base + 8:base + 16, CC_INVF] = invf
        cc[base:base + 8, CC_SGN] = -1.0
        cc[base + 8:base + 16, CC_SGN] = 1.0
    cc[:, CC_GQ] = np.tile(inp['diff_qk_norm_g'][l, 0], 2)
    cc[:, CC_GK] = np.tile(inp['diff_qk_norm_g'][l, 1], 2)
    cw = inp['mlstm_conv_w'][l]
    cb = inp['mlstm_conv_b'][l]
    cc[:, CC_CWQ:CC_CWQ + 4] = cw[:, h * 128:(h + 1) * 128].T
    cc[:, CC_CBQ] = cb[h * 128:(h + 1) * 128]
    cc[:, CC_CWK:CC_CWK + 4] = cw[:, GROUP + h * 128:GROUP + (h + 1) * 128].T
    cc[:, CC_CBK] = cb[GROUP + h * 128:GROUP + (h + 1) * 128]
    cc[:, CC_GXQ] = inp['xattn_qk_norm_g'][l, 0]
    cc[:, CC_GXK] = inp['xattn_qk_norm_g'][l, 1]
    cc[:, CC_LBL:CC_LBL + 2] = inp['hgrn_lb_logits'][:, h * 128:(h + 1) * 128].T
    cc[0:64, CC_IND0] = 1.0
    cc[64:128, CC_IND1] = 1.0
    cc[:, CC_GBI] = inp['mlstm_gate_b'][l, h]
    cc[:, CC_GBF] = inp['mlstm_gate_b'][l, 4 + h]
    cc[:, CC_EPS] = EPS
    cc[:, CC_ONE] = 1.0
    cc[:, CC_RANK + h] = 1.0
    crow = np.concatenate([inp['hgrn_norm_g'][l], inp['diff_subln_g'][l], inp['mlstm_norm_g'][l]])[None, :]
    wkv = inp['w_mem_kv'][l]
    wkv_h = np.concatenate([wkv[:, h * 128:(h + 1) * 128], wkv[:, GROUP + h * 128:GROUP + (h + 1) * 128]], axis=1)
    return {
        "xT": xT_b,
        "pos": np.ascontiguousarray(inp['positions'][b][None, :xT_b.shape[1]]).astype(np.int32),
        "w_fm": np.ascontiguousarray(w_in[:, fm]),
        "w_tm": np.ascontiguousarray(w_in[:, tm]),
        "memT": np.ascontiguousarray(inp['mem'][b].T),
        "w_kv": np.ascontiguousarray(wkv_h),
        "cmat": cmat,
        "ccol": cc,
        "crow": np.ascontiguousarray(crow.astype(np.float32)),
        "lam": np.ascontiguousarray(inp['diff_lambda'][l].reshape(1, 256)),
        "rmask": rmask,
    }


_CACHE = {}


def run_layer(l, xT, inp, S):
    key = ('L', l, S)
    if key not in _CACHE:
        _CACHE[key] = build_layer(S, l)
    nc = _CACHE[key]
    cmat = const_mats()
    rmask = np.ones((128, TB), np.float32)
    rmask[:, ::64] = 0.0
    in_maps = []
    for c in range(8):
        b, h = c // 4, c % 4
        in_maps.append(layer_inputs(l, b, h, np.ascontiguousarray(xT[b]), inp, cmat, rmask))
    res = run_bass_kernel_spmd(nc, in_maps, core_ids=list(range(8)))
    ys = [np.asarray(r["yT"]) for r in res.results]
    return np.stack([np.concatenate(ys[b * 4:(b + 1) * 4], axis=0) for b in range(BATCH)])


def perm_w_out(w):
    return np.ascontiguousarray(w.reshape(4, 4, 128, D_MODEL).transpose(1, 0, 2, 3).reshape(2048, D_MODEL))


def run_fused(xT, inp, S):
    key = ('F', S)
    if key not in _CACHE:
        _CACHE[key] = build_fused(S)
    nc = _CACHE[key]
    cmat = const_mats()
    rmask = np.ones((128, TB), np.float32)
    rmask[:, ::64] = 0.0
    wp = [perm_w_out(inp['w_out'][l]) for l in range(DEPTH)]
    in_maps = []
    for c in range(8):
        b, h = c // 4, c % 4
        xb_ = np.ascontiguousarray(xT[b])
        m = {}
        for l in range(DEPTH):
            li = layer_inputs(l, b, h, xb_, inp, cmat, rmask)
            for n_ in ('w_fm', 'w_tm', 'w_kv', 'ccol', 'crow', 'lam'):
                m[n_ + "_%d" % l] = li[n_]
            if l == 0:
                for n_ in ('xT', 'pos', 'memT', 'cmat', 'rmask'):
                    m[n_] = li[n_]
        fs = slice(h * 256, (h + 1) * 256)
        m['x0own'] = np.ascontiguousarray(xb_[fs])
        m['w_out0'] = wp[0]
        m['w_own0'] = np.ascontiguousarray(wp[0][:, fs])
        m['w_own1'] = np.ascontiguousarray(wp[1][:, fs])
        in_maps.append(m)
    res = run_bass_kernel_spmd(nc, in_maps, core_ids=list(range(8)))
    outs = [np.asarray(r["out"]) for r in res.results]
    return np.stack([np.concatenate(outs[b * 4:(b + 1) * 4], axis=0) for b in range(BATCH)])


def kernel(**inputs):
    inp = {k_: np.asarray(v) for k_, v in inputs.items()}
    x = inp['x']
    S = x.shape[1]
    xT = np.ascontiguousarray(x.transpose(0, 2, 1))
    oT = run_fused(xT, inp, S)
    return np.ascontiguousarray(oT.transpose(0, 2, 1)).astype(np.float32)
```

```python
import math
import os
from contextlib import ExitStack

import numpy as np
import concourse.bass as bass
import concourse.mybir as mybir
from concourse.bass_utils import run_bass_kernel_spmd

F32 = mybir.dt.float32
BF16 = mybir.dt.bfloat16
I32 = mybir.dt.int32
AF = mybir.ActivationFunctionType
ALU = mybir.AluOpType

D_MODEL = 1024
BATCH = 2
SEQ = 16384
DEPTH = 2
N_MEM = 256
GROUP = 512
EPS = 1e-6
ROPE_THETA = 500000.0
TB = 512
NFM = 7
NTM = 1026


class Sched:
    SAME = ('act', 'dve', 'pool')

    def __init__(self, nc, es):
        self.nc = nc
        self.engs = {'pe': nc.tensor, 'act': nc.scalar, 'dve': nc.vector, 'pool': nc.gpsimd, 'sp': nc.sync}
        self.sem = {e: es.enter_context(nc.semaphore('s_' + e)) for e in ('pe', 'act', 'dve', 'pool')}
        self.cnt = {e: 0 for e in self.sem}
        self.pending = {e: [] for e in self.sem}
        self.csem = {}
        self.ccnt = {}
        self.es = es
        self.waited = {}
        self.acc = {}
        self.ops = []
        self.nwaits = 0

    @staticmethod
    def box(ap):
        dims = ap.ap
        esz = mybir.dt.size(ap.dtype)
        off = ap.offset
        sp = str(ap.space)
        if sp == 'DRAM':
            ext = 1
            for s, c in dims:
                ext += (c - 1) * abs(s)
            return (ap.tensor.name, 0, 1, off * esz, (off + ext) * esz)
        pstep = dims[0][0]
        if pstep == 0:
            p0, f0 = 0, off
        else:
            p0, f0 = off // pstep, off % pstep
        ext = 1
        for s, c in dims[1:]:
            ext += (c - 1) * abs(s)
        if sp == 'PSUM':
            return (ap.tensor.name, (p0 // 32) * 32, ((p0 + dims[0][1] + 31) // 32) * 32, 0, 2048)
        return (ap.tensor.name, p0, p0 + dims[0][1], f0 * esz, (f0 + ext) * esz)

    def _sem_of(self, key):
        return self.sem[key] if key in self.sem else self.csem[key]

    def emit(self, eng, fn, outs=(), ins=(), sig=True, chan=None, inc=16):
        is_dma = chan is not None
        oboxes = [self.box(a) for a in outs]
        iboxes = [self.box(a) for a in ins]
        deps = set()
        for (name, p0, p1, f0, f1) in iboxes:
            for ent in self.acc.get(name, ()):
                if ent[5] and ent[0] < p1 and p0 < ent[1] and ent[2] < f1 and f0 < ent[3]:
                    deps.add(ent[4])
        for (name, p0, p1, f0, f1) in oboxes:
            for ent in self.acc.get(name, ()):
                if ent[0] < p1 and p0 < ent[1] and ent[2] < f1 and f0 < ent[3]:
                    deps.add(ent[4])
        need = {}
        for d in deps:
            deng, key, val = self.ops[d]
            if deng is not None and deng == eng and eng not in self.SAME:
                continue
            if val is None:
                raise RuntimeError('dependency on unsignalled op')
            if need.get(key, 0) < val:
                need[key] = val
        if is_dma:
            if chan not in self.csem:
                self.csem[chan] = self.es.enter_context(self.nc.semaphore('c_' + chan))
                self.ccnt[chan] = 0
            if self.ccnt[chan] > 0 and need.get(chan, 0) < self.ccnt[chan]:
                need[chan] = self.ccnt[chan]
        e = self.engs[eng]
        for key, val in need.items():
            if self.waited.get((eng, key), 0) >= val:
                continue
            e.wait_ge(self._sem_of(key), val)
            self.waited[(eng, key)] = val
            self.nwaits += 1
        inst = fn(e)
        opid = len(self.ops)
        if is_dma:
            self.ccnt[chan] += inc
            inst.then_inc(self.csem[chan], inc)
            self.ops.append([None, chan, self.ccnt[chan]])
        elif sig:
            self.cnt[eng] += 1
            inst.then_inc(self.sem[eng], 1)
            self.ops.append([eng, eng, self.cnt[eng]])
            for p in self.pending[eng]:
                self.ops[p][2] = self.cnt[eng]
            self.pending[eng] = []
        else:
            self.ops.append([eng, eng, None])
            self.pending[eng].append(opid)
        for (name, p0, p1, f0, f1) in oboxes:
            lst = self.acc.setdefault(name, [])
            lst[:] = [t for t in lst if not (p0 <= t[0] and t[1] <= p1 and f0 <= t[2] and t[3] <= f1)]
            lst.append((p0, p1, f0, f1, opid, True))
        rkey = None if is_dma else eng
        for (name, p0, p1, f0, f1) in iboxes:
            lst = self.acc.setdefault(name, [])
            for i, t in enumerate(lst):
                if (not t[5]) and t[0] == p0 and t[1] == p1 and t[2] == f0 and t[3] == f1 \
                        and self.ops[t[4]][0] == rkey and rkey is not None:
                    lst[i] = (p0, p1, f0, f1, opid, False)
                    break
            else:
                lst.append((p0, p1, f0, f1, opid, False))
        return inst

    def barrier(self):
        keys = [(key, self.cnt[key]) for key in self.sem] + [(key, self.ccnt[key]) for key in self.csem]
        for eng, e in self.engs.items():
            for key, val in keys:
                if val > 0 and self.waited.get((eng, key), 0) < val:
                    e.wait_ge(self._sem_of(key), val)
                    self.waited[(eng, key)] = val

    def finish(self):
        sp = self.engs['sp']
        for key in list(self.csem):
            if self.ccnt[key] > 0:
                sp.wait_ge(self.csem[key], self.ccnt[key])
        for key in self.sem:
            if self.cnt[key] > 0:
                sp.wait_ge(self.sem[key], self.cnt[key])


class _Stop(Exception):
    pass


def _lvl(n):
    if n > float(os.environ.get('KLVL', '99')):
        raise _Stop()


class K:
    def __init__(self, nc, es):
        self.nc = nc
        self.es = es
        self.s = Sched(nc, es)
        self.n = 0

    def sb(self, name, shape, dt):
        return self.es.enter_context(self.nc.sbuf_tensor(name, shape, dt))

    def ps(self, name, shape, dt=F32):
        return self.es.enter_context(self.nc.psum_tensor(name, shape, dt))

    def mm(self, out, lhsT, rhs, start=True, stop=True, sig=None, skip=False):
        if sig is None:
            sig = True
        return self.s.emit('pe', lambda e: e.matmul(out, lhsT, rhs, start=start, stop=stop,
                                                     skip_group_check=skip),
                           outs=[out], ins=[lhsT, rhs], sig=sig)

    def tr(self, out, in_, ident):
        return self.s.emit('pe', lambda e: e.transpose(out, in_, ident), outs=[out], ins=[in_, ident])

    def act(self, out, in_, func, bias=0.0, scale=1.0, accum_out=None):
        ins = [in_]
        outs = [out]
        if not isinstance(bias, (int, float)):
            ins.append(bias)
        if not isinstance(scale, (int, float)):
            ins.append(scale)
        if accum_out is not None:
            outs.append(accum_out)
        kw = {}
        if accum_out is not None:
            kw['accum_out'] = accum_out
        return self.s.emit('act', lambda e: e.activation(out, in_, func, bias=bias, scale=scale, **kw),
                           outs=outs, ins=ins)

    def ts(self, eng, out, in0, s1, s2=None, op0=ALU.mult, op1=None, accum_out=None):
        ins = [in0]
        outs = [out]
        for s_ in (s1, s2):
            if s_ is not None and not isinstance(s_, (int, float)):
                ins.append(s_)
        kw = {}
        if op1 is not None:
            kw['op1'] = op1
        if accum_out is not None:
            kw['accum_out'] = accum_out
            outs.append(accum_out)
        return self.s.emit(eng, lambda e: e.tensor_scalar(out, in0, s1, s2, op0, **kw), outs=outs, ins=ins)

    def tt(self, eng, out, in0, in1, op):
        return self.s.emit(eng, lambda e: e.tensor_tensor(out, in0, in1, op), outs=[out], ins=[in0, in1])

    def stt(self, out, in0, scalar, in1, op0, op1):
        ins = [in0, in1]
        if not isinstance(scalar, (int, float)):
            ins.append(scalar)
        return self.s.emit('dve', lambda e: e.scalar_tensor_tensor(out, in0, scalar, in1, op0, op1),
                           outs=[out], ins=ins)

    def cp(self, eng, out, in_):
        if eng == 'act':
            return self.s.emit('act', lambda e: e.copy(out, in_), outs=[out], ins=[in_])
        return self.s.emit(eng, lambda e: e.tensor_copy(out, in_), outs=[out], ins=[in_])

    def recip(self, out, in_):
        return self.s.emit('dve', lambda e: e.reciprocal(out, in_), outs=[out], ins=[in_])

    def memset(self, eng, out, val):
        return self.s.emit(eng, lambda e: e.memset(out, val), outs=[out], ins=[])

    def scan(self, out, d0, d1, init, op0, op1):
        return self.s.emit('dve', lambda e: e.tensor_tensor_scan(out, d0, d1, init, op0, op1),
                           outs=[out], ins=[d0, d1])

    def dma(self, out, in_, chan, eng='sp'):
        return self.s.emit(eng, lambda e: e.dma_start(out=out, in_=in_), outs=[out], ins=[in_], chan=chan)

    def cc(self, kind, op, groups, in_ap, out_ap, chan):
        return self.s.emit('pool', lambda e: e.collective_compute(kind, op, replica_groups=groups,
                                                                 ins=[in_ap.opt()], outs=[out_ap.opt()]),
                           outs=[out_ap], ins=[in_ap], chan=chan, inc=1)

    def rsqrt(self, out, in_, scale, epscol, tmp):
        self.act(tmp, in_, AF.Ln, bias=epscol, scale=scale)
        self.act(out, tmp, AF.Exp, scale=-0.5)


CM_IDENT, CM_TRI128, CM_TRI64, CM_BLK64, CM_ONES, CM_PROPE, CM_ZERO = range(7)
NCM = 7
(CC_NG, CC_MNG, CC_INVF, CC_SGN, CC_GQ, CC_GK, CC_CWQ, CC_CBQ, CC_CWK, CC_CBK, CC_GXQ, CC_GXK, CC_LBL,
 CC_IND0, CC_IND1, CC_GBI, CC_GBF, CC_EPS, CC_ONE, CC_HALFPI) = (0, 8, 16, 17, 18, 19, 20, 24, 25, 29, 30, 31,
                                                                 32, 34, 35, 36, 37, 38, 39, 40)
CC_RANK = 41
NCC = 45
CR_HG, CR_SUB, CR_MG = range(3)
TWO_PI = 2.0 * math.pi


def const_mats():
    m = np.zeros((NCM, 128, 128), np.float32)
    i = np.arange(128)
    m[CM_IDENT] = np.eye(128)
    m[CM_TRI128] = (i[None, :] >= i[:, None])
    same = (i[:, None] // 64) == (i[None, :] // 64)
    m[CM_TRI64] = m[CM_TRI128] * same
    m[CM_BLK64] = same
    m[CM_ONES] = 1.0
    for base in (0, 64):
        for j in range(8):
            m[CM_PROPE, base + 8 + j, base + j] = 1.0
            m[CM_PROPE, base + j, base + 8 + j] = 1.0
    return np.ascontiguousarray(m.transpose(1, 0, 2))


def emit_layer(nc, k, PSB, S, layer_idx, sfx, D, do_rope=True, fused=False):
    NBLK = S // TB
    NT = S // 128
    xT_d, pos_d, wfm_d, wtm_d = D['xT'], D['pos'], D['w_fm'], D['w_tm']
    memT_d, wkv_d, cmat_d, ccol_d, crow_d, lam_d, rm_d = (D['memT'], D['w_kv'], D['cmat'], D['ccol'], D['crow'],
                                                          D['lam'], D['rmask'])
    cs_d = D['cs']
    yT_d = D.get('yT')

    lam_init = 0.8 - 0.6 * math.exp(-0.3 * layer_idx)

    with ExitStack() as es:
        k.es = es
        sb = lambda name, shape, dt: k.sb(name + sfx, shape, dt)
        kT = sb("kT", [128, S], BF16)
        vA = sb("vA", [128, NT, 129], BF16)
        wfm = sb("wfm", [128, 8, NFM * 128], BF16)
        wtm = sb("wtm", [128, 8, NTM], BF16)
        cmf3 = sb("cmf3", [128, 3, 128], F32)
        cmb = sb("cmb", [128, NCM, 128], BF16)
        cc = sb("cc", [128, NCC], F32)
        crow = sb("crow_sb", [128, 3 * 128], F32)
        lamt = sb("lamt", [128, 256], F32)
        rmask = sb("rmask_sb", [128, TB], F32)
        zrow = sb("zrow", [128, TB], BF16)
        kmT = sb("kmT", [128, 256], BF16)
        vmA = sb("vmA", [128, 2, 129], BF16)
        S_h = sb("S_h", [128, 128], F32)
        Sb_h = sb("Sb_h", [128, 8, 128], BF16)
        dS = sb("dS", [128, 8, 128], F32)
        C_m = sb("C_m", [128, 129], F32)
        Cb_m = sb("Cb_m", [128, 8, 129], BF16)
        dC = sb("dC", [128, 8, 129], F32)
        sm = sb("sm", [128, 64], F32)
        stage = sb("stage", [128, NTM], F32)
        xt = sb("xt", [128, 4, TB], F32)
        xtm = xt[:].rearrange("p c t -> p (c t)").rearrange("p (c t) -> p c t", c=8)
        xb = sb("xb", [128, 8, TB], BF16)
        sqb = sb("sqb", [128, 2, TB], BF16)
        RX = sb("RX", [128, TB], F32)
        rxc = sb("rxc", [128, 8], F32)
        Tall = sb("Tall", [128, 6, TB], F32)
        T = [Tall[:, i, :] for i in range(6)]
        xt2 = Tall[:, 2:6, :]
        Bt = [sb(f"B{i}", [128, TB], BF16) for i in range(3)]
        cosT = sb("cosT", [128, TB], F32)
        sinT = sb("sinT", [128, TB], F32)
        qhT = sb("qhT", [128, TB], BF16)
        qtT = sb("qtT", [128, TB], BF16)
        ktT = sb("ktT", [128, TB], BF16)
        qcT = sb("qcT", [128, TB], BF16)
        kcT = sb("kcT", [128, TB], BF16)
        cqx = sb("cqx", [128, TB + 3], F32)
        ckx = sb("ckx", [128, TB + 3], F32)
        qxn = sb("qxn", [128, TB], BF16)
        bcum = sb("bcum", [128, TB], F32)
        av = sb("av", [128, 4, 128], BF16)
        cvA = sb("cvA", [128, 4, 129], BF16)
        gates = sb("gates", [128, 4, 5, 128], BF16)
        gcol = sb("gcol", [128, 4, 16], F32)
        kTM = sb("kTM", [128, 4, 128], BF16)
        kTM2 = sb("kTM2", [128, 4, 128], BF16)
        AT = sb("AT", [128, 2, 128], BF16)
        PT = sb("PT", [128, 4, TB], BF16)
        yTM = sb("yTM", [128, 4, 4, 128], BF16)
        yT = sb("yT_sb", [128, 4, TB], BF16)
        yTr = sb("yTr", [128, 2, 4, TB], BF16) if fused else None
        G = PSB[0:3]
        PS_S = PSB[3:5]
        PS_O = PSB[5:8]
        gi = [0]

        def gbank():
            b = G[gi[0] % 3]
            gi[0] += 1
            return b

        cmF = lambda i: cmf3[:, i - CM_TRI64, :]
        cmB = lambda i: cmb[:, i, :]
        col = lambda i, n=1: cc[:, i:i + n]
        epsc = col(CC_EPS)

        try:
            k.dma(stage[:, 0:NCM * 128], cmat_d.rearrange("p m n -> p (m n)"), "c0")
            k.dma(cmf3[:], cmat_d[:, CM_TRI64:CM_TRI64 + 3, :], "c5")
            k.dma(cc[:], ccol_d[:, :], "c1")
            k.dma(crow[:], crow_d[0:1, :].partition_broadcast(128), "c2")
            k.dma(lamt[:], lam_d[0:1, :].partition_broadcast(128), "c3")
            k.dma(rmask[:], rm_d[:, :], "c4")
            k.cp('dve', cmb[:].rearrange("p m n -> p (m n)"), stage[:, 0:NCM * 128])
            k.memset('pool', zrow[:], 0.0)
            k.memset('pool', vA[:, :, 128:129], 1.0)
            k.memset('pool', cvA[:, :, 128:129], 1.0)
            k.memset('pool', vmA[:, :, 128:129], 1.0)
            k.memset('pool', S_h[:], 0.0)
            k.memset('pool', C_m[:], 0.0)
            k.memset('pool', cqx[:, 0:3], 0.0)
            k.memset('pool', ckx[:, 0:3], 0.0)
            k.memset('pool', AT[:], 0.0)
            k.memset('dve', sm[:], 0.0)
            wv = wfm_d.rearrange("(c p) n -> p c n", p=128)
            for c in range(8):
                k.dma(stage[:, 0:NFM * 128], wv[:, c, :], "st")
                k.ts('dve', wfm[:, c, :], stage[:, 0:NFM * 128], col(CC_NG + c))
            wv = wtm_d.rearrange("(c p) n -> p c n", p=128)
            for c in range(8):
                k.dma(stage[:, 0:NTM], wv[:, c, :], "st")
                k.ts('dve', wtm[:, c, :], stage[:, 0:NTM], col(CC_NG + c))

            _lvl(2)
            LB, OMLB, NLAM, R1 = sm[:, 0:1], sm[:, 1:2], sm[:, 2:3], sm[:, 3:4]
            if layer_idx == 0:
                k.memset('dve', LB, 0.0)
            else:
                k.tt('dve', sm[:, 4:5], col(CC_LBL), col(CC_LBL + 1), ALU.subtract)
                k.act(sm[:, 5:6], sm[:, 4:5], AF.Exp)
                k.ts('dve', sm[:, 5:6], sm[:, 5:6], 1.0, None, op0=ALU.add)
                k.recip(LB, sm[:, 5:6])
            k.ts('dve', OMLB, LB, -1.0, 1.0, op0=ALU.mult, op1=ALU.add)
            k.tt('dve', T[0][:, 0:64], lamt[:, 0:64], lamt[:, 64:128], ALU.mult)
            k.tt('dve', T[0][:, 64:128], lamt[:, 128:192], lamt[:, 192:256], ALU.mult)
            k.s.emit('dve', lambda e: e.reduce_sum(sm[:, 6:7], T[0][:, 0:64], mybir.AxisListType.X),
                     outs=[sm[:, 6:7]], ins=[T[0][:, 0:64]])
            k.s.emit('dve', lambda e: e.reduce_sum(sm[:, 7:8], T[0][:, 64:128], mybir.AxisListType.X),
                     outs=[sm[:, 7:8]], ins=[T[0][:, 64:128]])
            k.act(sm[:, 8:10], sm[:, 6:8], AF.Exp)
            k.tt('dve', sm[:, 10:11], sm[:, 9:10], sm[:, 8:9], ALU.subtract)
            k.ts('dve', NLAM, sm[:, 10:11], -lam_init, None, op0=ALU.add)

            _lvl(3)
            mv = memT_d.rearrange("(c p) n -> p c n", p=128)
            k.dma(xtm, mv[:, :, :], "x0")
            wk = wkv_d.rearrange("(c p) n -> p c n", p=128)
            mw = xb[:, :, 256:512]
            for c in range(8):
                k.dma(stage[:, 0:256], wk[:, c, :], "st")
                k.ts('dve', mw[:, c, :], stage[:, 0:256], col(CC_MNG + c))
                k.cp('pool', xb[:, c, 0:256], xtm[:, c, :])
            for mt in range(2):
                pc = gbank()
                for c in range(8):
                    k.act(sqb[:, c % 2, 0:128], xtm[:, c, mt * 128:(mt + 1) * 128], AF.Square)
                    k.mm(pc[:, 0:1], sqb[:, c % 2, 0:128], cmB(CM_ONES)[:, 0:1], start=(c == 0), stop=(c == 7))
                k.rsqrt(gcol[:, 0, 0:1], pc[:, 0:1], 1.0 / D_MODEL, epsc, gcol[:, 0, 1:2])
                pk = gbank()
                for c in range(8):
                    k.mm(pk[:, 0:256], xb[:, c, mt * 128:(mt + 1) * 128], mw[:, c, :], start=(c == 0), stop=(c == 7))
                k.ts('dve', T[0][:, 0:128], pk[:, 0:128], gcol[:, 0, 0:1])
                k.ts('dve', vmA[:, mt, 0:128], pk[:, 128:256], gcol[:, 0, 0:1])
                k.act(T[1][:, 0:128], T[0][:, 0:128], AF.Square, accum_out=gcol[:, 0, 2:3])
                k.rsqrt(gcol[:, 0, 3:4], gcol[:, 0, 2:3], 1.0 / 128, epsc, gcol[:, 0, 4:5])
                k.ts('dve', Bt[0][:, 0:128], T[0][:, 0:128], gcol[:, 0, 3:4])
                pt_ = gbank()
                ptb = pt_[:, 0:64].bitcast(BF16)
                k.tr(ptb, Bt[0][:, 0:128], cmB(CM_IDENT))
                k.tt('dve', sm[:, 11:12], col(CC_GXQ), col(CC_GXK), ALU.mult)
                k.ts('dve', kmT[:, mt * 128:(mt + 1) * 128], ptb, sm[:, 11:12], 128 ** -0.5, op0=ALU.mult, op1=ALU.mult)

            _lvl(4)
            for bi in (range(NBLK) if do_rope else ()):
                tsl = slice(bi * TB, (bi + 1) * TB)
                pi_ = T[5].bitcast(I32)
                k.dma(pi_[:, :], pos_d[0:1, tsl].partition_broadcast(128), "x1")
                k.cp('dve', T[0][:], pi_[:, :])
                k.ts('dve', T[1][:], T[0][:], col(CC_INVF))
                for which, dst in ((0, cosT), (1, sinT)):
                    src = T[1]
                    if which == 0:
                        k.ts('dve', T[2][:], T[1][:], math.pi / 2, None, op0=ALU.add)
                        src = T[2]
                    k.ts('dve', T[3][:], src[:], 1.0 / TWO_PI, 12582912.0, op0=ALU.mult, op1=ALU.add)
                    k.ts('dve', T[3][:], T[3][:], -12582912.0, None, op0=ALU.add)
                    k.stt(T[4][:], T[3][:], -6.28125, src[:], ALU.mult, ALU.add)
                    k.stt(T[4][:], T[3][:], -(TWO_PI - 6.28125), T[4][:], ALU.mult, ALU.add)
                    k.ts('dve', T[4][:], T[4][:], 3.1415925, -3.1415925, op0=ALU.min, op1=ALU.max)
                    k.act(dst[:], T[4][:], AF.Sin)
                k.ts('dve', sinT[:], sinT[:], col(CC_SGN))
                k.dma(cs_d[0, :, tsl], cosT[:], "y0")
                k.dma(cs_d[1, :, tsl], sinT[:], "y1")

            xv = xT_d.rearrange("(c p) t -> p c t", p=128)
            yv = yT_d.rearrange("(g p) t -> p g t", p=128) if yT_d is not None else None

            for bi in range(NBLK):
                t0 = bi * TB
                tsl = slice(t0, t0 + TB)
                _lvl(5)
                if bi == 0:
                    k.dma(cosT[:], cs_d[0, :, tsl], "x2")
                    k.dma(sinT[:], cs_d[1, :, tsl], "x3")
                    k.dma(xt[:], xv[:, 0:4, tsl], "x0")
                prx = gbank()
                prc = gbank()
                if bi == 0:
                    k.dma(xt2, xv[:, 4:8, tsl], "x1")
                for c in range(8):
                    xsrc = xt[:, c, :] if c < 4 else xt2[:, c - 4, :]
                    k.cp('act', xb[:, c, :], xsrc)
                    k.act(sqb[:, c % 2, :], xsrc, AF.Square)
                    k.mm(prx[:], cmB(CM_ONES), sqb[:, c % 2, :], start=(c == 0), stop=(c == 7))
                    for tt_ in range(4):
                        k.mm(prc[:, tt_:tt_ + 1], sqb[:, c % 2, tt_ * 128:(tt_ + 1) * 128], cmB(CM_ONES)[:, 0:1],
                             start=(c == 0 and tt_ == 0), stop=(c == 7), skip=True)
                if bi + 1 < NBLK:
                    k.dma(xt[:], xv[:, 0:4, t0 + TB:t0 + 2 * TB], "x0")
                k.rsqrt(RX[:], prx[:], 1.0 / D_MODEL, epsc, T[0][:])
                k.rsqrt(rxc[:, 0:4], prc[:, 0:4], 1.0 / D_MODEL, epsc, rxc[:, 4:8])
                k.ts('dve', rxc[:, 4:8], rxc[:, 0:4], -1.0)

                def fm_proj(j):
                    p = gbank()
                    for c in range(8):
                        k.mm(p[:], wfm[:, c, j * 128:(j + 1) * 128], xb[:, c, :], start=(c == 0), stop=(c == 7))
                    return p

                _lvl(6)
                p = fm_proj(0)
                k.tt('dve', T[0][:], p[:], RX[:], ALU.mult)
                k.act(T[1][:], T[0][:], AF.Exp, scale=-1.0)
                k.ts('dve', T[1][:], T[1][:], 1.0, None, op0=ALU.add)
                k.recip(T[1][:], T[1][:])
                k.stt(T[0][:], T[0][:], 128 ** -0.5, T[1][:], ALU.mult, ALU.mult)
                p = fm_proj(1)
                k.tt('dve', T[1][:], p[:], RX[:], ALU.mult)
                k.act(T[2][:], T[1][:], AF.Exp, scale=-1.0)
                k.ts('dve', T[2][:], T[2][:], 1.0, None, op0=ALU.add)
                k.recip(T[2][:], T[2][:])
                k.ts('dve', T[2][:], T[2][:], OMLB, LB, op0=ALU.mult, op1=ALU.add)
                k.act(T[3][:], T[2][:], AF.Ln)
                k.ts('dve', T[2][:], T[2][:], -1.0, 1.0, op0=ALU.mult, op1=ALU.add)
                k.scan(bcum[:], rmask[:], T[3][:], 0.0, ALU.mult, ALU.add)
                for c8 in range(8):
                    cs_ = slice(c8 * 64, (c8 + 1) * 64)
                    k.ts('dve', T[3][:, cs_], bcum[:, cs_], bcum[:, c8 * 64 + 31:c8 * 64 + 32], None, op0=ALU.subtract)
                k.act(T[4][:], T[3][:], AF.Exp)
                k.act(T[5][:], T[3][:], AF.Exp, scale=-1.0)
                k.tt('dve', qtT[:], T[0][:], T[4][:], ALU.mult)
                k.tt('dve', ktT[:], T[2][:], T[5][:], ALU.mult)
                EREF, ELAST, CFAC = sm[:, 16:24], sm[:, 24:32], sm[:, 32:40]
                k.act(EREF, bcum[:, 31:TB:64], AF.Exp)
                k.act(ELAST, bcum[:, 63:TB:64], AF.Exp)
                k.tt('dve', sm[:, 40:48], bcum[:, 63:TB:64], bcum[:, 31:TB:64], ALU.subtract)
                k.act(CFAC, sm[:, 40:48], AF.Exp)

                _lvl(7)
                def qk_norm_rope(j, gcol_, scale_, dst):
                    p_ = fm_proj(j)
                    k.tt('dve', T[0][:], p_[:], RX[:], ALU.mult)
                    k.act(Bt[0][:], T[0][:], AF.Square)
                    pss = gbank()
                    k.mm(pss[:], cmB(CM_BLK64), Bt[0][:])
                    k.rsqrt(T[1][:], pss[:], 1.0 / 64, epsc, T[2][:])
                    k.ts('dve', Bt[1][:], T[0][:], gcol_)
                    pp = gbank()
                    k.mm(pp[:], cmB(CM_PROPE), Bt[1][:])
                    k.tt('dve', T[2][:], Bt[1][:], cosT[:], ALU.mult)
                    k.tt('dve', T[3][:], pp[:], sinT[:], ALU.mult)
                    k.tt('dve', T[2][:], T[2][:], T[3][:], ALU.add)
                    k.stt(dst, T[2][:], scale_, T[1][:], ALU.mult, ALU.mult)

                qk_norm_rope(2, col(CC_GQ), 0.125, qhT[:])
                qk_norm_rope(3, col(CC_GK), 1.0, kT[:, tsl])
                if bi + 1 < NBLK:
                    k.dma(cosT[:], cs_d[0, :, t0 + TB:t0 + 2 * TB], "x2")
                    k.dma(sinT[:], cs_d[1, :, t0 + TB:t0 + 2 * TB], "x3")

                _lvl(8)
                def conv_silu(j, ext, wcol, bcol, scale_, dst):
                    p_ = fm_proj(j)
                    k.tt('dve', ext[:, 3:TB + 3], p_[:], RX[:], ALU.mult)
                    k.ts('dve', T[0][:], ext[:, 0:TB], cc[:, wcol:wcol + 1], cc[:, bcol:bcol + 1], op0=ALU.mult, op1=ALU.add)
                    for jj in range(1, 4):
                        k.stt(T[0][:], ext[:, jj:TB + jj], cc[:, wcol + jj:wcol + jj + 1], T[0][:], ALU.mult, ALU.add)
                    k.cp('pool', ext[:, 0:3], ext[:, TB:TB + 3])
                    k.act(T[1][:], T[0][:], AF.Exp, scale=-1.0)
                    k.ts('dve', T[1][:], T[1][:], 1.0, None, op0=ALU.add)
                    k.recip(T[1][:], T[1][:])
                    k.stt(dst, T[0][:], scale_, T[1][:], ALU.mult, ALU.mult)

                conv_silu(4, cqx, CC_CWQ, CC_CBQ, 1.0, qcT[:])
                conv_silu(5, ckx, CC_CWK, CC_CBK, 128 ** -0.5, kcT[:])

                _lvl(9)
                p = fm_proj(6)
                k.tt('dve', T[0][:], p[:], RX[:], ALU.mult)
                k.act(Bt[0][:], T[0][:], AF.Square)
                pss = gbank()
                k.mm(pss[:], cmB(CM_ONES), Bt[0][:])
                k.rsqrt(T[1][:], pss[:], 1.0 / 128, epsc, T[2][:])
                k.tt('dve', qxn[:], T[0][:], T[1][:], ALU.mult)
                for mt in range(2):
                    pS = gbank()
                    k.mm(pS[:], kmT[:, mt * 128:(mt + 1) * 128], qxn[:])
                    k.act(PT[:, mt, :], pS[:], AF.Exp)

                _lvl(10)
                for tt_ in range(4):
                    tcs = slice(tt_ * 128, (tt_ + 1) * 128)
                    pA, pB, pC = gbank(), gbank(), gbank()
                    for c in range(8):
                        k.mm(pA[:], xb[:, c, tcs], wtm[:, c, 0:512], start=(c == 0), stop=(c == 7))
                    for c in range(8):
                        k.mm(pB[:], xb[:, c, tcs], wtm[:, c, 512:1024], start=(c == 0), stop=(c == 7))
                    for c in range(8):
                        k.mm(pC[:, 0:2], xb[:, c, tcs], wtm[:, c, 1024:1026], start=(c == 0), stop=(c == 7))
                    rc = rxc[:, tt_:tt_ + 1]
                    nrc = rxc[:, 4 + tt_:5 + tt_]
                    k.act(T[0][:], pA[:], AF.Exp, scale=nrc)
                    k.ts('dve', T[0][:], T[0][:], 1.0, None, op0=ALU.add)
                    k.recip(T[0][:], T[0][:])
                    k.stt(gates[:, tt_, 0:4, :].rearrange("p g d -> p (g d)"), pA[:], rc, T[0][:], ALU.mult, ALU.mult)
                    k.act(T[1][:, 0:128], pB[:, 0:128], AF.Exp, scale=nrc)
                    k.ts('dve', T[1][:, 0:128], T[1][:, 0:128], 1.0, None, op0=ALU.add)
                    k.recip(gates[:, tt_, 4, :], T[1][:, 0:128])
                    k.ts('dve', av[:, tt_, :], pB[:, 128:256], rc)
                    k.ts('dve', vA[:, bi * 4 + tt_, 0:128], pB[:, 256:384], rc)
                    k.ts('dve', cvA[:, tt_, 0:128], pB[:, 384:512], rc)
                    g_ = gcol[:, tt_, :]
                    k.stt(g_[:, 0:1], pC[:, 0:1], rc, col(CC_GBI), ALU.mult, ALU.add)
                    k.stt(g_[:, 2:3], pC[:, 1:2], rc, col(CC_GBF), ALU.mult, ALU.add)
                    k.act(g_[:, 3:4], g_[:, 2:3], AF.Exp, scale=-1.0)
                    k.act(g_[:, 4:5], g_[:, 3:4], AF.Ln, bias=col(CC_ONE))
                    k.ts('dve', g_[:, 1:2], g_[:, 4:5], -1.0)
                    k.ts('dve', g_[:, 5:6], g_[:, 1:2], col(CC_IND0))
                    k.ts('dve', g_[:, 6:7], g_[:, 1:2], col(CC_IND1))
                    pc2 = gbank()
                    k.mm(pc2[:, 0:1], cmF(CM_TRI64), g_[:, 1:2])
                    k.mm(pc2[:, 1:2], cmF(CM_BLK64), g_[:, 1:2], skip=True)
                    k.mm(pc2[:, 2:4], cmF(CM_ONES), g_[:, 5:7], skip=True)
                    k.cp('dve', g_[:, 7:11], pc2[:, 0:4])
                    k.tt('dve', g_[:, 11:12], g_[:, 0:1], g_[:, 7:8], ALU.subtract)
                    k.tt('dve', g_[:, 12:13], g_[:, 11:12], g_[:, 8:9], ALU.add)
                    k.act(g_[:, 13:14], g_[:, 11:12], AF.Exp)
                    k.act(g_[:, 14:15], g_[:, 12:13], AF.Exp)
                    k.act(g_[:, 15:16], g_[:, 7:8], AF.Exp)
                    k.act(sm[:, 48 + 2 * tt_:50 + 2 * tt_], g_[:, 9:11], AF.Exp)

                _lvl(11)
                for tt_ in range(4):
                    tcs = slice(tt_ * 128, (tt_ + 1) * 128)
                    po = gbank()
                    for mt in range(2):
                        k.mm(po[:, 0:129], PT[:, mt, tcs], vmA[:, mt, :], start=(mt == 0), stop=(mt == 1))
                    k.recip(gcol[:, tt_, 5:6], po[:, 128:129])
                    k.stt(yTM[:, tt_, 3, :], po[:, 0:128], gcol[:, tt_, 5:6], gates[:, tt_, 3, :], ALU.mult, ALU.mult)

                _lvl(12)
                for tt_ in range(4):
                    tcs = slice(tt_ * 128, (tt_ + 1) * 128)
                    ptr = gbank()
                    ptb = ptr[:, 0:64].bitcast(BF16)
                    k.tr(ptb, ktT[:, tcs], cmB(CM_IDENT))
                    k.cp('dve', kTM[:, tt_, :], ptb)
                _lvl(12.1)
                for tt_ in range(4):
                    pd = gbank()
                    for hh in range(2):
                        c8 = tt_ * 2 + hh
                        rs = slice(hh * 64, hh * 64 + 64)
                        k.mm(pd[:, hh * 128:(hh + 1) * 128], kTM[rs, tt_, :], av[rs, tt_, :], skip=(hh == 1))
                        k.ts('dve', dS[:, c8, :], pd[:, hh * 128:(hh + 1) * 128], CFAC[:, c8:c8 + 1])
                _lvl(12.2)
                for c8 in range(8):
                    k.ts('dve', Sb_h[:, c8, :], S_h[:], EREF[:, c8:c8 + 1])
                    k.stt(S_h[:], S_h[:], ELAST[:, c8:c8 + 1], dS[:, c8, :], ALU.mult, ALU.add)
                _lvl(12.3)
                for tt_ in range(4):
                    tcs = slice(tt_ * 128, (tt_ + 1) * 128)
                    pa = gbank()
                    for hh in range(2):
                        cs_ = slice(tt_ * 128 + hh * 64, tt_ * 128 + hh * 64 + 64)
                        rs = slice(hh * 64, hh * 64 + 64)
                        k.mm(pa[rs, hh * 64:hh * 64 + 64], ktT[:, cs_], qtT[:, cs_], skip=(hh == 1))
                        k.stt(AT[rs, 0, hh * 64:hh * 64 + 64], pa[rs, hh * 64:hh * 64 + 64], 3.0e38,
                              cmB(CM_TRI64)[rs, hh * 64:hh * 64 + 64], ALU.min, ALU.mult)
                    _lvl(12.4)
                    po = gbank()
                    for hh in range(2):
                        cs_ = slice(tt_ * 128 + hh * 64, tt_ * 128 + hh * 64 + 64)
                        rs = slice(hh * 64, hh * 64 + 64)
                        k.mm(po[rs, 0:128], AT[rs, 0, hh * 64:hh * 64 + 64], av[rs, tt_, :], start=True, stop=False, skip=True)
                        k.mm(po[rs, 0:128], qtT[:, cs_], Sb_h[:, tt_ * 2 + hh, :], start=False, stop=True, skip=True)
                    g_ = gcol[:, tt_, :]
                    k.act(T[0][:, 0:128], po[:, 0:128], AF.Square, accum_out=g_[:, 5:6])
                    k.rsqrt(g_[:, 6:7], g_[:, 5:6], 1.0 / 128, epsc, g_[:, 5:6])
                    k.stt(T[0][:, 0:128], po[:, 0:128], g_[:, 6:7], crow[:, CR_HG * 128:(CR_HG + 1) * 128], ALU.mult, ALU.mult)
                    k.tt('dve', yTM[:, tt_, 0, :], T[0][:, 0:128], gates[:, tt_, 0, :], ALU.mult)

                _lvl(13)
                for tt_ in range(4):
                    tcs = slice(tt_ * 128, (tt_ + 1) * 128)
                    ptr = gbank()
                    ptb = ptr[:, 0:64].bitcast(BF16)
                    k.tr(ptb, kcT[:, tcs], cmB(CM_IDENT))
                    k.ts('dve', kTM2[:, tt_, :], ptb, gcol[:, tt_, 14:15])
                for tt_ in range(4):
                    for hh in range(2):
                        c8 = tt_ * 2 + hh
                        rs = slice(hh * 64, hh * 64 + 64)
                        pd = gbank()
                        k.mm(pd[:, 0:129], kTM2[rs, tt_, :], cvA[rs, tt_, :])
                        k.cp('dve', dC[:, c8, :], pd[:, 0:129])
                for c8 in range(8):
                    k.cp('dve', Cb_m[:, c8, :], C_m[:])
                    k.stt(C_m[:], C_m[:], sm[:, 48 + c8:49 + c8], dC[:, c8, :], ALU.mult, ALU.add)
                for tt_ in range(4):
                    tcs = slice(tt_ * 128, (tt_ + 1) * 128)
                    g_ = gcol[:, tt_, :]
                    pa = gbank()
                    k.mm(pa[:, 0:128], kcT[:, tcs], qcT[:, tcs])
                    k.stt(AT[:, 1, :], pa[:, 0:128], g_[:, 13:14], cmB(CM_TRI64), ALU.mult, ALU.mult)
                    po = gbank()
                    for hh in range(2):
                        cs_ = slice(tt_ * 128 + hh * 64, tt_ * 128 + hh * 64 + 64)
                        rs = slice(hh * 64, hh * 64 + 64)
                        k.mm(po[rs, 0:129], AT[rs, 1, hh * 64:hh * 64 + 64], cvA[rs, tt_, :], start=True, stop=False, skip=True)
                        k.mm(po[rs, 0:129], qcT[:, cs_], Cb_m[:, tt_ * 2 + hh, :], start=False, stop=True, skip=True)
                    k.act(g_[:, 5:6], po[:, 128:129], AF.Abs, scale=g_[:, 15:16])
                    k.ts('dve', g_[:, 5:6], g_[:, 5:6], 1.0, None, op0=ALU.max)
                    k.recip(g_[:, 5:6], g_[:, 5:6])
                    k.tt('dve', g_[:, 5:6], g_[:, 5:6], g_[:, 15:16], ALU.mult)
                    k.stt(T[0][:, 0:128], po[:, 0:128], g_[:, 5:6], gates[:, tt_, 4, :], ALU.mult, ALU.mult)
                    k.act(T[1][:, 0:128], T[0][:, 0:128], AF.Square, accum_out=g_[:, 6:7])
                    k.rsqrt(g_[:, 5:6], g_[:, 6:7], 1.0 / 128, epsc, g_[:, 6:7])
                    k.stt(T[0][:, 0:128], T[0][:, 0:128], g_[:, 5:6], crow[:, CR_MG * 128:(CR_MG + 1) * 128], ALU.mult, ALU.mult)
                    k.tt('dve', yTM[:, tt_, 2, :], T[0][:, 0:128], gates[:, tt_, 2, :], ALU.mult)

                _lvl(14)
                if bi + 1 < NBLK:
                    k.dma(xt2, xv[:, 4:8, t0 + TB:t0 + 2 * TB], "x1")
                for ob in range(3):
                    k.mm(PS_O[ob][:], cmB(CM_ZERO), zrow[:], start=True, stop=False, sig=True, skip=True)
                oslot = {}
                for mp in range(2):
                    for qq in range(4):
                        idx = mp * 4 + qq
                        oslot[(mp, qq)] = (PS_O[idx // 3], (idx % 3) * 129)
                nkt = bi * 4 + 4
                SBK = [[PS_S[0], PSB[0]], [PS_S[1], PSB[1]]]

                def s_pair(j_):
                    q0_ = max(j_ - bi * 4, 0) * 128
                    for mp_ in range(2):
                        rs_ = slice(mp_ * 64, mp_ * 64 + 64)
                        k.mm(SBK[mp_][j_ % 2][:, q0_:TB], kT[rs_, j_ * 128:(j_ + 1) * 128], qhT[rs_, q0_:TB])

                s_pair(0)
                if nkt > 1:
                    s_pair(1)
                for j in range(nkt):
                    jj = j - bi * 4
                    q0 = max(jj, 0) * 128
                    for mp in range(2):
                        pt_ = PT[:, 2 * (j % 2) + mp, :]
                        k.act(pt_[:, q0:TB], SBK[mp][j % 2][:, q0:TB], AF.Exp)
                        if jj >= 0:
                            k.tt('pool', pt_[:, q0:q0 + 128], pt_[:, q0:q0 + 128], cmB(CM_TRI128), ALU.mult)
                    for mp in range(2):
                        pt_ = PT[:, 2 * (j % 2) + mp, :]
                        for qq in range(max(jj, 0), 4):
                            ob, oc = oslot[(mp, qq)]
                            last = (j == min(nkt - 1, bi * 4 + qq))
                            k.mm(ob[:, oc:oc + 129], pt_[:, qq * 128:(qq + 1) * 128], vA[:, j, :],
                                 start=False, stop=last, sig=last, skip=True)
                    if j + 2 < nkt:
                        s_pair(j + 2)
                for tt_ in range(4):
                    g_ = gcol[:, tt_, :]
                    ob1, oc1 = oslot[(0, tt_)]
                    ob2, oc2 = oslot[(1, tt_)]
                    k.recip(g_[:, 5:6], ob1[:, oc1 + 128:oc1 + 129])
                    k.recip(g_[:, 6:7], ob2[:, oc2 + 128:oc2 + 129])
                    k.tt('dve', g_[:, 6:7], g_[:, 6:7], NLAM, ALU.mult)
                    k.ts('dve', T[0][:, 0:128], ob1[:, oc1:oc1 + 128], g_[:, 5:6])
                    k.stt(T[0][:, 0:128], ob2[:, oc2:oc2 + 128], g_[:, 6:7], T[0][:, 0:128], ALU.mult, ALU.add)
                    k.act(T[1][:, 0:128], T[0][:, 0:128], AF.Square, accum_out=g_[:, 5:6])
                    k.rsqrt(g_[:, 6:7], g_[:, 5:6], 1.0 / 128, epsc, g_[:, 5:6])
                    k.ts('dve', g_[:, 6:7], g_[:, 6:7], 1.0 - lam_init)
                    k.stt(T[0][:, 0:128], T[0][:, 0:128], g_[:, 6:7], crow[:, CR_SUB * 128:(CR_SUB + 1) * 128], ALU.mult, ALU.mult)
                    k.tt('dve', yTM[:, tt_, 1, :], T[0][:, 0:128], gates[:, tt_, 1, :], ALU.mult)

                _lvl(15)
                for tt_ in range(4):
                    pty = gbank()
                    ptb = pty[:, 0:256].bitcast(BF16)
                    for g4 in range(4):
                        k.tr(ptb[:, g4 * 128:(g4 + 1) * 128], yTM[:, tt_, g4, :], cmB(CM_IDENT))
                    k.cp('dve', yT[:, :, tt_ * 128:(tt_ + 1) * 128], ptb.rearrange("p (g t) -> p g t", g=4))
                if not fused:
                    k.dma(yv[:, :, tsl], yT[:], "yu", eng='pool')
                else:
                    ch, half = bi // 2, bi % 2
                    for r in range(4):
                        k.act(yTr[:, r % 2, :, :], yT[:], AF.Copy, scale=col(CC_RANK + r))
                        k.dma(D['yin'][ch, r].rearrange("(g p) t -> p g t", p=128)[:, :, half * TB:(half + 1) * TB],
                              yTr[:, r % 2, :, :], "ys%d" % r, eng='pool')
                    if half == 1 or bi == NBLK - 1:
                        k.cc("AllReduce", ALU.add, [[0, 1, 2, 3], [4, 5, 6, 7]],
                             D['yin'][ch].rearrange("r p t -> (r p) t"), D['yout'][ch], "cc")

        except _Stop:
            pass
        k.s.barrier()


def layer_dram(nc, S, sfx, shared=None):
    dr = lambda name, shape, dt=F32: nc.dram_tensor(name, shape, dt, kind="ExternalInput").ap()
    D = {}
    if shared is None:
        D['pos'] = dr("pos", [1, S], I32)
        D['memT'] = dr("memT", [D_MODEL, N_MEM])
        D['cmat'] = dr("cmat", [128, NCM, 128])
        D['rmask'] = dr("rmask", [128, TB])
        D['cs'] = nc.dram_tensor("cs_scr", [2, 128, S], F32, kind="Internal").ap()
    else:
        for n_ in ('pos', 'memT', 'cmat', 'rmask', 'cs'):
            D[n_] = shared[n_]
    D['w_fm'] = dr("w_fm" + sfx, [D_MODEL, NFM * 128])
    D['w_tm'] = dr("w_tm" + sfx, [D_MODEL, NTM])
    D['w_kv'] = dr("w_kv" + sfx, [D_MODEL, 256])
    D['ccol'] = dr("ccol" + sfx, [128, NCC])
    D['crow'] = dr("crow" + sfx, [1, 3 * 128])
    D['lam'] = dr("lam" + sfx, [1, 256])
    return D


def build_layer(S, layer_idx):
    nc = bass.Bass("TRN2", target_bir_lowering=False)
    D = layer_dram(nc, S, "")
    D['xT'] = nc.dram_tensor("xT", [D_MODEL, S], F32, kind="ExternalInput").ap()
    D['yT'] = nc.dram_tensor("yT", [4 * 128, S], BF16, kind="ExternalOutput").ap()
    with ExitStack() as es:
        es.enter_context(nc.allow_low_precision(reason="bf16 matmul operands"))
        k = K(nc, es)
        PSB = [k.ps(f"pb{i}", [128, 512]) for i in range(8)]
        emit_layer(nc, k, PSB, S, layer_idx, "", D, do_rope=True, fused=False)
        k.s.finish()
    return nc


def emit_outproj(nc, k, PSB, S, sfx, D, full):
    NBLK = S // TB
    with ExitStack() as es:
        k.es = es
        sb = lambda name, shape, dt: k.sb(name + sfx, shape, dt)
        stage = sb("ostage", [128, D_MODEL], F32)
        wo = sb("wo", [128, 16, 256], BF16)
        yt = sb("yt", [128, 16, TB], BF16)
        xo = sb("xo", [128, 2, TB], F32)
        oo = sb("oo", [128, 2, TB], F32)
        wv = D['w_own'].rearrange("(c p) n -> p c n", p=128)
        for c in range(16):
            k.dma(stage[:, 0:256], wv[:, c, :], "st")
            k.cp('dve', wo[:, c, :], stage[:, 0:256])
        if full:
            w = sb("wfull", [128, 16, D_MODEL], BF16)
            xt = sb("oxt", [128, 8, TB], F32)
            ot = sb("oot", [128, 8, TB], F32)
            wv = D['w_out'].rearrange("(c p) n -> p c n", p=128)
            for c in range(16):
                k.dma(stage[:], wv[:, c, :], "st")
                k.cp('dve', w[:, c, :], stage[:])
            xv = D['x_prev'].rearrange("(c p) t -> p c t", p=128)
            x1v = D['x1'].rearrange("(c p) t -> p c t", p=128)
        xov = D['x_own_prev'].rearrange("(c p) t -> p c t", p=128)
        outv = (D['x1own'] if full else D['out']).rearrange("(c p) t -> p c t", p=128)
        pi = [0]
        for bi in range(NBLK):
            tsl = slice(bi * TB, (bi + 1) * TB)
            ch, half = bi // 2, bi % 2
            k.dma(yt[:], D['yout'][ch].rearrange("(c p) t -> p c t", p=128)[:, :, half * TB:(half + 1) * TB], "x0")
            k.dma(xo[:], xov[:, :, tsl], "x1")
            if full:
                k.dma(xt[:], xv[:, :, tsl], "x2")
                for oc in range(8):
                    p = PSB[pi[0] % 8]
                    pi[0] += 1
                    for c in range(16):
                        k.mm(p[:], w[:, c, oc * 128:(oc + 1) * 128], yt[:, c, :], start=(c == 0), stop=(c == 15),
                             sig=(c == 15))
                    k.tt('dve', ot[:, oc, :], p[:], xt[:, oc, :], ALU.add)
                k.dma(x1v[:, :, tsl], ot[:], "y0")
            for oc in range(2):
                p = PSB[pi[0] % 8]
                pi[0] += 1
                for c in range(16):
                    k.mm(p[:], wo[:, c, oc * 128:(oc + 1) * 128], yt[:, c, :], start=(c == 0), stop=(c == 15),
                         sig=(c == 15))
                k.tt('dve', oo[:, oc, :], p[:], xo[:, oc, :], ALU.add)
            k.dma(outv[:, :, tsl], oo[:], "y1")
        k.s.barrier()


def build_fused(S):
    nc = bass.Bass("TRN2", target_bir_lowering=False)
    NCH = max(S // 1024, 1)
    D0 = layer_dram(nc, S, "_0")
    D1 = layer_dram(nc, S, "_1", shared=D0)
    x0 = nc.dram_tensor("xT", [D_MODEL, S], F32, kind="ExternalInput").ap()
    x0own = nc.dram_tensor("x0own", [256, S], F32, kind="ExternalInput").ap()
    w_out0 = nc.dram_tensor("w_out0", [2048, D_MODEL], F32, kind="ExternalInput").ap()
    w_own0 = nc.dram_tensor("w_own0", [2048, 256], F32, kind="ExternalInput").ap()
    w_own1 = nc.dram_tensor("w_own1", [2048, 256], F32, kind="ExternalInput").ap()
    out = nc.dram_tensor("out", [256, S], F32, kind="ExternalOutput").ap()
    yin = nc.dram_tensor("yin", [NCH, 4, 512, min(S, 1024)], BF16).ap()
    yout = nc.dram_tensor("yout", [NCH, 2048, min(S, 1024)], BF16).ap()
    x1 = nc.dram_tensor("x1", [D_MODEL, S], F32).ap()
    x1own = nc.dram_tensor("x1own", [256, S], F32).ap()
    for D in (D0, D1):
        D['yin'], D['yout'] = yin, yout
    D0['xT'] = x0
    D1['xT'] = x1
    with ExitStack() as es:
        es.enter_context(nc.allow_low_precision(reason="bf16 matmul operands"))
        k = K(nc, es)
        PSB = [k.ps(f"pb{i}", [128, 512]) for i in range(8)]
        emit_layer(nc, k, PSB, S, 0, "_a", D0, do_rope=True, fused=True)
        emit_outproj(nc, k, PSB, S, "_p", {'w_own': w_own0, 'w_out': w_out0, 'x_prev': x0, 'x1': x1,
                                           'x_own_prev': x0own, 'x1own': x1own, 'yout': yout}, full=True)
        emit_layer(nc, k, PSB, S, 1, "_b", D1, do_rope=False, fused=True)
        emit_outproj(nc, k, PSB, S, "_q", {'w_own': w_own1, 'x_own_prev': x1own, 'out': out, 'yout': yout}, full=False)
        k.s.finish()
    return nc


def head_cols(h):
    G = GROUP
    base = np.arange(128)
    off = {}
    names = ['a_q', 'a_f', 'a_i', 'a_g', 'b_q', 'b_k', 'b_v', 'b_g', 'c_q', 'c_k', 'c_v', 'c_o']
    o = 0
    for n in names:
        off[n] = o
        o += G
    off['c_i'] = o
    o += 4
    off['c_f'] = o
    o += 4
    off['c_g'] = o
    o += G
    off['x_q'] = o
    o += G
    off['x_g'] = o
    o += G
    hc = lambda n: off[n] + h * 128 + base
    fm = np.concatenate([hc('a_q'), hc('a_f'), hc('b_q'), hc('b_k'), hc('c_q'), hc('c_k'), hc('x_q')])
    tm = np.concatenate([hc('a_g'), hc('b_g'), hc('c_g'), hc('x_g'), hc('c_o'), hc('a_i'), hc('b_v'), hc('c_v'),
                         np.array([off['c_i'] + h, off['c_f'] + h])])
    return fm, tm


def layer_inputs(l, b, h, xT_b, inp, cmat, rmask):
    fm, tm = head_cols(h)
    w_in = inp['w_in'][l]
    cc = np.zeros((128, NCC), np.float32)
    cc[:, CC_NG:CC_NG + 8] = inp['norm_g'][l].reshape(8, 128).T
    cc[:, CC_MNG:CC_MNG + 8] = inp['mem_norm_g'][l].reshape(8, 128).T
    invf = (ROPE_THETA ** (-np.arange(0, 16, 2, dtype=np.float32) / 16)).astype(np.float32)
    for base in (0, 64):
        cc[base:base + 8, CC_INVF] = invf
        cc[base + 8:base + 16, CC_INVF] = invf
        cc[base:base + 8, CC_SGN] = -1.0
        cc[base + 8:base + 16, CC_SGN] = 1.0
    cc[:, CC_GQ] = np.tile(inp['diff_qk_norm_g'][l, 0], 2)
    cc[:, CC_GK] = np.tile(inp['diff_qk_norm_g'][l, 1], 2)
    cw = inp['mlstm_conv_w'][l]
    cb = inp['mlstm_conv_b'][l]
    cc[:, CC_CWQ:CC_CWQ + 4] = cw[:, h * 128:(h + 1) * 128].T
    cc[:, CC_CBQ] = cb[h * 128:(h + 1) * 128]
    cc[:, CC_CWK:CC_CWK + 4] = cw[:, GROUP + h * 128:GROUP + (h + 1) * 128].T
    cc[:, CC_CBK] = cb[GROUP + h * 128:GROUP + (h + 1) * 128]
    cc[:, CC_GXQ] = inp['xattn_qk_norm_g'][l, 0]
    cc[:, CC_GXK] = inp['xattn_qk_norm_g'][l, 1]
    cc[:, CC_LBL:CC_LBL + 2] = inp['hgrn_lb_logits'][:, h * 128:(h + 1) * 128].T
    cc[0:64, CC_IND0] = 1.0
    cc[64:128, CC_IND1] = 1.0
    cc[:, CC_GBI] = inp['mlstm_gate_b'][l, h]
    cc[:, CC_GBF] = inp['mlstm_gate_b'][l, 4 + h]
    cc[:, CC_EPS] = EPS
    cc[:, CC_ONE] = 1.0
    cc[:, CC_RANK + h] = 1.0
    crow = np.concatenate([inp['hgrn_norm_g'][l], inp['diff_subln_g'][l], inp['mlstm_norm_g'][l]])[None, :]
    wkv = inp['w_mem_kv'][l]
    wkv_h = np.concatenate([wkv[:, h * 128:(h + 1) * 128], wkv[:, GROUP + h * 128:GROUP + (h + 1) * 128]], axis=1)
    return {
        "xT": xT_b,
        "pos": np.ascontiguousarray(inp['positions'][b][None, :xT_b.shape[1]]).astype(np.int32),
        "w_fm": np.ascontiguousarray(w_in[:, fm]),
        "w_tm": np.ascontiguousarray(w_in[:, tm]),
        "memT": np.ascontiguousarray(inp['mem'][b].T),
        "w_kv": np.ascontiguousarray(wkv_h),
        "cmat": cmat,
        "ccol": cc,
        "crow": np.ascontiguousarray(crow.astype(np.float32)),
        "lam": np.ascontiguousarray(inp['diff_lambda'][l].reshape(1, 256)),
        "rmask": rmask,
    }


_CACHE = {}


def run_layer(l, xT, inp, S):
    key = ('L', l, S)
    if key not in _CACHE:
        _CACHE[key] = build_layer(S, l)
    nc = _CACHE[key]
    cmat = const_mats()
    rmask = np.ones((128, TB), np.float32)
    rmask[:, ::64] = 0.0
    in_maps = []
    for c in range(8):
        b, h = c // 4, c % 4
        in_maps.append(layer_inputs(l, b, h, np.ascontiguousarray(xT[b]), inp, cmat, rmask))
    res = run_bass_kernel_spmd(nc, in_maps, core_ids=list(range(8)))
    ys = [np.asarray(r["yT"]) for r in res.results]
    return np.stack([np.concatenate(ys[b * 4:(b + 1) * 4], axis=0) for b in range(BATCH)])


def perm_w_out(w):
    return np.ascontiguousarray(w.reshape(4, 4, 128, D_MODEL).transpose(1, 0, 2, 3).reshape(2048, D_MODEL))


def run_fused(xT, inp, S):
    key = ('F', S)
    if key not in _CACHE:
        _CACHE[key] = build_fused(S)
    nc = _CACHE[key]
    cmat = const_mats()
    rmask = np.ones((128, TB), np.float32)
    rmask[:, ::64] = 0.0
    wp = [perm_w_out(inp['w_out'][l]) for l in range(DEPTH)]
    in_maps = []
    for c in range(8):
        b, h = c // 4, c % 4
        xb_ = np.ascontiguousarray(xT[b])
        m = {}
        for l in range(DEPTH):
            li = layer_inputs(l, b, h, xb_, inp, cmat, rmask)
            for n_ in ('w_fm', 'w_tm', 'w_kv', 'ccol', 'crow', 'lam'):
                m[n_ + "_%d" % l] = li[n_]
            if l == 0:
                for n_ in ('xT', 'pos', 'memT', 'cmat', 'rmask'):
                    m[n_] = li[n_]
        fs = slice(h * 256, (h + 1) * 256)
        m['x0own'] = np.ascontiguousarray(xb_[fs])
        m['w_out0'] = wp[0]
        m['w_own0'] = np.ascontiguousarray(wp[0][:, fs])
        m['w_own1'] = np.ascontiguousarray(wp[1][:, fs])
        in_maps.append(m)
    res = run_bass_kernel_spmd(nc, in_maps, core_ids=list(range(8)))
    outs = [np.asarray(r["out"]) for r in res.results]
    return np.stack([np.concatenate(outs[b * 4:(b + 1) * 4], axis=0) for b in range(BATCH)])


def kernel(**inputs):
    inp = {k_: np.asarray(v) for k_, v in inputs.items()}
    x = inp['x']
    S = x.shape[1]
    xT = np.ascontiguousarray(x.transpose(0, 2, 1))
    oT = run_fused(xT, inp, S)
    return np.ascontiguousarray(oT.transpose(0, 2, 1)).astype(np.float32)
```

```python
import math
import os
from contextlib import ExitStack

import numpy as np
import concourse.bass as bass
import concourse.mybir as mybir
from concourse.bass_utils import run_bass_kernel_spmd

F32 = mybir.dt.float32
BF16 = mybir.dt.bfloat16
I32 = mybir.dt.int32
AF = mybir.ActivationFunctionType
ALU = mybir.AluOpType

D_MODEL = 1024
BATCH = 2
SEQ = 16384
DEPTH = 2
N_MEM = 256
GROUP = 512
EPS = 1e-6
ROPE_THETA = 500000.0
TB = 512
NFM = 7
NTM = 1026


class Sched:
    SAME = ('act', 'dve', 'pool')

    def __init__(self, nc, es):
        self.nc = nc
        self.engs = {'pe': nc.tensor, 'act': nc.scalar, 'dve': nc.vector, 'pool': nc.gpsimd, 'sp': nc.sync}
        self.sem = {e: es.enter_context(nc.semaphore('s_' + e)) for e in ('pe', 'act', 'dve', 'pool')}
        self.cnt = {e: 0 for e in self.sem}
        self.pending = {e: [] for e in self.sem}
        self.csem = {}
        self.ccnt = {}
        self.es = es
        self.waited = {}
        self.acc = {}
        self.ops = []
        self.nwaits = 0

    @staticmethod
    def box(ap):
        dims = ap.ap
        esz = mybir.dt.size(ap.dtype)
        off = ap.offset
        sp = str(ap.space)
        if sp == 'DRAM':
            ext = 1
            for s, c in dims:
                ext += (c - 1) * abs(s)
            return (ap.tensor.name, 0, 1, off * esz, (off + ext) * esz)
        pstep = dims[0][0]
        if pstep == 0:
            p0, f0 = 0, off
        else:
            p0, f0 = off // pstep, off % pstep
        ext = 1
        for s, c in dims[1:]:
            ext += (c - 1) * abs(s)
        if sp == 'PSUM':
            return (ap.tensor.name, (p0 // 32) * 32, ((p0 + dims[0][1] + 31) // 32) * 32, 0, 2048)
        return (ap.tensor.name, p0, p0 + dims[0][1], f0 * esz, (f0 + ext) * esz)

    def _sem_of(self, key):
        return self.sem[key] if key in self.sem else self.csem[key]

    def emit(self, eng, fn, outs=(), ins=(), sig=True, chan=None, inc=16):
        is_dma = chan is not None
        oboxes = [self.box(a) for a in outs]
        iboxes = [self.box(a) for a in ins]
        deps = set()
        for (name, p0, p1, f0, f1) in iboxes:
            for ent in self.acc.get(name, ()):
                if ent[5] and ent[0] < p1 and p0 < ent[1] and ent[2] < f1 and f0 < ent[3]:
                    deps.add(ent[4])
        for (name, p0, p1, f0, f1) in oboxes:
            for ent in self.acc.get(name, ()):
                if ent[0] < p1 and p0 < ent[1] and ent[2] < f1 and f0 < ent[3]:
                    deps.add(ent[4])
        need = {}
        for d in deps:
            deng, key, val = self.ops[d]
            if deng is not None and deng == eng and eng not in self.SAME:
                continue
            if val is None:
                raise RuntimeError('dependency on unsignalled op')
            if need.get(key, 0) < val:
                need[key] = val
        if is_dma:
            if chan not in self.csem:
                self.csem[chan] = self.es.enter_context(self.nc.semaphore('c_' + chan))
                self.ccnt[chan] = 0
            if self.ccnt[chan] > 0 and need.get(chan, 0) < self.ccnt[chan]:
                need[chan] = self.ccnt[chan]
        e = self.engs[eng]
        for key, val in need.items():
            if self.waited.get((eng, key), 0) >= val:
                continue
            e.wait_ge(self._sem_of(key), val)
            self.waited[(eng, key)] = val
            self.nwaits += 1
        inst = fn(e)
        opid = len(self.ops)
        if is_dma:
            self.ccnt[chan] += inc
            inst.then_inc(self.csem[chan], inc)
            self.ops.append([None, chan, self.ccnt[chan]])
        elif sig:
            self.cnt[eng] += 1
            inst.then_inc(self.sem[eng], 1)
            self.ops.append([eng, eng, self.cnt[eng]])
            for p in self.pending[eng]:
                self.ops[p][2] = self.cnt[eng]
            self.pending[eng] = []
        else:
            self.ops.append([eng, eng, None])
            self.pending[eng].append(opid)
        for (name, p0, p1, f0, f1) in oboxes:
            lst = self.acc.setdefault(name, [])
            lst[:] = [t for t in lst if not (p0 <= t[0] and t[1] <= p1 and f0 <= t[2] and t[3] <= f1)]
            lst.append((p0, p1, f0, f1, opid, True))
        rkey = None if is_dma else eng
        for (name, p0, p1, f0, f1) in iboxes:
            lst = self.acc.setdefault(name, [])
            for i, t in enumerate(lst):
                if (not t[5]) and t[0] == p0 and t[1] == p1 and t[2] == f0 and t[3] == f1 \
                        and self.ops[t[4]][0] == rkey and rkey is not None:
                    lst[i] = (p0, p1, f0, f1, opid, False)
                    break
            else:
                lst.append((p0, p1, f0, f1, opid, False))
        return inst

    def barrier(self):
        keys = [(key, self.cnt[key]) for key in self.sem] + [(key, self.ccnt[key]) for key in self.csem]
        for eng, e in self.engs.items():
            for key, val in keys:
                if val > 0 and self.waited.get((eng, key), 0) < val:
                    e.wait_ge(self._sem_of(key), val)
                    self.waited[(eng, key)] = val

    def finish(self):
        sp = self.engs['sp']
        for key in list(self.csem):
            if self.ccnt[key] > 0:
                sp.wait_ge(self.csem[key], self.ccnt[key])
        for key in self.sem:
            if self.cnt[key] > 0:
                sp.wait_ge(self.sem[key], self.cnt[key])


class _Stop(Exception):
    pass


def _lvl(n):
    if n > float(os.environ.get('KLVL', '99')):
        raise _Stop()


class K:
    def __init__(self, nc, es):
        self.nc = nc
        self.es = es
        self.s = Sched(nc, es)
        self.n = 0

    def sb(self, name, shape, dt):
        return self.es.enter_context(self.nc.sbuf_tensor(name, shape, dt))

    def ps(self, name, shape, dt=F32):
        return self.es.enter_context(self.nc.psum_tensor(name, shape, dt))

    def mm(self, out, lhsT, rhs, start=True, stop=True, sig=None, skip=False):
        if sig is None:
            sig = True
        return self.s.emit('pe', lambda e: e.matmul(out, lhsT, rhs, start=start, stop=stop,
                                                     skip_group_check=skip),
                           outs=[out], ins=[lhsT, rhs], sig=sig)

    def tr(self, out, in_, ident):
        return self.s.emit('pe', lambda e: e.transpose(out, in_, ident), outs=[out], ins=[in_, ident])

    def act(self, out, in_, func, bias=0.0, scale=1.0, accum_out=None):
        ins = [in_]
        outs = [out]
        if not isinstance(bias, (int, float)):
            ins.append(bias)
        if not isinstance(scale, (int, float)):
            ins.append(scale)
        if accum_out is not None:
            outs.append(accum_out)
        kw = {}
        if accum_out is not None:
            kw['accum_out'] = accum_out
        return self.s.emit('act', lambda e: e.activation(out, in_, func, bias=bias, scale=scale, **kw),
                           outs=outs, ins=ins)

    def ts(self, eng, out, in0, s1, s2=None, op0=ALU.mult, op1=None, accum_out=None):
        ins = [in0]
        outs = [out]
        for s_ in (s1, s2):
            if s_ is not None and not isinstance(s_, (int, float)):
                ins.append(s_)
        kw = {}
        if op1 is not None:
            kw['op1'] = op1
        if accum_out is not None:
            kw['accum_out'] = accum_out
            outs.append(accum_out)
        return self.s.emit(eng, lambda e: e.tensor_scalar(out, in0, s1, s2, op0, **kw), outs=outs, ins=ins)

    def tt(self, eng, out, in0, in1, op):
        return self.s.emit(eng, lambda e: e.tensor_tensor(out, in0, in1, op), outs=[out], ins=[in0, in1])

    def stt(self, out, in0, scalar, in1, op0, op1):
        ins = [in0, in1]
        if not isinstance(scalar, (int, float)):
            ins.append(scalar)
        return self.s.emit('dve', lambda e: e.scalar_tensor_tensor(out, in0, scalar, in1, op0, op1),
                           outs=[out], ins=ins)

    def cp(self, eng, out, in_):
        if eng == 'act':
            return self.s.emit('act', lambda e: e.copy(out, in_), outs=[out], ins=[in_])
        return self.s.emit(eng, lambda e: e.tensor_copy(out, in_), outs=[out], ins=[in_])

    def recip(self, out, in_):
        return self.s.emit('dve', lambda e: e.reciprocal(out, in_), outs=[out], ins=[in_])

    def memset(self, eng, out, val):
        return self.s.emit(eng, lambda e: e.memset(out, val), outs=[out], ins=[])

    def scan(self, out, d0, d1, init, op0, op1):
        return self.s.emit('dve', lambda e: e.tensor_tensor_scan(out, d0, d1, init, op0, op1),
                           outs=[out], ins=[d0, d1])

    def dma(self, out, in_, chan, eng='sp'):
        return self.s.emit(eng, lambda e: e.dma_start(out=out, in_=in_), outs=[out], ins=[in_], chan=chan)

    def cc(self, kind, op, groups, in_ap, out_ap, chan):
        return self.s.emit('pool', lambda e: e.collective_compute(kind, op, replica_groups=groups,
                                                                 ins=[in_ap.opt()], outs=[out_ap.opt()]),
                           outs=[out_ap], ins=[in_ap], chan=chan, inc=1)

    def rsqrt(self, out, in_, scale, epscol, tmp):
        self.act(tmp, in_, AF.Ln, bias=epscol, scale=scale)
        self.act(out, tmp, AF.Exp, scale=-0.5)


CM_IDENT, CM_TRI128, CM_TRI64, CM_BLK64, CM_ONES, CM_PROPE, CM_ZERO = range(7)
NCM = 7
(CC_NG, CC_MNG, CC_INVF, CC_SGN, CC_GQ, CC_GK, CC_CWQ, CC_CBQ, CC_CWK, CC_CBK, CC_GXQ, CC_GXK, CC_LBL,
 CC_IND0, CC_IND1, CC_GBI, CC_GBF, CC_EPS, CC_ONE, CC_HALFPI) = (0, 8, 16, 17, 18, 19, 20, 24, 25, 29, 30, 31,
                                                                 32, 34, 35, 36, 37, 38, 39, 40)
CC_RANK = 41
NCC = 45
CR_HG, CR_SUB, CR_MG = range(3)
TWO_PI = 2.0 * math.pi


def const_mats():
    m = np.zeros((NCM, 128, 128), np.float32)
    i = np.arange(128)
    m[CM_IDENT] = np.eye(128)
    m[CM_TRI128] = (i[None, :] >= i[:, None])
    same = (i[:, None] // 64) == (i[None, :] // 64)
    m[CM_TRI64] = m[CM_TRI128] * same
    m[CM_BLK64] = same
    m[CM_ONES] = 1.0
    for base in (0, 64):
        for j in range(8):
            m[CM_PROPE, base + 8 + j, base + j] = 1.0
            m[CM_PROPE, base + j, base + 8 + j] = 1.0
    return np.ascontiguousarray(m.transpose(1, 0, 2))


def emit_layer(nc, k, PSB, S, layer_idx, sfx, D, do_rope=True, fused=False):
    NBLK = S // TB
    NT = S // 128
    xT_d, pos_d, wfm_d, wtm_d = D['xT'], D['pos'], D['w_fm'], D['w_tm']
    memT_d, wkv_d, cmat_d, ccol_d, crow_d, lam_d, rm_d = (D['memT'], D['w_kv'], D['cmat'], D['ccol'], D['crow'],
                                                          D['lam'], D['rmask'])
    cs_d = D['cs']
    yT_d = D.get('yT')

    lam_init = 0.8 - 0.6 * math.exp(-0.3 * layer_idx)

    with ExitStack() as es:
        k.es = es
        sb = lambda name, shape, dt: k.sb(name + sfx, shape, dt)
        kT = sb("kT", [128, S], BF16)
        vA = sb("vA", [128, NT, 129], BF16)
        wfm = sb("wfm", [128, 8, NFM * 128], BF16)
        wtm = sb("wtm", [128, 8, NTM], BF16)
        cmf3 = sb("cmf3", [128, 3, 128], F32)
        cmb = sb("cmb", [128, NCM, 128], BF16)
        cc = sb("cc", [128, NCC], F32)
        crow = sb("crow_sb", [128, 3 * 128], F32)
        lamt = sb("lamt", [128, 256], F32)
        rmask = sb("rmask_sb", [128, TB], F32)
        zrow = sb("zrow", [128, TB], BF16)
        kmT = sb("kmT", [128, 256], BF16)
        vmA = sb("vmA", [128, 2, 129], BF16)
        S_h = sb("S_h", [128, 128], F32)
        Sb_h = sb("Sb_h", [128, 8, 128], BF16)
        dS = sb("dS", [128, 8, 128], F32)
        C_m = sb("C_m", [128, 129], F32)
        Cb_m = sb("Cb_m", [128, 8, 129], BF16)
        dC = sb("dC", [128, 8, 129], F32)
        sm = sb("sm", [128, 64], F32)
        stage = sb("stage", [128, NTM], F32)
        xt = sb("xt", [128, 4, TB], F32)
        xtm = xt[:].rearrange("p c t -> p (c t)").rearrange("p (c t) -> p c t", c=8)
        xb = sb("xb", [128, 8, TB], BF16)
        sqb = sb("sqb", [128, 2, TB], BF16)
        RX = sb("RX", [128, TB], F32)
        rxc = sb("rxc", [128, 8], F32)
        Tall = sb("Tall", [128, 6, TB], F32)
        T = [Tall[:, i, :] for i in range(6)]
        xt2 = Tall[:, 2:6, :]
        Bt = [sb(f"B{i}", [128, TB], BF16) for i in range(3)]
        cosT = sb("cosT", [128, TB], F32)
        sinT = sb("sinT", [128, TB], F32)
        qhT = sb("qhT", [128, TB], BF16)
        qtT = sb("qtT", [128, TB], BF16)
        ktT = sb("ktT", [128, TB], BF16)
        qcT = sb("qcT", [128, TB], BF16)
        kcT = sb("kcT", [128, TB], BF16)
        cqx = sb("cqx", [128, TB + 3], F32)
        ckx = sb("ckx", [128, TB + 3], F32)
        qxn = sb("qxn", [128, TB], BF16)
        bcum = sb("bcum", [128, TB], F32)
        av = sb("av", [128, 4, 128], BF16)
        cvA = sb("cvA", [128, 4, 129], BF16)
        gates = sb("gates", [128, 4, 5, 128], BF16)
        gcol = sb("gcol", [128, 4, 16], F32)
        kTM = sb("kTM", [128, 4, 128], BF16)
        kTM2 = sb("kTM2", [128, 4, 128], BF16)
        AT = sb("AT", [128, 2, 128], BF16)
        PT = sb("PT", [128, 4, TB], BF16)
        yTM = sb("yTM", [128, 4, 4, 128], BF16)
        yT = sb("yT_sb", [128, 4, TB], BF16)
        yTr = sb("yTr", [128, 2, 4, TB], BF16) if fused else None
        G = PSB[0:3]
        PS_S = PSB[3:5]
        PS_O = PSB[5:8]
        gi = [0]

        def gbank():
            b = G[gi[0] % 3]
            gi[0] += 1
            return b

        cmF = lambda i: cmf3[:, i - CM_TRI64, :]
        cmB = lambda i: cmb[:, i, :]
        col = lambda i, n=1: cc[:, i:i + n]
        epsc = col(CC_EPS)

        try:
            k.dma(stage[:, 0:NCM * 128], cmat_d.rearrange("p m n -> p (m n)"), "c0")
            k.dma(cmf3[:], cmat_d[:, CM_TRI64:CM_TRI64 + 3, :], "c5")
            k.dma(cc[:], ccol_d[:, :], "c1")
            k.dma(crow[:], crow_d[0:1, :].partition_broadcast(128), "c2")
            k.dma(lamt[:], lam_d[0:1, :].partition_broadcast(128), "c3")
            k.dma(rmask[:], rm_d[:, :], "c4")
            k.cp('dve', cmb[:].rearrange("p m n -> p (m n)"), stage[:, 0:NCM * 128])
            k.memset('pool', zrow[:], 0.0)
            k.memset('pool', vA[:, :, 128:129], 1.0)
            k.memset('pool', cvA[:, :, 128:129], 1.0)
            k.memset('pool', vmA[:, :, 128:129], 1.0)
            k.memset('pool', S_h[:], 0.0)
            k.memset('pool', C_m[:], 0.0)
            k.memset('pool', cqx[:, 0:3], 0.0)
            k.memset('pool', ckx[:, 0:3], 0.0)
            k.memset('pool', AT[:], 0.0)
            k.memset('dve', sm[:], 0.0)
            wv = wfm_d.rearrange("(c p) n -> p c n", p=128)
            for c in range(8):
                k.dma(stage[:, 0:NFM * 128], wv[:, c, :], "st")
                k.ts('dve', wfm[:, c, :], stage[:, 0:NFM * 128], col(CC_NG + c))
            wv = wtm_d.rearrange("(c p) n -> p c n", p=128)
            for c in range(8):
                k.dma(stage[:, 0:NTM], wv[:, c, :], "st")
                k.ts('dve', wtm[:, c, :], stage[:, 0:NTM], col(CC_NG + c))

            _lvl(2)
            LB, OMLB, NLAM, R1 = sm[:, 0:1], sm[:, 1:2], sm[:, 2:3], sm[:, 3:4]
            if layer_idx == 0:
                k.memset('dve', LB, 0.0)
            else:
                k.tt('dve', sm[:, 4:5], col(CC_LBL), col(CC_LBL + 1), ALU.subtract)
                k.act(sm[:, 5:6], sm[:, 4:5], AF.Exp)
                k.ts('dve', sm[:, 5:6], sm[:, 5:6], 1.0, None, op0=ALU.add)
                k.recip(LB, sm[:, 5:6])
            k.ts('dve', OMLB, LB, -1.0, 1.0, op0=ALU.mult, op1=ALU.add)
            k.tt('dve', T[0][:, 0:64], lamt[:, 0:64], lamt[:, 64:128], ALU.mult)
            k.tt('dve', T[0][:, 64:128], lamt[:, 128:192], lamt[:, 192:256], ALU.mult)
            k.s.emit('dve', lambda e: e.reduce_sum(sm[:, 6:7], T[0][:, 0:64], mybir.AxisListType.X),
                     outs=[sm[:, 6:7]], ins=[T[0][:, 0:64]])
            k.s.emit('dve', lambda e: e.reduce_sum(sm[:, 7:8], T[0][:, 64:128], mybir.AxisListType.X),
                     outs=[sm[:, 7:8]], ins=[T[0][:, 64:128]])
            k.act(sm[:, 8:10], sm[:, 6:8], AF.Exp)
            k.tt('dve', sm[:, 10:11], sm[:, 9:10], sm[:, 8:9], ALU.subtract)
            k.ts('dve', NLAM, sm[:, 10:11], -lam_init, None, op0=ALU.add)

            _lvl(3)
            mv = memT_d.rearrange("(c p) n -> p c n", p=128)
            k.dma(xtm, mv[:, :, :], "x0")
            wk = wkv_d.rearrange("(c p) n -> p c n", p=128)
            mw = xb[:, :, 256:512]
            for c in range(8):
                k.dma(stage[:, 0:256], wk[:, c, :], "st")
                k.ts('dve', mw[:, c, :], stage[:, 0:256], col(CC_MNG + c))
                k.cp('pool', xb[:, c, 0:256], xtm[:, c, :])
            for mt in range(2):
                pc = gbank()
                for c in range(8):
                    k.act(sqb[:, c % 2, 0:128], xtm[:, c, mt * 128:(mt + 1) * 128], AF.Square)
                    k.mm(pc[:, 0:1], sqb[:, c % 2, 0:128], cmB(CM_ONES)[:, 0:1], start=(c == 0), stop=(c == 7))
                k.rsqrt(gcol[:, 0, 0:1], pc[:, 0:1], 1.0 / D_MODEL, epsc, gcol[:, 0, 1:2])
                pk = gbank()
                for c in range(8):
                    k.mm(pk[:, 0:256], xb[:, c, mt * 128:(mt + 1) * 128], mw[:, c, :], start=(c == 0), stop=(c == 7))
                k.ts('dve', T[0][:, 0:128], pk[:, 0:128], gcol[:, 0, 0:1])
                k.ts('dve', vmA[:, mt, 0:128], pk[:, 128:256], gcol[:, 0, 0:1])
                k.act(T[1][:, 0:128], T[0][:, 0:128], AF.Square, accum_out=gcol[:, 0, 2:3])
                k.rsqrt(gcol[:, 0, 3:4], gcol[:, 0, 2:3], 1.0 / 128, epsc, gcol[:, 0, 4:5])
                k.ts('dve', Bt[0][:, 0:128], T[0][:, 0:128], gcol[:, 0, 3:4])
                pt_ = gbank()
                ptb = pt_[:, 0:64].bitcast(BF16)
                k.tr(ptb, Bt[0][:, 0:128], cmB(CM_IDENT))
                k.tt('dve', sm[:, 11:12], col(CC_GXQ), col(CC_GXK), ALU.mult)
                k.ts('dve', kmT[:, mt * 128:(mt + 1) * 128], ptb, sm[:, 11:12], 128 ** -0.5, op0=ALU.mult, op1=ALU.mult)

            _lvl(4)
            for bi in (range(NBLK) if do_rope else ()):
                tsl = slice(bi * TB, (bi + 1) * TB)
                pi_ = T[5].bitcast(I32)
                k.dma(pi_[:, :], pos_d[0:1, tsl].partition_broadcast(128), "x1")
                k.cp('dve', T[0][:], pi_[:, :])
                k.ts('dve', T[1][:], T[0][:], col(CC_INVF))
                for which, dst in ((0, cosT), (1, sinT)):
                    src = T[1]
                    if which == 0:
                        k.ts('dve', T[2][:], T[1][:], math.pi / 2, None, op0=ALU.add)
                        src = T[2]
                    k.ts('dve', T[3][:], src[:], 1.0 / TWO_PI, 12582912.0, op0=ALU.mult, op1=ALU.add)
                    k.ts('dve', T[3][:], T[3][:], -12582912.0, None, op0=ALU.add)
                    k.stt(T[4][:], T[3][:], -6.28125, src[:], ALU.mult, ALU.add)
                    k.stt(T[4][:], T[3][:], -(TWO_PI - 6.28125), T[4][:], ALU.mult, ALU.add)
                    k.ts('dve', T[4][:], T[4][:], 3.1415925, -3.1415925, op0=ALU.min, op1=ALU.max)
                    k.act(dst[:], T[4][:], AF.Sin)
                k.ts('dve', sinT[:], sinT[:], col(CC_SGN))
                k.dma(cs_d[0, :, tsl], cosT[:], "y0")
                k.dma(cs_d[1, :, tsl], sinT[:], "y1")

            xv = xT_d.rearrange("(c p) t -> p c t", p=128)
            yv = yT_d.rearrange("(g p) t -> p g t", p=128) if yT_d is not None else None

            for bi in range(NBLK):
                t0 = bi * TB
                tsl = slice(t0, t0 + TB)
                _lvl(5)
                if bi == 0:
                    k.dma(cosT[:], cs_d[0, :, tsl], "x2")
                    k.dma(sinT[:], cs_d[1, :, tsl], "x3")
                    k.dma(xt[:], xv[:, 0:4, tsl], "x0")
                prx = gbank()
                prc = gbank()
                if bi == 0:
                    k.dma(xt2, xv[:, 4:8, tsl], "x1")
                for c in range(8):
                    xsrc = xt[:, c, :] if c < 4 else xt2[:, c - 4, :]
                    k.cp('act', xb[:, c, :], xsrc)
                    k.tt('dve', sqb[:, c % 2, :], xsrc, xsrc, ALU.mult)
                    k.mm(prx[:], cmB(CM_ONES), sqb[:, c % 2, :], start=(c == 0), stop=(c == 7))
                    for tt_ in range(4):
                        k.mm(prc[:, tt_:tt_ + 1], sqb[:, c % 2, tt_ * 128:(tt_ + 1) * 128], cmB(CM_ONES)[:, 0:1],
                             start=(c == 0 and tt_ == 0), stop=(c == 7), skip=True)
                if bi + 1 < NBLK:
                    k.dma(xt[:], xv[:, 0:4, t0 + TB:t0 + 2 * TB], "x0")
                k.rsqrt(RX[:], prx[:], 1.0 / D_MODEL, epsc, T[0][:])
                k.rsqrt(rxc[:, 0:4], prc[:, 0:4], 1.0 / D_MODEL, epsc, rxc[:, 4:8])
                k.ts('dve', rxc[:, 4:8], rxc[:, 0:4], -1.0)

                def fm_proj(j):
                    p = gbank()
                    for c in range(8):
                        k.mm(p[:], wfm[:, c, j * 128:(j + 1) * 128], xb[:, c, :], start=(c == 0), stop=(c == 7))
                    return p

                _lvl(6)
                p = fm_proj(0)
                k.tt('dve', T[0][:], p[:], RX[:], ALU.mult)
                k.act(T[1][:], T[0][:], AF.Exp, scale=-1.0)
                k.ts('dve', T[1][:], T[1][:], 1.0, None, op0=ALU.add)
                k.recip(T[1][:], T[1][:])
                k.stt(T[0][:], T[0][:], 128 ** -0.5, T[1][:], ALU.mult, ALU.mult)
                p = fm_proj(1)
                k.tt('dve', T[1][:], p[:], RX[:], ALU.mult)
                k.act(T[2][:], T[1][:], AF.Exp, scale=-1.0)
                k.ts('dve', T[2][:], T[2][:], 1.0, None, op0=ALU.add)
                k.recip(T[2][:], T[2][:])
                k.ts('dve', T[2][:], T[2][:], OMLB, LB, op0=ALU.mult, op1=ALU.add)
                k.act(T[3][:], T[2][:], AF.Ln)
                k.ts('dve', T[2][:], T[2][:], -1.0, 1.0, op0=ALU.mult, op1=ALU.add)
                k.scan(bcum[:], rmask[:], T[3][:], 0.0, ALU.mult, ALU.add)
                for c8 in range(8):
                    cs_ = slice(c8 * 64, (c8 + 1) * 64)
                    k.ts('dve', T[3][:, cs_], bcum[:, cs_], bcum[:, c8 * 64 + 31:c8 * 64 + 32], None, op0=ALU.subtract)
                k.act(T[4][:], T[3][:], AF.Exp)
                k.act(T[5][:], T[3][:], AF.Exp, scale=-1.0)
                k.tt('dve', qtT[:], T[0][:], T[4][:], ALU.mult)
                k.tt('dve', ktT[:], T[2][:], T[5][:], ALU.mult)
                EREF, ELAST, CFAC = sm[:, 16:24], sm[:, 24:32], sm[:, 32:40]
                k.act(EREF, bcum[:, 31:TB:64], AF.Exp)
                k.act(ELAST, bcum[:, 63:TB:64], AF.Exp)
                k.tt('dve', sm[:, 40:48], bcum[:, 63:TB:64], bcum[:, 31:TB:64], ALU.subtract)
                k.act(CFAC, sm[:, 40:48], AF.Exp)

                _lvl(7)
                def qk_norm_rope(j, gcol_, scale_, dst):
                    p_ = fm_proj(j)
                    k.tt('dve', T[0][:], p_[:], RX[:], ALU.mult)
                    k.act(Bt[0][:], T[0][:], AF.Square)
                    pss = gbank()
                    k.mm(pss[:], cmB(CM_BLK64), Bt[0][:])
                    k.rsqrt(T[1][:], pss[:], 1.0 / 64, epsc, T[2][:])
                    k.ts('dve', Bt[1][:], T[0][:], gcol_)
                    pp = gbank()
                    k.mm(pp[:], cmB(CM_PROPE), Bt[1][:])
                    k.tt('dve', T[2][:], Bt[1][:], cosT[:], ALU.mult)
                    k.tt('dve', T[3][:], pp[:], sinT[:], ALU.mult)
                    k.tt('dve', T[2][:], T[2][:], T[3][:], ALU.add)
                    k.stt(dst, T[2][:], scale_, T[1][:], ALU.mult, ALU.mult)

                qk_norm_rope(2, col(CC_GQ), 0.125, qhT[:])
                qk_norm_rope(3, col(CC_GK), 1.0, kT[:, tsl])
                if bi + 1 < NBLK:
                    k.dma(cosT[:], cs_d[0, :, t0 + TB:t0 + 2 * TB], "x2")
                    k.dma(sinT[:], cs_d[1, :, t0 + TB:t0 + 2 * TB], "x3")

                _lvl(8)
                def conv_silu(j, ext, wcol, bcol, scale_, dst):
                    p_ = fm_proj(j)
                    k.tt('dve', ext[:, 3:TB + 3], p_[:], RX[:], ALU.mult)
                    k.ts('dve', T[0][:], ext[:, 0:TB], cc[:, wcol:wcol + 1], cc[:, bcol:bcol + 1], op0=ALU.mult, op1=ALU.add)
                    for jj in range(1, 4):
                        k.stt(T[0][:], ext[:, jj:TB + jj], cc[:, wcol + jj:wcol + jj + 1], T[0][:], ALU.mult, ALU.add)
                    k.cp('pool', ext[:, 0:3], ext[:, TB:TB + 3])
                    k.act(T[1][:], T[0][:], AF.Exp, scale=-1.0)
                    k.ts('dve', T[1][:], T[1][:], 1.0, None, op0=ALU.add)
                    k.recip(T[1][:], T[1][:])
                    k.stt(dst, T[0][:], scale_, T[1][:], ALU.mult, ALU.mult)

                conv_silu(4, cqx, CC_CWQ, CC_CBQ, 1.0, qcT[:])
                conv_silu(5, ckx, CC_CWK, CC_CBK, 128 ** -0.5, kcT[:])

                _lvl(9)
                p = fm_proj(6)
                k.tt('dve', T[0][:], p[:], RX[:], ALU.mult)
                k.act(Bt[0][:], T[0][:], AF.Square)
                pss = gbank()
                k.mm(pss[:], cmB(CM_ONES), Bt[0][:])
                k.rsqrt(T[1][:], pss[:], 1.0 / 128, epsc, T[2][:])
                k.tt('dve', qxn[:], T[0][:], T[1][:], ALU.mult)
                for mt in range(2):
                    pS = gbank()
                    k.mm(pS[:], kmT[:, mt * 128:(mt + 1) * 128], qxn[:])
                    k.act(PT[:, mt, :], pS[:], AF.Exp)

                _lvl(10)
                for tt_ in range(4):
                    tcs = slice(tt_ * 128, (tt_ + 1) * 128)
                    pA, pB, pC = (PSB[2], PSB[3], PSB[4]) if tt_ % 2 == 0 else (PSB[5], PSB[6], PSB[7])
                    for c in range(8):
                        k.mm(pA[:], xb[:, c, tcs], wtm[:, c, 0:512], start=(c == 0), stop=(c == 7))
                    for c in range(8):
                        k.mm(pB[:], xb[:, c, tcs], wtm[:, c, 512:1024], start=(c == 0), stop=(c == 7))
                    for c in range(8):
                        k.mm(pC[:, 0:2], xb[:, c, tcs], wtm[:, c, 1024:1026], start=(c == 0), stop=(c == 7))
                    rc = rxc[:, tt_:tt_ + 1]
                    nrc = rxc[:, 4 + tt_:5 + tt_]
                    k.act(T[0][:], pA[:], AF.Exp, scale=nrc)
                    k.ts('dve', T[0][:], T[0][:], 1.0, None, op0=ALU.add)
                    k.recip(T[0][:], T[0][:])
                    k.stt(gates[:, tt_, 0:4, :].rearrange("p g d -> p (g d)"), pA[:], rc, T[0][:], ALU.mult, ALU.mult)
                    k.act(T[1][:, 0:128], pB[:, 0:128], AF.Exp, scale=nrc)
                    k.ts('dve', T[1][:, 0:128], T[1][:, 0:128], 1.0, None, op0=ALU.add)
                    k.recip(gates[:, tt_, 4, :], T[1][:, 0:128])
                    k.ts('dve', av[:, tt_, :], pB[:, 128:256], rc)
                    k.ts('dve', vA[:, bi * 4 + tt_, 0:128], pB[:, 256:384], rc)
                    k.ts('dve', cvA[:, tt_, 0:128], pB[:, 384:512], rc)
                    g_ = gcol[:, tt_, :]
                    k.stt(g_[:, 0:1], pC[:, 0:1], rc, col(CC_GBI), ALU.mult, ALU.add)
                    k.stt(g_[:, 2:3], pC[:, 1:2], rc, col(CC_GBF), ALU.mult, ALU.add)
                    k.act(g_[:, 3:4], g_[:, 2:3], AF.Exp, scale=-1.0)
                    k.act(g_[:, 4:5], g_[:, 3:4], AF.Ln, bias=col(CC_ONE))
                    k.ts('dve', g_[:, 1:2], g_[:, 4:5], -1.0)
                    k.ts('dve', g_[:, 5:6], g_[:, 1:2], col(CC_IND0))
                    k.ts('dve', g_[:, 6:7], g_[:, 1:2], col(CC_IND1))
                    pc2 = gbank()
                    k.mm(pc2[:, 0:1], cmF(CM_TRI64), g_[:, 1:2])
                    k.mm(pc2[:, 1:2], cmF(CM_BLK64), g_[:, 1:2], skip=True)
                    k.mm(pc2[:, 2:4], cmF(CM_ONES), g_[:, 5:7], skip=True)
                    k.cp('dve', g_[:, 7:11], pc2[:, 0:4])
                    k.tt('dve', g_[:, 11:12], g_[:, 0:1], g_[:, 7:8], ALU.subtract)
                    k.tt('dve', g_[:, 12:13], g_[:, 11:12], g_[:, 8:9], ALU.add)
                    k.act(g_[:, 13:14], g_[:, 11:12], AF.Exp)
                    k.act(g_[:, 14:15], g_[:, 12:13], AF.Exp)
                    k.act(g_[:, 15:16], g_[:, 7:8], AF.Exp)
                    k.act(sm[:, 48 + 2 * tt_:50 + 2 * tt_], g_[:, 9:11], AF.Exp)

                _lvl(11)
                for tt_ in range(4):
                    tcs = slice(tt_ * 128, (tt_ + 1) * 128)
                    po = gbank()
                    for mt in range(2):
                        k.mm(po[:, 0:129], PT[:, mt, tcs], vmA[:, mt, :], start=(mt == 0), stop=(mt == 1))
                    k.recip(gcol[:, tt_, 5:6], po[:, 128:129])
                    k.stt(yTM[:, tt_, 3, :], po[:, 0:128], gcol[:, tt_, 5:6], gates[:, tt_, 3, :], ALU.mult, ALU.mult)

                _lvl(12)
                for tt_ in range(4):
                    tcs = slice(tt_ * 128, (tt_ + 1) * 128)
                    ptr = gbank()
                    ptb = ptr[:, 0:64].bitcast(BF16)
                    k.tr(ptb, ktT[:, tcs], cmB(CM_IDENT))
                    k.cp('dve', kTM[:, tt_, :], ptb)
                _lvl(12.1)
                for tt_ in range(4):
                    pd = gbank()
                    for hh in range(2):
                        c8 = tt_ * 2 + hh
                        rs = slice(hh * 64, hh * 64 + 64)
                        k.mm(pd[:, hh * 128:(hh + 1) * 128], kTM[rs, tt_, :], av[rs, tt_, :], skip=(hh == 1))
                        k.ts('dve', dS[:, c8, :], pd[:, hh * 128:(hh + 1) * 128], CFAC[:, c8:c8 + 1])
                _lvl(12.2)
                for c8 in range(8):
                    k.ts('dve', Sb_h[:, c8, :], S_h[:], EREF[:, c8:c8 + 1])
                    k.stt(S_h[:], S_h[:], ELAST[:, c8:c8 + 1], dS[:, c8, :], ALU.mult, ALU.add)
                _lvl(12.3)
                for tt_ in range(4):
                    tcs = slice(tt_ * 128, (tt_ + 1) * 128)
                    pa = gbank()
                    for hh in range(2):
                        cs_ = slice(tt_ * 128 + hh * 64, tt_ * 128 + hh * 64 + 64)
                        rs = slice(hh * 64, hh * 64 + 64)
                        k.mm(pa[rs, hh * 64:hh * 64 + 64], ktT[:, cs_], qtT[:, cs_], skip=(hh == 1))
                        k.stt(AT[rs, 0, hh * 64:hh * 64 + 64], pa[rs, hh * 64:hh * 64 + 64], 3.0e38,
                              cmB(CM_TRI64)[rs, hh * 64:hh * 64 + 64], ALU.min, ALU.mult)
                    _lvl(12.4)
                    po = gbank()
                    for hh in range(2):
                        cs_ = slice(tt_ * 128 + hh * 64, tt_ * 128 + hh * 64 + 64)
                        rs = slice(hh * 64, hh * 64 + 64)
                        k.mm(po[rs, 0:128], AT[rs, 0, hh * 64:hh * 64 + 64], av[rs, tt_, :], start=True, stop=False, skip=True)
                        k.mm(po[rs, 0:128], qtT[:, cs_], Sb_h[:, tt_ * 2 + hh, :], start=False, stop=True, skip=True)
                    g_ = gcol[:, tt_, :]
                    k.act(T[0][:, 0:128], po[:, 0:128], AF.Square, accum_out=g_[:, 5:6])
                    k.rsqrt(g_[:, 6:7], g_[:, 5:6], 1.0 / 128, epsc, g_[:, 5:6])
                    k.stt(T[0][:, 0:128], po[:, 0:128], g_[:, 6:7], crow[:, CR_HG * 128:(CR_HG + 1) * 128], ALU.mult, ALU.mult)
                    k.tt('dve', yTM[:, tt_, 0, :], T[0][:, 0:128], gates[:, tt_, 0, :], ALU.mult)

                _lvl(13)
                for tt_ in range(4):
                    tcs = slice(tt_ * 128, (tt_ + 1) * 128)
                    ptr = gbank()
                    ptb = ptr[:, 0:64].bitcast(BF16)
                    k.tr(ptb, kcT[:, tcs], cmB(CM_IDENT))
                    k.ts('dve', kTM2[:, tt_, :], ptb, gcol[:, tt_, 14:15])
                for tt_ in range(4):
                    for hh in range(2):
                        c8 = tt_ * 2 + hh
                        rs = slice(hh * 64, hh * 64 + 64)
                        pd = gbank()
                        k.mm(pd[:, 0:129], kTM2[rs, tt_, :], cvA[rs, tt_, :])
                        k.cp('dve', dC[:, c8, :], pd[:, 0:129])
                for c8 in range(8):
                    k.cp('dve', Cb_m[:, c8, :], C_m[:])
                    k.stt(C_m[:], C_m[:], sm[:, 48 + c8:49 + c8], dC[:, c8, :], ALU.mult, ALU.add)
                for tt_ in range(4):
                    tcs = slice(tt_ * 128, (tt_ + 1) * 128)
                    g_ = gcol[:, tt_, :]
                    pa = gbank()
                    k.mm(pa[:, 0:128], kcT[:, tcs], qcT[:, tcs])
                    k.stt(AT[:, 1, :], pa[:, 0:128], g_[:, 13:14], cmB(CM_TRI64), ALU.mult, ALU.mult)
                    po = gbank()
                    for hh in range(2):
                        cs_ = slice(tt_ * 128 + hh * 64, tt_ * 128 + hh * 64 + 64)
                        rs = slice(hh * 64, hh * 64 + 64)
                        k.mm(po[rs, 0:129], AT[rs, 1, hh * 64:hh * 64 + 64], cvA[rs, tt_, :], start=True, stop=False, skip=True)
                        k.mm(po[rs, 0:129], qcT[:, cs_], Cb_m[:, tt_ * 2 + hh, :], start=False, stop=True, skip=True)
                    k.act(g_[:, 5:6], po[:, 128:129], AF.Abs, scale=g_[:, 15:16])
                    k.ts('dve', g_[:, 5:6], g_[:, 5:6], 1.0, None, op0=ALU.max)
                    k.recip(g_[:, 5:6], g_[:, 5:6])
                    k.tt('dve', g_[:, 5:6], g_[:, 5:6], g_[:, 15:16], ALU.mult)
                    k.stt(T[0][:, 0:128], po[:, 0:128], g_[:, 5:6], gates[:, tt_, 4, :], ALU.mult, ALU.mult)
                    k.act(T[1][:, 0:128], T[0][:, 0:128], AF.Square, accum_out=g_[:, 6:7])
                    k.rsqrt(g_[:, 5:6], g_[:, 6:7], 1.0 / 128, epsc, g_[:, 6:7])
                    k.stt(T[0][:, 0:128], T[0][:, 0:128], g_[:, 5:6], crow[:, CR_MG * 128:(CR_MG + 1) * 128], ALU.mult, ALU.mult)
                    k.tt('dve', yTM[:, tt_, 2, :], T[0][:, 0:128], gates[:, tt_, 2, :], ALU.mult)

                _lvl(14)
                if bi + 1 < NBLK:
                    k.dma(xt2, xv[:, 4:8, t0 + TB:t0 + 2 * TB], "x1")
                for ob in range(3):
                    k.mm(PS_O[ob][:], cmB(CM_ZERO), zrow[:], start=True, stop=False, sig=True, skip=True)
                oslot = {}
                for mp in range(2):
                    for qq in range(4):
                        idx = mp * 4 + qq
                        oslot[(mp, qq)] = (PS_O[idx // 3], (idx % 3) * 129)
                nkt = bi * 4 + 4
                SBK = [[PS_S[0], PSB[0]], [PS_S[1], PSB[1]]]

                def s_pair(j_):
                    q0_ = max(j_ - bi * 4, 0) * 128
                    for mp_ in range(2):
                        rs_ = slice(mp_ * 64, mp_ * 64 + 64)
                        k.mm(SBK[mp_][j_ % 2][:, q0_:TB], kT[rs_, j_ * 128:(j_ + 1) * 128], qhT[rs_, q0_:TB])

                s_pair(0)
                if nkt > 1:
                    s_pair(1)
                for j in range(nkt):
                    jj = j - bi * 4
                    q0 = max(jj, 0) * 128
                    for mp in range(2):
                        pt_ = PT[:, 2 * (j % 2) + mp, :]
                        k.act(pt_[:, q0:TB], SBK[mp][j % 2][:, q0:TB], AF.Exp)
                        if jj >= 0:
                            k.tt('pool', pt_[:, q0:q0 + 128], pt_[:, q0:q0 + 128], cmB(CM_TRI128), ALU.mult)
                    for mp in range(2):
                        pt_ = PT[:, 2 * (j % 2) + mp, :]
                        for qq in range(max(jj, 0), 4):
                            ob, oc = oslot[(mp, qq)]
                            last = (j == min(nkt - 1, bi * 4 + qq))
                            k.mm(ob[:, oc:oc + 129], pt_[:, qq * 128:(qq + 1) * 128], vA[:, j, :],
                                 start=False, stop=last, sig=last, skip=True)
                    if j + 2 < nkt:
                        s_pair(j + 2)
                for tt_ in range(4):
                    g_ = gcol[:, tt_, :]
                    ob1, oc1 = oslot[(0, tt_)]
                    ob2, oc2 = oslot[(1, tt_)]
                    k.recip(g_[:, 5:6], ob1[:, oc1 + 128:oc1 + 129])
                    k.recip(g_[:, 6:7], ob2[:, oc2 + 128:oc2 + 129])
                    k.tt('dve', g_[:, 6:7], g_[:, 6:7], NLAM, ALU.mult)
                    k.ts('dve', T[0][:, 0:128], ob1[:, oc1:oc1 + 128], g_[:, 5:6])
                    k.stt(T[0][:, 0:128], ob2[:, oc2:oc2 + 128], g_[:, 6:7], T[0][:, 0:128], ALU.mult, ALU.add)
                    k.act(T[1][:, 0:128], T[0][:, 0:128], AF.Square, accum_out=g_[:, 5:6])
                    k.rsqrt(g_[:, 6:7], g_[:, 5:6], 1.0 / 128, epsc, g_[:, 5:6])
                    k.ts('dve', g_[:, 6:7], g_[:, 6:7], 1.0 - lam_init)
                    k.stt(T[0][:, 0:128], T[0][:, 0:128], g_[:, 6:7], crow[:, CR_SUB * 128:(CR_SUB + 1) * 128], ALU.mult, ALU.mult)
                    k.tt('dve', yTM[:, tt_, 1, :], T[0][:, 0:128], gates[:, tt_, 1, :], ALU.mult)

                _lvl(15)
                for tt_ in range(4):
                    pty = gbank()
                    ptb = pty[:, 0:256].bitcast(BF16)
                    for g4 in range(4):
                        k.tr(ptb[:, g4 * 128:(g4 + 1) * 128], yTM[:, tt_, g4, :], cmB(CM_IDENT))
                    k.cp('dve', yT[:, :, tt_ * 128:(tt_ + 1) * 128], ptb.rearrange("p (g t) -> p g t", g=4))
                if not fused:
                    k.dma(yv[:, :, tsl], yT[:], "yu", eng='pool')
                else:
                    ch, half = bi // 2, bi % 2
                    for r in range(4):
                        k.act(yTr[:, r % 2, :, :], yT[:], AF.Copy, scale=col(CC_RANK + r))
                        k.dma(D['yin'][ch, r].rearrange("(g p) t -> p g t", p=128)[:, :, half * TB:(half + 1) * TB],
                              yTr[:, r % 2, :, :], "ys%d" % r, eng='pool')
                    if half == 1 or bi == NBLK - 1:
                        k.cc("AllReduce", ALU.add, [[0, 1, 2, 3], [4, 5, 6, 7]],
                             D['yin'][ch].rearrange("r p t -> (r p) t"), D['yout'][ch], "cc")

        except _Stop:
            pass
        k.s.barrier()


def layer_dram(nc, S, sfx, shared=None):
    dr = lambda name, shape, dt=F32: nc.dram_tensor(name, shape, dt, kind="ExternalInput").ap()
    D = {}
    if shared is None:
        D['pos'] = dr("pos", [1, S], I32)
        D['memT'] = dr("memT", [D_MODEL, N_MEM])
        D['cmat'] = dr("cmat", [128, NCM, 128])
        D['rmask'] = dr("rmask", [128, TB])
        D['cs'] = nc.dram_tensor("cs_scr", [2, 128, S], F32, kind="Internal").ap()
    else:
        for n_ in ('pos', 'memT', 'cmat', 'rmask', 'cs'):
            D[n_] = shared[n_]
    D['w_fm'] = dr("w_fm" + sfx, [D_MODEL, NFM * 128])
    D['w_tm'] = dr("w_tm" + sfx, [D_MODEL, NTM])
    D['w_kv'] = dr("w_kv" + sfx, [D_MODEL, 256])
    D['ccol'] = dr("ccol" + sfx, [128, NCC])
    D['crow'] = dr("crow" + sfx, [1, 3 * 128])
    D['lam'] = dr("lam" + sfx, [1, 256])
    return D


def build_layer(S, layer_idx):
    nc = bass.Bass("TRN2", target_bir_lowering=False)
    D = layer_dram(nc, S, "")
    D['xT'] = nc.dram_tensor("xT", [D_MODEL, S], F32, kind="ExternalInput").ap()
    D['yT'] = nc.dram_tensor("yT", [4 * 128, S], BF16, kind="ExternalOutput").ap()
    with ExitStack() as es:
        es.enter_context(nc.allow_low_precision(reason="bf16 matmul operands"))
        k = K(nc, es)
        PSB = [k.ps(f"pb{i}", [128, 512]) for i in range(8)]
        emit_layer(nc, k, PSB, S, layer_idx, "", D, do_rope=True, fused=False)
        k.s.finish()
    return nc


def emit_outproj(nc, k, PSB, S, sfx, D, full):
    NBLK = S // TB
    with ExitStack() as es:
        k.es = es
        sb = lambda name, shape, dt: k.sb(name + sfx, shape, dt)
        stage = sb("ostage", [128, D_MODEL], F32)
        wo = sb("wo", [128, 16, 256], BF16)
        yt = sb("yt", [128, 16, TB], BF16)
        xo = sb("xo", [128, 2, TB], F32)
        oo = sb("oo", [128, 2, TB], F32)
        wv = D['w_own'].rearrange("(c p) n -> p c n", p=128)
        for c in range(16):
            k.dma(stage[:, 0:256], wv[:, c, :], "st")
            k.cp('dve', wo[:, c, :], stage[:, 0:256])
        if full:
            w = sb("wfull", [128, 16, D_MODEL], BF16)
            xt = sb("oxt", [128, 8, TB], F32)
            ot = sb("oot", [128, 8, TB], F32)
            wv = D['w_out'].rearrange("(c p) n -> p c n", p=128)
            for c in range(16):
                k.dma(stage[:], wv[:, c, :], "st")
                k.cp('dve', w[:, c, :], stage[:])
            xv = D['x_prev'].rearrange("(c p) t -> p c t", p=128)
            x1v = D['x1'].rearrange("(c p) t -> p c t", p=128)
        xov = D['x_own_prev'].rearrange("(c p) t -> p c t", p=128)
        outv = (D['x1own'] if full else D['out']).rearrange("(c p) t -> p c t", p=128)
        pi = [0]
        for bi in range(NBLK):
            tsl = slice(bi * TB, (bi + 1) * TB)
            ch, half = bi // 2, bi % 2
            k.dma(yt[:], D['yout'][ch].rearrange("(c p) t -> p c t", p=128)[:, :, half * TB:(half + 1) * TB], "x0")
            k.dma(xo[:], xov[:, :, tsl], "x1")
            if full:
                k.dma(xt[:], xv[:, :, tsl], "x2")
                for oc in range(8):
                    p = PSB[pi[0] % 8]
                    pi[0] += 1
                    for c in range(16):
                        k.mm(p[:], w[:, c, oc * 128:(oc + 1) * 128], yt[:, c, :], start=(c == 0), stop=(c == 15),
                             sig=(c == 15))
                    k.tt('dve', ot[:, oc, :], p[:], xt[:, oc, :], ALU.add)
                k.dma(x1v[:, :, tsl], ot[:], "y0")
            for oc in range(2):
                p = PSB[pi[0] % 8]
                pi[0] += 1
                for c in range(16):
                    k.mm(p[:], wo[:, c, oc * 128:(oc + 1) * 128], yt[:, c, :], start=(c == 0), stop=(c == 15),
                         sig=(c == 15))
                k.tt('dve', oo[:, oc, :], p[:], xo[:, oc, :], ALU.add)
            k.dma(outv[:, :, tsl], oo[:], "y1")
        k.s.barrier()


def build_fused(S):
    nc = bass.Bass("TRN2", target_bir_lowering=False)
    NCH = max(S // 1024, 1)
    D0 = layer_dram(nc, S, "_0")
    D1 = layer_dram(nc, S, "_1", shared=D0)
    x0 = nc.dram_tensor("xT", [D_MODEL, S], F32, kind="ExternalInput").ap()
    x0own = nc.dram_tensor("x0own", [256, S], F32, kind="ExternalInput").ap()
    w_out0 = nc.dram_tensor("w_out0", [2048, D_MODEL], F32, kind="ExternalInput").ap()
    w_own0 = nc.dram_tensor("w_own0", [2048, 256], F32, kind="ExternalInput").ap()
    w_own1 = nc.dram_tensor("w_own1", [2048, 256], F32, kind="ExternalInput").ap()
    out = nc.dram_tensor("out", [256, S], F32, kind="ExternalOutput").ap()
    yin = nc.dram_tensor("yin", [NCH, 4, 512, min(S, 1024)], BF16).ap()
    yout = nc.dram_tensor("yout", [NCH, 2048, min(S, 1024)], BF16).ap()
    x1 = nc.dram_tensor("x1", [D_MODEL, S], F32).ap()
    x1own = nc.dram_tensor("x1own", [256, S], F32).ap()
    for D in (D0, D1):
        D['yin'], D['yout'] = yin, yout
    D0['xT'] = x0
    D1['xT'] = x1
    with ExitStack() as es:
        es.enter_context(nc.allow_low_precision(reason="bf16 matmul operands"))
        k = K(nc, es)
        PSB = [k.ps(f"pb{i}", [128, 512]) for i in range(8)]
        emit_layer(nc, k, PSB, S, 0, "_a", D0, do_rope=True, fused=True)
        emit_outproj(nc, k, PSB, S, "_p", {'w_own': w_own0, 'w_out': w_out0, 'x_prev': x0, 'x1': x1,
                                           'x_own_prev': x0own, 'x1own': x1own, 'yout': yout}, full=True)
        emit_layer(nc, k, PSB, S, 1, "_b", D1, do_rope=False, fused=True)
        emit_outproj(nc, k, PSB, S, "_q", {'w_own': w_own1, 'x_own_prev': x1own, 'out': out, 'yout': yout}, full=False)
        k.s.finish()
    return nc


def head_cols(h):
    G = GROUP
    base = np.arange(128)
    off = {}
    names = ['a_q', 'a_f', 'a_i', 'a_g', 'b_q', 'b_k', 'b_v', 'b_g', 'c_q', 'c_k', 'c_v', 'c_o']
    o = 0
    for n in names:
        off[n] = o
        o += G
    off['c_i'] = o
    o += 4
    off['c_f'] = o
    o += 4
    off['c_g'] = o
    o += G
    off['x_q'] = o
    o += G
    off['x_g'] = o
    o += G
    hc = lambda n: off[n] + h * 128 + base
    fm = np.concatenate([hc('a_q'), hc('a_f'), hc('b_q'), hc('b_k'), hc('c_q'), hc('c_k'), hc('x_q')])
    tm = np.concatenate([hc('a_g'), hc('b_g'), hc('c_g'), hc('x_g'), hc('c_o'), hc('a_i'), hc('b_v'), hc('c_v'),
                         np.array([off['c_i'] + h, off['c_f'] + h])])
    return fm, tm


def layer_inputs(l, b, h, xT_b, inp, cmat, rmask):
    fm, tm = head_cols(h)
    w_in = inp['w_in'][l]
    cc = np.zeros((128, NCC), np.float32)
    cc[:, CC_NG:CC_NG + 8] = inp['norm_g'][l].reshape(8, 128).T
    cc[:, CC_MNG:CC_MNG + 8] = inp['mem_norm_g'][l].reshape(8, 128).T
    invf = (ROPE_THETA ** (-np.arange(0, 16, 2, dtype=np.float32) / 16)).astype(np.float32)
    for base in (0, 64):
        cc[base:base + 8, CC_INVF] = invf
        cc[base + 8:base + 16, CC_INVF] = invf
        cc[base:base + 8, CC_SGN] = -1.0
        cc[base + 8:base + 16, CC_SGN] = 1.0
    cc[:, CC_GQ] = np.tile(inp['diff_qk_norm_g'][l, 0], 2)
    cc[:, CC_GK] = np.tile(inp['diff_qk_norm_g'][l, 1], 2)
    cw = inp['mlstm_conv_w'][l]
    cb = inp['mlstm_conv_b'][l]
    cc[:, CC_CWQ:CC_CWQ + 4] = cw[:, h * 128:(h + 1) * 128].T
    cc[:, CC_CBQ] = cb[h * 128:(h + 1) * 128]
    cc[:, CC_CWK:CC_CWK + 4] = cw[:, GROUP + h * 128:GROUP + (h + 1) * 128].T
    cc[:, CC_CBK] = cb[GROUP + h * 128:GROUP + (h + 1) * 128]
    cc[:, CC_GXQ] = inp['xattn_qk_norm_g'][l, 0]
    cc[:, CC_GXK] = inp['xattn_qk_norm_g'][l, 1]
    cc[:, CC_LBL:CC_LBL + 2] = inp['hgrn_lb_logits'][:, h * 128:(h + 1) * 128].T
    cc[0:64, CC_IND0] = 1.0
    cc[64:128, CC_IND1] = 1.0
    cc[:, CC_GBI] = inp['mlstm_gate_b'][l, h]
    cc[:, CC_GBF] = inp['mlstm_gate_b'][l, 4 + h]
    cc[:, CC_EPS] = EPS
    cc[:, CC_ONE] = 1.0
    cc[:, CC_RANK + h] = 1.0
    crow = np.concatenate([inp['hgrn_norm_g'][l], inp['diff_subln_g'][l], inp['mlstm_norm_g'][l]])[None, :]
    wkv = inp['w_mem_kv'][l]
    wkv_h = np.concatenate([wkv[:, h * 128:(h + 1) * 128], wkv[:, GROUP + h * 128:GROUP + (h + 1) * 128]], axis=1)
    return {
        "xT": xT_b,
        "pos": np.ascontiguousarray(inp['positions'][b][None, :xT_b.shape[1]]).astype(np.int32),
        "w_fm": np.ascontiguousarray(w_in[:, fm]),
        "w_tm": np.ascontiguousarray(w_in[:, tm]),
        "memT": np.ascontiguousarray(inp['mem'][b].T),
        "w_kv": np.ascontiguousarray(wkv_h),
        "cmat": cmat,
        "ccol": cc,
        "crow": np.ascontiguousarray(crow.astype(np.float32)),
        "lam": np.ascontiguousarray(inp['diff_lambda'][l].reshape(1, 256)),
        "rmask": rmask,
    }


_CACHE = {}


def run_layer(l, xT, inp, S):
    key = ('L', l, S)
    if key not in _CACHE:
        _CACHE[key] = build_layer(S, l)
    nc = _CACHE[key]
    cmat = const_mats()
    rmask = np.ones((128, TB), np.float32)
    rmask[:, ::64] = 0.0
    in_maps = []
    for c in range(8):
        b, h = c // 4, c % 4
        in_maps.append(layer_inputs(l, b, h, np.ascontiguousarray(xT[b]), inp, cmat, rmask))
    res = run_bass_kernel_spmd(nc, in_maps, core_ids=list(range(8)))
    ys = [np.asarray(r["yT"]) for r in res.results]
    return np.stack([np.concatenate(ys[b * 4:(b + 1) * 4], axis=0) for b in range(BATCH)])


def perm_w_out(w):
    return np.ascontiguousarray(w.reshape(4, 4, 128, D_MODEL).transpose(1, 0, 2, 3).reshape(2048, D_MODEL))


def run_fused(xT, inp, S):
    key = ('F', S)
    if key not in _CACHE:
        _CACHE[key] = build_fused(S)
    nc = _CACHE[key]
    cmat = const_mats()
    rmask = np.ones((128, TB), np.float32)
    rmask[:, ::64] = 0.0
    wp = [perm_w_out(inp['w_out'][l]) for l in range(DEPTH)]
    in_maps = []
    for c in range(8):
        b, h = c // 4, c % 4
        xb_ = np.ascontiguousarray(xT[b])
        m = {}
        for l in range(DEPTH):
            li = layer_inputs(l, b, h, xb_, inp, cmat, rmask)
            for n_ in ('w_fm', 'w_tm', 'w_kv', 'ccol', 'crow', 'lam'):
                m[n_ + "_%d" % l] = li[n_]
            if l == 0:
                for n_ in ('xT', 'pos', 'memT', 'cmat', 'rmask'):
                    m[n_] = li[n_]
        fs = slice(h * 256, (h + 1) * 256)
        m['x0own'] = np.ascontiguousarray(xb_[fs])
        m['w_out0'] = wp[0]
        m['w_own0'] = np.ascontiguousarray(wp[0][:, fs])
        m['w_own1'] = np.ascontiguousarray(wp[1][:, fs])
        in_maps.append(m)
    res = run_bass_kernel_spmd(nc, in_maps, core_ids=list(range(8)))
    outs = [np.asarray(r["out"]) for r in res.results]
    return np.stack([np.concatenate(outs[b * 4:(b + 1) * 4], axis=0) for b in range(BATCH)])


def kernel(**inputs):
    inp = {k_: np.asarray(v) for k_, v in inputs.items()}
    x = inp['x']
    S = x.shape[1]
    xT = np.ascontiguousarray(x.transpose(0, 2, 1))
    oT = run_fused(xT, inp, S)
    return np.ascontiguousarray(oT.transpose(0, 2, 1)).astype(np.float32)
```

```python
import math
import os
from contextlib import ExitStack

import numpy as np
import concourse.bass as bass
import concourse.mybir as mybir
from concourse.bass_utils import run_bass_kernel_spmd

F32 = mybir.dt.float32
BF16 = mybir.dt.bfloat16
I32 = mybir.dt.int32
AF = mybir.ActivationFunctionType
ALU = mybir.AluOpType

D_MODEL = 1024
BATCH = 2
SEQ = 16384
DEPTH = 2
N_MEM = 256
GROUP = 512
EPS = 1e-6
ROPE_THETA = 500000.0
TB = 512
NFM = 7
NTM = 1026


class Sched:
    SAME = ('act', 'dve', 'pool')

    def __init__(self, nc, es):
        self.nc = nc
        self.engs = {'pe': nc.tensor, 'act': nc.scalar, 'dve': nc.vector, 'pool': nc.gpsimd, 'sp': nc.sync}
        self.sem = {e: es.enter_context(nc.semaphore('s_' + e)) for e in ('pe', 'act', 'dve', 'pool')}
        self.cnt = {e: 0 for e in self.sem}
        self.pending = {e: [] for e in self.sem}
        self.csem = {}
        self.ccnt = {}
        self.es = es
        self.waited = {}
        self.acc = {}
        self.ops = []
        self.nwaits = 0

    @staticmethod
    def box(ap):
        dims = ap.ap
        esz = mybir.dt.size(ap.dtype)
        off = ap.offset
        sp = str(ap.space)
        if sp == 'DRAM':
            ext = 1
            for s, c in dims:
                ext += (c - 1) * abs(s)
            return (ap.tensor.name, 0, 1, off * esz, (off + ext) * esz)
        pstep = dims[0][0]
        if pstep == 0:
            p0, f0 = 0, off
        else:
            p0, f0 = off // pstep, off % pstep
        ext = 1
        for s, c in dims[1:]:
            ext += (c - 1) * abs(s)
        if sp == 'PSUM':
            return (ap.tensor.name, (p0 // 32) * 32, ((p0 + dims[0][1] + 31) // 32) * 32, 0, 2048)
        return (ap.tensor.name, p0, p0 + dims[0][1], f0 * esz, (f0 + ext) * esz)

    def _sem_of(self, key):
        return self.sem[key] if key in self.sem else self.csem[key]

    def emit(self, eng, fn, outs=(), ins=(), sig=True, chan=None, inc=16):
        is_dma = chan is not None
        oboxes = [self.box(a) for a in outs]
        iboxes = [self.box(a) for a in ins]
        deps = set()
        for (name, p0, p1, f0, f1) in iboxes:
            for ent in self.acc.get(name, ()):
                if ent[5] and ent[0] < p1 and p0 < ent[1] and ent[2] < f1 and f0 < ent[3]:
                    deps.add(ent[4])
        for (name, p0, p1, f0, f1) in oboxes:
            for ent in self.acc.get(name, ()):
                if ent[0] < p1 and p0 < ent[1] and ent[2] < f1 and f0 < ent[3]:
                    deps.add(ent[4])
        need = {}
        for d in deps:
            deng, key, val = self.ops[d]
            if deng is not None and deng == eng and eng not in self.SAME:
                continue
            if val is None:
                raise RuntimeError('dependency on unsignalled op')
            if need.get(key, 0) < val:
                need[key] = val
        if is_dma:
            if chan not in self.csem:
                self.csem[chan] = self.es.enter_context(self.nc.semaphore('c_' + chan))
                self.ccnt[chan] = 0
            if self.ccnt[chan] > 0 and need.get(chan, 0) < self.ccnt[chan]:
                need[chan] = self.ccnt[chan]
        e = self.engs[eng]
        for key, val in need.items():
            if self.waited.get((eng, key), 0) >= val:
                continue
            e.wait_ge(self._sem_of(key), val)
            self.waited[(eng, key)] = val
            self.nwaits += 1
        inst = fn(e)
        opid = len(self.ops)
        if is_dma:
            self.ccnt[chan] += inc
            inst.then_inc(self.csem[chan], inc)
            self.ops.append([None, chan, self.ccnt[chan]])
        elif sig:
            self.cnt[eng] += 1
            inst.then_inc(self.sem[eng], 1)
            self.ops.append([eng, eng, self.cnt[eng]])
            for p in self.pending[eng]:
                self.ops[p][2] = self.cnt[eng]
            self.pending[eng] = []
        else:
            self.ops.append([eng, eng, None])
            self.pending[eng].append(opid)
        for (name, p0, p1, f0, f1) in oboxes:
            lst = self.acc.setdefault(name, [])
            lst[:] = [t for t in lst if not (p0 <= t[0] and t[1] <= p1 and f0 <= t[2] and t[3] <= f1)]
            lst.append((p0, p1, f0, f1, opid, True))
        rkey = None if is_dma else eng
        for (name, p0, p1, f0, f1) in iboxes:
            lst = self.acc.setdefault(name, [])
            for i, t in enumerate(lst):
                if (not t[5]) and t[0] == p0 and t[1] == p1 and t[2] == f0 and t[3] == f1 \
                        and self.ops[t[4]][0] == rkey and rkey is not None:
                    lst[i] = (p0, p1, f0, f1, opid, False)
                    break
            else:
                lst.append((p0, p1, f0, f1, opid, False))
        return inst

    def barrier(self):
        keys = [(key, self.cnt[key]) for key in self.sem] + [(key, self.ccnt[key]) for key in self.csem]
        for eng, e in self.engs.items():
            for key, val in keys:
                if val > 0 and self.waited.get((eng, key), 0) < val:
                    e.wait_ge(self._sem_of(key), val)
                    self.waited[(eng, key)] = val

    def finish(self):
        sp = self.engs['sp']
        for key in list(self.csem):
            if self.ccnt[key] > 0:
                sp.wait_ge(self.csem[key], self.ccnt[key])
        for key in self.sem:
            if self.cnt[key] > 0:
                sp.wait_ge(self.sem[key], self.cnt[key])


class _Stop(Exception):
    pass


def _lvl(n):
    if n > float(os.environ.get('KLVL', '99')):
        raise _Stop()


class K:
    def __init__(self, nc, es):
        self.nc = nc
        self.es = es
        self.s = Sched(nc, es)
        self.n = 0

    def sb(self, name, shape, dt):
        return self.es.enter_context(self.nc.sbuf_tensor(name, shape, dt))

    def ps(self, name, shape, dt=F32):
        return self.es.enter_context(self.nc.psum_tensor(name, shape, dt))

    def mm(self, out, lhsT, rhs, start=True, stop=True, sig=None, skip=False):
        if sig is None:
            sig = True
        return self.s.emit('pe', lambda e: e.matmul(out, lhsT, rhs, start=start, stop=stop,
                                                     skip_group_check=skip),
                           outs=[out], ins=[lhsT, rhs], sig=sig)

    def tr(self, out, in_, ident):
        return self.s.emit('pe', lambda e: e.transpose(out, in_, ident), outs=[out], ins=[in_, ident])

    def act(self, out, in_, func, bias=0.0, scale=1.0, accum_out=None):
        ins = [in_]
        outs = [out]
        if not isinstance(bias, (int, float)):
            ins.append(bias)
        if not isinstance(scale, (int, float)):
            ins.append(scale)
        if accum_out is not None:
            outs.append(accum_out)
        kw = {}
        if accum_out is not None:
            kw['accum_out'] = accum_out
        return self.s.emit('act', lambda e: e.activation(out, in_, func, bias=bias, scale=scale, **kw),
                           outs=outs, ins=ins)

    def ts(self, eng, out, in0, s1, s2=None, op0=ALU.mult, op1=None, accum_out=None):
        ins = [in0]
        outs = [out]
        for s_ in (s1, s2):
            if s_ is not None and not isinstance(s_, (int, float)):
                ins.append(s_)
        kw = {}
        if op1 is not None:
            kw['op1'] = op1
        if accum_out is not None:
            kw['accum_out'] = accum_out
            outs.append(accum_out)
        return self.s.emit(eng, lambda e: e.tensor_scalar(out, in0, s1, s2, op0, **kw), outs=outs, ins=ins)

    def tt(self, eng, out, in0, in1, op):
        return self.s.emit(eng, lambda e: e.tensor_tensor(out, in0, in1, op), outs=[out], ins=[in0, in1])

    def stt(self, out, in0, scalar, in1, op0, op1):
        ins = [in0, in1]
        if not isinstance(scalar, (int, float)):
            ins.append(scalar)
        return self.s.emit('dve', lambda e: e.scalar_tensor_tensor(out, in0, scalar, in1, op0, op1),
                           outs=[out], ins=ins)

    def cp(self, eng, out, in_):
        if eng == 'act':
            return self.s.emit('act', lambda e: e.copy(out, in_), outs=[out], ins=[in_])
        return self.s.emit(eng, lambda e: e.tensor_copy(out, in_), outs=[out], ins=[in_])

    def recip(self, out, in_):
        return self.s.emit('dve', lambda e: e.reciprocal(out, in_), outs=[out], ins=[in_])

    def memset(self, eng, out, val):
        return self.s.emit(eng, lambda e: e.memset(out, val), outs=[out], ins=[])

    def scan(self, out, d0, d1, init, op0, op1):
        return self.s.emit('dve', lambda e: e.tensor_tensor_scan(out, d0, d1, init, op0, op1),
                           outs=[out], ins=[d0, d1])

    def dma(self, out, in_, chan, eng='sp'):
        return self.s.emit(eng, lambda e: e.dma_start(out=out, in_=in_), outs=[out], ins=[in_], chan=chan)

    def cc(self, kind, op, groups, in_ap, out_ap, chan):
        return self.s.emit('pool', lambda e: e.collective_compute(kind, op, replica_groups=groups,
                                                                 ins=[in_ap.opt()], outs=[out_ap.opt()]),
                           outs=[out_ap], ins=[in_ap], chan=chan, inc=1)

    def rsqrt(self, out, in_, scale, epscol, tmp):
        self.act(tmp, in_, AF.Ln, bias=epscol, scale=scale)
        self.act(out, tmp, AF.Exp, scale=-0.5)


CM_IDENT, CM_TRI128, CM_TRI64, CM_BLK64, CM_ONES, CM_PROPE, CM_ZERO = range(7)
NCM = 7
(CC_NG, CC_MNG, CC_INVF, CC_SGN, CC_GQ, CC_GK, CC_CWQ, CC_CBQ, CC_CWK, CC_CBK, CC_GXQ, CC_GXK, CC_LBL,
 CC_IND0, CC_IND1, CC_GBI, CC_GBF, CC_EPS, CC_ONE, CC_HALFPI) = (0, 8, 16, 17, 18, 19, 20, 24, 25, 29, 30, 31,
                                                                 32, 34, 35, 36, 37, 38, 39, 40)
CC_RANK = 41
NCC = 45
CR_HG, CR_SUB, CR_MG = range(3)
TWO_PI = 2.0 * math.pi


def const_mats():
    m = np.zeros((NCM, 128, 128), np.float32)
    i = np.arange(128)
    m[CM_IDENT] = np.eye(128)
    m[CM_TRI128] = (i[None, :] >= i[:, None])
    same = (i[:, None] // 64) == (i[None, :] // 64)
    m[CM_TRI64] = m[CM_TRI128] * same
    m[CM_BLK64] = same
    m[CM_ONES] = 1.0
    for base in (0, 64):
        for j in range(8):
            m[CM_PROPE, base + 8 + j, base + j] = 1.0
            m[CM_PROPE, base + j, base + 8 + j] = 1.0
    return np.ascontiguousarray(m.transpose(1, 0, 2))


def emit_layer(nc, k, PSB, S, layer_idx, sfx, D, do_rope=True, fused=False):
    NBLK = S // TB
    NT = S // 128
    xT_d, pos_d, wfm_d, wtm_d = D['xT'], D['pos'], D['w_fm'], D['w_tm']
    memT_d, wkv_d, cmat_d, ccol_d, crow_d, lam_d, rm_d = (D['memT'], D['w_kv'], D['cmat'], D['ccol'], D['crow'],
                                                          D['lam'], D['rmask'])
    cs_d = D['cs']
    yT_d = D.get('yT')

    lam_init = 0.8 - 0.6 * math.exp(-0.3 * layer_idx)

    with ExitStack() as es:
        k.es = es
        sb = lambda name, shape, dt: k.sb(name + sfx, shape, dt)
        kT = sb("kT", [128, S], BF16)
        vA = sb("vA", [128, NT, 129], BF16)
        wfm = sb("wfm", [128, 8, NFM * 128], BF16)
        wtm = sb("wtm", [128, 8, NTM], BF16)
        cmf3 = sb("cmf3", [128, 3, 128], F32)
        cmb = sb("cmb", [128, NCM, 128], BF16)
        cc = sb("cc", [128, NCC], F32)
        crow = sb("crow_sb", [128, 3 * 128], F32)
        lamt = sb("lamt", [128, 256], F32)
        rmask = sb("rmask_sb", [128, TB], F32)
        zrow = sb("zrow", [128, TB], BF16)
        kmT = sb("kmT", [128, 256], BF16)
        vmA = sb("vmA", [128, 2, 129], BF16)
        S_h = sb("S_h", [128, 128], F32)
        Sb_h = sb("Sb_h", [128, 8, 128], BF16)
        dS = sb("dS", [128, 8, 128], F32)
        C_m = sb("C_m", [128, 129], F32)
        Cb_m = sb("Cb_m", [128, 8, 129], BF16)
        dC = sb("dC", [128, 8, 129], F32)
        sm = sb("sm", [128, 64], F32)
        stage = sb("stage", [128, NTM], F32)
        xt = sb("xt", [128, 4, TB], F32)
        xtm = xt[:].rearrange("p c t -> p (c t)").rearrange("p (c t) -> p c t", c=8)
        xb = sb("xb", [128, 8, TB], BF16)
        sqb = sb("sqb", [128, 2, TB], BF16)
        RX = sb("RX", [128, TB], F32)
        rxc = sb("rxc", [128, 8], F32)
        Tall = sb("Tall", [128, 6, TB], F32)
        T = [Tall[:, i, :] for i in range(6)]
        xt2 = Tall[:, 2:6, :]
        Bt = [sb(f"B{i}", [128, TB], BF16) for i in range(3)]
        cosT = sb("cosT", [128, TB], F32)
        sinT = sb("sinT", [128, TB], F32)
        qhT = sb("qhT", [128, TB], BF16)
        qtT = sb("qtT", [128, TB], BF16)
        ktT = sb("ktT", [128, TB], BF16)
        qcT = sb("qcT", [128, TB], BF16)
        kcT = sb("kcT", [128, TB], BF16)
        cqx = sb("cqx", [128, TB + 3], F32)
        ckx = sb("ckx", [128, TB + 3], F32)
        qxn = sb("qxn", [128, TB], BF16)
        bcum = sb("bcum", [128, TB], F32)
        av = sb("av", [128, 4, 128], BF16)
        cvA = sb("cvA", [128, 4, 129], BF16)
        gates = sb("gates", [128, 4, 5, 128], BF16)
        gcol = sb("gcol", [128, 4, 16], F32)
        kTM = sb("kTM", [128, 4, 128], BF16)
        kTM2 = sb("kTM2", [128, 4, 128], BF16)
        AT = sb("AT", [128, 2, 128], BF16)
        PT = sb("PT", [128, 4, TB], BF16)
        yTM = sb("yTM", [128, 4, 4, 128], BF16)
        yT = sb("yT_sb", [128, 4, TB], BF16)
        yTr = sb("yTr", [128, 2, 4, TB], BF16) if fused else None
        G = PSB[0:3]
        PS_S = PSB[3:5]
        PS_O = PSB[5:8]
        gi = [0]

        def gbank():
            b = G[gi[0] % 3]
            gi[0] += 1
            return b

        cmF = lambda i: cmf3[:, i - CM_TRI64, :]
        cmB = lambda i: cmb[:, i, :]
        col = lambda i, n=1: cc[:, i:i + n]
        epsc = col(CC_EPS)

        try:
            k.dma(stage[:, 0:NCM * 128], cmat_d.rearrange("p m n -> p (m n)"), "c0")
            k.dma(cmf3[:], cmat_d[:, CM_TRI64:CM_TRI64 + 3, :], "c5")
            k.dma(cc[:], ccol_d[:, :], "c1")
            k.dma(crow[:], crow_d[0:1, :].partition_broadcast(128), "c2")
            k.dma(lamt[:], lam_d[0:1, :].partition_broadcast(128), "c3")
            k.dma(rmask[:], rm_d[:, :], "c4")
            k.cp('dve', cmb[:].rearrange("p m n -> p (m n)"), stage[:, 0:NCM * 128])
            k.memset('pool', zrow[:], 0.0)
            k.memset('pool', vA[:, :, 128:129], 1.0)
            k.memset('pool', cvA[:, :, 128:129], 1.0)
            k.memset('pool', vmA[:, :, 128:129], 1.0)
            k.memset('pool', S_h[:], 0.0)
            k.memset('pool', C_m[:], 0.0)
            k.memset('pool', cqx[:, 0:3], 0.0)
            k.memset('pool', ckx[:, 0:3], 0.0)
            k.memset('pool', AT[:], 0.0)
            k.memset('dve', sm[:], 0.0)
            wv = wfm_d.rearrange("(c p) n -> p c n", p=128)
            for c in range(8):
                k.dma(stage[:, 0:NFM * 128], wv[:, c, :], "st")
                k.ts('dve', wfm[:, c, :], stage[:, 0:NFM * 128], col(CC_NG + c))
            wv = wtm_d.rearrange("(c p) n -> p c n", p=128)
            for c in range(8):
                k.dma(stage[:, 0:NTM], wv[:, c, :], "st")
                k.ts('dve', wtm[:, c, :], stage[:, 0:NTM], col(CC_NG + c))

            _lvl(2)
            LB, OMLB, NLAM, R1 = sm[:, 0:1], sm[:, 1:2], sm[:, 2:3], sm[:, 3:4]
            if layer_idx == 0:
                k.memset('dve', LB, 0.0)
            else:
                k.tt('dve', sm[:, 4:5], col(CC_LBL), col(CC_LBL + 1), ALU.subtract)
                k.act(sm[:, 5:6], sm[:, 4:5], AF.Exp)
                k.ts('dve', sm[:, 5:6], sm[:, 5:6], 1.0, None, op0=ALU.add)
                k.recip(LB, sm[:, 5:6])
            k.ts('dve', OMLB, LB, -1.0, 1.0, op0=ALU.mult, op1=ALU.add)
            k.tt('dve', T[0][:, 0:64], lamt[:, 0:64], lamt[:, 64:128], ALU.mult)
            k.tt('dve', T[0][:, 64:128], lamt[:, 128:192], lamt[:, 192:256], ALU.mult)
            k.s.emit('dve', lambda e: e.reduce_sum(sm[:, 6:7], T[0][:, 0:64], mybir.AxisListType.X),
                     outs=[sm[:, 6:7]], ins=[T[0][:, 0:64]])
            k.s.emit('dve', lambda e: e.reduce_sum(sm[:, 7:8], T[0][:, 64:128], mybir.AxisListType.X),
                     outs=[sm[:, 7:8]], ins=[T[0][:, 64:128]])
            k.act(sm[:, 8:10], sm[:, 6:8], AF.Exp)
            k.tt('dve', sm[:, 10:11], sm[:, 9:10], sm[:, 8:9], ALU.subtract)
            k.ts('dve', NLAM, sm[:, 10:11], -lam_init, None, op0=ALU.add)

            _lvl(3)
            mv = memT_d.rearrange("(c p) n -> p c n", p=128)
            k.dma(xtm, mv[:, :, :], "x0")
            wk = wkv_d.rearrange("(c p) n -> p c n", p=128)
            mw = xb[:, :, 256:512]
            for c in range(8):
                k.dma(stage[:, 0:256], wk[:, c, :], "st")
                k.ts('dve', mw[:, c, :], stage[:, 0:256], col(CC_MNG + c))
                k.cp('pool', xb[:, c, 0:256], xtm[:, c, :])
            for mt in range(2):
                pc = gbank()
                for c in range(8):
                    k.act(sqb[:, c % 2, 0:128], xtm[:, c, mt * 128:(mt + 1) * 128], AF.Square)
                    k.mm(pc[:, 0:1], sqb[:, c % 2, 0:128], cmB(CM_ONES)[:, 0:1], start=(c == 0), stop=(c == 7))
                k.rsqrt(gcol[:, 0, 0:1], pc[:, 0:1], 1.0 / D_MODEL, epsc, gcol[:, 0, 1:2])
                pk = gbank()
                for c in range(8):
                    k.mm(pk[:, 0:256], xb[:, c, mt * 128:(mt + 1) * 128], mw[:, c, :], start=(c == 0), stop=(c == 7))
                k.ts('dve', T[0][:, 0:128], pk[:, 0:128], gcol[:, 0, 0:1])
                k.ts('dve', vmA[:, mt, 0:128], pk[:, 128:256], gcol[:, 0, 0:1])
                k.act(T[1][:, 0:128], T[0][:, 0:128], AF.Square, accum_out=gcol[:, 0, 2:3])
                k.rsqrt(gcol[:, 0, 3:4], gcol[:, 0, 2:3], 1.0 / 128, epsc, gcol[:, 0, 4:5])
                k.ts('dve', Bt[0][:, 0:128], T[0][:, 0:128], gcol[:, 0, 3:4])
                pt_ = gbank()
                ptb = pt_[:, 0:64].bitcast(BF16)
                k.tr(ptb, Bt[0][:, 0:128], cmB(CM_IDENT))
                k.tt('dve', sm[:, 11:12], col(CC_GXQ), col(CC_GXK), ALU.mult)
                k.ts('dve', kmT[:, mt * 128:(mt + 1) * 128], ptb, sm[:, 11:12], 128 ** -0.5, op0=ALU.mult, op1=ALU.mult)

            _lvl(4)
            for bi in (range(NBLK) if do_rope else ()):
                tsl = slice(bi * TB, (bi + 1) * TB)
                pi_ = T[5].bitcast(I32)
                k.dma(pi_[:, :], pos_d[0:1, tsl].partition_broadcast(128), "x1")
                k.cp('dve', T[0][:], pi_[:, :])
                k.ts('dve', T[1][:], T[0][:], col(CC_INVF))
                for which, dst in ((0, cosT), (1, sinT)):
                    src = T[1]
                    if which == 0:
                        k.ts('dve', T[2][:], T[1][:], math.pi / 2, None, op0=ALU.add)
                        src = T[2]
                    k.ts('dve', T[3][:], src[:], 1.0 / TWO_PI, 12582912.0, op0=ALU.mult, op1=ALU.add)
                    k.ts('dve', T[3][:], T[3][:], -12582912.0, None, op0=ALU.add)
                    k.stt(T[4][:], T[3][:], -6.28125, src[:], ALU.mult, ALU.add)
                    k.stt(T[4][:], T[3][:], -(TWO_PI - 6.28125), T[4][:], ALU.mult, ALU.add)
                    k.ts('dve', T[4][:], T[4][:], 3.1415925, -3.1415925, op0=ALU.min, op1=ALU.max)
                    k.act(dst[:], T[4][:], AF.Sin)
                k.ts('dve', sinT[:], sinT[:], col(CC_SGN))
                k.dma(cs_d[0, :, tsl], cosT[:], "y0")
                k.dma(cs_d[1, :, tsl], sinT[:], "y1")

            xv = xT_d.rearrange("(c p) t -> p c t", p=128)
            yv = yT_d.rearrange("(g p) t -> p g t", p=128) if yT_d is not None else None

            for bi in range(NBLK):
                t0 = bi * TB
                tsl = slice(t0, t0 + TB)
                _lvl(5)
                if bi == 0:
                    k.dma(cosT[:], cs_d[0, :, tsl], "x2")
                    k.dma(sinT[:], cs_d[1, :, tsl], "x3")
                    k.dma(xt[:], xv[:, 0:4, tsl], "x0")
                prx = gbank()
                prc = gbank()
                if bi == 0:
                    k.dma(xt2, xv[:, 4:8, tsl], "x1")
                for c in range(8):
                    xsrc = xt[:, c, :] if c < 4 else xt2[:, c - 4, :]
                    k.cp('act', xb[:, c, :], xsrc)
                    k.tt('dve', sqb[:, c % 2, :], xsrc, xsrc, ALU.mult)
                    k.mm(prx[:], cmB(CM_ONES), sqb[:, c % 2, :], start=(c == 0), stop=(c == 7))
                    for tt_ in range(4):
                        k.mm(prc[:, tt_:tt_ + 1], sqb[:, c % 2, tt_ * 128:(tt_ + 1) * 128], cmB(CM_ONES)[:, 0:1],
                             start=(c == 0 and tt_ == 0), stop=(c == 7), skip=True)
                if bi + 1 < NBLK:
                    k.dma(xt[:], xv[:, 0:4, t0 + TB:t0 + 2 * TB], "x0")
                k.rsqrt(RX[:], prx[:], 1.0 / D_MODEL, epsc, T[0][:])
                k.rsqrt(rxc[:, 0:4], prc[:, 0:4], 1.0 / D_MODEL, epsc, rxc[:, 4:8])
                k.ts('dve', rxc[:, 4:8], rxc[:, 0:4], -1.0)

                def fm_proj(j):
                    p = gbank()
                    for c in range(8):
                        k.mm(p[:], wfm[:, c, j * 128:(j + 1) * 128], xb[:, c, :], start=(c == 0), stop=(c == 7))
                    return p

                _lvl(6)
                p = fm_proj(0)
                k.tt('dve', T[0][:], p[:], RX[:], ALU.mult)
                k.act(T[1][:], T[0][:], AF.Exp, scale=-1.0)
                k.ts('dve', T[1][:], T[1][:], 1.0, None, op0=ALU.add)
                k.recip(T[1][:], T[1][:])
                k.stt(T[0][:], T[0][:], 128 ** -0.5, T[1][:], ALU.mult, ALU.mult)
                p = fm_proj(1)
                k.tt('dve', T[1][:], p[:], RX[:], ALU.mult)
                k.act(T[2][:], T[1][:], AF.Exp, scale=-1.0)
                k.ts('dve', T[2][:], T[2][:], 1.0, None, op0=ALU.add)
                k.recip(T[2][:], T[2][:])
                k.ts('dve', T[2][:], T[2][:], OMLB, LB, op0=ALU.mult, op1=ALU.add)
                k.act(T[3][:], T[2][:], AF.Ln)
                k.ts('dve', T[2][:], T[2][:], -1.0, 1.0, op0=ALU.mult, op1=ALU.add)
                k.scan(bcum[:], rmask[:], T[3][:], 0.0, ALU.mult, ALU.add)
                for c8 in range(8):
                    cs_ = slice(c8 * 64, (c8 + 1) * 64)
                    k.ts('dve', T[3][:, cs_], bcum[:, cs_], bcum[:, c8 * 64 + 31:c8 * 64 + 32], None, op0=ALU.subtract)
                k.act(T[4][:], T[3][:], AF.Exp)
                k.act(T[5][:], T[3][:], AF.Exp, scale=-1.0)
                k.tt('dve', qtT[:], T[0][:], T[4][:], ALU.mult)
                k.tt('dve', ktT[:], T[2][:], T[5][:], ALU.mult)
                EREF, ELAST, CFAC = sm[:, 16:24], sm[:, 24:32], sm[:, 32:40]
                k.act(EREF, bcum[:, 31:TB:64], AF.Exp)
                k.act(ELAST, bcum[:, 63:TB:64], AF.Exp)
                k.tt('dve', sm[:, 40:48], bcum[:, 63:TB:64], bcum[:, 31:TB:64], ALU.subtract)
                k.act(CFAC, sm[:, 40:48], AF.Exp)

                _lvl(7)
                def qk_norm_rope(j, gcol_, scale_, dst):
                    p_ = fm_proj(j)
                    k.tt('dve', T[0][:], p_[:], RX[:], ALU.mult)
                    k.act(Bt[0][:], T[0][:], AF.Square)
                    pss = gbank()
                    k.mm(pss[:], cmB(CM_BLK64), Bt[0][:])
                    k.rsqrt(T[1][:], pss[:], 1.0 / 64, epsc, T[2][:])
                    k.ts('dve', Bt[1][:], T[0][:], gcol_)
                    pp = gbank()
                    k.mm(pp[:], cmB(CM_PROPE), Bt[1][:])
                    k.tt('dve', T[2][:], Bt[1][:], cosT[:], ALU.mult)
                    k.tt('dve', T[3][:], pp[:], sinT[:], ALU.mult)
                    k.tt('dve', T[2][:], T[2][:], T[3][:], ALU.add)
                    k.stt(dst, T[2][:], scale_, T[1][:], ALU.mult, ALU.mult)

                qk_norm_rope(2, col(CC_GQ), 0.125, qhT[:])
                qk_norm_rope(3, col(CC_GK), 1.0, kT[:, tsl])
                if bi + 1 < NBLK:
                    k.dma(cosT[:], cs_d[0, :, t0 + TB:t0 + 2 * TB], "x2")
                    k.dma(sinT[:], cs_d[1, :, t0 + TB:t0 + 2 * TB], "x3")

                _lvl(8)
                def conv_silu(j, ext, wcol, bcol, scale_, dst):
                    p_ = fm_proj(j)
                    k.tt('dve', ext[:, 3:TB + 3], p_[:], RX[:], ALU.mult)
                    k.ts('dve', T[0][:], ext[:, 0:TB], cc[:, wcol:wcol + 1], cc[:, bcol:bcol + 1], op0=ALU.mult, op1=ALU.add)
                    for jj in range(1, 4):
                        k.stt(T[0][:], ext[:, jj:TB + jj], cc[:, wcol + jj:wcol + jj + 1], T[0][:], ALU.mult, ALU.add)
                    k.cp('pool', ext[:, 0:3], ext[:, TB:TB + 3])
                    k.act(T[1][:], T[0][:], AF.Exp, scale=-1.0)
                    k.ts('dve', T[1][:], T[1][:], 1.0, None, op0=ALU.add)
                    k.recip(T[1][:], T[1][:])
                    k.stt(dst, T[0][:], scale_, T[1][:], ALU.mult, ALU.mult)

                conv_silu(4, cqx, CC_CWQ, CC_CBQ, 1.0, qcT[:])
                conv_silu(5, ckx, CC_CWK, CC_CBK, 128 ** -0.5, kcT[:])

                _lvl(9)
                p = fm_proj(6)
                k.tt('dve', T[0][:], p[:], RX[:], ALU.mult)
                k.act(Bt[0][:], T[0][:], AF.Square)
                pss = gbank()
                k.mm(pss[:], cmB(CM_ONES), Bt[0][:])
                k.rsqrt(T[1][:], pss[:], 1.0 / 128, epsc, T[2][:])
                k.tt('dve', qxn[:], T[0][:], T[1][:], ALU.mult)
                for mt in range(2):
                    pS = gbank()
                    k.mm(pS[:], kmT[:, mt * 128:(mt + 1) * 128], qxn[:])
                    k.act(PT[:, mt, :], pS[:], AF.Exp)

                _lvl(10)
                for tt_ in range(4):
                    tcs = slice(tt_ * 128, (tt_ + 1) * 128)
                    pA, pB, pC = (PSB[2], PSB[3], PSB[4]) if tt_ % 2 == 0 else (PSB[5], PSB[6], PSB[7])
                    for c in range(8):
                        k.mm(pA[:], xb[:, c, tcs], wtm[:, c, 0:512], start=(c == 0), stop=(c == 7))
                    for c in range(8):
                        k.mm(pB[:], xb[:, c, tcs], wtm[:, c, 512:1024], start=(c == 0), stop=(c == 7))
                    for c in range(8):
                        k.mm(pC[:, 0:2], xb[:, c, tcs], wtm[:, c, 1024:1026], start=(c == 0), stop=(c == 7))
                    rc = rxc[:, tt_:tt_ + 1]
                    nrc = rxc[:, 4 + tt_:5 + tt_]
                    k.act(T[0][:], pA[:], AF.Exp, scale=nrc)
                    k.ts('dve', T[0][:], T[0][:], 1.0, None, op0=ALU.add)
                    k.recip(T[0][:], T[0][:])
                    k.stt(gates[:, tt_, 0:4, :].rearrange("p g d -> p (g d)"), pA[:], rc, T[0][:], ALU.mult, ALU.mult)
                    k.act(T[1][:, 0:128], pB[:, 0:128], AF.Exp, scale=nrc)
                    k.ts('dve', T[1][:, 0:128], T[1][:, 0:128], 1.0, None, op0=ALU.add)
                    k.recip(gates[:, tt_, 4, :], T[1][:, 0:128])
                    k.act(av[:, tt_, :], pB[:, 128:256], AF.Copy, scale=rc)
                    k.act(vA[:, bi * 4 + tt_, 0:128], pB[:, 256:384], AF.Copy, scale=rc)
                    k.act(cvA[:, tt_, 0:128], pB[:, 384:512], AF.Copy, scale=rc)
                    g_ = gcol[:, tt_, :]
                    k.stt(g_[:, 0:1], pC[:, 0:1], rc, col(CC_GBI), ALU.mult, ALU.add)
                    k.stt(g_[:, 2:3], pC[:, 1:2], rc, col(CC_GBF), ALU.mult, ALU.add)
                    k.act(g_[:, 3:4], g_[:, 2:3], AF.Exp, scale=-1.0)
                    k.act(g_[:, 4:5], g_[:, 3:4], AF.Ln, bias=col(CC_ONE))
                    k.ts('dve', g_[:, 1:2], g_[:, 4:5], -1.0)
                    k.ts('dve', g_[:, 5:6], g_[:, 1:2], col(CC_IND0))
                    k.ts('dve', g_[:, 6:7], g_[:, 1:2], col(CC_IND1))
                    pc2 = gbank()
                    k.mm(pc2[:, 0:1], cmF(CM_TRI64), g_[:, 1:2])
                    k.mm(pc2[:, 1:2], cmF(CM_BLK64), g_[:, 1:2], skip=True)
                    k.mm(pc2[:, 2:4], cmF(CM_ONES), g_[:, 5:7], skip=True)
                    k.cp('dve', g_[:, 7:11], pc2[:, 0:4])
                    k.tt('dve', g_[:, 11:12], g_[:, 0:1], g_[:, 7:8], ALU.subtract)
                    k.tt('dve', g_[:, 12:13], g_[:, 11:12], g_[:, 8:9], ALU.add)
                    k.act(g_[:, 13:14], g_[:, 11:12], AF.Exp)
                    k.act(g_[:, 14:15], g_[:, 12:13], AF.Exp)
                    k.act(g_[:, 15:16], g_[:, 7:8], AF.Exp)
                    k.act(sm[:, 48 + 2 * tt_:50 + 2 * tt_], g_[:, 9:11], AF.Exp)

                _lvl(11)
                for tt_ in range(4):
                    tcs = slice(tt_ * 128, (tt_ + 1) * 128)
                    po = gbank()
                    for mt in range(2):
                        k.mm(po[:, 0:129], PT[:, mt, tcs], vmA[:, mt, :], start=(mt == 0), stop=(mt == 1))
                    k.recip(gcol[:, tt_, 5:6], po[:, 128:129])
                    k.stt(yTM[:, tt_, 3, :], po[:, 0:128], gcol[:, tt_, 5:6], gates[:, tt_, 3, :], ALU.mult, ALU.mult)

                _lvl(12)
                for tt_ in range(4):
                    tcs = slice(tt_ * 128, (tt_ + 1) * 128)
                    ptr = gbank()
                    ptb = ptr[:, 0:64].bitcast(BF16)
                    k.tr(ptb, ktT[:, tcs], cmB(CM_IDENT))
                    k.cp('dve', kTM[:, tt_, :], ptb)
                _lvl(12.1)
                for tt_ in range(4):
                    pd = gbank()
                    for hh in range(2):
                        c8 = tt_ * 2 + hh
                        rs = slice(hh * 64, hh * 64 + 64)
                        k.mm(pd[:, hh * 128:(hh + 1) * 128], kTM[rs, tt_, :], av[rs, tt_, :], skip=(hh == 1))
                        k.ts('dve', dS[:, c8, :], pd[:, hh * 128:(hh + 1) * 128], CFAC[:, c8:c8 + 1])
                _lvl(12.2)
                for c8 in range(8):
                    k.ts('dve', Sb_h[:, c8, :], S_h[:], EREF[:, c8:c8 + 1])
                    k.stt(S_h[:], S_h[:], ELAST[:, c8:c8 + 1], dS[:, c8, :], ALU.mult, ALU.add)
                _lvl(12.3)
                for tt_ in range(4):
                    tcs = slice(tt_ * 128, (tt_ + 1) * 128)
                    pa = gbank()
                    for hh in range(2):
                        cs_ = slice(tt_ * 128 + hh * 64, tt_ * 128 + hh * 64 + 64)
                        rs = slice(hh * 64, hh * 64 + 64)
                        k.mm(pa[rs, hh * 64:hh * 64 + 64], ktT[:, cs_], qtT[:, cs_], skip=(hh == 1))
                        k.stt(AT[rs, 0, hh * 64:hh * 64 + 64], pa[rs, hh * 64:hh * 64 + 64], 3.0e38,
                              cmB(CM_TRI64)[rs, hh * 64:hh * 64 + 64], ALU.min, ALU.mult)
                    _lvl(12.4)
                    po = gbank()
                    for hh in range(2):
                        cs_ = slice(tt_ * 128 + hh * 64, tt_ * 128 + hh * 64 + 64)
                        rs = slice(hh * 64, hh * 64 + 64)
                        k.mm(po[rs, 0:128], AT[rs, 0, hh * 64:hh * 64 + 64], av[rs, tt_, :], start=True, stop=False, skip=True)
                        k.mm(po[rs, 0:128], qtT[:, cs_], Sb_h[:, tt_ * 2 + hh, :], start=False, stop=True, skip=True)
                    g_ = gcol[:, tt_, :]
                    k.act(T[0][:, 0:128], po[:, 0:128], AF.Square, accum_out=g_[:, 5:6])
                    k.rsqrt(g_[:, 6:7], g_[:, 5:6], 1.0 / 128, epsc, g_[:, 5:6])
                    k.stt(T[0][:, 0:128], po[:, 0:128], g_[:, 6:7], crow[:, CR_HG * 128:(CR_HG + 1) * 128], ALU.mult, ALU.mult)
                    k.tt('dve', yTM[:, tt_, 0, :], T[0][:, 0:128], gates[:, tt_, 0, :], ALU.mult)

                _lvl(13)
                for tt_ in range(4):
                    tcs = slice(tt_ * 128, (tt_ + 1) * 128)
                    ptr = gbank()
                    ptb = ptr[:, 0:64].bitcast(BF16)
                    k.tr(ptb, kcT[:, tcs], cmB(CM_IDENT))
                    k.ts('dve', kTM2[:, tt_, :], ptb, gcol[:, tt_, 14:15])
                for tt_ in range(4):
                    for hh in range(2):
                        c8 = tt_ * 2 + hh
                        rs = slice(hh * 64, hh * 64 + 64)
                        pd = gbank()
                        k.mm(pd[:, 0:129], kTM2[rs, tt_, :], cvA[rs, tt_, :])
                        k.cp('dve', dC[:, c8, :], pd[:, 0:129])
                for c8 in range(8):
                    k.cp('dve', Cb_m[:, c8, :], C_m[:])
                    k.stt(C_m[:], C_m[:], sm[:, 48 + c8:49 + c8], dC[:, c8, :], ALU.mult, ALU.add)
                for tt_ in range(4):
                    tcs = slice(tt_ * 128, (tt_ + 1) * 128)
                    g_ = gcol[:, tt_, :]
                    pa = gbank()
                    k.mm(pa[:, 0:128], kcT[:, tcs], qcT[:, tcs])
                    k.stt(AT[:, 1, :], pa[:, 0:128], g_[:, 13:14], cmB(CM_TRI64), ALU.mult, ALU.mult)
                    po = gbank()
                    for hh in range(2):
                        cs_ = slice(tt_ * 128 + hh * 64, tt_ * 128 + hh * 64 + 64)
                        rs = slice(hh * 64, hh * 64 + 64)
                        k.mm(po[rs, 0:129], AT[rs, 1, hh * 64:hh * 64 + 64], cvA[rs, tt_, :], start=True, stop=False, skip=True)
                        k.mm(po[rs, 0:129], qcT[:, cs_], Cb_m[:, tt_ * 2 + hh, :], start=False, stop=True, skip=True)
                    k.act(g_[:, 5:6], po[:, 128:129], AF.Abs, scale=g_[:, 15:16])
                    k.ts('dve', g_[:, 5:6], g_[:, 5:6], 1.0, None, op0=ALU.max)
                    k.recip(g_[:, 5:6], g_[:, 5:6])
                    k.tt('dve', g_[:, 5:6], g_[:, 5:6], g_[:, 15:16], ALU.mult)
                    k.stt(T[0][:, 0:128], po[:, 0:128], g_[:, 5:6], gates[:, tt_, 4, :], ALU.mult, ALU.mult)
                    k.act(T[1][:, 0:128], T[0][:, 0:128], AF.Square, accum_out=g_[:, 6:7])
                    k.rsqrt(g_[:, 5:6], g_[:, 6:7], 1.0 / 128, epsc, g_[:, 6:7])
                    k.stt(T[0][:, 0:128], T[0][:, 0:128], g_[:, 5:6], crow[:, CR_MG * 128:(CR_MG + 1) * 128], ALU.mult, ALU.mult)
                    k.tt('dve', yTM[:, tt_, 2, :], T[0][:, 0:128], gates[:, tt_, 2, :], ALU.mult)

                _lvl(14)
                if bi + 1 < NBLK:
                    k.dma(xt2, xv[:, 4:8, t0 + TB:t0 + 2 * TB], "x1")
                for ob in range(3):
                    k.mm(PS_O[ob][:], cmB(CM_ZERO), zrow[:], start=True, stop=False, sig=True, skip=True)
                oslot = {}
                for mp in range(2):
                    for qq in range(4):
                        idx = mp * 4 + qq
                        oslot[(mp, qq)] = (PS_O[idx // 3], (idx % 3) * 129)
                nkt = bi * 4 + 4
                SBK = [[PS_S[0], PSB[0]], [PS_S[1], PSB[1]]]

                def s_pair(j_):
                    q0_ = max(j_ - bi * 4, 0) * 128
                    for mp_ in range(2):
                        rs_ = slice(mp_ * 64, mp_ * 64 + 64)
                        k.mm(SBK[mp_][j_ % 2][:, q0_:TB], kT[rs_, j_ * 128:(j_ + 1) * 128], qhT[rs_, q0_:TB])

                s_pair(0)
                if nkt > 1:
                    s_pair(1)
                for j in range(nkt):
                    jj = j - bi * 4
                    q0 = max(jj, 0) * 128
                    for mp in range(2):
                        pt_ = PT[:, 2 * (j % 2) + mp, :]
                        k.act(pt_[:, q0:TB], SBK[mp][j % 2][:, q0:TB], AF.Exp)
                        if jj >= 0:
                            k.tt('pool', pt_[:, q0:q0 + 128], pt_[:, q0:q0 + 128], cmB(CM_TRI128), ALU.mult)
                    for mp in range(2):
                        pt_ = PT[:, 2 * (j % 2) + mp, :]
                        for qq in range(max(jj, 0), 4):
                            ob, oc = oslot[(mp, qq)]
                            last = (j == min(nkt - 1, bi * 4 + qq))
                            k.mm(ob[:, oc:oc + 129], pt_[:, qq * 128:(qq + 1) * 128], vA[:, j, :],
                                 start=False, stop=last, sig=last, skip=True)
                    if j + 2 < nkt:
                        s_pair(j + 2)
                for tt_ in range(4):
                    g_ = gcol[:, tt_, :]
                    ob1, oc1 = oslot[(0, tt_)]
                    ob2, oc2 = oslot[(1, tt_)]
                    k.recip(g_[:, 5:6], ob1[:, oc1 + 128:oc1 + 129])
                    k.recip(g_[:, 6:7], ob2[:, oc2 + 128:oc2 + 129])
                    k.tt('dve', g_[:, 6:7], g_[:, 6:7], NLAM, ALU.mult)
                    k.ts('dve', T[0][:, 0:128], ob1[:, oc1:oc1 + 128], g_[:, 5:6])
                    k.stt(T[0][:, 0:128], ob2[:, oc2:oc2 + 128], g_[:, 6:7], T[0][:, 0:128], ALU.mult, ALU.add)
                    k.act(T[1][:, 0:128], T[0][:, 0:128], AF.Square, accum_out=g_[:, 5:6])
                    k.rsqrt(g_[:, 6:7], g_[:, 5:6], 1.0 / 128, epsc, g_[:, 5:6])
                    k.ts('dve', g_[:, 6:7], g_[:, 6:7], 1.0 - lam_init)
                    k.stt(T[0][:, 0:128], T[0][:, 0:128], g_[:, 6:7], crow[:, CR_SUB * 128:(CR_SUB + 1) * 128], ALU.mult, ALU.mult)
                    k.tt('dve', yTM[:, tt_, 1, :], T[0][:, 0:128], gates[:, tt_, 1, :], ALU.mult)

                _lvl(15)
                for tt_ in range(4):
                    pty = gbank()
                    ptb = pty[:, 0:256].bitcast(BF16)
                    for g4 in range(4):
                        k.tr(ptb[:, g4 * 128:(g4 + 1) * 128], yTM[:, tt_, g4, :], cmB(CM_IDENT))
                    k.cp('dve', yT[:, :, tt_ * 128:(tt_ + 1) * 128], ptb.rearrange("p (g t) -> p g t", g=4))
                if not fused:
                    k.dma(yv[:, :, tsl], yT[:], "yu", eng='pool')
                else:
                    ch, half = bi // 2, bi % 2
                    for r in range(4):
                        k.act(yTr[:, r % 2, :, :], yT[:], AF.Copy, scale=col(CC_RANK + r))
                        k.dma(D['yin'][ch, r].rearrange("(g p) t -> p g t", p=128)[:, :, half * TB:(half + 1) * TB],
                              yTr[:, r % 2, :, :], "ys%d" % r, eng='pool')
                    if half == 1 or bi == NBLK - 1:
                        k.cc("AllReduce", ALU.add, [[0, 1, 2, 3], [4, 5, 6, 7]],
                             D['yin'][ch].rearrange("r p t -> (r p) t"), D['yout'][ch], "cc")

        except _Stop:
            pass
        k.s.barrier()


def layer_dram(nc, S, sfx, shared=None):
    dr = lambda name, shape, dt=F32: nc.dram_tensor(name, shape, dt, kind="ExternalInput").ap()
    D = {}
    if shared is None:
        D['pos'] = dr("pos", [1, S], I32)
        D['memT'] = dr("memT", [D_MODEL, N_MEM])
        D['cmat'] = dr("cmat", [128, NCM, 128])
        D['rmask'] = dr("rmask", [128, TB])
        D['cs'] = nc.dram_tensor("cs_scr", [2, 128, S], F32, kind="Internal").ap()
    else:
        for n_ in ('pos', 'memT', 'cmat', 'rmask', 'cs'):
            D[n_] = shared[n_]
    D['w_fm'] = dr("w_fm" + sfx, [D_MODEL, NFM * 128])
    D['w_tm'] = dr("w_tm" + sfx, [D_MODEL, NTM])
    D['w_kv'] = dr("w_kv" + sfx, [D_MODEL, 256])
    D['ccol'] = dr("ccol" + sfx, [128, NCC])
    D['crow'] = dr("crow" + sfx, [1, 3 * 128])
    D['lam'] = dr("lam" + sfx, [1, 256])
    return D


def build_layer(S, layer_idx):
    nc = bass.Bass("TRN2", target_bir_lowering=False)
    D = layer_dram(nc, S, "")
    D['xT'] = nc.dram_tensor("xT", [D_MODEL, S], F32, kind="ExternalInput").ap()
    D['yT'] = nc.dram_tensor("yT", [4 * 128, S], BF16, kind="ExternalOutput").ap()
    with ExitStack() as es:
        es.enter_context(nc.allow_low_precision(reason="bf16 matmul operands"))
        k = K(nc, es)
        PSB = [k.ps(f"pb{i}", [128, 512]) for i in range(8)]
        emit_layer(nc, k, PSB, S, layer_idx, "", D, do_rope=True, fused=False)
        k.s.finish()
    return nc


def emit_outproj(nc, k, PSB, S, sfx, D, full):
    NBLK = S // TB
    with ExitStack() as es:
        k.es = es
        sb = lambda name, shape, dt: k.sb(name + sfx, shape, dt)
        stage = sb("ostage", [128, D_MODEL], F32)
        wo = sb("wo", [128, 16, 256], BF16)
        yt = sb("yt", [128, 16, TB], BF16)
        xo = sb("xo", [128, 2, TB], F32)
        oo = sb("oo", [128, 2, TB], F32)
        wv = D['w_own'].rearrange("(c p) n -> p c n", p=128)
        for c in range(16):
            k.dma(stage[:, 0:256], wv[:, c, :], "st")
            k.cp('dve', wo[:, c, :], stage[:, 0:256])
        if full:
            w = sb("wfull", [128, 16, D_MODEL], BF16)
            xt = sb("oxt", [128, 8, TB], F32)
            ot = sb("oot", [128, 8, TB], F32)
            wv = D['w_out'].rearrange("(c p) n -> p c n", p=128)
            for c in range(16):
                k.dma(stage[:], wv[:, c, :], "st")
                k.cp('dve', w[:, c, :], stage[:])
            xv = D['x_prev'].rearrange("(c p) t -> p c t", p=128)
            x1v = D['x1'].rearrange("(c p) t -> p c t", p=128)
        xov = D['x_own_prev'].rearrange("(c p) t -> p c t", p=128)
        outv = (D['x1own'] if full else D['out']).rearrange("(c p) t -> p c t", p=128)
        pi = [0]
        for bi in range(NBLK):
            tsl = slice(bi * TB, (bi + 1) * TB)
            ch, half = bi // 2, bi % 2
            k.dma(yt[:], D['yout'][ch].rearrange("(c p) t -> p c t", p=128)[:, :, half * TB:(half + 1) * TB], "x0")
            k.dma(xo[:], xov[:, :, tsl], "x1")
            if full:
                k.dma(xt[:], xv[:, :, tsl], "x2")
                for oc in range(8):
                    p = PSB[pi[0] % 8]
                    pi[0] += 1
                    for c in range(16):
                        k.mm(p[:], w[:, c, oc * 128:(oc + 1) * 128], yt[:, c, :], start=(c == 0), stop=(c == 15),
                             sig=(c == 15))
                    k.tt('dve', ot[:, oc, :], p[:], xt[:, oc, :], ALU.add)
                k.dma(x1v[:, :, tsl], ot[:], "y0")
            for oc in range(2):
                p = PSB[pi[0] % 8]
                pi[0] += 1
                for c in range(16):
                    k.mm(p[:], wo[:, c, oc * 128:(oc + 1) * 128], yt[:, c, :], start=(c == 0), stop=(c == 15),
                         sig=(c == 15))
                k.tt('dve', oo[:, oc, :], p[:], xo[:, oc, :], ALU.add)
            k.dma(outv[:, :, tsl], oo[:], "y1")
        k.s.barrier()


def build_fused(S):
    nc = bass.Bass("TRN2", target_bir_lowering=False)
    NCH = max(S // 1024, 1)
    D0 = layer_dram(nc, S, "_0")
    D1 = layer_dram(nc, S, "_1", shared=D0)
    x0 = nc.dram_tensor("xT", [D_MODEL, S], F32, kind="ExternalInput").ap()
    x0own = nc.dram_tensor("x0own", [256, S], F32, kind="ExternalInput").ap()
    w_out0 = nc.dram_tensor("w_out0", [2048, D_MODEL], F32, kind="ExternalInput").ap()
    w_own0 = nc.dram_tensor("w_own0", [2048, 256], F32, kind="ExternalInput").ap()
    w_own1 = nc.dram_tensor("w_own1", [2048, 256], F32, kind="ExternalInput").ap()
    out = nc.dram_tensor("out", [256, S], F32, kind="ExternalOutput").ap()
    yin = nc.dram_tensor("yin", [NCH, 4, 512, min(S, 1024)], BF16).ap()
    yout = nc.dram_tensor("yout", [NCH, 2048, min(S, 1024)], BF16).ap()
    x1 = nc.dram_tensor("x1", [D_MODEL, S], F32).ap()
    x1own = nc.dram_tensor("x1own", [256, S], F32).ap()
    for D in (D0, D1):
        D['yin'], D['yout'] = yin, yout
    D0['xT'] = x0
    D1['xT'] = x1
    with ExitStack() as es:
        es.enter_context(nc.allow_low_precision(reason="bf16 matmul operands"))
        k = K(nc, es)
        PSB = [k.ps(f"pb{i}", [128, 512]) for i in range(8)]
        emit_layer(nc, k, PSB, S, 0, "_a", D0, do_rope=True, fused=True)
        emit_outproj(nc, k, PSB, S, "_p", {'w_own': w_own0, 'w_out': w_out0, 'x_prev': x0, 'x1': x1,
                                           'x_own_prev': x0own, 'x1own': x1own, 'yout': yout}, full=True)
        emit_layer(nc, k, PSB, S, 1, "_b", D1, do_rope=False, fused=True)
        emit_outproj(nc, k, PSB, S, "_q", {'w_own': w_own1, 'x_own_prev': x1own, 'out': out, 'yout': yout}, full=False)
        k.s.finish()
    return nc


def head_cols(h):
    G = GROUP
    base = np.arange(128)
    off = {}
    names = ['a_q', 'a_f', 'a_i', 'a_g', 'b_q', 'b_k', 'b_v', 'b_g', 'c_q', 'c_k', 'c_v', 'c_o']
    o = 0
    for n in names:
        off[n] = o
        o += G
    off['c_i'] = o
    o += 4
    off['c_f'] = o
    o += 4
    off['c_g'] = o
    o += G
    off['x_q'] = o
    o += G
    off['x_g'] = o
    o += G
    hc = lambda n: off[n] + h * 128 + base
    fm = np.concatenate([hc('a_q'), hc('a_f'), hc('b_q'), hc('b_k'), hc('c_q'), hc('c_k'), hc('x_q')])
    tm = np.concatenate([hc('a_g'), hc('b_g'), hc('c_g'), hc('x_g'), hc('c_o'), hc('a_i'), hc('b_v'), hc('c_v'),
                         np.array([off['c_i'] + h, off['c_f'] + h])])
    return fm, tm


def layer_inputs(l, b, h, xT_b, inp, cmat, rmask):
    fm, tm = head_cols(h)
    w_in = inp['w_in'][l]
    cc = np.zeros((128, NCC), np.float32)
    cc[:, CC_NG:CC_NG + 8] = inp['norm_g'][l].reshape(8, 128).T
    cc[:, CC_MNG:CC_MNG + 8] = inp['mem_norm_g'][l].reshape(8, 128).T
    invf = (ROPE_THETA ** (-np.arange(0, 16, 2, dtype=np.float32) / 16)).astype(np.float32)
    for base in (0, 64):
        cc[base:base + 8, CC_INVF] = invf
        cc[base + 8:base + 16, CC_INVF] = invf
        cc[base:base + 8, CC_SGN] = -1.0
        cc[base + 8:base + 16, CC_SGN] = 1.0
    cc[:, CC_GQ] = np.tile(inp['diff_qk_norm_g'][l, 0], 2)
    cc[:, CC_GK] = np.tile(inp['diff_qk_norm_g'][l, 1], 2)
    cw = inp['mlstm_conv_w'][l]
    cb = inp['mlstm_conv_b'][l]
    cc[:, CC_CWQ:CC_CWQ + 4] = cw[:, h * 128:(h + 1) * 128].T
    cc[:, CC_CBQ] = cb[h * 128:(h + 1) * 128]
    cc[:, CC_CWK:CC_CWK + 4] = cw[:, GROUP + h * 128:GROUP + (h + 1) * 128].T
    cc[:, CC_CBK] = cb[GROUP + h * 128:GROUP + (h + 1) * 128]
    cc[:, CC_GXQ] = inp['xattn_qk_norm_g'][l, 0]
    cc[:, CC_GXK] = inp['xattn_qk_norm_g'][l, 1]
    cc[:, CC_LBL:CC_LBL + 2] = inp['hgrn_lb_logits'][:, h * 128:(h + 1) * 128].T
    cc[0:64, CC_IND0] = 1.0
    cc[64:128, CC_IND1] = 1.0
    cc[:, CC_GBI] = inp['mlstm_gate_b'][l, h]
    cc[:, CC_GBF] = inp['mlstm_gate_b'][l, 4 + h]
    cc[:, CC_EPS] = EPS
    cc[:, CC_ONE] = 1.0
    cc[:, CC_RANK + h] = 1.0
    crow = np.concatenate([inp['hgrn_norm_g'][l], inp['diff_subln_g'][l], inp['mlstm_norm_g'][l]])[None, :]
    wkv = inp['w_mem_kv'][l]
    wkv_h = np.concatenate([wkv[:, h * 128:(h + 1) * 128], wkv[:, GROUP + h * 128:GROUP + (h + 1) * 128]], axis=1)
    return {
        "xT": xT_b,
        "pos": np.ascontiguousarray(inp['positions'][b][None, :xT_b.shape[1]]).astype(np.int32),
        "w_fm": np.ascontiguousarray(w_in[:, fm]),
        "w_tm": np.ascontiguousarray(w_in[:, tm]),
        "memT": np.ascontiguousarray(inp['mem'][b].T),
        "w_kv": np.ascontiguousarray(wkv_h),
        "cmat": cmat,
        "ccol": cc,
        "crow": np.ascontiguousarray(crow.astype(np.float32)),
        "lam": np.ascontiguousarray(inp['diff_lambda'][l].reshape(1, 256)),
        "rmask": rmask,
    }


_CACHE = {}


def run_layer(l, xT, inp, S):
    key = ('L', l, S)
    if key not in _CACHE:
        _CACHE[key] = build_layer(S, l)
    nc = _CACHE[key]
    cmat = const_mats()
    rmask = np.ones((128, TB), np.float32)
    rmask[:, ::64] = 0.0
    in_maps = []
    for c in range(8):
        b, h = c // 4, c % 4
        in_maps.append(layer_inputs(l, b, h, np.ascontiguousarray(xT[b]), inp, cmat, rmask))
    res = run_bass_kernel_spmd(nc, in_maps, core_ids=list(range(8)))
    ys = [np.asarray(r["yT"]) for r in res.results]
    return np.stack([np.concatenate(ys[b * 4:(b + 1) * 4], axis=0) for b in range(BATCH)])


def perm_w_out(w):
    return np.ascontiguousarray(w.reshape(4, 4, 128, D_MODEL).transpose(1, 0, 2, 3).reshape(2048, D_MODEL))


def run_fused(xT, inp, S):
    key = ('F', S)
    if key not in _CACHE:
        _CACHE[key] = build_fused(S)
    nc = _CACHE[key]
    cmat = const_mats()
    rmask = np.ones((128, TB), np.float32)
    rmask[:, ::64] = 0.0
    wp = [perm_w_out(inp['w_out'][l]) for l in range(DEPTH)]
    in_maps = []
    for c in range(8):
        b, h = c // 4, c % 4
        xb_ = np.ascontiguousarray(xT[b])
        m = {}
        for l in range(DEPTH):
            li = layer_inputs(l, b, h, xb_, inp, cmat, rmask)
            for n_ in ('w_fm', 'w_tm', 'w_kv', 'ccol', 'crow', 'lam'):
                m[n_ + "_%d" % l] = li[n_]
            if l == 0:
                for n_ in ('xT', 'pos', 'memT', 'cmat', 'rmask'):
                    m[n_] = li[n_]
        fs = slice(h * 256, (h + 1) * 256)
        m['x0own'] = np.ascontiguousarray(xb_[fs])
        m['w_out0'] = wp[0]
        m['w_own0'] = np.ascontiguousarray(wp[0][:, fs])
        m['w_own1'] = np.ascontiguousarray(wp[1][:, fs])
        in_maps.append(m)
    res = run_bass_kernel_spmd(nc, in_maps, core_ids=list(range(8)))
    outs = [np.asarray(r["out"]) for r in res.results]
    return np.stack([np.concatenate(outs[b * 4:(b + 1) * 4], axis=0) for b in range(BATCH)])


def kernel(**inputs):
    inp = {k_: np.asarray(v) for k_, v in inputs.items()}
    x = inp['x']
    S = x.shape[1]
    xT = np.ascontiguousarray(x.transpose(0, 2, 1))
    oT = run_fused(xT, inp, S)
    return np.ascontiguousarray(oT.transpose(0, 2, 1)).astype(np.float32)
```

```python
import math
import os
from contextlib import ExitStack

import numpy as np
import concourse.bass as bass
import concourse.mybir as mybir
from concourse.bass_utils import run_bass_kernel_spmd

F32 = mybir.dt.float32
BF16 = mybir.dt.bfloat16
I32 = mybir.dt.int32
AF = mybir.ActivationFunctionType
ALU = mybir.AluOpType

D_MODEL = 1024
BATCH = 2
SEQ = 16384
DEPTH = 2
N_MEM = 256
GROUP = 512
EPS = 1e-6
ROPE_THETA = 500000.0
TB = 512
NFM = 7
NTM = 1026


class Sched:
    SAME = ('act', 'dve', 'pool')

    def __init__(self, nc, es):
        self.nc = nc
        self.engs = {'pe': nc.tensor, 'act': nc.scalar, 'dve': nc.vector, 'pool': nc.gpsimd, 'sp': nc.sync}
        self.sem = {e: es.enter_context(nc.semaphore('s_' + e)) for e in ('pe', 'act', 'dve', 'pool')}
        self.cnt = {e: 0 for e in self.sem}
        self.pending = {e: [] for e in self.sem}
        self.csem = {}
        self.ccnt = {}
        self.es = es
        self.waited = {}
        self.acc = {}
        self.ops = []
        self.nwaits = 0

    @staticmethod
    def box(ap):
        dims = ap.ap
        esz = mybir.dt.size(ap.dtype)
        off = ap.offset
        sp = str(ap.space)
        if sp == 'DRAM':
            ext = 1
            for s, c in dims:
                ext += (c - 1) * abs(s)
            return (ap.tensor.name, 0, 1, off * esz, (off + ext) * esz)
        pstep = dims[0][0]
        if pstep == 0:
            p0, f0 = 0, off
        else:
            p0, f0 = off // pstep, off % pstep
        ext = 1
        for s, c in dims[1:]:
            ext += (c - 1) * abs(s)
        if sp == 'PSUM':
            return (ap.tensor.name, (p0 // 32) * 32, ((p0 + dims[0][1] + 31) // 32) * 32, 0, 2048)
        return (ap.tensor.name, p0, p0 + dims[0][1], f0 * esz, (f0 + ext) * esz)

    def _sem_of(self, key):
        return self.sem[key] if key in self.sem else self.csem[key]

    def emit(self, eng, fn, outs=(), ins=(), sig=True, chan=None, inc=16):
        is_dma = chan is not None
        oboxes = [self.box(a) for a in outs]
        iboxes = [self.box(a) for a in ins]
        deps = set()
        for (name, p0, p1, f0, f1) in iboxes:
            for ent in self.acc.get(name, ()):
                if ent[5] and ent[0] < p1 and p0 < ent[1] and ent[2] < f1 and f0 < ent[3]:
                    deps.add(ent[4])
        for (name, p0, p1, f0, f1) in oboxes:
            for ent in self.acc.get(name, ()):
                if ent[0] < p1 and p0 < ent[1] and ent[2] < f1 and f0 < ent[3]:
                    deps.add(ent[4])
        need = {}
        for d in deps:
            deng, key, val = self.ops[d]
            if deng is not None and deng == eng and eng not in self.SAME:
                continue
            if val is None:
                raise RuntimeError('dependency on unsignalled op')
            if need.get(key, 0) < val:
                need[key] = val
        if is_dma:
            if chan not in self.csem:
                self.csem[chan] = self.es.enter_context(self.nc.semaphore('c_' + chan))
                self.ccnt[chan] = 0
            if self.ccnt[chan] > 0 and need.get(chan, 0) < self.ccnt[chan]:
                need[chan] = self.ccnt[chan]
        e = self.engs[eng]
        for key, val in need.items():
            if self.waited.get((eng, key), 0) >= val:
                continue
            e.wait_ge(self._sem_of(key), val)
            self.waited[(eng, key)] = val
            self.nwaits += 1
        inst = fn(e)
        opid = len(self.ops)
        if is_dma:
            self.ccnt[chan] += inc
            inst.then_inc(self.csem[chan], inc)
            self.ops.append([None, chan, self.ccnt[chan]])
        elif sig:
            self.cnt[eng] += 1
            inst.then_inc(self.sem[eng], 1)
            self.ops.append([eng, eng, self.cnt[eng]])
            for p in self.pending[eng]:
                self.ops[p][2] = self.cnt[eng]
            self.pending[eng] = []
        else:
            self.ops.append([eng, eng, None])
            self.pending[eng].append(opid)
        for (name, p0, p1, f0, f1) in oboxes:
            lst = self.acc.setdefault(name, [])
            lst[:] = [t for t in lst if not (p0 <= t[0] and t[1] <= p1 and f0 <= t[2] and t[3] <= f1)]
            lst.append((p0, p1, f0, f1, opid, True))
        rkey = None if is_dma else eng
        for (name, p0, p1, f0, f1) in iboxes:
            lst = self.acc.setdefault(name, [])
            for i, t in enumerate(lst):
                if (not t[5]) and t[0] == p0 and t[1] == p1 and t[2] == f0 and t[3] == f1 \
                        and self.ops[t[4]][0] == rkey and rkey is not None:
                    lst[i] = (p0, p1, f0, f1, opid, False)
                    break
            else:
                lst.append((p0, p1, f0, f1, opid, False))
        return inst

    def barrier(self):
        keys = [(key, self.cnt[key]) for key in self.sem] + [(key, self.ccnt[key]) for key in self.csem]
        for eng, e in self.engs.items():
            for key, val in keys:
                if val > 0 and self.waited.get((eng, key), 0) < val:
                    e.wait_ge(self._sem_of(key), val)
                    self.waited[(eng, key)] = val

    def finish(self):
        sp = self.engs['sp']
        for key in list(self.csem):
            if self.ccnt[key] > 0:
                sp.wait_ge(self.csem[key], self.ccnt[key])
        for key in self.sem:
            if self.cnt[key] > 0:
                sp.wait_ge(self.sem[key], self.cnt[key])


class _Stop(Exception):
    pass


def _lvl(n):
    if n > float(os.environ.get('KLVL', '99')):
        raise _Stop()


class K:
    def __init__(self, nc, es):
        self.nc = nc
        self.es = es
        self.s = Sched(nc, es)
        self.n = 0

    def sb(self, name, shape, dt):
        return self.es.enter_context(self.nc.sbuf_tensor(name, shape, dt))

    def ps(self, name, shape, dt=F32):
        return self.es.enter_context(self.nc.psum_tensor(name, shape, dt))

    def mm(self, out, lhsT, rhs, start=True, stop=True, sig=None, skip=False):
        if sig is None:
            sig = True
        return self.s.emit('pe', lambda e: e.matmul(out, lhsT, rhs, start=start, stop=stop,
                                                     skip_group_check=skip),
                           outs=[out], ins=[lhsT, rhs], sig=sig)

    def tr(self, out, in_, ident):
        return self.s.emit('pe', lambda e: e.transpose(out, in_, ident), outs=[out], ins=[in_, ident])

    def act(self, out, in_, func, bias=0.0, scale=1.0, accum_out=None):
        ins = [in_]
        outs = [out]
        if not isinstance(bias, (int, float)):
            ins.append(bias)
        if not isinstance(scale, (int, float)):
            ins.append(scale)
        if accum_out is not None:
            outs.append(accum_out)
        kw = {}
        if accum_out is not None:
            kw['accum_out'] = accum_out
        return self.s.emit('act', lambda e: e.activation(out, in_, func, bias=bias, scale=scale, **kw),
                           outs=outs, ins=ins)

    def ts(self, eng, out, in0, s1, s2=None, op0=ALU.mult, op1=None, accum_out=None):
        ins = [in0]
        outs = [out]
        for s_ in (s1, s2):
            if s_ is not None and not isinstance(s_, (int, float)):
                ins.append(s_)
        kw = {}
        if op1 is not None:
            kw['op1'] = op1
        if accum_out is not None:
            kw['accum_out'] = accum_out
            outs.append(accum_out)
        return self.s.emit(eng, lambda e: e.tensor_scalar(out, in0, s1, s2, op0, **kw), outs=outs, ins=ins)

    def tt(self, eng, out, in0, in1, op):
        return self.s.emit(eng, lambda e: e.tensor_tensor(out, in0, in1, op), outs=[out], ins=[in0, in1])

    def stt(self, out, in0, scalar, in1, op0, op1):
        ins = [in0, in1]
        if not isinstance(scalar, (int, float)):
            ins.append(scalar)
        return self.s.emit('dve', lambda e: e.scalar_tensor_tensor(out, in0, scalar, in1, op0, op1),
                           outs=[out], ins=ins)

    def cp(self, eng, out, in_):
        if eng == 'act':
            return self.s.emit('act', lambda e: e.copy(out, in_), outs=[out], ins=[in_])
        return self.s.emit(eng, lambda e: e.tensor_copy(out, in_), outs=[out], ins=[in_])

    def recip(self, out, in_):
        return self.s.emit('dve', lambda e: e.reciprocal(out, in_), outs=[out], ins=[in_])

    def memset(self, eng, out, val):
        return self.s.emit(eng, lambda e: e.memset(out, val), outs=[out], ins=[])

    def scan(self, out, d0, d1, init, op0, op1):
        return self.s.emit('dve', lambda e: e.tensor_tensor_scan(out, d0, d1, init, op0, op1),
                           outs=[out], ins=[d0, d1])

    def dma(self, out, in_, chan, eng='sp'):
        return self.s.emit(eng, lambda e: e.dma_start(out=out, in_=in_), outs=[out], ins=[in_], chan=chan)

    def cc(self, kind, op, groups, in_ap, out_ap, chan):
        return self.s.emit('pool', lambda e: e.collective_compute(kind, op, replica_groups=groups,
                                                                 ins=[in_ap.opt()], outs=[out_ap.opt()]),
                           outs=[out_ap], ins=[in_ap], chan=chan, inc=1)

    def rsqrt(self, out, in_, scale, epscol, tmp):
        self.act(tmp, in_, AF.Ln, bias=epscol, scale=scale)
        self.act(out, tmp, AF.Exp, scale=-0.5)


CM_IDENT, CM_TRI128, CM_TRI64, CM_BLK64, CM_ONES, CM_PROPE, CM_ZERO = range(7)
NCM = 7
(CC_NG, CC_MNG, CC_INVF, CC_SGN, CC_GQ, CC_GK, CC_CWQ, CC_CBQ, CC_CWK, CC_CBK, CC_GXQ, CC_GXK, CC_LBL,
 CC_IND0, CC_IND1, CC_GBI, CC_GBF, CC_EPS, CC_ONE, CC_HALFPI) = (0, 8, 16, 17, 18, 19, 20, 24, 25, 29, 30, 31,
                                                                 32, 34, 35, 36, 37, 38, 39, 40)
CC_RANK = 41
NCC = 45
CR_HG, CR_SUB, CR_MG = range(3)
TWO_PI = 2.0 * math.pi


def const_mats():
    m = np.zeros((NCM, 128, 128), np.float32)
    i = np.arange(128)
    m[CM_IDENT] = np.eye(128)
    m[CM_TRI128] = (i[None, :] >= i[:, None])
    same = (i[:, None] // 64) == (i[None, :] // 64)
    m[CM_TRI64] = m[CM_TRI128] * same
    m[CM_BLK64] = same
    m[CM_ONES] = 1.0
    for base in (0, 64):
        for j in range(8):
            m[CM_PROPE, base + 8 + j, base + j] = 1.0
            m[CM_PROPE, base + j, base + 8 + j] = 1.0
    return np.ascontiguousarray(m.transpose(1, 0, 2))


def emit_layer(nc, k, PSB, S, layer_idx, sfx, D, do_rope=True, fused=False):
    NBLK = S // TB
    NT = S // 128
    xT_d, pos_d, wfm_d, wtm_d = D['xT'], D['pos'], D['w_fm'], D['w_tm']
    memT_d, wkv_d, cmat_d, ccol_d, crow_d, lam_d, rm_d = (D['memT'], D['w_kv'], D['cmat'], D['ccol'], D['crow'],
                                                          D['lam'], D['rmask'])
    cs_d = D['cs']
    yT_d = D.get('yT')

    lam_init = 0.8 - 0.6 * math.exp(-0.3 * layer_idx)

    with ExitStack() as es:
        k.es = es
        sb = lambda name, shape, dt: k.sb(name + sfx, shape, dt)
        kT = sb("kT", [128, S], BF16)
        vA = sb("vA", [128, NT, 129], BF16)
        wfm = sb("wfm", [128, 8, NFM * 128], BF16)
        wtm = sb("wtm", [128, 8, NTM], BF16)
        cmf3 = sb("cmf3", [128, 3, 128], F32)
        cmb = sb("cmb", [128, NCM, 128], BF16)
        cc = sb("cc", [128, NCC], F32)
        crow = sb("crow_sb", [128, 3 * 128], F32)
        lamt = sb("lamt", [128, 256], F32)
        rmask = sb("rmask_sb", [128, TB], F32)
        zrow = sb("zrow", [128, TB], BF16)
        kmT = sb("kmT", [128, 256], BF16)
        vmA = sb("vmA", [128, 2, 129], BF16)
        S_h = sb("S_h", [128, 128], F32)
        Sb_h = sb("Sb_h", [128, 8, 128], BF16)
        dS = sb("dS", [128, 8, 128], F32)
        C_m = sb("C_m", [128, 129], F32)
        Cb_m = sb("Cb_m", [128, 8, 129], BF16)
        dC = sb("dC", [128, 8, 129], F32)
        sm = sb("sm", [128, 64], F32)
        stage = sb("stage", [128, NTM], F32)
        xt = sb("xt", [128, 4, TB], F32)
        xtm = xt[:].rearrange("p c t -> p (c t)").rearrange("p (c t) -> p c t", c=8)
        xb = sb("xb", [128, 8, TB], BF16)
        sqb = sb("sqb", [128, 2, TB], BF16)
        RX = sb("RX", [128, TB], F32)
        rxc = sb("rxc", [128, 8], F32)
        Tall = sb("Tall", [128, 6, TB], F32)
        T = [Tall[:, i, :] for i in range(6)]
        xt2 = Tall[:, 2:6, :]
        Bt = [sb(f"B{i}", [128, TB], BF16) for i in range(3)]
        cosT = sb("cosT", [128, TB], F32)
        sinT = sb("sinT", [128, TB], F32)
        qhT = sb("qhT", [128, TB], BF16)
        qtT = sb("qtT", [128, TB], BF16)
        ktT = sb("ktT", [128, TB], BF16)
        qcT = sb("qcT", [128, TB], BF16)
        kcT = sb("kcT", [128, TB], BF16)
        cqx = sb("cqx", [128, TB + 3], F32)
        ckx = sb("ckx", [128, TB + 3], F32)
        qxn = sb("qxn", [128, TB], BF16)
        bcum = sb("bcum", [128, TB], F32)
        av = sb("av", [128, 4, 128], BF16)
        cvA = sb("cvA", [128, 4, 129], BF16)
        gates = sb("gates", [128, 4, 5, 128], BF16)
        gcol = sb("gcol", [128, 4, 16], F32)
        kTM = sb("kTM", [128, 4, 128], BF16)
        kTM2 = sb("kTM2", [128, 4, 128], BF16)
        AT = sb("AT", [128, 2, 128], BF16)
        PT = sb("PT", [128, 4, TB], BF16)
        yTM = sb("yTM", [128, 4, 4, 128], BF16)
        yT = sb("yT_sb", [128, 4, TB], BF16)
        yTr = sb("yTr", [128, 2, 4, TB], BF16) if fused else None
        G = PSB[0:3]
        PS_S = PSB[3:5]
        PS_O = PSB[5:8]
        gi = [0]

        def gbank():
            b = G[gi[0] % 3]
            gi[0] += 1
            return b

        cmF = lambda i: cmf3[:, i - CM_TRI64, :]
        cmB = lambda i: cmb[:, i, :]
        col = lambda i, n=1: cc[:, i:i + n]
        epsc = col(CC_EPS)

        try:
            k.dma(stage[:, 0:NCM * 128], cmat_d.rearrange("p m n -> p (m n)"), "c0")
            k.dma(cmf3[:], cmat_d[:, CM_TRI64:CM_TRI64 + 3, :], "c5")
            k.dma(cc[:], ccol_d[:, :], "c1")
            k.dma(crow[:], crow_d[0:1, :].partition_broadcast(128), "c2")
            k.dma(lamt[:], lam_d[0:1, :].partition_broadcast(128), "c3")
            k.dma(rmask[:], rm_d[:, :], "c4")
            k.cp('dve', cmb[:].rearrange("p m n -> p (m n)"), stage[:, 0:NCM * 128])
            k.memset('pool', zrow[:], 0.0)
            k.memset('pool', vA[:, :, 128:129], 1.0)
            k.memset('pool', cvA[:, :, 128:129], 1.0)
            k.memset('pool', vmA[:, :, 128:129], 1.0)
            k.memset('pool', S_h[:], 0.0)
            k.memset('pool', C_m[:], 0.0)
            k.memset('pool', cqx[:, 0:3], 0.0)
            k.memset('pool', ckx[:, 0:3], 0.0)
            k.memset('pool', AT[:], 0.0)
            k.memset('dve', sm[:], 0.0)
            wv = wfm_d.rearrange("(c p) n -> p c n", p=128)
            for c in range(8):
                k.dma(stage[:, 0:NFM * 128], wv[:, c, :], "st")
                k.ts('dve', wfm[:, c, :], stage[:, 0:NFM * 128], col(CC_NG + c))
            wv = wtm_d.rearrange("(c p) n -> p c n", p=128)
            for c in range(8):
                k.dma(stage[:, 0:NTM], wv[:, c, :], "st")
                k.ts('dve', wtm[:, c, :], stage[:, 0:NTM], col(CC_NG + c))

            _lvl(2)
            LB, OMLB, NLAM, R1 = sm[:, 0:1], sm[:, 1:2], sm[:, 2:3], sm[:, 3:4]
            if layer_idx == 0:
                k.memset('dve', LB, 0.0)
            else:
                k.tt('dve', sm[:, 4:5], col(CC_LBL), col(CC_LBL + 1), ALU.subtract)
                k.act(sm[:, 5:6], sm[:, 4:5], AF.Exp)
                k.ts('dve', sm[:, 5:6], sm[:, 5:6], 1.0, None, op0=ALU.add)
                k.recip(LB, sm[:, 5:6])
            k.ts('dve', OMLB, LB, -1.0, 1.0, op0=ALU.mult, op1=ALU.add)
            k.tt('dve', T[0][:, 0:64], lamt[:, 0:64], lamt[:, 64:128], ALU.mult)
            k.tt('dve', T[0][:, 64:128], lamt[:, 128:192], lamt[:, 192:256], ALU.mult)
            k.s.emit('dve', lambda e: e.reduce_sum(sm[:, 6:7], T[0][:, 0:64], mybir.AxisListType.X),
                     outs=[sm[:, 6:7]], ins=[T[0][:, 0:64]])
            k.s.emit('dve', lambda e: e.reduce_sum(sm[:, 7:8], T[0][:, 64:128], mybir.AxisListType.X),
                     outs=[sm[:, 7:8]], ins=[T[0][:, 64:128]])
            k.act(sm[:, 8:10], sm[:, 6:8], AF.Exp)
            k.tt('dve', sm[:, 10:11], sm[:, 9:10], sm[:, 8:9], ALU.subtract)
            k.ts('dve', NLAM, sm[:, 10:11], -lam_init, None, op0=ALU.add)

            _lvl(3)
            mv = memT_d.rearrange("(c p) n -> p c n", p=128)
            k.dma(xtm, mv[:, :, :], "x0")
            wk = wkv_d.rearrange("(c p) n -> p c n", p=128)
            mw = xb[:, :, 256:512]
            for c in range(8):
                k.dma(stage[:, 0:256], wk[:, c, :], "st")
                k.ts('dve', mw[:, c, :], stage[:, 0:256], col(CC_MNG + c))
                k.cp('pool', xb[:, c, 0:256], xtm[:, c, :])
            for mt in range(2):
                pc = gbank()
                for c in range(8):
                    k.act(sqb[:, c % 2, 0:128], xtm[:, c, mt * 128:(mt + 1) * 128], AF.Square)
                    k.mm(pc[:, 0:1], sqb[:, c % 2, 0:128], cmB(CM_ONES)[:, 0:1], start=(c == 0), stop=(c == 7))
                k.rsqrt(gcol[:, 0, 0:1], pc[:, 0:1], 1.0 / D_MODEL, epsc, gcol[:, 0, 1:2])
                pk = gbank()
                for c in range(8):
                    k.mm(pk[:, 0:256], xb[:, c, mt * 128:(mt + 1) * 128], mw[:, c, :], start=(c == 0), stop=(c == 7))
                k.ts('dve', T[0][:, 0:128], pk[:, 0:128], gcol[:, 0, 0:1])
                k.ts('dve', vmA[:, mt, 0:128], pk[:, 128:256], gcol[:, 0, 0:1])
                k.act(T[1][:, 0:128], T[0][:, 0:128], AF.Square, accum_out=gcol[:, 0, 2:3])
                k.rsqrt(gcol[:, 0, 3:4], gcol[:, 0, 2:3], 1.0 / 128, epsc, gcol[:, 0, 4:5])
                k.ts('dve', Bt[0][:, 0:128], T[0][:, 0:128], gcol[:, 0, 3:4])
                pt_ = gbank()
                ptb = pt_[:, 0:64].bitcast(BF16)
                k.tr(ptb, Bt[0][:, 0:128], cmB(CM_IDENT))
                k.tt('dve', sm[:, 11:12], col(CC_GXQ), col(CC_GXK), ALU.mult)
                k.ts('dve', kmT[:, mt * 128:(mt + 1) * 128], ptb, sm[:, 11:12], 128 ** -0.5, op0=ALU.mult, op1=ALU.mult)

            _lvl(4)
            for bi in (range(NBLK) if do_rope else ()):
                tsl = slice(bi * TB, (bi + 1) * TB)
                pi_ = T[5].bitcast(I32)
                k.dma(pi_[:, :], pos_d[0:1, tsl].partition_broadcast(128), "x1")
                k.cp('dve', T[0][:], pi_[:, :])
                k.ts('dve', T[1][:], T[0][:], col(CC_INVF))
                for which, dst in ((0, cosT), (1, sinT)):
                    src = T[1]
                    if which == 0:
                        k.ts('dve', T[2][:], T[1][:], math.pi / 2, None, op0=ALU.add)
                        src = T[2]
                    k.ts('dve', T[3][:], src[:], 1.0 / TWO_PI, 12582912.0, op0=ALU.mult, op1=ALU.add)
                    k.ts('dve', T[3][:], T[3][:], -12582912.0, None, op0=ALU.add)
                    k.stt(T[4][:], T[3][:], -6.28125, src[:], ALU.mult, ALU.add)
                    k.stt(T[4][:], T[3][:], -(TWO_PI - 6.28125), T[4][:], ALU.mult, ALU.add)
                    k.ts('dve', T[4][:], T[4][:], 3.1415925, -3.1415925, op0=ALU.min, op1=ALU.max)
                    k.act(dst[:], T[4][:], AF.Sin)
                k.ts('dve', sinT[:], sinT[:], col(CC_SGN))
                k.dma(cs_d[0, :, tsl], cosT[:], "y0")
                k.dma(cs_d[1, :, tsl], sinT[:], "y1")

            xv = xT_d.rearrange("(c p) t -> p c t", p=128)
            yv = yT_d.rearrange("(g p) t -> p g t", p=128) if yT_d is not None else None

            for bi in range(NBLK):
                t0 = bi * TB
                tsl = slice(t0, t0 + TB)
                _lvl(5)
                if bi == 0:
                    k.dma(cosT[:], cs_d[0, :, tsl], "x2")
                    k.dma(sinT[:], cs_d[1, :, tsl], "x3")
                    k.dma(xt[:], xv[:, 0:4, tsl], "x0")
                prx = gbank()
                prc = gbank()
                if bi == 0:
                    k.dma(xt2, xv[:, 4:8, tsl], "x1")
                for c in range(8):
                    xsrc = xt[:, c, :] if c < 4 else xt2[:, c - 4, :]
                    k.cp('act', xb[:, c, :], xsrc)
                    k.tt('dve', sqb[:, c % 2, :], xsrc, xsrc, ALU.mult)
                    k.mm(prx[:], cmB(CM_ONES), sqb[:, c % 2, :], start=(c == 0), stop=(c == 7))
                    for tt_ in range(4):
                        k.mm(prc[:, tt_:tt_ + 1], sqb[:, c % 2, tt_ * 128:(tt_ + 1) * 128], cmB(CM_ONES)[:, 0:1],
                             start=(c == 0 and tt_ == 0), stop=(c == 7), skip=True)
                if bi + 1 < NBLK:
                    k.dma(xt[:], xv[:, 0:4, t0 + TB:t0 + 2 * TB], "x0")
                k.rsqrt(RX[:], prx[:], 1.0 / D_MODEL, epsc, T[0][:])
                k.rsqrt(rxc[:, 0:4], prc[:, 0:4], 1.0 / D_MODEL, epsc, rxc[:, 4:8])
                k.ts('dve', rxc[:, 4:8], rxc[:, 0:4], -1.0)

                def fm_proj(j):
                    p = gbank()
                    for c in range(8):
                        k.mm(p[:], wfm[:, c, j * 128:(j + 1) * 128], xb[:, c, :], start=(c == 0), stop=(c == 7))
                    return p

                _lvl(6)
                p = fm_proj(0)
                k.tt('dve', T[0][:], p[:], RX[:], ALU.mult)
                k.act(T[1][:], T[0][:], AF.Exp, scale=-1.0)
                k.ts('dve', T[1][:], T[1][:], 1.0, None, op0=ALU.add)
                k.recip(T[1][:], T[1][:])
                k.stt(T[0][:], T[0][:], 128 ** -0.5, T[1][:], ALU.mult, ALU.mult)
                p = fm_proj(1)
                k.tt('dve', T[1][:], p[:], RX[:], ALU.mult)
                k.act(T[2][:], T[1][:], AF.Exp, scale=-1.0)
                k.ts('dve', T[2][:], T[2][:], 1.0, None, op0=ALU.add)
                k.recip(T[2][:], T[2][:])
                k.ts('dve', T[2][:], T[2][:], OMLB, LB, op0=ALU.mult, op1=ALU.add)
                k.act(T[3][:], T[2][:], AF.Ln)
                k.ts('dve', T[2][:], T[2][:], -1.0, 1.0, op0=ALU.mult, op1=ALU.add)
                k.scan(bcum[:], rmask[:], T[3][:], 0.0, ALU.mult, ALU.add)
                for c8 in range(8):
                    cs_ = slice(c8 * 64, (c8 + 1) * 64)
                    k.ts('dve', T[3][:, cs_], bcum[:, cs_], bcum[:, c8 * 64 + 31:c8 * 64 + 32], None, op0=ALU.subtract)
                k.act(T[4][:], T[3][:], AF.Exp)
                k.act(T[5][:], T[3][:], AF.Exp, scale=-1.0)
                k.tt('dve', qtT[:], T[0][:], T[4][:], ALU.mult)
                k.tt('dve', ktT[:], T[2][:], T[5][:], ALU.mult)
                EREF, ELAST, CFAC = sm[:, 16:24], sm[:, 24:32], sm[:, 32:40]
                k.act(EREF, bcum[:, 31:TB:64], AF.Exp)
                k.act(ELAST, bcum[:, 63:TB:64], AF.Exp)
                k.tt('dve', sm[:, 40:48], bcum[:, 63:TB:64], bcum[:, 31:TB:64], ALU.subtract)
                k.act(CFAC, sm[:, 40:48], AF.Exp)

                _lvl(7)
                def qk_norm_rope(j, gcol_, scale_, dst):
                    p_ = fm_proj(j)
                    k.tt('dve', T[0][:], p_[:], RX[:], ALU.mult)
                    k.act(Bt[0][:], T[0][:], AF.Square)
                    pss = gbank()
                    k.mm(pss[:], cmB(CM_BLK64), Bt[0][:])
                    k.rsqrt(T[1][:], pss[:], 1.0 / 64, epsc, T[2][:])
                    k.ts('dve', Bt[1][:], T[0][:], gcol_)
                    pp = gbank()
                    k.mm(pp[:], cmB(CM_PROPE), Bt[1][:])
                    k.tt('dve', T[2][:], Bt[1][:], cosT[:], ALU.mult)
                    k.tt('dve', T[3][:], pp[:], sinT[:], ALU.mult)
                    k.tt('dve', T[2][:], T[2][:], T[3][:], ALU.add)
                    k.stt(dst, T[2][:], scale_, T[1][:], ALU.mult, ALU.mult)

                qk_norm_rope(2, col(CC_GQ), 0.125, qhT[:])
                qk_norm_rope(3, col(CC_GK), 1.0, kT[:, tsl])
                if bi + 1 < NBLK:
                    k.dma(cosT[:], cs_d[0, :, t0 + TB:t0 + 2 * TB], "x2")
                    k.dma(sinT[:], cs_d[1, :, t0 + TB:t0 + 2 * TB], "x3")

                _lvl(8)
                def conv_silu(j, ext, wcol, bcol, scale_, dst):
                    p_ = fm_proj(j)
                    k.tt('dve', ext[:, 3:TB + 3], p_[:], RX[:], ALU.mult)
                    k.ts('dve', T[0][:], ext[:, 0:TB], cc[:, wcol:wcol + 1], cc[:, bcol:bcol + 1], op0=ALU.mult, op1=ALU.add)
                    for jj in range(1, 4):
                        k.stt(T[0][:], ext[:, jj:TB + jj], cc[:, wcol + jj:wcol + jj + 1], T[0][:], ALU.mult, ALU.add)
                    k.cp('pool', ext[:, 0:3], ext[:, TB:TB + 3])
                    k.act(T[1][:], T[0][:], AF.Exp, scale=-1.0)
                    k.ts('dve', T[1][:], T[1][:], 1.0, None, op0=ALU.add)
                    k.recip(T[1][:], T[1][:])
                    k.stt(dst, T[0][:], scale_, T[1][:], ALU.mult, ALU.mult)

                conv_silu(4, cqx, CC_CWQ, CC_CBQ, 1.0, qcT[:])
                conv_silu(5, ckx, CC_CWK, CC_CBK, 128 ** -0.5, kcT[:])

                _lvl(9)
                p = fm_proj(6)
                k.tt('dve', T[0][:], p[:], RX[:], ALU.mult)
                k.act(Bt[0][:], T[0][:], AF.Square)
                pss = gbank()
                k.mm(pss[:], cmB(CM_ONES), Bt[0][:])
                k.rsqrt(T[1][:], pss[:], 1.0 / 128, epsc, T[2][:])
                k.tt('dve', qxn[:], T[0][:], T[1][:], ALU.mult)
                for mt in range(2):
                    pS = gbank()
                    k.mm(pS[:], kmT[:, mt * 128:(mt + 1) * 128], qxn[:])
                    k.act(PT[:, mt, :], pS[:], AF.Exp)

                _lvl(10)
                for tt_ in range(4):
                    tcs = slice(tt_ * 128, (tt_ + 1) * 128)
                    pA, pB, pC = (PSB[2], PSB[3], PSB[4]) if tt_ % 2 == 0 else (PSB[5], PSB[6], PSB[7])
                    for c in range(8):
                        k.mm(pA[:], xb[:, c, tcs], wtm[:, c, 0:512], start=(c == 0), stop=(c == 7))
                    for c in range(8):
                        k.mm(pB[:], xb[:, c, tcs], wtm[:, c, 512:1024], start=(c == 0), stop=(c == 7))
                    for c in range(8):
                        k.mm(pC[:, 0:2], xb[:, c, tcs], wtm[:, c, 1024:1026], start=(c == 0), stop=(c == 7))
                    rc = rxc[:, tt_:tt_ + 1]
                    nrc = rxc[:, 4 + tt_:5 + tt_]
                    k.act(T[0][:], pA[:], AF.Exp, scale=nrc)
                    k.ts('dve', T[0][:], T[0][:], 1.0, None, op0=ALU.add)
                    k.recip(T[0][:], T[0][:])
                    k.stt(gates[:, tt_, 0:4, :].rearrange("p g d -> p (g d)"), pA[:], rc, T[0][:], ALU.mult, ALU.mult)
                    k.act(T[1][:, 0:128], pB[:, 0:128], AF.Exp, scale=nrc)
                    k.ts('dve', T[1][:, 0:128], T[1][:, 0:128], 1.0, None, op0=ALU.add)
                    k.recip(gates[:, tt_, 4, :], T[1][:, 0:128])
                    k.act(av[:, tt_, :], pB[:, 128:256], AF.Copy, scale=rc)
                    k.act(vA[:, bi * 4 + tt_, 0:128], pB[:, 256:384], AF.Copy, scale=rc)
                    k.act(cvA[:, tt_, 0:128], pB[:, 384:512], AF.Copy, scale=rc)
                    g_ = gcol[:, tt_, :]
                    k.stt(g_[:, 0:1], pC[:, 0:1], rc, col(CC_GBI), ALU.mult, ALU.add)
                    k.stt(g_[:, 2:3], pC[:, 1:2], rc, col(CC_GBF), ALU.mult, ALU.add)

                gv = lambda i, n=1: gcol[:, :, i:i + n]
                k.act(gv(3), gv(2), AF.Exp, scale=-1.0)
                k.act(gv(4), gv(3), AF.Ln, bias=col(CC_ONE))
                k.ts('dve', gv(1), gv(4), -1.0)
                k.ts('dve', gv(5), gv(1), col(CC_IND0))
                k.ts('dve', gv(6), gv(1), col(CC_IND1))
                pc2 = gbank()
                k.mm(pc2[:, 0:4], cmF(CM_TRI64), gcol[:, :, 1])
                k.mm(pc2[:, 4:8], cmF(CM_BLK64), gcol[:, :, 1], skip=True)
                k.mm(pc2[:, 8:16], cmF(CM_ONES), gcol[:, :, 5:7], skip=True)
                k.cp('dve', gcol[:, :, 7], pc2[:, 0:4])
                k.cp('dve', gcol[:, :, 8], pc2[:, 4:8])
                k.cp('dve', gcol[:, :, 9:11], pc2[:, 8:16].rearrange("p (t c) -> p t c", c=2))
                k.tt('dve', gv(11), gv(0), gv(7), ALU.subtract)
                k.tt('dve', gv(12), gv(11), gv(8), ALU.add)
                k.act(gv(13), gv(11), AF.Exp)
                k.act(gv(14), gv(12), AF.Exp)
                k.act(gv(15), gv(7), AF.Exp)
                k.act(sm[:, 48:56].rearrange("p (t c) -> p t c", c=2), gcol[:, :, 9:11], AF.Exp)

                _lvl(11)
                for tt_ in range(4):
                    tcs = slice(tt_ * 128, (tt_ + 1) * 128)
                    po = gbank()
                    for mt in range(2):
                        k.mm(po[:, 0:129], PT[:, mt, tcs], vmA[:, mt, :], start=(mt == 0), stop=(mt == 1))
                    k.recip(gcol[:, tt_, 5:6], po[:, 128:129])
                    k.stt(yTM[:, tt_, 3, :], po[:, 0:128], gcol[:, tt_, 5:6], gates[:, tt_, 3, :], ALU.mult, ALU.mult)

                _lvl(12)
                for tt_ in range(4):
                    tcs = slice(tt_ * 128, (tt_ + 1) * 128)
                    ptr = gbank()
                    ptb = ptr[:, 0:64].bitcast(BF16)
                    k.tr(ptb, ktT[:, tcs], cmB(CM_IDENT))
                    k.cp('dve', kTM[:, tt_, :], ptb)
                _lvl(12.1)
                for tt_ in range(4):
                    pd = gbank()
                    for hh in range(2):
                        c8 = tt_ * 2 + hh
                        rs = slice(hh * 64, hh * 64 + 64)
                        k.mm(pd[:, hh * 128:(hh + 1) * 128], kTM[rs, tt_, :], av[rs, tt_, :], skip=(hh == 1))
                        k.ts('dve', dS[:, c8, :], pd[:, hh * 128:(hh + 1) * 128], CFAC[:, c8:c8 + 1])
                _lvl(12.2)
                for c8 in range(8):
                    k.ts('dve', Sb_h[:, c8, :], S_h[:], EREF[:, c8:c8 + 1])
                    k.stt(S_h[:], S_h[:], ELAST[:, c8:c8 + 1], dS[:, c8, :], ALU.mult, ALU.add)
                _lvl(12.3)
                for tt_ in range(4):
                    tcs = slice(tt_ * 128, (tt_ + 1) * 128)
                    pa = gbank()
                    for hh in range(2):
                        cs_ = slice(tt_ * 128 + hh * 64, tt_ * 128 + hh * 64 + 64)
                        rs = slice(hh * 64, hh * 64 + 64)
                        k.mm(pa[rs, hh * 64:hh * 64 + 64], ktT[:, cs_], qtT[:, cs_], skip=(hh == 1))
                        k.stt(AT[rs, 0, hh * 64:hh * 64 + 64], pa[rs, hh * 64:hh * 64 + 64], 3.0e38,
                              cmB(CM_TRI64)[rs, hh * 64:hh * 64 + 64], ALU.min, ALU.mult)
                    _lvl(12.4)
                    po = gbank()
                    for hh in range(2):
                        cs_ = slice(tt_ * 128 + hh * 64, tt_ * 128 + hh * 64 + 64)
                        rs = slice(hh * 64, hh * 64 + 64)
                        k.mm(po[rs, 0:128], AT[rs, 0, hh * 64:hh * 64 + 64], av[rs, tt_, :], start=True, stop=False, skip=True)
                        k.mm(po[rs, 0:128], qtT[:, cs_], Sb_h[:, tt_ * 2 + hh, :], start=False, stop=True, skip=True)
                    g_ = gcol[:, tt_, :]
                    k.act(T[0][:, 0:128], po[:, 0:128], AF.Square, accum_out=g_[:, 5:6])
                    k.rsqrt(g_[:, 6:7], g_[:, 5:6], 1.0 / 128, epsc, g_[:, 5:6])
                    k.stt(T[0][:, 0:128], po[:, 0:128], g_[:, 6:7], crow[:, CR_HG * 128:(CR_HG + 1) * 128], ALU.mult, ALU.mult)
                    k.tt('dve', yTM[:, tt_, 0, :], T[0][:, 0:128], gates[:, tt_, 0, :], ALU.mult)

                _lvl(13)
                for tt_ in range(4):
                    tcs = slice(tt_ * 128, (tt_ + 1) * 128)
                    ptr = gbank()
                    ptb = ptr[:, 0:64].bitcast(BF16)
                    k.tr(ptb, kcT[:, tcs], cmB(CM_IDENT))
                    k.ts('dve', kTM2[:, tt_, :], ptb, gcol[:, tt_, 14:15])
                for tt_ in range(4):
                    for hh in range(2):
                        c8 = tt_ * 2 + hh
                        rs = slice(hh * 64, hh * 64 + 64)
                        pd = gbank()
                        k.mm(pd[:, 0:129], kTM2[rs, tt_, :], cvA[rs, tt_, :])
                        k.cp('dve', dC[:, c8, :], pd[:, 0:129])
                for c8 in range(8):
                    k.cp('dve', Cb_m[:, c8, :], C_m[:])
                    k.stt(C_m[:], C_m[:], sm[:, 48 + c8:49 + c8], dC[:, c8, :], ALU.mult, ALU.add)
                for tt_ in range(4):
                    tcs = slice(tt_ * 128, (tt_ + 1) * 128)
                    g_ = gcol[:, tt_, :]
                    pa = gbank()
                    k.mm(pa[:, 0:128], kcT[:, tcs], qcT[:, tcs])
                    k.stt(AT[:, 1, :], pa[:, 0:128], g_[:, 13:14], cmB(CM_TRI64), ALU.mult, ALU.mult)
                    po = gbank()
                    for hh in range(2):
                        cs_ = slice(tt_ * 128 + hh * 64, tt_ * 128 + hh * 64 + 64)
                        rs = slice(hh * 64, hh * 64 + 64)
                        k.mm(po[rs, 0:129], AT[rs, 1, hh * 64:hh * 64 + 64], cvA[rs, tt_, :], start=True, stop=False, skip=True)
                        k.mm(po[rs, 0:129], qcT[:, cs_], Cb_m[:, tt_ * 2 + hh, :], start=False, stop=True, skip=True)
                    k.act(g_[:, 5:6], po[:, 128:129], AF.Abs, scale=g_[:, 15:16])
                    k.ts('dve', g_[:, 5:6], g_[:, 5:6], 1.0, None, op0=ALU.max)
                    k.recip(g_[:, 5:6], g_[:, 5:6])
                    k.tt('dve', g_[:, 5:6], g_[:, 5:6], g_[:, 15:16], ALU.mult)
                    k.stt(T[0][:, 0:128], po[:, 0:128], g_[:, 5:6], gates[:, tt_, 4, :], ALU.mult, ALU.mult)
                    k.act(T[1][:, 0:128], T[0][:, 0:128], AF.Square, accum_out=g_[:, 6:7])
                    k.rsqrt(g_[:, 5:6], g_[:, 6:7], 1.0 / 128, epsc, g_[:, 6:7])
                    k.stt(T[0][:, 0:128], T[0][:, 0:128], g_[:, 5:6], crow[:, CR_MG * 128:(CR_MG + 1) * 128], ALU.mult, ALU.mult)
                    k.tt('dve', yTM[:, tt_, 2, :], T[0][:, 0:128], gates[:, tt_, 2, :], ALU.mult)

                _lvl(14)
                if bi + 1 < NBLK:
                    k.dma(xt2, xv[:, 4:8, t0 + TB:t0 + 2 * TB], "x1")
                for ob in range(3):
                    k.mm(PS_O[ob][:], cmB(CM_ZERO), zrow[:], start=True, stop=False, sig=True, skip=True)
                oslot = {}
                for mp in range(2):
                    for qq in range(4):
                        idx = mp * 4 + qq
                        oslot[(mp, qq)] = (PS_O[idx // 3], (idx % 3) * 129)
                nkt = bi * 4 + 4
                SBK = [[PS_S[0], PSB[0]], [PS_S[1], PSB[1]]]

                def s_pair(j_):
                    q0_ = max(j_ - bi * 4, 0) * 128
                    for mp_ in range(2):
                        rs_ = slice(mp_ * 64, mp_ * 64 + 64)
                        k.mm(SBK[mp_][j_ % 2][:, q0_:TB], kT[rs_, j_ * 128:(j_ + 1) * 128], qhT[rs_, q0_:TB])

                s_pair(0)
                if nkt > 1:
                    s_pair(1)
                for j in range(nkt):
                    jj = j - bi * 4
                    q0 = max(jj, 0) * 128
                    for mp in range(2):
                        pt_ = PT[:, 2 * (j % 2) + mp, :]
                        k.act(pt_[:, q0:TB], SBK[mp][j % 2][:, q0:TB], AF.Exp)
                        if jj >= 0:
                            k.tt('pool', pt_[:, q0:q0 + 128], pt_[:, q0:q0 + 128], cmB(CM_TRI128), ALU.mult)
                    for mp in range(2):
                        pt_ = PT[:, 2 * (j % 2) + mp, :]
                        for qq in range(max(jj, 0), 4):
                            ob, oc = oslot[(mp, qq)]
                            last = (j == min(nkt - 1, bi * 4 + qq))
                            k.mm(ob[:, oc:oc + 129], pt_[:, qq * 128:(qq + 1) * 128], vA[:, j, :],
                                 start=False, stop=last, sig=last, skip=True)
                    if j + 2 < nkt:
                        s_pair(j + 2)
                for tt_ in range(4):
                    g_ = gcol[:, tt_, :]
                    ob1, oc1 = oslot[(0, tt_)]
                    ob2, oc2 = oslot[(1, tt_)]
                    k.recip(g_[:, 5:6], ob1[:, oc1 + 128:oc1 + 129])
                    k.recip(g_[:, 6:7], ob2[:, oc2 + 128:oc2 + 129])
                    k.tt('dve', g_[:, 6:7], g_[:, 6:7], NLAM, ALU.mult)
                    k.ts('dve', T[0][:, 0:128], ob1[:, oc1:oc1 + 128], g_[:, 5:6])
                    k.stt(T[0][:, 0:128], ob2[:, oc2:oc2 + 128], g_[:, 6:7], T[0][:, 0:128], ALU.mult, ALU.add)
                    k.act(T[1][:, 0:128], T[0][:, 0:128], AF.Square, accum_out=g_[:, 5:6])
                    k.rsqrt(g_[:, 6:7], g_[:, 5:6], 1.0 / 128, epsc, g_[:, 5:6])
                    k.ts('dve', g_[:, 6:7], g_[:, 6:7], 1.0 - lam_init)
                    k.stt(T[0][:, 0:128], T[0][:, 0:128], g_[:, 6:7], crow[:, CR_SUB * 128:(CR_SUB + 1) * 128], ALU.mult, ALU.mult)
                    k.tt('dve', yTM[:, tt_, 1, :], T[0][:, 0:128], gates[:, tt_, 1, :], ALU.mult)

                _lvl(15)
                for tt_ in range(4):
                    pty = gbank()
                    ptb = pty[:, 0:256].bitcast(BF16)
                    for g4 in range(4):
                        k.tr(ptb[:, g4 * 128:(g4 + 1) * 128], yTM[:, tt_, g4, :], cmB(CM_IDENT))
                    k.cp('dve', yT[:, :, tt_ * 128:(tt_ + 1) * 128], ptb.rearrange("p (g t) -> p g t", g=4))
                if not fused:
                    k.dma(yv[:, :, tsl], yT[:], "yu", eng='pool')
                else:
                    ch, half = bi // 2, bi % 2
                    for r in range(4):
                        k.act(yTr[:, r % 2, :, :], yT[:], AF.Copy, scale=col(CC_RANK + r))
                        k.dma(D['yin'][ch, r].rearrange("(g p) t -> p g t", p=128)[:, :, half * TB:(half + 1) * TB],
                              yTr[:, r % 2, :, :], "ys%d" % r, eng='pool')
                    if half == 1 or bi == NBLK - 1:
                        k.cc("AllReduce", ALU.add, [[0, 1, 2, 3], [4, 5, 6, 7]],
                             D['yin'][ch].rearrange("r p t -> (r p) t"), D['yout'][ch], "cc")

        except _Stop:
            pass
        k.s.barrier()


def layer_dram(nc, S, sfx, shared=None):
    dr = lambda name, shape, dt=F32: nc.dram_tensor(name, shape, dt, kind="ExternalInput").ap()
    D = {}
    if shared is None:
        D['pos'] = dr("pos", [1, S], I32)
        D['memT'] = dr("memT", [D_MODEL, N_MEM])
        D['cmat'] = dr("cmat", [128, NCM, 128])
        D['rmask'] = dr("rmask", [128, TB])
        D['cs'] = nc.dram_tensor("cs_scr", [2, 128, S], F32, kind="Internal").ap()
    else:
        for n_ in ('pos', 'memT', 'cmat', 'rmask', 'cs'):
            D[n_] = shared[n_]
    D['w_fm'] = dr("w_fm" + sfx, [D_MODEL, NFM * 128])
    D['w_tm'] = dr("w_tm" + sfx, [D_MODEL, NTM])
    D['w_kv'] = dr("w_kv" + sfx, [D_MODEL, 256])
    D['ccol'] = dr("ccol" + sfx, [128, NCC])
    D['crow'] = dr("crow" + sfx, [1, 3 * 128])
    D['lam'] = dr("lam" + sfx, [1, 256])
    return D


def build_layer(S, layer_idx):
    nc = bass.Bass("TRN2", target_bir_lowering=False)
    D = layer_dram(nc, S, "")
    D['xT'] = nc.dram_tensor("xT", [D_MODEL, S], F32, kind="ExternalInput").ap()
    D['yT'] = nc.dram_tensor("yT", [4 * 128, S], BF16, kind="ExternalOutput").ap()
    with ExitStack() as es:
        es.enter_context(nc.allow_low_precision(reason="bf16 matmul operands"))
        k = K(nc, es)
        PSB = [k.ps(f"pb{i}", [128, 512]) for i in range(8)]
        emit_layer(nc, k, PSB, S, layer_idx, "", D, do_rope=True, fused=False)
        k.s.finish()
    return nc


def emit_outproj(nc, k, PSB, S, sfx, D, full):
    NBLK = S // TB
    with ExitStack() as es:
        k.es = es
        sb = lambda name, shape, dt: k.sb(name + sfx, shape, dt)
        stage = sb("ostage", [128, D_MODEL], F32)
        wo = sb("wo", [128, 16, 256], BF16)
        yt = sb("yt", [128, 16, TB], BF16)
        xo = sb("xo", [128, 2, TB], F32)
        oo = sb("oo", [128, 2, TB], F32)
        wv = D['w_own'].rearrange("(c p) n -> p c n", p=128)
        for c in range(16):
            k.dma(stage[:, 0:256], wv[:, c, :], "st")
            k.cp('dve', wo[:, c, :], stage[:, 0:256])
        if full:
            w = sb("wfull", [128, 16, D_MODEL], BF16)
            xt = sb("oxt", [128, 8, TB], F32)
            ot = sb("oot", [128, 8, TB], F32)
            wv = D['w_out'].rearrange("(c p) n -> p c n", p=128)
            for c in range(16):
                k.dma(stage[:], wv[:, c, :], "st")
                k.cp('dve', w[:, c, :], stage[:])
            xv = D['x_prev'].rearrange("(c p) t -> p c t", p=128)
            x1v = D['x1'].rearrange("(c p) t -> p c t", p=128)
        xov = D['x_own_prev'].rearrange("(c p) t -> p c t", p=128)
        outv = (D['x1own'] if full else D['out']).rearrange("(c p) t -> p c t", p=128)
        pi = [0]
        for bi in range(NBLK):
            tsl = slice(bi * TB, (bi + 1) * TB)
            ch, half = bi // 2, bi % 2
            k.dma(yt[:], D['yout'][ch].rearrange("(c p) t -> p c t", p=128)[:, :, half * TB:(half + 1) * TB], "x0")
            k.dma(xo[:], xov[:, :, tsl], "x1")
            if full:
                k.dma(xt[:], xv[:, :, tsl], "x2")
                for oc in range(8):
                    p = PSB[pi[0] % 8]
                    pi[0] += 1
                    for c in range(16):
                        k.mm(p[:], w[:, c, oc * 128:(oc + 1) * 128], yt[:, c, :], start=(c == 0), stop=(c == 15),
                             sig=(c == 15))
                    k.tt('dve', ot[:, oc, :], p[:], xt[:, oc, :], ALU.add)
                k.dma(x1v[:, :, tsl], ot[:], "y0")
            for oc in range(2):
                p = PSB[pi[0] % 8]
                pi[0] += 1
                for c in range(16):
                    k.mm(p[:], wo[:, c, oc * 128:(oc + 1) * 128], yt[:, c, :], start=(c == 0), stop=(c == 15),
                         sig=(c == 15))
                k.tt('dve', oo[:, oc, :], p[:], xo[:, oc, :], ALU.add)
            k.dma(outv[:, :, tsl], oo[:], "y1")
        k.s.barrier()


def build_fused(S):
    nc = bass.Bass("TRN2", target_bir_lowering=False)
    NCH = max(S // 1024, 1)
    D0 = layer_dram(nc, S, "_0")
    D1 = layer_dram(nc, S, "_1", shared=D0)
    x0 = nc.dram_tensor("xT", [D_MODEL, S], F32, kind="ExternalInput").ap()
    x0own = nc.dram_tensor("x0own", [256, S], F32, kind="ExternalInput").ap()
    w_out0 = nc.dram_tensor("w_out0", [2048, D_MODEL], F32, kind="ExternalInput").ap()
    w_own0 = nc.dram_tensor("w_own0", [2048, 256], F32, kind="ExternalInput").ap()
    w_own1 = nc.dram_tensor("w_own1", [2048, 256], F32, kind="ExternalInput").ap()
    out = nc.dram_tensor("out", [256, S], F32, kind="ExternalOutput").ap()
    yin = nc.dram_tensor("yin", [NCH, 4, 512, min(S, 1024)], BF16).ap()
    yout = nc.dram_tensor("yout", [NCH, 2048, min(S, 1024)], BF16).ap()
    x1 = nc.dram_tensor("x1", [D_MODEL, S], F32).ap()
    x1own = nc.dram_tensor("x1own", [256, S], F32).ap()
    for D in (D0, D1):
        D['yin'], D['yout'] = yin, yout
    D0['xT'] = x0
    D1['xT'] = x1
    with ExitStack() as es:
        es.enter_context(nc.allow_low_precision(reason="bf16 matmul operands"))
        k = K(nc, es)
        PSB = [k.ps(f"pb{i}", [128, 512]) for i in range(8)]
        emit_layer(nc, k, PSB, S, 0, "_a", D0, do_rope=True, fused=True)
        emit_outproj(nc, k, PSB, S, "_p", {'w_own': w_own0, 'w_out': w_out0, 'x_prev': x0, 'x1': x1,
                                           'x_own_prev': x0own, 'x1own': x1own, 'yout': yout}, full=True)
        emit_layer(nc, k, PSB, S, 1, "_b", D1, do_rope=False, fused=True)
        emit_outproj(nc, k, PSB, S, "_q", {'w_own': w_own1, 'x_own_prev': x1own, 'out': out, 'yout': yout}, full=False)
        k.s.finish()
    return nc


def head_cols(h):
    G = GROUP
    base = np.arange(128)
    off = {}
    names = ['a_q', 'a_f', 'a_i', 'a_g', 'b_q', 'b_k', 'b_v', 'b_g', 'c_q', 'c_k', 'c_v', 'c_o']
    o = 0
    for n in names:
        off[n] = o
        o += G
    off['c_i'] = o
    o += 4
    off['c_f'] = o
    o += 4
    off['c_g'] = o
    o += G
    off['x_q'] = o
    o += G
    off['x_g'] = o
    o += G
    hc = lambda n: off[n] + h * 128 + base
    fm = np.concatenate([hc('a_q'), hc('a_f'), hc('b_q'), hc('b_k'), hc('c_q'), hc('c_k'), hc('x_q')])
    tm = np.concatenate([hc('a_g'), hc('b_g'), hc('c_g'), hc('x_g'), hc('c_o'), hc('a_i'), hc('b_v'), hc('c_v'),
                         np.array([off['c_i'] + h, off['c_f'] + h])])
    return fm, tm


def layer_inputs(l, b, h, xT_b, inp, cmat, rmask):
    fm, tm = head_cols(h)
    w_in = inp['w_in'][l]
    cc = np.zeros((128, NCC), np.float32)
    cc[:, CC_NG:CC_NG + 8] = inp['norm_g'][l].reshape(8, 128).T
    cc[:, CC_MNG:CC_MNG + 8] = inp['mem_norm_g'][l].reshape(8, 128).T
    invf = (ROPE_THETA ** (-np.arange(0, 16, 2, dtype=np.float32) / 16)).astype(np.float32)
    for base in (0, 64):
        cc[base:base + 8, CC_INVF] = invf
        cc[base + 8:base + 16, CC_INVF] = invf
        cc[base:base + 8, CC_SGN] = -1.0
        cc[base + 8:base + 16, CC_SGN] = 1.0
    cc[:, CC_GQ] = np.tile(inp['diff_qk_norm_g'][l, 0], 2)
    cc[:, CC_GK] = np.tile(inp['diff_qk_norm_g'][l, 1], 2)
    cw = inp['mlstm_conv_w'][l]
    cb = inp['mlstm_conv_b'][l]
    cc[:, CC_CWQ:CC_CWQ + 4] = cw[:, h * 128:(h + 1) * 128].T
    cc[:, CC_CBQ] = cb[h * 128:(h + 1) * 128]
    cc[:, CC_CWK:CC_CWK + 4] = cw[:, GROUP + h * 128:GROUP + (h + 1) * 128].T
    cc[:, CC_CBK] = cb[GROUP + h * 128:GROUP + (h + 1) * 128]
    cc[:, CC_GXQ] = inp['xattn_qk_norm_g'][l, 0]
    cc[:, CC_GXK] = inp['xattn_qk_norm_g'][l, 1]
    cc[:, CC_LBL:CC_LBL + 2] = inp['hgrn_lb_logits'][:, h * 128:(h + 1) * 128].T
    cc[0:64, CC_IND0] = 1.0
    cc[64:128, CC_IND1] = 1.0
    cc[:, CC_GBI] = inp['mlstm_gate_b'][l, h]
    cc[:, CC_GBF] = inp['mlstm_gate_b'][l, 4 + h]
    cc[:, CC_EPS] = EPS
    cc[:, CC_ONE] = 1.0
    cc[:, CC_RANK + h] = 1.0
    crow = np.concatenate([inp['hgrn_norm_g'][l], inp['diff_subln_g'][l], inp['mlstm_norm_g'][l]])[None, :]
    wkv = inp['w_mem_kv'][l]
    wkv_h = np.concatenate([wkv[:, h * 128:(h + 1) * 128], wkv[:, GROUP + h * 128:GROUP + (h + 1) * 128]], axis=1)
    return {
        "xT": xT_b,
        "pos": np.ascontiguousarray(inp['positions'][b][None, :xT_b.shape[1]]).astype(np.int32),
        "w_fm": np.ascontiguousarray(w_in[:, fm]),
        "w_tm": np.ascontiguousarray(w_in[:, tm]),
        "memT": np.ascontiguousarray(inp['mem'][b].T),
        "w_kv": np.ascontiguousarray(wkv_h),
        "cmat": cmat,
        "ccol": cc,
        "crow": np.ascontiguousarray(crow.astype(np.float32)),
        "lam": np.ascontiguousarray(inp['diff_lambda'][l].reshape(1, 256)),
        "rmask": rmask,
    }


_CACHE = {}


def run_layer(l, xT, inp, S):
    key = ('L', l, S)
    if key not in _CACHE:
        _CACHE[key] = build_layer(S, l)
    nc = _CACHE[key]
    cmat = const_mats()
    rmask = np.ones((128, TB), np.float32)
    rmask[:, ::64] = 0.0
    in_maps = []
    for c in range(8):
        b, h = c // 4, c % 4
        in_maps.append(layer_inputs(l, b, h, np.ascontiguousarray(xT[b]), inp, cmat, rmask))
    res = run_bass_kernel_spmd(nc, in_maps, core_ids=list(range(8)))
    ys = [np.asarray(r["yT"]) for r in res.results]
    return np.stack([np.concatenate(ys[b * 4:(b + 1) * 4], axis=0) for b in range(BATCH)])


def perm_w_out(w):
    return np.ascontiguousarray(w.reshape(4, 4, 128, D_MODEL).transpose(1, 0, 2, 3).reshape(2048, D_MODEL))


def run_fused(xT, inp, S):
    key = ('F', S)
    if key not in _CACHE:
        _CACHE[key] = build_fused(S)
    nc = _CACHE[key]
    cmat = const_mats()
    rmask = np.ones((128, TB), np.float32)
    rmask[:, ::64] = 0.0
    wp = [perm_w_out(inp['w_out'][l]) for l in range(DEPTH)]
    in_maps = []
    for c in range(8):
        b, h = c // 4, c % 4
        xb_ = np.ascontiguousarray(xT[b])
        m = {}
        for l in range(DEPTH):
            li = layer_inputs(l, b, h, xb_, inp, cmat, rmask)
            for n_ in ('w_fm', 'w_tm', 'w_kv', 'ccol', 'crow', 'lam'):
                m[n_ + "_%d" % l] = li[n_]
            if l == 0:
                for n_ in ('xT', 'pos', 'memT', 'cmat', 'rmask'):
                    m[n_] = li[n_]
        fs = slice(h * 256, (h + 1) * 256)
        m['x0own'] = np.ascontiguousarray(xb_[fs])
        m['w_out0'] = wp[0]
        m['w_own0'] = np.ascontiguousarray(wp[0][:, fs])
        m['w_own1'] = np.ascontiguousarray(wp[1][:, fs])
        in_maps.append(m)
    res = run_bass_kernel_spmd(nc, in_maps, core_ids=list(range(8)))
    outs = [np.asarray(r["out"]) for r in res.results]
    return np.stack([np.concatenate(outs[b * 4:(b + 1) * 4], axis=0) for b in range(BATCH)])


def kernel(**inputs):
    inp = {k_: np.asarray(v) for k_, v in inputs.items()}
    x = inp['x']
    S = x.shape[1]
    xT = np.ascontiguousarray(x.transpose(0, 2, 1))
    oT = run_fused(xT, inp, S)
    return np.ascontiguousarray(oT.transpose(0, 2, 1)).astype(np.float32)
```
